# Optimizing a Trainium2 kernel written in Bass

```python
import math
import jax, jax.numpy as jnp
from jax import lax
import numpy as np

D_MODEL = 1024
BATCH = 8
SEQ = 2048
DEPTH = 2

GRID_W = 64
CTX_LEN = 256
ROPE_THETA = 10000.0
EPS = 1e-6
Q_BLOCK = 128

MLA_HEADS = 8
MLA_Q_RANK = 256
MLA_KV_RANK = 128
MLA_NOPE = 64
MLA_ROPE = 32
MLA_V = 64
MLA_QK = MLA_NOPE + MLA_ROPE

NA_HEADS = 8
NA_HEAD_DIM = 64
NA_ROWS_MAX = 8
NA_COLS = 16
NA_DIM = NA_HEADS * NA_HEAD_DIM

GQA_HEADS = 8
GQA_KV_HEADS = 2
GQA_HEAD_DIM = 64
GQA_WINDOW = 128

DIFF_HEADS = 4
DIFF_HEAD_DIM = 64

FFN_HIDDEN = -(-8 * D_MODEL // (3 * 256)) * 256

EVEN_IN = [MLA_Q_RANK, MLA_KV_RANK, MLA_ROPE, NA_DIM, NA_DIM, NA_DIM]
EVEN_MIX = MLA_HEADS * MLA_V + NA_DIM
ODD_IN = [GQA_HEADS * GQA_HEAD_DIM, GQA_KV_HEADS * GQA_HEAD_DIM, GQA_KV_HEADS * GQA_HEAD_DIM,
          DIFF_HEADS * 2 * DIFF_HEAD_DIM, DIFF_HEADS * 2 * DIFF_HEAD_DIM, DIFF_HEADS * 2 * DIFF_HEAD_DIM]
ODD_MIX = GQA_HEADS * GQA_HEAD_DIM + DIFF_HEADS * 2 * DIFF_HEAD_DIM

kernel_name = "hybrid_diffusion_mla_na_swa_diff"


def rms_norm(x, g):
    xf = x.astype(jnp.float32)
    y = xf * lax.rsqrt(jnp.mean(xf * xf, axis=-1, keepdims=True) + EPS)
    return (y * g.astype(jnp.float32)).astype(x.dtype)


def split_cols(z, sizes):
    offs = np.cumsum(sizes)[:-1].tolist()
    return jnp.split(z, offs, axis=-1)


def rope_1d(x, pos):
    d = x.shape[-1]
    freqs = ROPE_THETA ** (-jnp.arange(0, d, 2, dtype=jnp.float32) / d)
    ang = pos.astype(jnp.float32)[:, None] * freqs[None, :]
    bshape = (pos.shape[0],) + (1,) * (x.ndim - 3) + (d // 2,)
    cos = jnp.cos(ang).reshape(bshape).astype(x.dtype)
    sin = jnp.sin(ang).reshape(bshape).astype(x.dtype)
    x1, x2 = x[..., : d // 2], x[..., d // 2:]
    return jnp.concatenate([x1 * cos - x2 * sin, x2 * cos + x1 * sin], axis=-1)


def rope_2d(x, rows, cols):
    h = x.shape[-1] // 2
    return jnp.concatenate([rope_1d(x[..., :h], rows), rope_1d(x[..., h:], cols)], axis=-1)


def to_blocks(x):
    b, s = x.shape[:2]
    return jnp.moveaxis(x.reshape((b, s // Q_BLOCK, Q_BLOCK) + x.shape[2:]), 1, 0)


def from_blocks(y):
    nb, b, qb = y.shape[:3]
    return jnp.moveaxis(y, 0, 1).reshape((b, nb * qb) + y.shape[3:])


def dense_attend(q, k, v, scale):
    s = jnp.einsum('bqhd,bkhd->bhqk', q, k).astype(jnp.float32) * scale
    p = jax.nn.softmax(s, axis=-1).astype(v.dtype)
    return jnp.einsum('bhqk,bkhd->bqhd', p, v)


def ada_modulation(cond, w, b):
    return jnp.split(jax.nn.silu(cond) @ w + b, 6, axis=-1)


def modulate(x, g, shift, scale):
    return rms_norm(x, g) * (1 + scale) + shift


def swiglu(h, w_gate, w_up, w_down):
    return (jax.nn.silu(h @ w_gate) * (h @ w_up)) @ w_down


def mla_queries(cq, qa_g, w_uq, qn_g, rows, cols):
    b, t = cq.shape[:2]
    q = rms_norm((rms_norm(cq, qa_g) @ w_uq).reshape(b, t, MLA_HEADS, MLA_QK), qn_g)
    if rows is not None:
        q = jnp.concatenate([q[..., :MLA_NOPE], rope_2d(q[..., MLA_NOPE:], rows, cols)], axis=-1)
    return q


def mla_keys_values(ckv, kr, kva_g, w_ukv, kn_g, rows, cols):
    b, t = ckv.shape[:2]
    kv = (rms_norm(ckv, kva_g) @ w_ukv).reshape(b, t, MLA_HEADS, MLA_NOPE + MLA_V)
    k_rope = jnp.broadcast_to(kr[:, :, None, :], (b, t, MLA_HEADS, MLA_ROPE))
    k = rms_norm(jnp.concatenate([kv[..., :MLA_NOPE], k_rope], axis=-1), kn_g)
    if rows is not None:
        k = jnp.concatenate([k[..., :MLA_NOPE], rope_2d(k[..., MLA_NOPE:], rows, cols)], axis=-1)
    return k, kv[..., MLA_NOPE:]


def neighbourhood_tables(rows_n, rpb):
    kr = min(NA_ROWS_MAX, rows_n)
    kc = NA_COLS
    r = jnp.arange(rows_n)
    col = jnp.arange(GRID_W)
    key_r = jnp.clip(r - kr // 2, 0, rows_n - kr)[:, None] + jnp.arange(kr)[None, :]
    key_c = jnp.clip(col - kc // 2, 0, GRID_W - kc)[:, None] + jnp.arange(kc)[None, :]
    idx = (key_r[:, None, :, None] * GRID_W + key_c[None, :, None, :]).reshape(rows_n, GRID_W, kr * kc)
    off_r = key_r - r[:, None] + (NA_ROWS_MAX - 1)
    off_c = key_c - col[:, None] + (NA_COLS - 1)
    bias = rpb[:, off_r[:, None, :, None], off_c[None, :, None, :]]
    bias = jnp.moveaxis(bias.reshape(NA_HEADS, rows_n, GRID_W, kr * kc), 0, 1)
    return idx, bias


def neighbourhood_attention(q, k, v, k_ctx, v_ctx, idx, bias):
    b, s, h, d = q.shape
    rows_n = s // GRID_W
    n_nb = idx.shape[-1]
    scale = d ** -0.5
    q_rows = jnp.moveaxis(q.reshape(b, rows_n, GRID_W, h, d), 1, 0)

    def row_block(args):
        qb, idx_b, bias_b = args
        kg = k[:, idx_b]
        vg = v[:, idx_b]
        s_nb = jnp.einsum('bqhd,bqkhd->bhqk', qb, kg).astype(jnp.float32) * scale + bias_b[None].astype(jnp.float32)
        s_ctx = jnp.einsum('bqhd,bkhd->bhqk', qb, k_ctx).astype(jnp.float32) * scale
        p = jax.nn.softmax(jnp.concatenate([s_nb, s_ctx], axis=-1), axis=-1).astype(v.dtype)
        return (jnp.einsum('bhqk,bqkhd->bqhd', p[..., :n_nb], vg)
                + jnp.einsum('bhqk,bkhd->bqhd', p[..., n_nb:], v_ctx))

    out = lax.map(row_block, (q_rows, idx, bias))
    return jnp.moveaxis(out, 0, 1).reshape(b, s, h, d)


def windowed_gqa_latent(q, k, v, k_ctx, v_ctx, sink):
    b, s, hq, d = q.shape
    hkv = k.shape[2]
    g = hq // hkv
    nb = s // Q_BLOCK
    band = 3 * Q_BLOCK
    n_ctx = k_ctx.shape[1]
    scale = d ** -0.5

    def bands(z):
        zp = jnp.pad(z, ((0, 0), (Q_BLOCK, Q_BLOCK), (0, 0), (0, 0))).reshape(b, nb + 2, Q_BLOCK, hkv, d)
        return jnp.concatenate([zp[:, :-2], zp[:, 1:-1], zp[:, 2:]], axis=2)

    kb, vb = bands(k), bands(v)
    qb = q.reshape(b, nb, Q_BLOCK, hkv, g, d)
    t = jnp.arange(s).reshape(nb, Q_BLOCK)
    s_pos = (jnp.arange(nb)[:, None] - 1) * Q_BLOCK + jnp.arange(band)[None, :]
    valid = ((s_pos[:, None, :] >= 0) & (s_pos[:, None, :] < s)
             & (jnp.abs(t[:, :, None] - s_pos[:, None, :]) <= GQA_WINDOW))
    s_band = jnp.einsum('bnqhgd,bnkhd->bhgnqk', qb, kb).astype(jnp.float32) * scale
    s_band = jnp.where(valid, s_band, -jnp.inf)
    s_ctx = jnp.einsum('bnqhgd,bkhd->bhgnqk', qb, k_ctx).astype(jnp.float32) * scale
    s_sink = jnp.broadcast_to(sink.astype(jnp.float32).reshape(1, hkv, g, 1, 1, 1), s_band.shape[:-1] + (1,))
    p = jax.nn.softmax(jnp.concatenate([s_band, s_ctx, s_sink], axis=-1), axis=-1).astype(v.dtype)
    o = (jnp.einsum('bhgnqk,bnkhd->bnqhgd', p[..., :band], vb)
         + jnp.einsum('bhgnqk,bkhd->bnqhgd', p[..., band:band + n_ctx], v_ctx))
    return o.reshape(b, s, hq, d)


def gqa_context(q, k, v, sink):
    b, n, hq, d = q.shape
    hkv = k.shape[2]
    g = hq // hkv
    qg = q.reshape(b, n, hkv, g, d)
    s = jnp.einsum('bqhgd,bkhd->bhgqk', qg, k).astype(jnp.float32) * (d ** -0.5)
    s_sink = jnp.broadcast_to(sink.astype(jnp.float32).reshape(1, hkv, g, 1, 1), s.shape[:-1] + (1,))
    p = jax.nn.softmax(jnp.concatenate([s, s_sink], axis=-1), axis=-1)[..., :-1].astype(v.dtype)
    return jnp.einsum('bhgqk,bkhd->bqhgd', p, v).reshape(b, n, hq, d)


def diff_attend(q, k, v, lam, scale):
    s = jnp.einsum('bqhcd,bkhcd->bhcqk', q, k).astype(jnp.float32) * scale
    p = jax.nn.softmax(s, axis=-1)
    w = (p[:, :, 0] - lam * p[:, :, 1]).astype(v.dtype)
    return jnp.einsum('bhqk,bkhe->bqhe', w, v)


def even_mixer(h, hc, p, rows, cols, need_ctx):
    b, s = h.shape[:2]
    cq, ckv, kr, qn, kn, vn = split_cols(h @ p['w_in'], EVEN_IN)
    ccq, cckv, ckr, cqn, ckn, cvn = split_cols(hc @ p['w_in'], EVEN_IN)

    scale_a = MLA_QK ** -0.5
    q_a = mla_queries(cq, p['mla_qa_g'], p['mla_w_uq'], p['mla_qn_g'], rows, cols)
    k_a, v_a = mla_keys_values(ckv, kr, p['mla_kva_g'], p['mla_w_ukv'], p['mla_kn_g'], rows, cols)
    kc_a, vc_a = mla_keys_values(cckv, ckr, p['mla_kva_g'], p['mla_w_ukv'], p['mla_kn_g'], None, None)
    k_all = jnp.concatenate([k_a, kc_a], axis=1)
    v_all = jnp.concatenate([v_a, vc_a], axis=1)
    o_a = from_blocks(lax.map(lambda qb: dense_attend(qb, k_all, v_all, scale_a), to_blocks(q_a)))

    def na_heads(z):
        return z.reshape(z.shape[:2] + (NA_HEADS, NA_HEAD_DIM))
    q_b = rms_norm(na_heads(qn), p['na_qn_g'])
    k_b = rms_norm(na_heads(kn), p['na_kn_g'])
    kc_b = rms_norm(na_heads(ckn), p['na_kn_g'])
    vc_b = na_heads(cvn)
    idx, bias = neighbourhood_tables(s // GRID_W, p['na_rpb'])
    o_b = neighbourhood_attention(q_b, k_b, na_heads(vn), kc_b, vc_b, idx, bias)

    o = jnp.concatenate([o_a.reshape(b, s, -1), o_b.reshape(b, s, -1)], axis=-1) @ p['w_out']
    oc = None
    if need_ctx:
        n = hc.shape[1]
        qc_a = mla_queries(ccq, p['mla_qa_g'], p['mla_w_uq'], p['mla_qn_g'], None, None)
        oc_a = dense_attend(qc_a, kc_a, vc_a, scale_a)
        qc_b = rms_norm(na_heads(cqn), p['na_qn_g'])
        oc_b = dense_attend(qc_b, kc_b, vc_b, NA_HEAD_DIM ** -0.5)
        oc = jnp.concatenate([oc_a.reshape(b, n, -1), oc_b.reshape(b, n, -1)], axis=-1) @ p['w_out']
    return o, oc


def odd_mixer(h, hc, p, layer_idx, rows, cols, need_ctx):
    b, s = h.shape[:2]
    n = hc.shape[1]
    qg, kg, vg, qd, kd, vd = split_cols(h @ p['w_in'], ODD_IN)
    cqg, ckg, cvg, cqd, ckd, cvd = split_cols(hc @ p['w_in'], ODD_IN)

    def gq(z, nh):
        return z.reshape(z.shape[:2] + (nh, GQA_HEAD_DIM))
    q_c = rope_2d(rms_norm(gq(qg, GQA_HEADS), p['gqa_qn_g']), rows, cols)
    k_c = rope_2d(rms_norm(gq(kg, GQA_KV_HEADS), p['gqa_kn_g']), rows, cols)
    kc_c = rms_norm(gq(ckg, GQA_KV_HEADS), p['gqa_kn_g'])
    vc_c = gq(cvg, GQA_KV_HEADS)
    o_c = windowed_gqa_latent(q_c, k_c, gq(vg, GQA_KV_HEADS), kc_c, vc_c, p['gqa_sink'])

    def dqk(z):
        return z.reshape(z.shape[:2] + (DIFF_HEADS, 2, DIFF_HEAD_DIM))
    def dv(z):
        return z.reshape(z.shape[:2] + (DIFF_HEADS, 2 * DIFF_HEAD_DIM))
    lam_init = 0.8 - 0.6 * math.exp(-0.3 * layer_idx)
    f32 = jnp.float32
    lam = (jnp.exp(jnp.sum(p['diff_lq1'].astype(f32) * p['diff_lk1'].astype(f32)))
           - jnp.exp(jnp.sum(p['diff_lq2'].astype(f32) * p['diff_lk2'].astype(f32))) + lam_init)
    scale_d = DIFF_HEAD_DIM ** -0.5
    q_d = rope_2d(rms_norm(dqk(qd), p['diff_qn_g']), rows, cols)
    k_d = rope_2d(rms_norm(dqk(kd), p['diff_kn_g']), rows, cols)
    kc_d = rms_norm(dqk(ckd), p['diff_kn_g'])
    vc_d = dv(cvd)
    k_all = jnp.concatenate([k_d, kc_d], axis=1)
    v_all = jnp.concatenate([dv(vd), vc_d], axis=1)
    o_d = from_blocks(lax.map(lambda qb: diff_attend(qb, k_all, v_all, lam, scale_d), to_blocks(q_d)))
    o_d = rms_norm(o_d, p['diff_subln_g']) * (1 - lam_init)

    o = jnp.concatenate([o_c.reshape(b, s, -1), o_d.reshape(b, s, -1)], axis=-1) @ p['w_out']
    oc = None
    if need_ctx:
        qc_c = rms_norm(gq(cqg, GQA_HEADS), p['gqa_qn_g'])
        oc_c = gqa_context(qc_c, kc_c, vc_c, p['gqa_sink'])
        qc_d = rms_norm(dqk(cqd), p['diff_qn_g'])
        oc_d = rms_norm(diff_attend(qc_d, kc_d, vc_d, lam, scale_d), p['diff_subln_g']) * (1 - lam_init)
        oc = jnp.concatenate([oc_c.reshape(b, n, -1), oc_d.reshape(b, n, -1)], axis=-1) @ p['w_out']
    return o, oc


def trunk_layer(x, xc, cond, cond_ctx, p, layer_idx, rows, cols, need_ctx):
    sh1, sc1, g1, sh2, sc2, g2 = ada_modulation(cond, p['ada_w'], p['ada_b'])
    csh1, csc1, cg1, csh2, csc2, cg2 = ada_modulation(cond_ctx, p['ada_w'], p['ada_b'])
    h = modulate(x, p['norm1_g'], sh1, sc1)
    hc = modulate(xc, p['norm1_g'], csh1, csc1)
    if layer_idx % 2 == 0:
        o, oc = even_mixer(h, hc, p, rows, cols, need_ctx)
    else:
        o, oc = odd_mixer(h, hc, p, layer_idx, rows, cols, need_ctx)
    x = x + g1 * o
    x = x + g2 * swiglu(modulate(x, p['norm2_g'], sh2, sc2), p['ffn_w_gate'], p['ffn_w_up'], p['ffn_w_down'])
    if need_ctx:
        xc = xc + cg1 * oc
        xc = xc + cg2 * swiglu(modulate(xc, p['norm2_g'], csh2, csc2), p['ffn_w_gate'], p['ffn_w_up'], p['ffn_w_down'])
    return x, xc


def setup_inputs(seed: int = 0) -> dict:
    key = jax.random.key(seed)
    keys = list(jax.random.split(key, 48))

    def nrm(shape, s):
        return jax.random.normal(keys.pop(), shape, jnp.float32) * s

    def gain(n):
        return 1.0 + nrm((n,), 0.02)

    d = D_MODEL
    inp = {}
    inp['x'] = nrm((BATCH, SEQ, d), 1.0)
    inp['c'] = nrm((BATCH, d), 1.0)
    inp['ctx'] = nrm((BATCH, CTX_LEN, d), 1.0)
    inp['c_ctx'] = nrm((d,), 1.0)
    inp['l0_ada_w'] = nrm((d, 6 * d), 0.5 * d ** -0.5)
    inp['l0_ada_b'] = nrm((6 * d,), 0.02)
    inp['l0_norm1_g'] = gain(d)
    inp['l0_norm2_g'] = gain(d)
    inp['l0_w_in'] = nrm((d, sum(EVEN_IN)), d ** -0.5)
    inp['l0_mla_qa_g'] = gain(MLA_Q_RANK)
    inp['l0_mla_w_uq'] = nrm((MLA_Q_RANK, MLA_HEADS * MLA_QK), MLA_Q_RANK ** -0.5)
    inp['l0_mla_kva_g'] = gain(MLA_KV_RANK)
    inp['l0_mla_w_ukv'] = nrm((MLA_KV_RANK, MLA_HEADS * (MLA_NOPE + MLA_V)), MLA_KV_RANK ** -0.5)
    inp['l0_mla_qn_g'] = gain(MLA_QK)
    inp['l0_mla_kn_g'] = gain(MLA_QK)
    inp['l0_na_qn_g'] = gain(NA_HEAD_DIM)
    inp['l0_na_kn_g'] = gain(NA_HEAD_DIM)
    inp['l0_na_rpb'] = nrm((NA_HEADS, 2 * NA_ROWS_MAX - 1, 2 * NA_COLS - 1), 0.1)
    inp['l0_w_out'] = nrm((EVEN_MIX, d), EVEN_MIX ** -0.5)
    inp['l0_ffn_w_gate'] = nrm((d, FFN_HIDDEN), d ** -0.5)
    inp['l0_ffn_w_up'] = nrm((d, FFN_HIDDEN), d ** -0.5)
    inp['l0_ffn_w_down'] = nrm((FFN_HIDDEN, d), FFN_HIDDEN ** -0.5)
    inp['l1_ada_w'] = nrm((d, 6 * d), 0.5 * d ** -0.5)
    inp['l1_ada_b'] = nrm((6 * d,), 0.02)
    inp['l1_norm1_g'] = gain(d)
    inp['l1_norm2_g'] = gain(d)
    inp['l1_w_in'] = nrm((d, sum(ODD_IN)), d ** -0.5)
    inp['l1_gqa_qn_g'] = gain(GQA_HEAD_DIM)
    inp['l1_gqa_kn_g'] = gain(GQA_HEAD_DIM)
    inp['l1_gqa_sink'] = nrm((GQA_HEADS,), 0.5)
    inp['l1_diff_qn_g'] = gain(DIFF_HEAD_DIM)
    inp['l1_diff_kn_g'] = gain(DIFF_HEAD_DIM)
    inp['l1_diff_lq1'] = nrm((DIFF_HEAD_DIM,), 0.1)
    inp['l1_diff_lk1'] = nrm((DIFF_HEAD_DIM,), 0.1)
    inp['l1_diff_lq2'] = nrm((DIFF_HEAD_DIM,), 0.1)
    inp['l1_diff_lk2'] = nrm((DIFF_HEAD_DIM,), 0.1)
    inp['l1_diff_subln_g'] = gain(2 * DIFF_HEAD_DIM)
    inp['l1_w_out'] = nrm((ODD_MIX, d), ODD_MIX ** -0.5)
    inp['l1_ffn_w_gate'] = nrm((d, FFN_HIDDEN), d ** -0.5)
    inp['l1_ffn_w_up'] = nrm((d, FFN_HIDDEN), d ** -0.5)
    inp['l1_ffn_w_down'] = nrm((FFN_HIDDEN, d), FFN_HIDDEN ** -0.5)
    return inp


def reference(x, c, ctx, c_ctx,
              l0_ada_w, l0_ada_b, l0_norm1_g, l0_norm2_g, l0_w_in,
              l0_mla_qa_g, l0_mla_w_uq, l0_mla_kva_g, l0_mla_w_ukv, l0_mla_qn_g, l0_mla_kn_g,
              l0_na_qn_g, l0_na_kn_g, l0_na_rpb, l0_w_out,
              l0_ffn_w_gate, l0_ffn_w_up, l0_ffn_w_down,
              l1_ada_w, l1_ada_b, l1_norm1_g, l1_norm2_g, l1_w_in,
              l1_gqa_qn_g, l1_gqa_kn_g, l1_gqa_sink,
              l1_diff_qn_g, l1_diff_kn_g, l1_diff_lq1, l1_diff_lk1, l1_diff_lq2, l1_diff_lk2, l1_diff_subln_g,
              l1_w_out, l1_ffn_w_gate, l1_ffn_w_up, l1_ffn_w_down):
    s = x.shape[1]
    t = jnp.arange(s)
    rows, cols = t // GRID_W, t % GRID_W
    layers = (
        dict(ada_w=l0_ada_w, ada_b=l0_ada_b, norm1_g=l0_norm1_g, norm2_g=l0_norm2_g, w_in=l0_w_in,
             mla_qa_g=l0_mla_qa_g, mla_w_uq=l0_mla_w_uq, mla_kva_g=l0_mla_kva_g, mla_w_ukv=l0_mla_w_ukv,
             mla_qn_g=l0_mla_qn_g, mla_kn_g=l0_mla_kn_g,
             na_qn_g=l0_na_qn_g, na_kn_g=l0_na_kn_g, na_rpb=l0_na_rpb, w_out=l0_w_out,
             ffn_w_gate=l0_ffn_w_gate, ffn_w_up=l0_ffn_w_up, ffn_w_down=l0_ffn_w_down),
        dict(ada_w=l1_ada_w, ada_b=l1_ada_b, norm1_g=l1_norm1_g, norm2_g=l1_norm2_g, w_in=l1_w_in,
             gqa_qn_g=l1_gqa_qn_g, gqa_kn_g=l1_gqa_kn_g, gqa_sink=l1_gqa_sink,
             diff_qn_g=l1_diff_qn_g, diff_kn_g=l1_diff_kn_g, diff_lq1=l1_diff_lq1, diff_lk1=l1_diff_lk1,
             diff_lq2=l1_diff_lq2, diff_lk2=l1_diff_lk2, diff_subln_g=l1_diff_subln_g, w_out=l1_w_out,
             ffn_w_gate=l1_ffn_w_gate, ffn_w_up=l1_ffn_w_up, ffn_w_down=l1_ffn_w_down),
    )
    cond = c[:, None, :]
    cond_ctx = c_ctx[None, None, :]
    xc = ctx
    for i in range(DEPTH):
        x, xc = trunk_layer(x, xc, cond, cond_ctx, layers[i], i, rows, cols, i < DEPTH - 1)
    return x
```

```python
import os
import numpy as np
import concourse.bass as bass
import concourse.mybir as mybir
from concourse.bass_utils import run_bass_kernel_spmd
from concourse.alu_op_type import AluOpType as ALU
from contextlib import ExitStack

F32 = mybir.dt.float32
BF16 = mybir.dt.bfloat16
AF = mybir.ActivationFunctionType
AX = mybir.AxisListType


STRICT = int(os.environ.get('KSTRICT', '1'))


class Prog:
    ENG = ('pe', 'act', 'dve', 'pool', 'sp')

    def __init__(self, nc):
        self.nc = nc
        self.q = {e: [] for e in self.ENG}
        self.cnt = {}
        self.res = {}
        self.known = {e: {} for e in self.ENG}
        self.pending = {e: {} for e in self.ENG}

    def _need(self, eng, toks, waits, skip=None):
        for s, v in toks.items():
            if s == skip:
                continue
            if self.known[eng].get(s, 0) < v and waits.get(s, 0) < v:
                waits[s] = v

    def _record(self, eng, fn, reads, writes, tok, incspec, is_dma=False):
        waits = {}
        own = 'e_' + eng
        wskip = tok[0] if is_dma else (own if (eng == 'pe' or not STRICT) else None)
        for r in reads:
            st = self.res.get(r)
            if st is not None:
                self._need(eng, st[0], waits, skip=own if eng == 'pe' else None)
        for w in writes:
            st = self.res.get(w)
            if st is not None:
                self._need(eng, st[0], waits, skip=wskip)
                self._need(eng, st[1], waits, skip=wskip)
        for s, v in waits.items():
            self.known[eng][s] = v
        for s, v in self.pending[eng].items():
            if waits.get(s, 0) < v:
                waits[s] = v
        self.pending[eng] = {}
        self.q[eng].append((sorted(waits.items()), fn, incspec))
        for r in reads:
            st = self.res.setdefault(r, [{}, {}])
            if st[1].get(tok[0], 0) < tok[1]:
                st[1][tok[0]] = tok[1]
        for w in writes:
            self.res[w] = [{tok[0]: tok[1]}, {}]

    def capture(self):
        self.cap = []
        return self.cap

    def end_capture(self):
        c = self.cap
        self.cap = None
        return c

    def replay(self, lists):
        idx = [0] * len(lists)
        live = True
        while live:
            live = False
            for k, L in enumerate(lists):
                if idx[k] < len(L):
                    it = L[idx[k]]
                    idx[k] += 1
                    live = True
                    if it[0] == 'op':
                        self.op(*it[1:])
                    else:
                        self.dma(*it[1:-1], **it[-1])

    def op(self, eng, fn, reads=(), writes=(), inc=True):
        if getattr(self, 'cap', None) is not None:
            self.cap.append(('op', eng, fn, list(reads), list(writes), inc))
            return
        own = 'e_' + eng
        c = self.cnt.get(own, 0)
        if inc:
            self.cnt[own] = c + 1
        self._record(eng, fn, reads, writes, (own, c + 1), (own, 1) if inc else None)

    def dma(self, eng, out, in_, reads=(), writes=(), sem=None, **kw):
        if getattr(self, 'cap', None) is not None:
            self.cap.append(('dma', eng, out, in_, list(reads), list(writes), sem, kw))
            return
        s = 'd_' + sem
        c = self.cnt.get(s, 0) + 16
        self.cnt[s] = c
        self._record(eng, lambda e: e.dma_start(out=out, in_=in_, **kw), reads, writes, (s, c), (s, 16), is_dma=True)

    def finish(self, keys):
        self.op('sp', lambda e: e.nop(), reads=list(keys), inc=False)

    def emit(self):
        nc = self.nc
        names = sorted(self.cnt.keys())
        with ExitStack() as es:
            sems = {n: es.enter_context(nc.semaphore(n)) for n in names}
            with nc.Block() as block:
                def body(ename):
                    def f(e):
                        for waits, fn, incspec in self.q[ename]:
                            for s, v in waits:
                                e.wait_ge(sems[s], v)
                            ins = fn(e)
                            if incspec is not None:
                                ins.then_inc(sems[incspec[0]], incspec[1])
                        for s, v in sorted(self.pending[ename].items()):
                            e.wait_ge(sems[s], v)
                    return f
                block.tensor(body('pe'))
                block.scalar(body('act'))
                block.vector(body('dve'))
                block.gpsimd(body('pool'))
                block.sync(body('sp'))

    def check(self):
        sem = {}
        ptr = {e: 0 for e in self.ENG}
        prog = True
        while prog:
            prog = False
            for e in self.ENG:
                while ptr[e] < len(self.q[e]):
                    waits, fn, inc = self.q[e][ptr[e]]
                    if all(sem.get(s_, 0) >= v for s_, v in waits):
                        if inc is not None:
                            sem[inc[0]] = sem.get(inc[0], 0) + inc[1]
                        ptr[e] += 1
                        prog = True
                    else:
                        break
        stuck = {e: (ptr[e], len(self.q[e]), self.q[e][ptr[e]][0]) for e in self.ENG if ptr[e] < len(self.q[e])}
        return stuck, sem

    def barrier(self):
        snap = dict(self.cnt)
        for eng in self.ENG:
            waits = {}
            own = 'e_' + eng
            for s, v in snap.items():
                if s == own:
                    continue
                if self.known[eng].get(s, 0) < v:
                    waits[s] = v
                    self.known[eng][s] = v
            for s, v in waits.items():
                if self.pending[eng].get(s, 0) < v:
                    self.pending[eng][s] = v


D = 1024
T = 2304
NT = 18
NL = 16
FH = 2816
EPS = 1e-6
NEG = -30000.0
GRID_W = 64
G_OFF = {}
_o = 0
for _n, _w in [('qa', 256), ('kva', 128), ('qn96', 96), ('kn96', 96), ('naq', 64), ('nak', 64), ('gq', 64), ('gk', 64),
               ('dq', 64), ('dk', 64), ('sink', 8), ('lq1', 64), ('lk1', 64), ('lq2', 64), ('lk2', 64)]:
    G_OFF[_n] = (_o, _w)
    _o += _w
GW = _o


def _rope_tables():
    pos = np.arange(2048)
    rows, cols = pos // GRID_W, pos % GRID_W

    def tab(h):
        fr = (10000.0 ** (-np.arange(0, h, 2, dtype=np.float32) / np.float32(h))).astype(np.float32)
        ar = rows.astype(np.float32)[:, None] * fr[None, :]
        ac = cols.astype(np.float32)[:, None] * fr[None, :]
        cr, sr, cc, sc = np.cos(ar), np.sin(ar), np.cos(ac), np.sin(ac)
        cos = np.concatenate([cr, cr, cc, cc], axis=1)
        sin = np.concatenate([-sr, sr, -sc, sc], axis=1)
        return np.concatenate([cos, sin], axis=1).astype(np.float32)
    return tab(32), tab(16)


def _na_tables(rpb):
    kc = np.arange(64)[:, None]
    qc = np.arange(64)[None, :]
    c0 = np.clip(qc - 8, 0, 48)
    valid = (kc >= c0) & (kc < c0 + 16)
    offc = np.clip(kc - qc + 15, 0, 30)
    lib = np.zeros((128, 4, 960), np.float32)
    mask = np.zeros((128, 960), np.float32)
    for dr in range(-7, 8):
        col = (7 - dr) * 64
        for h in range(8):
            lib[(h % 2) * 64:(h % 2) * 64 + 64, h // 2, col:col + 64] = rpb[h, dr + 7][offc]
        mask[0:64, col:col + 64] = np.where(valid, 0.0, NEG)
        mask[64:128, col:col + 64] = np.where(valid, 0.0, NEG)
    return lib, mask


def _bc_mid(a, G):
    return bass.AP(a.tensor, a.offset, [list(a.ap[0]), [0, G], list(a.ap[1])])


def _bc_last(a, d):
    return bass.AP(a.tensor, a.offset, [list(a.ap[0]), list(a.ap[1]), [0, d]])


def _view(a, off, dims):
    return bass.AP(a.tensor, a.offset + off, [list(a.ap[0])] + [list(x) for x in dims])


def build(layers=(0, 1), stop_after=None):
    nc = bass.Bass("TRN2", target_bir_lowering=False)
    es = ExitStack()

    def din(name, shape):
        return nc.dram_tensor(name, list(shape), F32, kind="ExternalInput").ap()

    xin = din("xin", [T, D])
    cvecT = din("cvecT", [128, 16])
    gains_d = din("gains", [128, GW])
    pp_d = din("pp", [128, 33])
    ident_d = din("ident", [128, 128])
    tri_d = din("tri", [128, 256])
    sel_d = din("sel", [2, 256])
    cs64_d = din("cs64", [2048, 128])
    cs32_d = din("cs32", [2048, 64])
    lib_d = din("nalib", [128, 4 * 960])
    nam_d = din("namask", [128, 960])
    W = {}
    for l in (0, 1):
        W[l] = dict(
            ada_w=din(f"l{l}_ada_w", [D, 6 * D]), ada_b2=din(f"l{l}_ada_b2", [2, 6 * D]),
            w_in=din(f"l{l}_w_in", [D, 1952 if l == 0 else 2304]), w_out=din(f"l{l}_w_out", [D, D]),
            wg=din(f"l{l}_ffn_w_gate", [D, FH]), wu=din(f"l{l}_ffn_w_up", [D, FH]), wd=din(f"l{l}_ffn_w_down", [FH, D]))
    W[0]['w_uq'] = din("l0_mla_w_uq", [256, 768])
    W[0]['w_ukv'] = din("l0_mla_w_ukv", [128, 1024])
    y_out = nc.dram_tensor("y", [2048, D], F32, kind="ExternalOutput").ap()
    xs = {n: nc.dram_tensor(n, [T, D], F32).ap() for n in ('xs_a', 'xs_b', 'xs_c', 'xs_d')}
    xs['xin'] = xin
    hts = [nc.dram_tensor(f'hts{k}', [NT, 128, D], BF16).ap() for k in range(2)]
    xs['y'] = y_out

    with es:
        def sb(name, shape, dt=F32):
            return es.enter_context(nc.sbuf_tensor(name, list(shape), dt))

        def ps(name, shape, dt=F32):
            return es.enter_context(nc.psum_tensor(name, list(shape), dt))

        p = Prog(nc)
        ident = sb("ident_s", [128, 128])
        identb = sb("identb", [128, 128], BF16)
        onesb = sb("onesb", [128, 128], BF16)
        trib = sb("trib", [128, 256], BF16)
        sel = sb("sel_s", [2, 256])
        gains = sb("gains_s", [128, GW])
        pp = sb("pp_s", [128, 33])
        cT = sb("cT", [128, 16])
        cTb = sb("cTb", [128, 16], BF16)
        modT = sb("modT", [128, 64])
        AB = sb("AB", [128, 64])
        Gt = [[sb(f"G{k}{j}", [128, D]) for j in range(2)] for k in range(2)]
        small = sb("small", [128, 64])
        lamt = sb("lamt", [128, 8])
        es_sink = sb("es_sink", [128, 8])
        sublnS = sb("sublnS", [128, 1])
        naq_s = sb("naq_s", [128, 64])
        xa = [sb("xa0", [128, D])]
        recb = [sb(f"recb{i}", [128, 512]) for i in range(2)]
        xo = [sb(f"xo{i}", [128, D]) for i in range(2)]
        mrow = [sb(f"mrow{i}", [2, 512]) for i in range(2)]
        brow = [sb(f"brow{i}", [2, 512]) for i in range(2)]
        NS = 2

        class SSet:
            pass
        SS = []
        for i in range(NS):
            S_ = SSet()
            S_.i = i
            S_.xt = sb(f"xt{i}", [128, D])
            S_.buf = sb(f"buf{i}", [128, D])
            S_.raw = sb(f"raw{i}", [128, D])
            S_.y1 = sb(f"y1{i}", [128, D])
            S_.cs = sb(f"cs{i}", [128, 192])
            S_.small = small[:, i * 32:(i + 1) * 32]
            S_.xb = (3 * i, 3 * i + 1)
            S_.pj = 3 * i + 2
            S_.k = lambda n, i=i: (n, i)
            SS.append(S_)
        junk = SS[0].buf
        raw = SS[0].raw
        rec = recb
        of = [xo[1][:, 0:512], xo[1][:, 512:1024]]
        RECK = [('rec', 0), ('rec', 1)]
        OFK = [('xo', 1), ('xo', 1)]
        ARN = 63800
        arena = sb("arena", [128, ARN], BF16)
        pb = [ps(f"pb{i}", [128, 512]) for i in range(8)]
        ptbs = [pb[6][:, :].bitcast(BF16), pb[7][:, :].bitcast(BF16)]
        PB = [('pb', i) for i in range(8)]

        class Arena:
            def __init__(self):
                self.off = 0

            def reset(self):
                self.off = 0

            def alloc(self, *free):
                n = int(np.prod(free))
                a = arena[:, self.off:self.off + n]
                self.off += n
                assert self.off <= ARN, self.off
                if len(free) == 2:
                    a = a.rearrange("p (a b) -> p a b", a=free[0])
                elif len(free) == 3:
                    a = a.rearrange("p (a b c) -> p a b c", a=free[0], b=free[1])
                return a
        AR = Arena()

        def MM(out, lhsT, rhs, start, stop, reads, writes, inc=True):
            p.op('pe', lambda e: e.matmul(out, lhsT, rhs, start=start, stop=stop), reads, writes, inc)

        def TR(out, in_, idn, reads, writes, inc=True):
            p.op('pe', lambda e: e.transpose(out, in_, idn), reads, writes, inc)

        def ACT(out, in_, func, reads, writes, bias=None, scale=None):
            kw = {}
            if bias is not None:
                kw['bias'] = bias
            if scale is not None:
                kw['scale'] = scale
            p.op('act', lambda e: e.activation(out, in_, func, **kw), reads, writes)

        def TT(eng, out, a, b, op, reads, writes):
            p.op(eng, lambda e: e.tensor_tensor(out, a, b, op), reads, writes)

        def TS(eng, out, a, s1, s2, op0, op1, reads, writes):
            if s2 is None:
                p.op(eng, lambda e: e.tensor_scalar(out, a, s1, None, op0), reads, writes)
            else:
                p.op(eng, lambda e: e.tensor_scalar(out, a, s1, s2, op0, op1), reads, writes)

        def STT(out, a, s, b, op0, op1, reads, writes):
            p.op('dve', lambda e: e.scalar_tensor_tensor(out, a, s, b, op0, op1), reads, writes)

        def CP(eng, out, in_, reads, writes):
            if eng == 'act':
                ACT(out, in_, AF.Copy, reads, writes)
            else:
                p.op(eng, lambda e: e.tensor_copy(out, in_), reads, writes)

        def RED(out, in_, reads, writes):
            p.op('dve', lambda e: e.tensor_reduce(out, in_, AX.X, ALU.add), reads, writes)

        def RSTD(ap, n_inv, reads_writes):
            ACT(ap, ap, AF.Ln, [reads_writes], [reads_writes], bias=EPS, scale=n_inv)
            ACT(ap, ap, AF.Exp, [reads_writes], [reads_writes], scale=-0.5)

        def gain(name):
            o, w = G_OFF[name]
            return gains[:, o:o + w]

        p.dma('sp', ident[:], ident_d, writes=['ident'], sem='c0')
        p.dma('sp', sel[:], sel_d, writes=['sel'], sem='c1')
        p.dma('sp', gains[:], gains_d, writes=['gains'], sem='c2')
        p.dma('sp', pp[:], pp_d, writes=['pp'], sem='c3')
        p.dma('sp', cT[:], cvecT, writes=['cT'], sem='c4')
        p.dma('pool', identb[:], ident_d, writes=['identb'], sem='c5')
        p.dma('pool', trib[:], tri_d, writes=['trib'], sem='c6')
        p.op('pool', lambda e: e.memset(onesb[:], 1.0), writes=['onesb'])
        ACT(cTb[:], cT[:], AF.Silu, ['cT'], ['cTb'])
        TS('dve', naq_s[:], gain('naq'), 0.125, None, ALU.mult, None, ['gains'], ['naq_s'])
        lam_init = 0.8 - 0.6 * float(np.exp(-0.3 * 1))
        TT('dve', junk[:, 0:64], gain('lq1'), gain('lk1'), ALU.mult, ['gains'], [('buf', 0)])
        RED(lamt[:, 0:1], junk[:, 0:64], [('buf', 0)], ['lamt'])
        TT('dve', junk[:, 64:128], gain('lq2'), gain('lk2'), ALU.mult, ['gains'], [('buf', 0)])
        RED(lamt[:, 1:2], junk[:, 64:128], [('buf', 0)], ['lamt'])
        ACT(lamt[:, 0:2], lamt[:, 0:2], AF.Exp, ['lamt'], ['lamt'])
        TT('dve', lamt[:, 2:3], lamt[:, 1:2], lamt[:, 0:1], ALU.subtract, ['lamt'], ['lamt'])
        TS('dve', lamt[:, 3:4], lamt[:, 2:3], -lam_init, None, ALU.add, None, ['lamt'], ['lamt'])
        ACT(es_sink[:], gain('sink'), AF.Exp, ['gains'], ['es_sink'])
        TS('dve', sublnS[:], pp[:, 32:33], 1.0 - lam_init, None, ALU.mult, None, ['pp'], ['sublnS'])

        def ada_phase(l):
            AR.reset()
            p.barrier()
            awf = [arena[:, k_ * 8192:(k_ + 1) * 8192].bitcast(F32).rearrange("p (c n) -> p c n", c=8) for k_ in range(3)]
            awb = [arena[:, 24576 + k_ * 4096:24576 + (k_ + 1) * 4096].rearrange("p (c n) -> p c n", c=8) for k_ in range(3)]
            adaw = W[l]['ada_w'].rearrange("(c p) n -> p c n", p=128)
            for n in range(12):
                s = n % 3
                b2 = n % 2
                for c in range(8):
                    p.dma('sp' if c % 2 == 0 else 'act', awf[s][:, c, :], adaw[:, c, n * 512:(n + 1) * 512], writes=[('aw', s, c)],
                          sem=f'aw{s}_{c}')
                p.dma('sp', brow[b2][:], W[l]['ada_b2'][:, n * 512:(n + 1) * 512], writes=[('brow', b2)], sem=f'brow{b2}')
                for hh_ in range(2):
                    CP('dve' if hh_ == 0 else 'pool', awb[s][:, hh_ * 4:hh_ * 4 + 4, :], awf[s][:, hh_ * 4:hh_ * 4 + 4, :],
                       [('aw', s, c_) for c_ in range(hh_ * 4, hh_ * 4 + 4)], [('awb', s, hh_)])
                for c in range(8):
                    MM(pb[0][0:2, :], cTb[:, c * 2:c * 2 + 2], awb[s][:, c, :], c == 0, c == 7,
                       ['cTb', ('awb', s, c // 4)], [PB[0]], inc=(c == 7))
                TT('dve', mrow[b2][:], pb[0][0:2, :], brow[b2][:], ALU.add, [PB[0], ('brow', b2)], [('mrow', b2)])
                sec, half = n // 2, n % 2
                if sec in (2, 5):
                    k = 0 if sec == 2 else 1
                    for j in range(2):
                        MM(pb[1 + j][:, :], sel[0:2, j * 128:(j + 1) * 128], mrow[b2][:], True, True,
                           ['sel', ('mrow', b2)], [PB[1 + j]])
                        CP('act', Gt[k][j][:, half * 512:(half + 1) * 512], pb[1 + j][:, :], [PB[1 + j]], [('G', k, j)])
                else:
                    si = {0: 0, 1: 1, 3: 2, 4: 3}[sec]
                    for q in range(4):
                        c = half * 4 + q
                        TR(pb[3][:, (si * 8 + c) * 2:(si * 8 + c) * 2 + 2], mrow[b2][0:2, q * 128:(q + 1) * 128], ident[0:2, 0:2],
                           [('mrow', b2), 'ident'], [PB[3]])
            CP('dve', modT[:], pb[3][:, 0:64], [PB[3]], ['modT'])
            for k in range(2):
                sh = modT[:, (2 * k) * 16:(2 * k) * 16 + 16]
                sc = modT[:, (2 * k + 1) * 16:(2 * k + 1) * 16 + 16]
                gn = pp[:, l * 16 + k * 8:l * 16 + k * 8 + 8]
                A = AB[:, k * 32:k * 32 + 16]
                B = AB[:, k * 32 + 16:k * 32 + 32]
                TS('dve', A, sc, 1.0, None, ALU.add, None, ['modT'], ['AB'])
                TT('dve', A.rearrange("p (c j) -> p c j", j=2), A.rearrange("p (c j) -> p c j", j=2), _bc_last(gn, 2), ALU.mult,
                   ['AB', 'pp'], ['AB'])
                CP('dve', B, sh, ['modT'], ['AB'])

        state = dict(xa=0, xo=0)

        def emit_h(S, src, t, k, dst, hkey, banks=None):
            j = 0 if t < NL else 1
            i = S.i
            xb = banks if banks is not None else S.xb
            X = S.xt
            p.dma('sp', X[:], xs[src][t * 128:(t + 1) * 128, :], reads=[(src, t)], writes=[('xt', i)], sem=f'xt{i}')
            TT('dve', S.buf[:], X[:], X[:], ALU.mult, [('xt', i)], [('buf', i)])
            RED(S.small[:, 0:1], S.buf[:], [('buf', i)], [('small', i)])
            RSTD(S.small[:, 0:1], 1.0 / D, ('small', i))
            ACT(S.buf[:], X[:], AF.Identity, [('xt', i), ('small', i)], [('buf', i)], scale=S.small[:, 0:1])
            for rnd in range(2):
                for c in range(rnd * 4, rnd * 4 + 4):
                    TR(pb[xb[c // 4]][:, (c % 4) * 128:(c % 4 + 1) * 128], S.buf[:, c * 128:(c + 1) * 128], ident[:],
                       [('buf', i), 'ident'], [PB[xb[c // 4]]], inc=(c % 4 == 3))
                for c in range(rnd * 4, rnd * 4 + 4):
                    ACT(dst[:, c, :], pb[xb[c // 4]][:, (c % 4) * 128:(c % 4 + 1) * 128], AF.Identity,
                        [PB[xb[c // 4]], 'AB'], [hkey], scale=AB[:, k * 32 + c * 2 + j:k * 32 + c * 2 + j + 1],
                        bias=AB[:, k * 32 + 16 + c * 2 + j:k * 32 + 16 + c * 2 + j + 1])

        def get_h(S, src, t, k, dst, hkey, compute, banks=None):
            i = S.i
            if compute:
                emit_h(S, src, t, k, dst, hkey, banks=banks)
                p.dma('sp', hts[k][t].rearrange("p (c n) -> p c n", c=8), dst, reads=[hkey], writes=[('hts', k, t)], sem=f'hs{i}')
            else:
                p.dma('sp', dst, hts[k][t].rearrange("p (c n) -> p c n", c=8), reads=[('hts', k, t)], writes=[hkey], sem=f'hl{i}')

        def proj(hT, hkey, wsb, wkey, col0, ncols, bank, M0=0, M1=128):
            for c in range(8):
                MM(pb[bank][0:M1 - M0, 0:ncols], hT[:, c, M0:M1], wsb[:, c, col0:col0 + ncols], c == 0, c == 7,
                   [hkey, wkey], [PB[bank]], inc=(c == 7))

        def load_cs(S, t, which):
            i = S.i
            if which == 64:
                p.dma('sp', S.cs[:, 0:128], cs64_d[t * 128:(t + 1) * 128, :], writes=[('cs', i)], sem=f'cs{i}')
            else:
                p.dma('sp', S.cs[:, 0:64], cs32_d[t * 128:(t + 1) * 128, :], writes=[('cs', i)], sem=f'cs{i}')

        def prep(S, src, G, d, gain_ap, out_b, rope=None):
            i = S.i
            kr, kb, ky, ks, kyb = ('raw', i), ('buf', i), ('y1', i), ('small', i), ('yb', i)
            sq = _view(S.buf[:], 0, [[d, G], [1, d]])
            TT('dve', sq, src, src, ALU.mult, [kr], [kb])
            RED(S.small[:, 8:8 + G], sq, [kb], [ks])
            RSTD(S.small[:, 8:8 + G], 1.0 / d, ks)
            yv = _view(S.y1[:], 0, [[d, G], [1, d]])
            TT('dve', yv, src, _bc_last(S.small[:, 8:8 + G], d), ALU.mult, [kr, ks], [ky])
            if rope is None:
                TT('pool', out_b, yv, _bc_mid(gain_ap, G), ALU.mult, [ky, 'gains', 'naq_s'], [kyb])
                return
            off, bs = rope
            TT('pool', yv, yv, _bc_mid(gain_ap, G), ALU.mult, [ky, 'gains'], [ky])
            if off > 0:
                CP('act', out_b[:, :, 0:off], _view(S.y1[:], 0, [[d, G], [1, off]]), [ky], [kyb])
            w = 4 * bs
            r = _view(S.y1[:], off, [[d, G], [1, w]])
            cosv = _bc_mid(S.cs[:, 0:w], G)
            ta = _view(S.buf[:], 0, [[w, G], [1, w]])
            TT('dve', ta, r, cosv, ALU.mult, [ky, ('cs', i)], [kb])
            for s_ in range(2):
                o_ = _view(S.buf[:], 512 + s_ * bs, [[w, G], [2 * bs, 2], [1, bs]])
                i0 = _view(S.y1[:], off + (1 - s_) * bs, [[d, G], [2 * bs, 2], [1, bs]])
                i1 = _view(S.cs[:], w + s_ * bs, [[0, G], [2 * bs, 2], [1, bs]])
                TT('pool', o_, i0, i1, ALU.mult, [ky, ('cs', i)], [kb])
            tb = _view(S.buf[:], 512, [[w, G], [1, w]])
            TT('dve', out_b[:, :, off:off + w], ta, tb, ALU.add, [kb], [kyb])

        def to_fm(S, ins, width, dst_of, dkey):
            i = S.i
            pt = ptbs[i][:, 0:512]
            for g0 in range(0, len(ins), 4):
                grp = ins[g0:g0 + 4]
                n = len(grp)
                for g, a in enumerate(grp):
                    TR(pt[0:width, g * 128:(g + 1) * 128], a, identb[:], [('yb', i), 'identb'], [PB[6 + i]], inc=(g == n - 1))
                CP('act', dst_of(g0, n), pt[0:width, 0:n * 128].rearrange("p (g t) -> p g t", g=n), [PB[6 + i]], [dkey])

        def residual_store(banks, xacc_i, Gk, j, dst, t):
            i = 0
            for hf in range(2):
                TT('dve', xo[i][:, hf * 512:(hf + 1) * 512], pb[banks[hf]][:, :], Gt[Gk][j][:, hf * 512:(hf + 1) * 512], ALU.mult,
                   [PB[banks[hf]], ('G', Gk, j)], [('xo', i)])
            TT('pool', xo[i][:], xo[i][:], xa[xacc_i][:], ALU.add, [('xo', i), ('xa', xacc_i)], [('xo', i)])
            if dst == 'y':
                if t < NL:
                    p.dma('sp', xs['y'][t * 128:(t + 1) * 128, :], xo[i][:], reads=[('xo', i)], writes=[(dst, t)], sem=f'xo{i}')
            else:
                p.dma('sp', xs[dst][t * 128:(t + 1) * 128, :], xo[i][:], reads=[('xo', i)], writes=[(dst, t)], sem=f'xo{i}')

        def load_xa(src, t):
            i = 0
            p.dma('sp', xa[i][:], xs[src][t * 128:(t + 1) * 128, :], reads=[(src, t)], writes=[('xa', i)], sem=f'xa{i}')
            return i

        def load_w(dst3, src3, key, sem, nch):
            for c in range(nch):
                p.dma('pool', dst3[:, c, :], src3[:, c, :], writes=[key], sem=sem)

        def merged(fn, items):
            out = []
            for g0 in range(0, len(items), NS):
                lists = []
                for k_, it in enumerate(items[g0:g0 + NS]):
                    p.capture()
                    fn(SS[k_], it)
                    lists.append(p.end_capture())
                idx = [0] * len(lists)
                live = True
                while live:
                    live = False
                    for k_, L in enumerate(lists):
                        if idx[k_] < len(L):
                            out.append(L[idx[k_]])
                            idx[k_] += 1
                            live = True
            return out

        def overlapped_chunks(chunks_, q_tile_, att_chunk_):
            p.replay([merged(lambda S, tt: q_tile_(S, tt, 0), list(enumerate(chunks_[0])))])
            for ci, tiles in enumerate(chunks_):
                p.capture()
                att_chunk_(ci, tiles)
                A = p.end_capture()
                B = merged(lambda S, tt: q_tile_(S, tt, (ci + 1) % 2), list(enumerate(chunks_[ci + 1]))) if ci + 1 < len(chunks_) else []
                M = []
                nb_, ib = len(B), 0
                for ka, ia in enumerate(A):
                    M.append(ia)
                    tgt = ((ka + 1) * nb_) // max(len(A), 1)
                    while ib < tgt:
                        M.append(B[ib])
                        ib += 1
                M.extend(B[ib:])
                p.replay([M])

        def interleaved(fn, items):
            for g0 in range(0, len(items), NS):
                lists = []
                for k_, it in enumerate(items[g0:g0 + NS]):
                    p.capture()
                    fn(SS[k_], it)
                    lists.append(p.end_capture())
                p.replay(lists)

        def attend(N, units, scale, qT_of, dv, nb, PT, fused=None, qkey=('fm',), la=None):
            pN, pD = nb if nb is not None else (None, None)
            nu = len(units)

            LA = int(os.environ.get('ATT_LA', '2')) if la is None else la
            DUP = int(os.environ.get('ATT_DUP', '0'))
            SBK = ([0, 3, 5] if fused is not None else [0, 3, 6])[:LA + 1]

            def emit_S(i):
                u = units[i]
                sbk = SBK[i % len(SBK)]
                pS = pb[sbk]
                nk, c0, c1 = u['nk'], u['c0'], u['c1']
                n = c1 - c0
                for rep in range(1 + DUP):
                    MM(pS[0:nk, 0:n], u['kT'], qT_of(c0, c1), True, u.get('bias') is None, [('fm',), qkey], [PB[sbk]], inc=(u.get('bias') is None))
                    if u.get('bias') is not None:
                        MM(pS[0:nk, 0:n], u['bias'][0], u['bias'][1], False, True, ['identb', 'lib'], [PB[sbk]])

            for i0 in range(min(LA, nu)):
                emit_S(i0)
            for i, u in enumerate(units):
                sbk = SBK[i % len(SBK)]
                pS = pb[sbk]
                nk, c0, c1 = u['nk'], u['c0'], u['c1']
                n = c1 - c0
                PTi = i % 4
                ACT(PT[PTi][0:nk, 0:n], pS[0:nk, 0:n], AF.Exp, [PB[sbk]], [('PT', PTi)], scale=scale)
                if i + LA < nu:
                    emit_S(i + LA)
                for (mc, mk) in u.get('masks', []):
                    TT('pool', PT[PTi][:, mc:mc + 128], PT[PTi][:, mc:mc + 128], mk, ALU.mult, [('PT', PTi), 'trib'], [('PT', PTi)])
                if fused is not None:
                    MM(pb[fused][:, c0:c1], u['v'], PT[PTi][0:nk, 0:n], i == 0, i == nu - 1, [('PT', PTi), ('v',)], [PB[fused]], inc=True)
                    continue
                MM(pb[pN][0:dv, c0:c1], u['v'], PT[PTi][0:nk, 0:n], i == 0, i == nu - 1, [('PT', PTi), ('v',)], [PB[pN]], inc=False)
                MM(pb[pD][0:dv, c0:c1], onesb[0:nk, 0:dv], PT[PTi][0:nk, 0:n], i == 0, i == nu - 1, [('PT', PTi), 'onesb'], [PB[pD]],
                   inc=True)

        def norm_fused(h, N, bank, mix, add_sink=False):
            par = h % 2
            dlo, nlo = (64, 0) if par == 0 else (0, 64)
            r_ = rec[par]
            rk = RECK[par]
            if add_sink:
                TS('dve', r_[dlo:dlo + 64, 0:N], pb[bank][dlo:dlo + 64, 0:N], es_sink[dlo:dlo + 64, h:h + 1], None, ALU.add, None,
                   [PB[bank], 'es_sink'], [rk])
                p.op('dve', lambda e: e.reciprocal(r_[dlo:dlo + 64, 0:N], r_[dlo:dlo + 64, 0:N]), [rk], [rk])
            else:
                p.op('dve', lambda e: e.reciprocal(r_[dlo:dlo + 64, 0:N], pb[bank][dlo:dlo + 64, 0:N]), [PB[bank]], [rk])
            TT('dve', mix[nlo:nlo + 64, h // 2, 0:N], pb[bank][nlo:nlo + 64, 0:N], r_[dlo:dlo + 64, 0:N], ALU.mult, [PB[bank], rk], [('mix',)])

        def norm_heads(h, N, pN, pD, mix, add_sink=None):
            r_ = rec[h % 2]
            rk = RECK[h % 2]
            if add_sink is None:
                p.op('dve', lambda e: e.reciprocal(r_[0:64, 0:N], pb[pD][0:64, 0:N]), [PB[pD]], [rk])
            else:
                TS('dve', r_[0:64, 0:N], pb[pD][0:64, 0:N], add_sink, None, ALU.add, None, [PB[pD], 'es_sink'], [rk])
                p.op('dve', lambda e: e.reciprocal(r_[0:64, 0:N], r_[0:64, 0:N]), [rk], [rk])
            TT('dve', mix[0:64, h, 0:N], pb[pN][0:64, 0:N], r_[0:64, 0:N], ALU.mult, [PB[pN], rk], [('mix',)])

        def out_proj(l, mixv, K, nslot, wo, tiles, xacc_src, dst, Gk):
            for ti, t in enumerate(tiles):
                j = 0 if t < NL else 1
                xi = load_xa(xacc_src, t)
                for hf in range(2):
                    for s_ in range(nslot):
                        bk = (1 + hf) if ti % 2 == 0 else (4 + hf)
                        MM(pb[bk][:, :], mixv[0:K, s_, ti * 128:(ti + 1) * 128], wo[0:K, s_, hf * 512:(hf + 1) * 512],
                           s_ == 0, s_ == nslot - 1, [('mix',), 'wo'], [PB[bk]], inc=(s_ == nslot - 1))
                residual_store((1, 2) if ti % 2 == 0 else (4, 5), xi, Gk, j, dst, t)

        def mixer_pass(l, mx, src, acc, dst):
            AR.reset()
            p.barrier()
            ctxq = (l == 0)
            win_d = W[l]['w_in'].rearrange("(c p) n -> p c n", p=128)
            PT = [AR.alloc(512) for _ in range(4)]
            for S in SS:
                S.hT = AR.alloc(8, 128)
                S.yb = AR.alloc(1024)
                S.cT = AR.alloc(2, 128)
            HK = lambda S: ('hT', S.i)
            HCOMP = [mx in ('A', 'C')]
            chunks = [list(range(c * 4, c * 4 + 4)) for c in range(4)] + ([[16, 17]] if ctxq else [])
            NB = [(1, 2), (4, 5)]
            if mx == 'A':
                wi = AR.alloc(8, 416)
                load_w(wi, win_d[:, :, 0:416], 'wi', 'wi', 8)
                wuq = AR.alloc(2, 768)
                load_w(wuq, W[0]['w_uq'].rearrange("(c p) n -> p c n", p=128), 'wuq', 'wuq', 2)
                wukv = AR.alloc(1, 1024)
                load_w(wukv, W[0]['w_ukv'].rearrange("(c p) n -> p c n", p=128), 'wukv', 'wukv', 1)
                wo = AR.alloc(4, 1024)
                load_w(wo, W[l]['w_out'][0:512, :].rearrange("(s p) n -> p s n", p=128), 'wo', 'wo', 4)
                kT = AR.alloc(8, T)
                vv = AR.alloc(NT, 8, 128)
                p.op('pool', lambda e: e.memset(vv, 1.0), [], [('v',)])
                qTs = [AR.alloc(8, 512) for _ in range(2)]
                mix = AR.alloc(4, 512)

                def kv_tile(S, t):
                    lat = t < NL
                    i = S.i
                    if lat:
                        load_cs(S, t, 32)
                    get_h(S, src, t, 0, S.hT, HK(S), HCOMP[0])
                    proj(S.hT, HK(S), wi, 'wi', 256, 160, S.pj)
                    CP('act', S.raw[:, 0:160], pb[S.pj][:, 0:160], [PB[S.pj]], [('raw', i)])
                    prep(S, _view(S.raw[:], 0, [[128, 1], [1, 128]]), 1, 128, gain('kva'), _view(S.yb, 0, [[128, 1], [1, 128]]))
                    to_fm(S, [S.yb[:, 0:128]], 128, lambda g0, n: S.cT[:, 0:1, :], ('cT', i))
                    for hf in range(2):
                        MM(pb[S.xb[hf]][:, :], S.cT[:, 0, :], wukv[:, 0, hf * 512:(hf + 1) * 512], True, True, [('cT', i), 'wukv'], [PB[S.xb[hf]]])
                    for hf in range(2):
                        kvv = pb[S.xb[hf]][:, :].rearrange("p (h e) -> p h e", h=4)
                        for par in range(2):
                            CP('act', _view(vv, (t * 8 + hf * 4 + par) * 128 + par * 64, [[256, 2], [1, 64]]),
                               _view(pb[S.xb[hf]][:, :], par * 128 + 64, [[256, 2], [1, 64]]), [PB[S.xb[hf]]], [('v',)])
                        CP('act', _view(S.raw[:], 256 + hf * 4 * 96, [[96, 4], [1, 64]]), kvv[:, :, 0:64], [PB[S.xb[hf]]], [('raw', i)])
                    CP('pool', _view(S.raw[:], 256 + 64, [[96, 8], [1, 32]]), _bc_mid(S.raw[:, 128:160], 8), [('raw', i)], [('raw', i)])
                    ybv = _view(S.yb, 0, [[96, 8], [1, 96]])
                    prep(S, _view(S.raw[:], 256, [[96, 8], [1, 96]]), 8, 96, gain('kn96'), ybv, rope=(64, 8) if lat else None)
                    to_fm(S, [ybv[:, h, :] for h in range(8)], 96, lambda g0, n: kT[0:96, g0:g0 + n, t * 128:(t + 1) * 128], ('fm',))

                def q_tile(S, tt, qb):
                    ti, t = tt
                    lat = t < NL
                    i = S.i
                    bk = 6 + i
                    if lat:
                        load_cs(S, t, 32)
                    get_h(S, src, t, 0, S.hT, HK(S), HCOMP[0])
                    proj(S.hT, HK(S), wi, 'wi', 0, 256, bk)
                    CP('act', S.raw[:, 0:256], pb[bk][:, 0:256], [PB[bk]], [('raw', i)])
                    prep(S, _view(S.raw[:], 0, [[256, 1], [1, 256]]), 1, 256, gain('qa'), _view(S.yb, 0, [[256, 1], [1, 256]]))
                    to_fm(S, [S.yb[:, 0:128], S.yb[:, 128:256]], 128, lambda g0, n: S.cT[:, :, :], ('cT', i))
                    for (c0_, n_) in ((0, 512), (512, 256)):
                        for c in range(2):
                            MM(pb[bk][:, 0:n_], S.cT[:, c, :], wuq[:, c, c0_:c0_ + n_], c == 0, c == 1, [('cT', i), 'wuq'], [PB[bk]], inc=(c == 1))
                        CP('act', S.raw[:, c0_:c0_ + n_], pb[bk][:, 0:n_], [PB[bk]], [('raw', i)])
                    ybv = _view(S.yb, 0, [[96, 8], [1, 96]])
                    prep(S, _view(S.raw[:], 0, [[96, 8], [1, 96]]), 8, 96, gain('qn96'), ybv, rope=(64, 8) if lat else None)
                    to_fm(S, [ybv[:, h, :] for h in range(8)], 96, lambda g0, n: qTs[qb][0:96, g0:g0 + n, ti * 128:(ti + 1) * 128], ('qT', qb))

                KS = os.environ.get('KSTOP', '')
                interleaved(kv_tile, list(range(2 if KS == 'kv1' else NT)))
                HCOMP[0] = False
                if KS in ('kv1', 'kv'):
                    chunks = []

                def att_chunk(ci, tiles):
                    N = 128 * len(tiles)
                    lat = tiles[0] < NL
                    qT_ = qTs[ci % 2]
                    ktiles = [16, 17] + (list(range(16)) if lat else [])
                    for h in range(8):
                        units = [dict(kT=kT[0:96, h, kt * 128:(kt + 1) * 128], nk=128, c0=0, c1=N, v=vv[:, kt, h, :]) for kt in ktiles]
                        bank = (1, 2, 4)[h % 3]
                        attend(N, units, 96 ** -0.5, lambda c0, c1, h=h: qT_[0:96, h, c0:c1], 64, None, PT, fused=bank, qkey=('qT', ci % 2))
                        norm_fused(h, N, bank, mix)
                    out_proj(l, mix, 128, 4, wo, tiles, acc, dst, 0)
                if chunks:
                    overlapped_chunks(chunks, q_tile, att_chunk)
            elif mx == 'B':
                wi = AR.alloc(8, 1536)
                load_w(wi, win_d[:, :, 416:1952], 'wi', 'wi', 8)
                wo = AR.alloc(8, 1024)
                load_w(wo[0:64], W[l]['w_out'][512:1024, :].rearrange("(h p) n -> p h n", p=64), 'wo', 'wo', 8)
                kT = AR.alloc(4, T)
                vv = AR.alloc(32, 8, 64)
                vvc = AR.alloc(2, 8, 64)
                qT = AR.alloc(4, 512)
                mix = AR.alloc(8, 512)
                libt = AR.alloc(4, 960)
                p.dma('sp', SS[1].raw[:, 0:960], nam_d, writes=[('raw', 1)], sem='c7')
                for hc in range(4):
                    p.dma('sp', SS[0].raw[:, 0:960], lib_d[:, hc * 960:(hc + 1) * 960], writes=[('raw', 0)], sem='c8')
                    TT('pool', libt[:, hc, :], SS[0].raw[:, 0:960], SS[1].raw[:, 0:960], ALU.add, [('raw', 0), ('raw', 1)], ['lib'])

                def kv_tile(S, t):
                    lat = t < NL
                    i = S.i
                    get_h(S, src, t, 0, S.hT, HK(S), HCOMP[0])
                    proj(S.hT, HK(S), wi, 'wi', 512, 512, S.pj)
                    CP('act', S.raw[:, 0:512], pb[S.pj][:, :], [PB[S.pj]], [('raw', i)])
                    prep(S, _view(S.raw[:], 0, [[64, 8], [1, 64]]), 8, 64, gain('nak'), _view(S.yb, 0, [[64, 8], [1, 64]]))
                    to_fm(S, [S.yb[:, c * 128:(c + 1) * 128] for c in range(4)], 128, lambda g0, n: kT[:, g0:g0 + n, t * 128:(t + 1) * 128], ('fm',))
                    if lat:
                        for hf in range(2):
                            proj(S.hT, HK(S), wi, 'wi', 1024, 512, S.xb[hf], M0=hf * 64, M1=hf * 64 + 64)
                            CP('act', vv[0:64, 2 * t + hf, :, :], pb[S.xb[hf]][0:64, :].rearrange("p (h e) -> p h e", h=8), [PB[S.xb[hf]]], [('v',)])
                    else:
                        proj(S.hT, HK(S), wi, 'wi', 1024, 512, S.xb[0])
                        CP('act', vvc[:, t - NL, :, :], pb[S.xb[0]][:, :].rearrange("p (h e) -> p h e", h=8), [PB[S.xb[0]]], [('v',)])

                def q_tile(S, tt):
                    ti, t = tt
                    i = S.i
                    get_h(S, src, t, 0, S.hT, HK(S), HCOMP[0])
                    proj(S.hT, HK(S), wi, 'wi', 0, 512, S.pj)
                    CP('act', S.raw[:, 0:512], pb[S.pj][:, :], [PB[S.pj]], [('raw', i)])
                    prep(S, _view(S.raw[:], 0, [[64, 8], [1, 64]]), 8, 64, naq_s[:], _view(S.yb, 0, [[64, 8], [1, 64]]))
                    to_fm(S, [S.yb[:, c * 128:(c + 1) * 128] for c in range(4)], 128, lambda g0, n: qT[:, g0:g0 + n, ti * 128:(ti + 1) * 128], ('fm',))

                interleaved(kv_tile, list(range(NT)))
                HCOMP[0] = False
                for tiles in chunks:
                    N = 128 * len(tiles)
                    lat = tiles[0] < NL
                    interleaved(q_tile, list(enumerate(tiles)))
                    for h in range(8):
                        ho, hc = (h % 2) * 64, h // 2
                        units = [dict(kT=kT[ho:ho + 64, hc, (NL + i_) * 128:(NL + i_ + 1) * 128], nk=128, c0=0, c1=N, v=vvc[:, i_, h, :])
                                 for i_ in range(2)]
                        if lat:
                            r0 = (tiles[0] * 128) // 64
                            for kr in range(32):
                                rs = [r for r in range(r0, r0 + 8) if min(max(r - 4, 0), 24) <= kr <= min(max(r - 4, 0), 24) + 7]
                                if not rs:
                                    continue
                                ra, rb = rs[0], rs[-1]
                                c0, c1 = (ra - r0) * 64, (rb - r0 + 1) * 64
                                L0 = (7 - (kr - ra)) * 64
                                units.append(dict(kT=kT[ho:ho + 64, hc, kr * 64:(kr + 1) * 64], nk=64, c0=c0, c1=c1, v=vv[0:64, kr, h, :],
                                                  bias=(identb[ho:ho + 64, ho:ho + 64], libt[ho:ho + 64, hc, L0:L0 + (c1 - c0)])))
                        pN, pD = NB[h % 2]
                        attend(N, units, 1.0, lambda c0, c1, ho=ho, hc=hc: qT[ho:ho + 64, hc, c0:c1], 64, (pN, pD), PT)
                        norm_heads(h, N, pN, pD, mix)
                    out_proj(l, mix, 64, 8, wo, tiles, acc, dst, 0)
            elif mx == 'C':
                wi = AR.alloc(8, 768)
                load_w(wi, win_d[:, :, 0:768], 'wi', 'wi', 8)
                wo = AR.alloc(4, 1024)
                load_w(wo, W[l]['w_out'][0:512, :].rearrange("(s p) n -> p s n", p=128), 'wo', 'wo', 4)
                kT = AR.alloc(1, T)
                vvE = AR.alloc(NT, 2, 128)
                vvO = AR.alloc(NT, 2, 128)
                p.op('pool', lambda e: e.memset(vvE, 1.0), [], [('v',)])
                p.op('pool', lambda e: e.memset(vvO, 1.0), [], [('v',)])
                qTs = [AR.alloc(4, 512) for _ in range(2)]
                mix = AR.alloc(4, 512)

                def kv_tile(S, t):
                    lat = t < NL
                    i = S.i
                    if lat:
                        load_cs(S, t, 64)
                    get_h(S, src, t, 0, S.hT, HK(S), HCOMP[0])
                    proj(S.hT, HK(S), wi, 'wi', 512, 256, S.pj)
                    CP('act', S.raw[:, 0:256], pb[S.pj][:, 0:256], [PB[S.pj]], [('raw', i)])
                    prep(S, _view(S.raw[:], 0, [[64, 2], [1, 64]]), 2, 64, gain('gk'), _view(S.yb, 0, [[64, 2], [1, 64]]), rope=(0, 16) if lat else None)
                    to_fm(S, [S.yb[:, 0:128]], 128, lambda g0, n: kT[:, 0:1, t * 128:(t + 1) * 128], ('fm',))
                    CP('pool', vvE[:, t, :, 0:64], _view(S.raw[:], 128, [[64, 2], [1, 64]]), [('raw', i)], [('v',)])
                    CP('pool', vvO[:, t, :, 64:128], _view(S.raw[:], 128, [[64, 2], [1, 64]]), [('raw', i)], [('v',)])

                def q_tile(S, tt, qb):
                    ti, t = tt
                    i = S.i
                    bk = 6 + i
                    load_cs(S, t, 64)
                    get_h(S, src, t, 0, S.hT, HK(S), HCOMP[0])
                    proj(S.hT, HK(S), wi, 'wi', 0, 512, bk)
                    CP('act', S.raw[:, 0:512], pb[bk][:, :], [PB[bk]], [('raw', i)])
                    prep(S, _view(S.raw[:], 0, [[64, 8], [1, 64]]), 8, 64, gain('gq'), _view(S.yb, 0, [[64, 8], [1, 64]]), rope=(0, 16))
                    pt = ptbs[i][:, 0:512]
                    for h in range(8):
                        kvh, g = h // 4, h % 4
                        TR(pt[kvh * 64:(kvh + 1) * 64, g * 128:(g + 1) * 128], S.yb[:, h * 64:(h + 1) * 64], identb[:], [('yb', i), 'identb'],
                           [PB[6 + i]], inc=(h == 7))
                    CP('act', qTs[qb][:, :, ti * 128:(ti + 1) * 128], pt[:, 0:512].rearrange("p (g t) -> p g t", g=4), [PB[6 + i]], [('qT', qb)])

                interleaved(kv_tile, list(range(NT)))
                HCOMP[0] = False

                def att_chunk(cch, tiles):
                    N = 512
                    qT_ = qTs[cch % 2]
                    for h in range(8):
                        kvh, g = h // 4, h % 4
                        ho = kvh * 64
                        vv = vvE if h % 2 == 0 else vvO
                        units = [dict(kT=kT[ho:ho + 64, 0, (NL + i_) * 128:(NL + i_ + 1) * 128], nk=128, c0=0, c1=N, v=vv[:, NL + i_, kvh, :])
                                 for i_ in range(2)]
                        for m in range(max(4 * cch - 1, 0), min(4 * cch + 4, 15) + 1):
                            nlo, nhi = max(m - 1, 4 * cch), min(m + 1, 4 * cch + 3)
                            masks = []
                            for n_ in range(nlo, nhi + 1):
                                if n_ == m + 1:
                                    masks.append(((n_ - nlo) * 128, trib[:, 0:128]))
                                elif n_ == m - 1:
                                    masks.append(((n_ - nlo) * 128, trib[:, 128:256]))
                            units.append(dict(kT=kT[ho:ho + 64, 0, m * 128:(m + 1) * 128], nk=128, c0=(nlo - 4 * cch) * 128,
                                              c1=(nhi - 4 * cch + 1) * 128, v=vv[:, m, kvh, :], masks=masks))
                        bank = (1, 2, 4)[h % 3]
                        attend(N, units, 0.125, lambda c0, c1, ho=ho, g=g: qT_[ho:ho + 64, g, c0:c1], 64, None, PT, fused=bank, qkey=('qT', cch % 2))
                        norm_fused(h, N, bank, mix, add_sink=True)
                    out_proj(l, mix, 128, 4, wo, tiles, acc, dst, 0)
                overlapped_chunks([list(range(c_ * 4, c_ * 4 + 4)) for c_ in range(4)], q_tile, att_chunk)
            else:
                wi = AR.alloc(8, 1536)
                load_w(wi, win_d[:, :, 768:2304], 'wi', 'wi', 8)
                wo = AR.alloc(4, 1024)
                load_w(wo, W[l]['w_out'][512:1024, :].rearrange("(h p) n -> p h n", p=128), 'wo', 'wo', 4)
                kT = AR.alloc(4, T)
                vv = AR.alloc(NT, 4, 128)
                qTs = [AR.alloc(4, 512) for _ in range(2)]
                mix = AR.alloc(4, 512)

                def kv_tile(S, t):
                    lat = t < NL
                    i = S.i
                    if lat:
                        load_cs(S, t, 64)
                    get_h(S, src, t, 0, S.hT, HK(S), HCOMP[0])
                    proj(S.hT, HK(S), wi, 'wi', 512, 512, S.pj)
                    CP('act', S.raw[:, 0:512], pb[S.pj][:, :], [PB[S.pj]], [('raw', i)])
                    prep(S, _view(S.raw[:], 0, [[64, 8], [1, 64]]), 8, 64, gain('dk'), _view(S.yb, 0, [[64, 8], [1, 64]]), rope=(0, 16) if lat else None)
                    to_fm(S, [S.yb[:, c * 128:(c + 1) * 128] for c in range(4)], 128, lambda g0, n: kT[:, g0:g0 + n, t * 128:(t + 1) * 128], ('fm',))
                    proj(S.hT, HK(S), wi, 'wi', 1024, 512, S.xb[0])
                    CP('act', vv[:, t, :, :], pb[S.xb[0]][:, :].rearrange("p (h e) -> p h e", h=4), [PB[S.xb[0]]], [('v',)])

                def q_tile(S, tt, qb):
                    ti, t = tt
                    i = S.i
                    bk = 6 + i
                    load_cs(S, t, 64)
                    get_h(S, src, t, 0, S.hT, HK(S), HCOMP[0])
                    proj(S.hT, HK(S), wi, 'wi', 0, 512, bk)
                    CP('act', S.raw[:, 0:512], pb[bk][:, :], [PB[bk]], [('raw', i)])
                    prep(S, _view(S.raw[:], 0, [[64, 8], [1, 64]]), 8, 64, gain('dq'), _view(S.yb, 0, [[64, 8], [1, 64]]), rope=(0, 16))
                    to_fm(S, [S.yb[:, c * 128:(c + 1) * 128] for c in range(4)], 128, lambda g0, n: qTs[qb][:, g0:g0 + n, ti * 128:(ti + 1) * 128], ('qT', qb))

                interleaved(kv_tile, list(range(NT)))
                HCOMP[0] = False

                def att_chunk(cch, tiles):
                    N = 512
                    qT_ = qTs[cch % 2]
                    for hh in range(4):
                        for cc in range(2):
                            ho = cc * 64
                            units = [dict(kT=kT[ho:ho + 64, hh, kt * 128:(kt + 1) * 128], nk=128, c0=0, c1=N, v=vv[:, kt, hh, :])
                                     for kt in ([16, 17] + list(range(16)))]
                            attend(N, units, 0.125, lambda c0, c1, ho=ho, hh=hh: qT_[ho:ho + 64, hh, c0:c1], 128, NB[cc], PT, qkey=('qT', cch % 2), la=1)
                        for cc in range(2):
                            pN, pD = NB[cc]
                            p.op('dve', lambda e, cc=cc, pD=pD: e.reciprocal(rec[cc][:, :], pb[pD][:, :]), [PB[pD]], [RECK[cc]])
                            TT('dve', of[cc][:, :], pb[pN][:, :], rec[cc][:, :], ALU.mult, [PB[pN], RECK[cc]], [OFK[cc]])
                        STT(of[0][:, :], of[1][:, :], lamt[:, 3:4], of[0][:, :], ALU.mult, ALU.add, [OFK[0], 'lamt'], [OFK[0]])
                        TT('pool', PT[0][:, :], of[0][:, :], of[0][:, :], ALU.mult, [OFK[0]], [('PT', 0)])
                        MM(pb[0][:, :], onesb[:, :], PT[0][:, :], True, True, [('PT', 0), 'onesb'], [PB[0]])
                        ACT(rec[0][:, :], pb[0][:, :], AF.Ln, [PB[0]], [RECK[0]], bias=EPS, scale=1.0 / 128)
                        ACT(rec[0][:, :], rec[0][:, :], AF.Exp, [RECK[0]], [RECK[0]], scale=-0.5)
                        STT(mix[:, hh, :], of[0][:, :], sublnS[:, 0:1], rec[0][:, :], ALU.mult, ALU.mult, [OFK[0], RECK[0], 'sublnS'], [('mix',)])
                    out_proj(l, mix, 128, 4, wo, tiles, acc, dst, 0)
                overlapped_chunks([list(range(c_ * 4, c_ * 4 + 4)) for c_ in range(4)], q_tile, att_chunk)

        def ffn_half(l, hf, hsrc, acc, dst):
            AR.reset()
            p.barrier()
            wg = AR.alloc(8, 1408)
            wu = AR.alloc(8, 1408)
            wd = AR.alloc(11, 1024)
            load_w(wg, W[l]['wg'].rearrange("(c p) n -> p c n", p=128)[:, :, hf * 1408:(hf + 1) * 1408], 'wg', 'wg', 8)
            load_w(wu, W[l]['wu'].rearrange("(c p) n -> p c n", p=128)[:, :, hf * 1408:(hf + 1) * 1408], 'wu', 'wu', 8)
            load_w(wd, W[l]['wd'][hf * 1408:(hf + 1) * 1408, :].rearrange("(f p) n -> p f n", p=128), 'wd', 'wd', 11)
            h2s = [AR.alloc(8, 512) for _ in range(2)]
            act = AR.alloc(11, 512)
            sg = [AR.alloc(512) for _ in range(2)]
            chunks = [list(range(c * 4, c * 4 + 4)) for c in range(4)] + ([[16, 17]] if l == 0 else [])

            if hf == 0:
                for S in SS:
                    S.hT = AR.alloc(8, 128)

                def norm_tile(S, t):
                    get_h(S, hsrc, t, 1, S.hT, ('hT', S.i), True)
                interleaved(norm_tile, [t for tiles in chunks for t in tiles])

            def emit_h_tile(ci_, ti):
                t = chunks[ci_][ti]
                p.dma('sp', h2s[ci_ % 2][:, :, ti * 128:(ti + 1) * 128], hts[1][t].rearrange("p (c n) -> p c n", c=8),
                      reads=[('hts', 1, t)], writes=[('h2', ci_ % 2, ti)], sem=f'h2l{ci_ % 2}_{ti}')

            for ti in range(len(chunks[0])):
                emit_h_tile(0, ti)
            for ci_, tiles in enumerate(chunks):
                N = 128 * len(tiles)
                h2 = h2s[ci_ % 2]
                hks = [('h2', ci_ % 2, ti_) for ti_ in range(len(tiles))]
                nxt = len(chunks[ci_ + 1]) if ci_ + 1 < len(chunks) else 0
                for f in range(11):
                    bg, bu = (f % 2) * 2, (f % 2) * 2 + 1
                    for c in range(8):
                        MM(pb[bg][:, 0:N], wg[:, c, f * 128:(f + 1) * 128], h2[:, c, 0:N], c == 0, c == 7, hks + ['wg'], [PB[bg]], inc=(c == 7))
                    for c in range(8):
                        MM(pb[bu][:, 0:N], wu[:, c, f * 128:(f + 1) * 128], h2[:, c, 0:N], c == 0, c == 7, hks + ['wu'], [PB[bu]], inc=(c == 7))
                    ACT(sg[f % 2][:, 0:N], pb[bg][:, 0:N], AF.Silu, [PB[bg]], [('sg', f % 2)])
                    TT('dve', act[:, f, 0:N], pb[bu][:, 0:N], sg[f % 2][:, 0:N], ALU.mult, [PB[bu], ('sg', f % 2)], [('act',)])
                    if f % 2 == 1 and (f // 2) < nxt:
                        emit_h_tile(ci_ + 1, f // 2)
                for ti, t in enumerate(tiles):
                    j = 0 if t < NL else 1
                    xi2 = load_xa(acc, t)
                    for h2_ in range(2):
                        for f in range(11):
                            bk = (4 + h2_) if ti % 2 == 0 else (2 + h2_)
                            MM(pb[bk][:, :], act[:, f, ti * 128:(ti + 1) * 128], wd[:, f, h2_ * 512:(h2_ + 1) * 512], f == 0, f == 10,
                               [('act',), 'wd'], [PB[bk]], inc=(f == 10))
                    residual_store((4, 5) if ti % 2 == 0 else (2, 3), xi2, 1, j, dst, t)

        seq = []
        if 0 in layers:
            seq += [('ada', 0), ('mix', 0, 'A', 'xin', 'xin', 'xs_a'), ('mix', 0, 'B', 'xin', 'xs_a', 'xs_b'),
                    ('ffn', 0, 0, 'xs_b', 'xs_b', 'xs_c'), ('ffn', 0, 1, 'xs_b', 'xs_c', 'xs_d')]
        if 1 in layers:
            s1 = 'xs_d' if 0 in layers else 'xin'
            seq += [('ada', 1), ('mix', 1, 'C', s1, s1, 'xs_a'), ('mix', 1, 'D', s1, 'xs_a', 'xs_b'),
                    ('ffn', 1, 0, 'xs_b', 'xs_b', 'xs_c'), ('ffn', 1, 1, 'xs_b', 'xs_c', 'y')]
        if stop_after is not None:
            seq = seq[:stop_after]
        last_dst = None
        for st in seq:
            if st[0] == 'ada':
                ada_phase(st[1])
            elif st[0] == 'mix':
                mixer_pass(st[1], st[2], st[3], st[4], st[5])
                last_dst = st[5]
            else:
                ffn_half(st[1], st[2], st[3], st[4], st[5])
                last_dst = st[5]
        if last_dst is None:
            p.dma('sp', xs['y'][0:128, :], Gt[0][0][:], reads=[('G', 0, 0)], writes=[('y', 0)], sem='dbg0')
            p.dma('sp', xs['y'][128:256, :], Gt[0][1][:], reads=[('G', 0, 1)], writes=[('y', 1)], sem='dbg0')
            p.dma('sp', xs['y'][256:384, :], Gt[1][0][:], reads=[('G', 1, 0)], writes=[('y', 2)], sem='dbg0')
            p.dma('sp', xs['y'][384:512, 0:64], AB[:], reads=['AB'], writes=[('y', 3)], sem='dbg0')
            p.finish([('y', t) for t in range(4)])
            p.emit()
            return nc
        if last_dst != 'y':
            for t in range(NL):
                S = SS[t % 2]
                p.dma('sp', S.xt[:], xs[last_dst][t * 128:(t + 1) * 128, :], reads=[(last_dst, t)], writes=[('xt', S.i)], sem=f'xt{S.i}')
                p.dma('sp', xs['y'][t * 128:(t + 1) * 128, :], S.xt[:], reads=[('xt', S.i)], writes=[('y', t)], sem=f'dbg{t % 2}')
        p.finish([('y', t) for t in range(NL)])
        p.emit()
    return nc


_CACHE = {}


def _host_inputs(inputs, layers=(0, 1)):
    f = lambda a: np.ascontiguousarray(np.asarray(a, dtype=np.float32))
    x, c, ctx, c_ctx = f(inputs['x']), f(inputs['c']), f(inputs['ctx']), f(inputs['c_ctx'])
    cs64, cs32 = _rope_tables()
    lib, nam = _na_tables(f(inputs['l0_na_rpb']))
    gl = []
    for n in ['l0_mla_qa_g', 'l0_mla_kva_g', 'l0_mla_qn_g', 'l0_mla_kn_g', 'l0_na_qn_g', 'l0_na_kn_g', 'l1_gqa_qn_g', 'l1_gqa_kn_g',
              'l1_diff_qn_g', 'l1_diff_kn_g', 'l1_gqa_sink', 'l1_diff_lq1', 'l1_diff_lk1', 'l1_diff_lq2', 'l1_diff_lk2']:
        gl.append(f(inputs[n]).reshape(-1))
    gains = np.ascontiguousarray(np.broadcast_to(np.concatenate(gl)[None, :], (128, GW)))
    pp = np.zeros((128, 33), np.float32)
    for i, n in enumerate(['l0_norm1_g', 'l0_norm2_g', 'l1_norm1_g', 'l1_norm2_g']):
        pp[:, i * 8:(i + 1) * 8] = f(inputs[n]).reshape(8, 128).T
    pp[:, 32] = f(inputs['l1_diff_subln_g'])
    ident = np.eye(128, dtype=np.float32)
    jj = np.arange(128)[:, None]
    ii = np.arange(128)[None, :]
    tri = np.concatenate([(ii <= jj), (jj <= ii)], axis=1).astype(np.float32)
    sel = np.zeros((2, 256), np.float32)
    sel[0, 0:128] = 1.0
    sel[1, 128:256] = 1.0
    shared = dict(gains=gains, pp=pp, ident=ident, tri=tri, sel=sel, cs64=cs64, cs32=cs32,
                  nalib=np.ascontiguousarray(lib.reshape(128, 4 * 960)), namask=nam)
    for l in (0, 1):
        shared[f'l{l}_ada_w'] = f(inputs[f'l{l}_ada_w'])
        shared[f'l{l}_ada_b2'] = np.ascontiguousarray(np.broadcast_to(f(inputs[f'l{l}_ada_b'])[None, :], (2, 6 * D)))
        shared[f'l{l}_w_in'] = f(inputs[f'l{l}_w_in'])
        shared[f'l{l}_w_out'] = f(inputs[f'l{l}_w_out'])
        shared[f'l{l}_ffn_w_gate'] = f(inputs[f'l{l}_ffn_w_gate'])
        shared[f'l{l}_ffn_w_up'] = f(inputs[f'l{l}_ffn_w_up'])
        shared[f'l{l}_ffn_w_down'] = f(inputs[f'l{l}_ffn_w_down'])
    shared['l0_mla_w_uq'] = f(inputs['l0_mla_w_uq'])
    shared['l0_mla_w_ukv'] = f(inputs['l0_mla_w_ukv'])
    maps = []
    for b in range(x.shape[0]):
        m = dict(shared)
        m['xin'] = np.ascontiguousarray(np.concatenate([x[b], ctx[b]], axis=0))
        cv = np.stack([c[b], c_ctx], axis=0)
        m['cvecT'] = np.ascontiguousarray(cv.reshape(2, 8, 128).transpose(2, 1, 0).reshape(128, 16))
        maps.append(m)
    return maps


def kernel(**inputs):
    maps = _host_inputs(inputs)
    if 'nc' not in _CACHE:
        _CACHE['nc'] = build()
    res = run_bass_kernel_spmd(_CACHE['nc'], maps, core_ids=list(range(len(maps))))
    return np.stack([np.asarray(r['y'], dtype=np.float32) for r in res.results], axis=0)
```

```python
import os
import numpy as np
import concourse.bass as bass
import concourse.mybir as mybir
from concourse.bass_utils import run_bass_kernel_spmd
from concourse.alu_op_type import AluOpType as ALU
from contextlib import ExitStack

F32 = mybir.dt.float32
BF16 = mybir.dt.bfloat16
AF = mybir.ActivationFunctionType
AX = mybir.AxisListType


STRICT = int(os.environ.get('KSTRICT', '1'))


class Prog:
    ENG = ('pe', 'act', 'dve', 'pool', 'sp')

    def __init__(self, nc):
        self.nc = nc
        self.q = {e: [] for e in self.ENG}
        self.cnt = {}
        self.res = {}
        self.known = {e: {} for e in self.ENG}
        self.pending = {e: {} for e in self.ENG}

    def _need(self, eng, toks, waits, skip=None):
        for s, v in toks.items():
            if s == skip:
                continue
            if self.known[eng].get(s, 0) < v and waits.get(s, 0) < v:
                waits[s] = v

    def _record(self, eng, fn, reads, writes, tok, incspec, is_dma=False):
        waits = {}
        own = 'e_' + eng
        wskip = tok[0] if is_dma else (own if (eng == 'pe' or not STRICT) else None)
        for r in reads:
            st = self.res.get(r)
            if st is not None:
                self._need(eng, st[0], waits, skip=own if eng == 'pe' else None)
        for w in writes:
            st = self.res.get(w)
            if st is not None:
                self._need(eng, st[0], waits, skip=wskip)
                self._need(eng, st[1], waits, skip=wskip)
        for s, v in waits.items():
            self.known[eng][s] = v
        for s, v in self.pending[eng].items():
            if waits.get(s, 0) < v:
                waits[s] = v
        self.pending[eng] = {}
        self.q[eng].append((sorted(waits.items()), fn, incspec))
        for r in reads:
            st = self.res.setdefault(r, [{}, {}])
            if st[1].get(tok[0], 0) < tok[1]:
                st[1][tok[0]] = tok[1]
        for w in writes:
            self.res[w] = [{tok[0]: tok[1]}, {}]

    def capture(self):
        self.cap = []
        return self.cap

    def end_capture(self):
        c = self.cap
        self.cap = None
        return c

    def replay(self, lists):
        idx = [0] * len(lists)
        live = True
        while live:
            live = False
            for k, L in enumerate(lists):
                if idx[k] < len(L):
                    it = L[idx[k]]
                    idx[k] += 1
                    live = True
                    if it[0] == 'op':
                        self.op(*it[1:])
                    else:
                        self.dma(*it[1:-1], **it[-1])

    def op(self, eng, fn, reads=(), writes=(), inc=True):
        if getattr(self, 'cap', None) is not None:
            self.cap.append(('op', eng, fn, list(reads), list(writes), inc))
            return
        own = 'e_' + eng
        c = self.cnt.get(own, 0)
        if inc:
            self.cnt[own] = c + 1
        self._record(eng, fn, reads, writes, (own, c + 1), (own, 1) if inc else None)

    def dma(self, eng, out, in_, reads=(), writes=(), sem=None, **kw):
        if getattr(self, 'cap', None) is not None:
            self.cap.append(('dma', eng, out, in_, list(reads), list(writes), sem, kw))
            return
        s = 'd_' + sem
        c = self.cnt.get(s, 0) + 16
        self.cnt[s] = c
        self._record(eng, lambda e: e.dma_start(out=out, in_=in_, **kw), reads, writes, (s, c), (s, 16), is_dma=True)

    def finish(self, keys):
        self.op('sp', lambda e: e.nop(), reads=list(keys), inc=False)

    def emit(self):
        nc = self.nc
        names = sorted(self.cnt.keys())
        with ExitStack() as es:
            sems = {n: es.enter_context(nc.semaphore(n)) for n in names}
            with nc.Block() as block:
                def body(ename):
                    def f(e):
                        for waits, fn, incspec in self.q[ename]:
                            for s, v in waits:
                                e.wait_ge(sems[s], v)
                            ins = fn(e)
                            if incspec is not None:
                                ins.then_inc(sems[incspec[0]], incspec[1])
                        for s, v in sorted(self.pending[ename].items()):
                            e.wait_ge(sems[s], v)
                    return f
                block.tensor(body('pe'))
                block.scalar(body('act'))
                block.vector(body('dve'))
                block.gpsimd(body('pool'))
                block.sync(body('sp'))

    def check(self):
        sem = {}
        ptr = {e: 0 for e in self.ENG}
        prog = True
        while prog:
            prog = False
            for e in self.ENG:
                while ptr[e] < len(self.q[e]):
                    waits, fn, inc = self.q[e][ptr[e]]
                    if all(sem.get(s_, 0) >= v for s_, v in waits):
                        if inc is not None:
                            sem[inc[0]] = sem.get(inc[0], 0) + inc[1]
                        ptr[e] += 1
                        prog = True
                    else:
                        break
        stuck = {e: (ptr[e], len(self.q[e]), self.q[e][ptr[e]][0]) for e in self.ENG if ptr[e] < len(self.q[e])}
        return stuck, sem

    def barrier(self):
        snap = dict(self.cnt)
        for eng in self.ENG:
            waits = {}
            own = 'e_' + eng
            for s, v in snap.items():
                if s == own:
                    continue
                if self.known[eng].get(s, 0) < v:
                    waits[s] = v
                    self.known[eng][s] = v
            for s, v in waits.items():
                if self.pending[eng].get(s, 0) < v:
                    self.pending[eng][s] = v


D = 1024
T = 2304
NT = 18
NL = 16
FH = 2816
EPS = 1e-6
NEG = -30000.0
GRID_W = 64
G_OFF = {}
_o = 0
for _n, _w in [('qa', 256), ('kva', 128), ('qn96', 96), ('kn96', 96), ('naq', 64), ('nak', 64), ('gq', 64), ('gk', 64),
               ('dq', 64), ('dk', 64), ('sink', 8), ('lq1', 64), ('lk1', 64), ('lq2', 64), ('lk2', 64)]:
    G_OFF[_n] = (_o, _w)
    _o += _w
GW = _o


def _rope_tables():
    pos = np.arange(2048)
    rows, cols = pos // GRID_W, pos % GRID_W

    def tab(h):
        fr = (10000.0 ** (-np.arange(0, h, 2, dtype=np.float32) / np.float32(h))).astype(np.float32)
        ar = rows.astype(np.float32)[:, None] * fr[None, :]
        ac = cols.astype(np.float32)[:, None] * fr[None, :]
        cr, sr, cc, sc = np.cos(ar), np.sin(ar), np.cos(ac), np.sin(ac)
        cos = np.concatenate([cr, cr, cc, cc], axis=1)
        sin = np.concatenate([-sr, sr, -sc, sc], axis=1)
        return np.concatenate([cos, sin], axis=1).astype(np.float32)
    return tab(32), tab(16)


def _na_tables(rpb):
    kc = np.arange(64)[:, None]
    qc = np.arange(64)[None, :]
    c0 = np.clip(qc - 8, 0, 48)
    valid = (kc >= c0) & (kc < c0 + 16)
    offc = np.clip(kc - qc + 15, 0, 30)
    lib = np.zeros((128, 4, 960), np.float32)
    mask = np.zeros((128, 960), np.float32)
    for dr in range(-7, 8):
        col = (7 - dr) * 64
        for h in range(8):
            lib[(h % 2) * 64:(h % 2) * 64 + 64, h // 2, col:col + 64] = rpb[h, dr + 7][offc]
        mask[0:64, col:col + 64] = np.where(valid, 0.0, NEG)
        mask[64:128, col:col + 64] = np.where(valid, 0.0, NEG)
    return lib, mask


def _bc_mid(a, G):
    return bass.AP(a.tensor, a.offset, [list(a.ap[0]), [0, G], list(a.ap[1])])


def _bc_last(a, d):
    return bass.AP(a.tensor, a.offset, [list(a.ap[0]), list(a.ap[1]), [0, d]])


def _view(a, off, dims):
    return bass.AP(a.tensor, a.offset + off, [list(a.ap[0])] + [list(x) for x in dims])


def build(layers=(0, 1), stop_after=None):
    nc = bass.Bass("TRN2", target_bir_lowering=False)
    es = ExitStack()

    def din(name, shape):
        return nc.dram_tensor(name, list(shape), F32, kind="ExternalInput").ap()

    xin = din("xin", [T, D])
    cvecT = din("cvecT", [128, 16])
    gains_d = din("gains", [128, GW])
    pp_d = din("pp", [128, 33])
    ident_d = din("ident", [128, 128])
    tri_d = din("tri", [128, 256])
    sel_d = din("sel", [2, 256])
    cs64_d = din("cs64", [2048, 128])
    cs32_d = din("cs32", [2048, 64])
    lib_d = din("nalib", [128, 4 * 960])
    nam_d = din("namask", [128, 960])
    W = {}
    for l in (0, 1):
        W[l] = dict(
            ada_w=din(f"l{l}_ada_w", [D, 6 * D]), ada_b2=din(f"l{l}_ada_b2", [2, 6 * D]),
            w_in=din(f"l{l}_w_in", [D, 1952 if l == 0 else 2304]), w_out=din(f"l{l}_w_out", [D, D]),
            wg=din(f"l{l}_ffn_w_gate", [D, FH]), wu=din(f"l{l}_ffn_w_up", [D, FH]), wd=din(f"l{l}_ffn_w_down", [FH, D]))
    W[0]['w_uq'] = din("l0_mla_w_uq", [256, 768])
    W[0]['w_ukv'] = din("l0_mla_w_ukv", [128, 1024])
    y_out = nc.dram_tensor("y", [2048, D], F32, kind="ExternalOutput").ap()
    xs = {n: nc.dram_tensor(n, [T, D], F32).ap() for n in ('xs_a', 'xs_b', 'xs_c', 'xs_d')}
    xs['xin'] = xin
    hts = [nc.dram_tensor(f'hts{k}', [NT, 128, D], BF16).ap() for k in range(2)]
    xs['y'] = y_out

    with es:
        def sb(name, shape, dt=F32):
            return es.enter_context(nc.sbuf_tensor(name, list(shape), dt))

        def ps(name, shape, dt=F32):
            return es.enter_context(nc.psum_tensor(name, list(shape), dt))

        p = Prog(nc)
        ident = sb("ident_s", [128, 128])
        identb = sb("identb", [128, 128], BF16)
        onesb = sb("onesb", [128, 128], BF16)
        trib = sb("trib", [128, 256], BF16)
        sel = sb("sel_s", [2, 256])
        gains = sb("gains_s", [128, GW])
        pp = sb("pp_s", [128, 33])
        cT = sb("cT", [128, 16])
        cTb = sb("cTb", [128, 16], BF16)
        modT = sb("modT", [128, 64])
        AB = sb("AB", [128, 64])
        Gt = [[sb(f"G{k}{j}", [128, D]) for j in range(2)] for k in range(2)]
        small = sb("small", [128, 64])
        lamt = sb("lamt", [128, 8])
        es_sink = sb("es_sink", [128, 8])
        sublnS = sb("sublnS", [128, 1])
        naq_s = sb("naq_s", [128, 64])
        xa = [sb("xa0", [128, D])]
        recb = [sb(f"recb{i}", [128, 512]) for i in range(2)]
        xo = [sb(f"xo{i}", [128, D]) for i in range(2)]
        mrow = [sb(f"mrow{i}", [2, 512]) for i in range(2)]
        brow = [sb(f"brow{i}", [2, 512]) for i in range(2)]
        NS = 2

        class SSet:
            pass
        SS = []
        for i in range(NS):
            S_ = SSet()
            S_.i = i
            S_.xt = sb(f"xt{i}", [128, D])
            S_.buf = sb(f"buf{i}", [128, D])
            S_.raw = sb(f"raw{i}", [128, D])
            S_.y1 = sb(f"y1{i}", [128, D])
            S_.cs = sb(f"cs{i}", [128, 192])
            S_.small = small[:, i * 32:(i + 1) * 32]
            S_.xb = (3 * i, 3 * i + 1)
            S_.pj = 3 * i + 2
            S_.k = lambda n, i=i: (n, i)
            SS.append(S_)
        junk = SS[0].buf
        raw = SS[0].raw
        rec = recb
        of = [xo[1][:, 0:512], xo[1][:, 512:1024]]
        RECK = [('rec', 0), ('rec', 1)]
        OFK = [('xo', 1), ('xo', 1)]
        ARN = 63800
        arena = sb("arena", [128, ARN], BF16)
        pb = [ps(f"pb{i}", [128, 512]) for i in range(8)]
        ptbs = [pb[6][:, :].bitcast(BF16), pb[7][:, :].bitcast(BF16)]
        PB = [('pb', i) for i in range(8)]

        class Arena:
            def __init__(self):
                self.off = 0

            def reset(self):
                self.off = 0

            def alloc(self, *free):
                n = int(np.prod(free))
                a = arena[:, self.off:self.off + n]
                self.off += n
                assert self.off <= ARN, self.off
                if len(free) == 2:
                    a = a.rearrange("p (a b) -> p a b", a=free[0])
                elif len(free) == 3:
                    a = a.rearrange("p (a b c) -> p a b c", a=free[0], b=free[1])
                return a
        AR = Arena()

        def MM(out, lhsT, rhs, start, stop, reads, writes, inc=True):
            p.op('pe', lambda e: e.matmul(out, lhsT, rhs, start=start, stop=stop), reads, writes, inc)

        def TR(out, in_, idn, reads, writes, inc=True):
            p.op('pe', lambda e: e.transpose(out, in_, idn), reads, writes, inc)

        def ACT(out, in_, func, reads, writes, bias=None, scale=None):
            kw = {}
            if bias is not None:
                kw['bias'] = bias
            if scale is not None:
                kw['scale'] = scale
            p.op('act', lambda e: e.activation(out, in_, func, **kw), reads, writes)

        def TT(eng, out, a, b, op, reads, writes):
            p.op(eng, lambda e: e.tensor_tensor(out, a, b, op), reads, writes)

        def TS(eng, out, a, s1, s2, op0, op1, reads, writes):
            if s2 is None:
                p.op(eng, lambda e: e.tensor_scalar(out, a, s1, None, op0), reads, writes)
            else:
                p.op(eng, lambda e: e.tensor_scalar(out, a, s1, s2, op0, op1), reads, writes)

        def STT(out, a, s, b, op0, op1, reads, writes):
            p.op('dve', lambda e: e.scalar_tensor_tensor(out, a, s, b, op0, op1), reads, writes)

        def CP(eng, out, in_, reads, writes):
            if eng == 'act':
                ACT(out, in_, AF.Copy, reads, writes)
            else:
                p.op(eng, lambda e: e.tensor_copy(out, in_), reads, writes)

        def RED(out, in_, reads, writes):
            p.op('dve', lambda e: e.tensor_reduce(out, in_, AX.X, ALU.add), reads, writes)

        def RSTD(ap, n_inv, reads_writes):
            ACT(ap, ap, AF.Ln, [reads_writes], [reads_writes], bias=EPS, scale=n_inv)
            ACT(ap, ap, AF.Exp, [reads_writes], [reads_writes], scale=-0.5)

        def gain(name):
            o, w = G_OFF[name]
            return gains[:, o:o + w]

        p.dma('sp', ident[:], ident_d, writes=['ident'], sem='c0')
        p.dma('sp', sel[:], sel_d, writes=['sel'], sem='c1')
        p.dma('sp', gains[:], gains_d, writes=['gains'], sem='c2')
        p.dma('sp', pp[:], pp_d, writes=['pp'], sem='c3')
        p.dma('sp', cT[:], cvecT, writes=['cT'], sem='c4')
        p.dma('pool', identb[:], ident_d, writes=['identb'], sem='c5')
        p.dma('pool', trib[:], tri_d, writes=['trib'], sem='c6')
        p.op('pool', lambda e: e.memset(onesb[:], 1.0), writes=['onesb'])
        ACT(cTb[:], cT[:], AF.Silu, ['cT'], ['cTb'])
        TS('dve', naq_s[:], gain('naq'), 0.125, None, ALU.mult, None, ['gains'], ['naq_s'])
        lam_init = 0.8 - 0.6 * float(np.exp(-0.3 * 1))
        TT('dve', junk[:, 0:64], gain('lq1'), gain('lk1'), ALU.mult, ['gains'], [('buf', 0)])
        RED(lamt[:, 0:1], junk[:, 0:64], [('buf', 0)], ['lamt'])
        TT('dve', junk[:, 64:128], gain('lq2'), gain('lk2'), ALU.mult, ['gains'], [('buf', 0)])
        RED(lamt[:, 1:2], junk[:, 64:128], [('buf', 0)], ['lamt'])
        ACT(lamt[:, 0:2], lamt[:, 0:2], AF.Exp, ['lamt'], ['lamt'])
        TT('dve', lamt[:, 2:3], lamt[:, 1:2], lamt[:, 0:1], ALU.subtract, ['lamt'], ['lamt'])
        TS('dve', lamt[:, 3:4], lamt[:, 2:3], -lam_init, None, ALU.add, None, ['lamt'], ['lamt'])
        ACT(es_sink[:], gain('sink'), AF.Exp, ['gains'], ['es_sink'])
        TS('dve', sublnS[:], pp[:, 32:33], 1.0 - lam_init, None, ALU.mult, None, ['pp'], ['sublnS'])

        def ada_phase(l):
            AR.reset()
            p.barrier()
            aw = [AR.alloc(8, 512) for _ in range(3)]
            adaw = W[l]['ada_w'].rearrange("(c p) n -> p c n", p=128)
            for n in range(12):
                s = n % 3
                b2 = n % 2
                for c in range(8):
                    p.dma('pool', aw[s][:, c, :], adaw[:, c, n * 512:(n + 1) * 512], writes=[('aw', s)], sem=f'aw{s}')
                p.dma('sp', brow[b2][:], W[l]['ada_b2'][:, n * 512:(n + 1) * 512], writes=[('brow', b2)], sem=f'brow{b2}')
                for c in range(8):
                    MM(pb[0][0:2, :], cTb[:, c * 2:c * 2 + 2], aw[s][:, c, :], c == 0, c == 7,
                       ['cTb', ('aw', s)], [PB[0]], inc=(c == 7))
                TT('dve', mrow[b2][:], pb[0][0:2, :], brow[b2][:], ALU.add, [PB[0], ('brow', b2)], [('mrow', b2)])
                sec, half = n // 2, n % 2
                if sec in (2, 5):
                    k = 0 if sec == 2 else 1
                    for j in range(2):
                        MM(pb[1 + j][:, :], sel[0:2, j * 128:(j + 1) * 128], mrow[b2][:], True, True,
                           ['sel', ('mrow', b2)], [PB[1 + j]])
                        CP('act', Gt[k][j][:, half * 512:(half + 1) * 512], pb[1 + j][:, :], [PB[1 + j]], [('G', k, j)])
                else:
                    si = {0: 0, 1: 1, 3: 2, 4: 3}[sec]
                    for q in range(4):
                        c = half * 4 + q
                        TR(pb[3][:, (si * 8 + c) * 2:(si * 8 + c) * 2 + 2], mrow[b2][0:2, q * 128:(q + 1) * 128], ident[0:2, 0:2],
                           [('mrow', b2), 'ident'], [PB[3]])
            CP('dve', modT[:], pb[3][:, 0:64], [PB[3]], ['modT'])
            for k in range(2):
                sh = modT[:, (2 * k) * 16:(2 * k) * 16 + 16]
                sc = modT[:, (2 * k + 1) * 16:(2 * k + 1) * 16 + 16]
                gn = pp[:, l * 16 + k * 8:l * 16 + k * 8 + 8]
                A = AB[:, k * 32:k * 32 + 16]
                B = AB[:, k * 32 + 16:k * 32 + 32]
                TS('dve', A, sc, 1.0, None, ALU.add, None, ['modT'], ['AB'])
                TT('dve', A.rearrange("p (c j) -> p c j", j=2), A.rearrange("p (c j) -> p c j", j=2), _bc_last(gn, 2), ALU.mult,
                   ['AB', 'pp'], ['AB'])
                CP('dve', B, sh, ['modT'], ['AB'])

        state = dict(xa=0, xo=0)

        def emit_h(S, src, t, k, dst, hkey, banks=None):
            j = 0 if t < NL else 1
            i = S.i
            xb = banks if banks is not None else S.xb
            X = S.xt
            p.dma('sp', X[:], xs[src][t * 128:(t + 1) * 128, :], reads=[(src, t)], writes=[('xt', i)], sem=f'xt{i}')
            TT('dve', S.buf[:], X[:], X[:], ALU.mult, [('xt', i)], [('buf', i)])
            RED(S.small[:, 0:1], S.buf[:], [('buf', i)], [('small', i)])
            RSTD(S.small[:, 0:1], 1.0 / D, ('small', i))
            ACT(S.buf[:], X[:], AF.Identity, [('xt', i), ('small', i)], [('buf', i)], scale=S.small[:, 0:1])
            for rnd in range(2):
                for c in range(rnd * 4, rnd * 4 + 4):
                    TR(pb[xb[c // 4]][:, (c % 4) * 128:(c % 4 + 1) * 128], S.buf[:, c * 128:(c + 1) * 128], ident[:],
                       [('buf', i), 'ident'], [PB[xb[c // 4]]], inc=(c % 4 == 3))
                for c in range(rnd * 4, rnd * 4 + 4):
                    ACT(dst[:, c, :], pb[xb[c // 4]][:, (c % 4) * 128:(c % 4 + 1) * 128], AF.Identity,
                        [PB[xb[c // 4]], 'AB'], [hkey], scale=AB[:, k * 32 + c * 2 + j:k * 32 + c * 2 + j + 1],
                        bias=AB[:, k * 32 + 16 + c * 2 + j:k * 32 + 16 + c * 2 + j + 1])

        def get_h(S, src, t, k, dst, hkey, compute, banks=None):
            i = S.i
            if compute:
                emit_h(S, src, t, k, dst, hkey, banks=banks)
                p.dma('sp', hts[k][t].rearrange("p (c n) -> p c n", c=8), dst, reads=[hkey], writes=[('hts', k, t)], sem=f'hs{i}')
            else:
                p.dma('sp', dst, hts[k][t].rearrange("p (c n) -> p c n", c=8), reads=[('hts', k, t)], writes=[hkey], sem=f'hl{i}')

        def proj(hT, hkey, wsb, wkey, col0, ncols, bank, M0=0, M1=128):
            for c in range(8):
                MM(pb[bank][0:M1 - M0, 0:ncols], hT[:, c, M0:M1], wsb[:, c, col0:col0 + ncols], c == 0, c == 7,
                   [hkey, wkey], [PB[bank]], inc=(c == 7))

        def load_cs(S, t, which):
            i = S.i
            if which == 64:
                p.dma('sp', S.cs[:, 0:128], cs64_d[t * 128:(t + 1) * 128, :], writes=[('cs', i)], sem=f'cs{i}')
            else:
                p.dma('sp', S.cs[:, 0:64], cs32_d[t * 128:(t + 1) * 128, :], writes=[('cs', i)], sem=f'cs{i}')

        def prep(S, src, G, d, gain_ap, out_b, rope=None):
            i = S.i
            kr, kb, ky, ks, kyb = ('raw', i), ('buf', i), ('y1', i), ('small', i), ('yb', i)
            sq = _view(S.buf[:], 0, [[d, G], [1, d]])
            TT('dve', sq, src, src, ALU.mult, [kr], [kb])
            RED(S.small[:, 8:8 + G], sq, [kb], [ks])
            RSTD(S.small[:, 8:8 + G], 1.0 / d, ks)
            yv = _view(S.y1[:], 0, [[d, G], [1, d]])
            TT('dve', yv, src, _bc_last(S.small[:, 8:8 + G], d), ALU.mult, [kr, ks], [ky])
            if rope is None:
                TT('pool', out_b, yv, _bc_mid(gain_ap, G), ALU.mult, [ky, 'gains', 'naq_s'], [kyb])
                return
            off, bs = rope
            TT('pool', yv, yv, _bc_mid(gain_ap, G), ALU.mult, [ky, 'gains'], [ky])
            if off > 0:
                CP('act', out_b[:, :, 0:off], _view(S.y1[:], 0, [[d, G], [1, off]]), [ky], [kyb])
            w = 4 * bs
            r = _view(S.y1[:], off, [[d, G], [1, w]])
            cosv = _bc_mid(S.cs[:, 0:w], G)
            ta = _view(S.buf[:], 0, [[w, G], [1, w]])
            TT('dve', ta, r, cosv, ALU.mult, [ky, ('cs', i)], [kb])
            for s_ in range(2):
                o_ = _view(S.buf[:], 512 + s_ * bs, [[w, G], [2 * bs, 2], [1, bs]])
                i0 = _view(S.y1[:], off + (1 - s_) * bs, [[d, G], [2 * bs, 2], [1, bs]])
                i1 = _view(S.cs[:], w + s_ * bs, [[0, G], [2 * bs, 2], [1, bs]])
                TT('pool', o_, i0, i1, ALU.mult, [ky, ('cs', i)], [kb])
            tb = _view(S.buf[:], 512, [[w, G], [1, w]])
            TT('dve', out_b[:, :, off:off + w], ta, tb, ALU.add, [kb], [kyb])

        def to_fm(S, ins, width, dst_of, dkey):
            i = S.i
            pt = ptbs[i][:, 0:512]
            for g0 in range(0, len(ins), 4):
                grp = ins[g0:g0 + 4]
                n = len(grp)
                for g, a in enumerate(grp):
                    TR(pt[0:width, g * 128:(g + 1) * 128], a, identb[:], [('yb', i), 'identb'], [PB[6 + i]], inc=(g == n - 1))
                CP('act', dst_of(g0, n), pt[0:width, 0:n * 128].rearrange("p (g t) -> p g t", g=n), [PB[6 + i]], [dkey])

        def residual_store(banks, xacc_i, Gk, j, dst, t):
            i = 0
            for hf in range(2):
                TT('dve', xo[i][:, hf * 512:(hf + 1) * 512], pb[banks[hf]][:, :], Gt[Gk][j][:, hf * 512:(hf + 1) * 512], ALU.mult,
                   [PB[banks[hf]], ('G', Gk, j)], [('xo', i)])
            TT('pool', xo[i][:], xo[i][:], xa[xacc_i][:], ALU.add, [('xo', i), ('xa', xacc_i)], [('xo', i)])
            if dst == 'y':
                if t < NL:
                    p.dma('sp', xs['y'][t * 128:(t + 1) * 128, :], xo[i][:], reads=[('xo', i)], writes=[(dst, t)], sem=f'xo{i}')
            else:
                p.dma('sp', xs[dst][t * 128:(t + 1) * 128, :], xo[i][:], reads=[('xo', i)], writes=[(dst, t)], sem=f'xo{i}')

        def load_xa(src, t):
            i = 0
            p.dma('sp', xa[i][:], xs[src][t * 128:(t + 1) * 128, :], reads=[(src, t)], writes=[('xa', i)], sem=f'xa{i}')
            return i

        def load_w(dst3, src3, key, sem, nch):
            for c in range(nch):
                p.dma('pool', dst3[:, c, :], src3[:, c, :], writes=[key], sem=sem)

        def merged(fn, items, nsets=NS):
            out = []
            for g0 in range(0, len(items), nsets):
                lists = []
                for k_, it in enumerate(items[g0:g0 + nsets]):
                    p.capture()
                    fn(SS[k_], it)
                    lists.append(p.end_capture())
                idx = [0] * len(lists)
                live = True
                while live:
                    live = False
                    for k_, L in enumerate(lists):
                        if idx[k_] < len(L):
                            out.append(L[idx[k_]])
                            idx[k_] += 1
                            live = True
            return out

        def overlapped_chunks(chunks_, q_tile_, att_chunk_, nsets_b=NS):
            p.replay([merged(lambda S, tt: q_tile_(S, tt, 0), list(enumerate(chunks_[0])))])
            for ci, tiles in enumerate(chunks_):
                p.capture()
                att_chunk_(ci, tiles)
                A = p.end_capture()
                B = merged(lambda S, tt: q_tile_(S, tt, (ci + 1) % 2), list(enumerate(chunks_[ci + 1])), nsets_b) if ci + 1 < len(chunks_) else []
                M = []
                nb_, ib = len(B), 0
                for ka, ia in enumerate(A):
                    M.append(ia)
                    tgt = ((ka + 1) * nb_) // max(len(A), 1)
                    while ib < tgt:
                        M.append(B[ib])
                        ib += 1
                M.extend(B[ib:])
                p.replay([M])

        def interleaved(fn, items):
            for g0 in range(0, len(items), NS):
                lists = []
                for k_, it in enumerate(items[g0:g0 + NS]):
                    p.capture()
                    fn(SS[k_], it)
                    lists.append(p.end_capture())
                p.replay(lists)

        def attend(N, units, scale, qT_of, dv, nb, PT, fused=None, qkey=('fm',), la=None, sbk=None):
            pN, pD = nb if nb is not None else (None, None)
            nu = len(units)

            LA = int(os.environ.get('ATT_LA', '2')) if la is None else la
            DUP = int(os.environ.get('ATT_DUP', '0'))
            SBK = (sbk if sbk is not None else ([0, 3, 5] if fused is not None else [0, 3, 6]))[:LA + 1]

            def emit_S(i):
                u = units[i]
                sbk = SBK[i % len(SBK)]
                pS = pb[sbk]
                nk, c0, c1 = u['nk'], u['c0'], u['c1']
                n = c1 - c0
                for rep in range(1 + DUP):
                    MM(pS[0:nk, 0:n], u['kT'], qT_of(c0, c1), True, u.get('bias') is None, [('fm',), qkey], [PB[sbk]], inc=(u.get('bias') is None))
                    if u.get('bias') is not None:
                        MM(pS[0:nk, 0:n], u['bias'][0], u['bias'][1], False, True, ['identb', 'lib'], [PB[sbk]])

            for i0 in range(min(LA, nu)):
                emit_S(i0)
            for i, u in enumerate(units):
                sbk = SBK[i % len(SBK)]
                pS = pb[sbk]
                nk, c0, c1 = u['nk'], u['c0'], u['c1']
                n = c1 - c0
                PTi = i % 4
                ACT(PT[PTi][0:nk, 0:n], pS[0:nk, 0:n], AF.Exp, [PB[sbk]], [('PT', PTi)], scale=scale)
                if i + LA < nu:
                    emit_S(i + LA)
                for (mc, mk) in u.get('masks', []):
                    TT('pool', PT[PTi][:, mc:mc + 128], PT[PTi][:, mc:mc + 128], mk, ALU.mult, [('PT', PTi), 'trib'], [('PT', PTi)])
                if fused is not None:
                    MM(pb[fused][:, c0:c1], u['v'], PT[PTi][0:nk, 0:n], i == 0, i == nu - 1, [('PT', PTi), ('v',)], [PB[fused]], inc=True)
                    continue
                MM(pb[pN][0:dv, c0:c1], u['v'], PT[PTi][0:nk, 0:n], i == 0, i == nu - 1, [('PT', PTi), ('v',)], [PB[pN]], inc=False)
                MM(pb[pD][0:dv, c0:c1], onesb[0:nk, 0:dv], PT[PTi][0:nk, 0:n], i == 0, i == nu - 1, [('PT', PTi), 'onesb'], [PB[pD]],
                   inc=True)

        def norm_fused(h, N, bank, mix, add_sink=False):
            par = h % 2
            dlo, nlo = (64, 0) if par == 0 else (0, 64)
            r_ = rec[par]
            rk = RECK[par]
            if add_sink:
                TS('dve', r_[dlo:dlo + 64, 0:N], pb[bank][dlo:dlo + 64, 0:N], es_sink[dlo:dlo + 64, h:h + 1], None, ALU.add, None,
                   [PB[bank], 'es_sink'], [rk])
                p.op('dve', lambda e: e.reciprocal(r_[dlo:dlo + 64, 0:N], r_[dlo:dlo + 64, 0:N]), [rk], [rk])
            else:
                p.op('dve', lambda e: e.reciprocal(r_[dlo:dlo + 64, 0:N], pb[bank][dlo:dlo + 64, 0:N]), [PB[bank]], [rk])
            TT('dve', mix[nlo:nlo + 64, h // 2, 0:N], pb[bank][nlo:nlo + 64, 0:N], r_[dlo:dlo + 64, 0:N], ALU.mult, [PB[bank], rk], [('mix',)])

        def norm_heads(h, N, pN, pD, mix, add_sink=None):
            r_ = rec[h % 2]
            rk = RECK[h % 2]
            if add_sink is None:
                p.op('dve', lambda e: e.reciprocal(r_[0:64, 0:N], pb[pD][0:64, 0:N]), [PB[pD]], [rk])
            else:
                TS('dve', r_[0:64, 0:N], pb[pD][0:64, 0:N], add_sink, None, ALU.add, None, [PB[pD], 'es_sink'], [rk])
                p.op('dve', lambda e: e.reciprocal(r_[0:64, 0:N], r_[0:64, 0:N]), [rk], [rk])
            TT('dve', mix[0:64, h, 0:N], pb[pN][0:64, 0:N], r_[0:64, 0:N], ALU.mult, [PB[pN], rk], [('mix',)])

        def out_proj(l, mixv, K, nslot, wo, tiles, xacc_src, dst, Gk):
            for ti, t in enumerate(tiles):
                j = 0 if t < NL else 1
                xi = load_xa(xacc_src, t)
                for hf in range(2):
                    for s_ in range(nslot):
                        bk = (1 + hf) if ti % 2 == 0 else (4 + hf)
                        MM(pb[bk][:, :], mixv[0:K, s_, ti * 128:(ti + 1) * 128], wo[0:K, s_, hf * 512:(hf + 1) * 512],
                           s_ == 0, s_ == nslot - 1, [('mix',), 'wo'], [PB[bk]], inc=(s_ == nslot - 1))
                residual_store((1, 2) if ti % 2 == 0 else (4, 5), xi, Gk, j, dst, t)

        def mixer_pass(l, mx, src, acc, dst):
            AR.reset()
            p.barrier()
            ctxq = (l == 0)
            win_d = W[l]['w_in'].rearrange("(c p) n -> p c n", p=128)
            PT = [AR.alloc(512) for _ in range(4)]
            for S in SS:
                S.hT = AR.alloc(8, 128)
                S.yb = AR.alloc(1024)
                S.cT = AR.alloc(2, 128)
            HK = lambda S: ('hT', S.i)
            HCOMP = [mx in ('A', 'C')]
            chunks = [list(range(c * 4, c * 4 + 4)) for c in range(4)] + ([[16, 17]] if ctxq else [])
            NB = [(1, 2), (4, 5)]
            if mx == 'A':
                wi = AR.alloc(8, 416)
                load_w(wi, win_d[:, :, 0:416], 'wi', 'wi', 8)
                wuq = AR.alloc(2, 768)
                load_w(wuq, W[0]['w_uq'].rearrange("(c p) n -> p c n", p=128), 'wuq', 'wuq', 2)
                wukv = AR.alloc(1, 1024)
                load_w(wukv, W[0]['w_ukv'].rearrange("(c p) n -> p c n", p=128), 'wukv', 'wukv', 1)
                wo = AR.alloc(4, 1024)
                load_w(wo, W[l]['w_out'][0:512, :].rearrange("(s p) n -> p s n", p=128), 'wo', 'wo', 4)
                kT = AR.alloc(8, T)
                vv = AR.alloc(NT, 8, 128)
                p.op('pool', lambda e: e.memset(vv, 1.0), [], [('v',)])
                qTs = [AR.alloc(8, 512) for _ in range(2)]
                mix = AR.alloc(4, 512)

                def kv_tile(S, t):
                    lat = t < NL
                    i = S.i
                    if lat:
                        load_cs(S, t, 32)
                    get_h(S, src, t, 0, S.hT, HK(S), HCOMP[0])
                    proj(S.hT, HK(S), wi, 'wi', 256, 160, S.pj)
                    CP('act', S.raw[:, 0:160], pb[S.pj][:, 0:160], [PB[S.pj]], [('raw', i)])
                    prep(S, _view(S.raw[:], 0, [[128, 1], [1, 128]]), 1, 128, gain('kva'), _view(S.yb, 0, [[128, 1], [1, 128]]))
                    to_fm(S, [S.yb[:, 0:128]], 128, lambda g0, n: S.cT[:, 0:1, :], ('cT', i))
                    for hf in range(2):
                        MM(pb[S.xb[hf]][:, :], S.cT[:, 0, :], wukv[:, 0, hf * 512:(hf + 1) * 512], True, True, [('cT', i), 'wukv'], [PB[S.xb[hf]]])
                    for hf in range(2):
                        kvv = pb[S.xb[hf]][:, :].rearrange("p (h e) -> p h e", h=4)
                        for par in range(2):
                            CP('act', _view(vv, (t * 8 + hf * 4 + par) * 128 + par * 64, [[256, 2], [1, 64]]),
                               _view(pb[S.xb[hf]][:, :], par * 128 + 64, [[256, 2], [1, 64]]), [PB[S.xb[hf]]], [('v',)])
                        CP('act', _view(S.raw[:], 256 + hf * 4 * 96, [[96, 4], [1, 64]]), kvv[:, :, 0:64], [PB[S.xb[hf]]], [('raw', i)])
                    CP('pool', _view(S.raw[:], 256 + 64, [[96, 8], [1, 32]]), _bc_mid(S.raw[:, 128:160], 8), [('raw', i)], [('raw', i)])
                    ybv = _view(S.yb, 0, [[96, 8], [1, 96]])
                    prep(S, _view(S.raw[:], 256, [[96, 8], [1, 96]]), 8, 96, gain('kn96'), ybv, rope=(64, 8) if lat else None)
                    to_fm(S, [ybv[:, h, :] for h in range(8)], 96, lambda g0, n: kT[0:96, g0:g0 + n, t * 128:(t + 1) * 128], ('fm',))

                def q_tile(S, tt, qb):
                    ti, t = tt
                    lat = t < NL
                    i = S.i
                    bk = 6 + i
                    if lat:
                        load_cs(S, t, 32)
                    get_h(S, src, t, 0, S.hT, HK(S), HCOMP[0])
                    proj(S.hT, HK(S), wi, 'wi', 0, 256, bk)
                    CP('act', S.raw[:, 0:256], pb[bk][:, 0:256], [PB[bk]], [('raw', i)])
                    prep(S, _view(S.raw[:], 0, [[256, 1], [1, 256]]), 1, 256, gain('qa'), _view(S.yb, 0, [[256, 1], [1, 256]]))
                    to_fm(S, [S.yb[:, 0:128], S.yb[:, 128:256]], 128, lambda g0, n: S.cT[:, :, :], ('cT', i))
                    for (c0_, n_) in ((0, 512), (512, 256)):
                        for c in range(2):
                            MM(pb[bk][:, 0:n_], S.cT[:, c, :], wuq[:, c, c0_:c0_ + n_], c == 0, c == 1, [('cT', i), 'wuq'], [PB[bk]], inc=(c == 1))
                        CP('act', S.raw[:, c0_:c0_ + n_], pb[bk][:, 0:n_], [PB[bk]], [('raw', i)])
                    ybv = _view(S.yb, 0, [[96, 8], [1, 96]])
                    prep(S, _view(S.raw[:], 0, [[96, 8], [1, 96]]), 8, 96, gain('qn96'), ybv, rope=(64, 8) if lat else None)
                    to_fm(S, [ybv[:, h, :] for h in range(8)], 96, lambda g0, n: qTs[qb][0:96, g0:g0 + n, ti * 128:(ti + 1) * 128], ('qT', qb))

                KS = os.environ.get('KSTOP', '')
                interleaved(kv_tile, list(range(2 if KS == 'kv1' else NT)))
                HCOMP[0] = False
                if KS in ('kv1', 'kv'):
                    chunks = []

                def att_chunk(ci, tiles):
                    N = 128 * len(tiles)
                    lat = tiles[0] < NL
                    qT_ = qTs[ci % 2]
                    ktiles = [16, 17] + (list(range(16)) if lat else [])
                    for h in range(8):
                        units = [dict(kT=kT[0:96, h, kt * 128:(kt + 1) * 128], nk=128, c0=0, c1=N, v=vv[:, kt, h, :]) for kt in ktiles]
                        bank = (1, 2, 4)[h % 3]
                        attend(N, units, 96 ** -0.5, lambda c0, c1, h=h: qT_[0:96, h, c0:c1], 64, None, PT, fused=bank, qkey=('qT', ci % 2))
                        norm_fused(h, N, bank, mix)
                    out_proj(l, mix, 128, 4, wo, tiles, acc, dst, 0)
                if chunks:
                    overlapped_chunks(chunks, q_tile, att_chunk)
            elif mx == 'B':
                wi = AR.alloc(8, 1536)
                load_w(wi, win_d[:, :, 416:1952], 'wi', 'wi', 8)
                wo = AR.alloc(8, 1024)
                load_w(wo[0:64], W[l]['w_out'][512:1024, :].rearrange("(h p) n -> p h n", p=64), 'wo', 'wo', 8)
                kT = AR.alloc(4, T)
                vv = AR.alloc(32, 8, 64)
                vvc = AR.alloc(2, 8, 64)
                qT = AR.alloc(4, 512)
                mix = AR.alloc(8, 512)
                libt = AR.alloc(4, 960)
                p.dma('sp', SS[1].raw[:, 0:960], nam_d, writes=[('raw', 1)], sem='c7')
                for hc in range(4):
                    p.dma('sp', SS[0].raw[:, 0:960], lib_d[:, hc * 960:(hc + 1) * 960], writes=[('raw', 0)], sem='c8')
                    TT('pool', libt[:, hc, :], SS[0].raw[:, 0:960], SS[1].raw[:, 0:960], ALU.add, [('raw', 0), ('raw', 1)], ['lib'])

                def kv_tile(S, t):
                    lat = t < NL
                    i = S.i
                    get_h(S, src, t, 0, S.hT, HK(S), HCOMP[0])
                    proj(S.hT, HK(S), wi, 'wi', 512, 512, S.pj)
                    CP('act', S.raw[:, 0:512], pb[S.pj][:, :], [PB[S.pj]], [('raw', i)])
                    prep(S, _view(S.raw[:], 0, [[64, 8], [1, 64]]), 8, 64, gain('nak'), _view(S.yb, 0, [[64, 8], [1, 64]]))
                    to_fm(S, [S.yb[:, c * 128:(c + 1) * 128] for c in range(4)], 128, lambda g0, n: kT[:, g0:g0 + n, t * 128:(t + 1) * 128], ('fm',))
                    if lat:
                        for hf in range(2):
                            proj(S.hT, HK(S), wi, 'wi', 1024, 512, S.xb[hf], M0=hf * 64, M1=hf * 64 + 64)
                            CP('act', vv[0:64, 2 * t + hf, :, :], pb[S.xb[hf]][0:64, :].rearrange("p (h e) -> p h e", h=8), [PB[S.xb[hf]]], [('v',)])
                    else:
                        proj(S.hT, HK(S), wi, 'wi', 1024, 512, S.xb[0])
                        CP('act', vvc[:, t - NL, :, :], pb[S.xb[0]][:, :].rearrange("p (h e) -> p h e", h=8), [PB[S.xb[0]]], [('v',)])

                def q_tile(S, tt):
                    ti, t = tt
                    i = S.i
                    get_h(S, src, t, 0, S.hT, HK(S), HCOMP[0])
                    proj(S.hT, HK(S), wi, 'wi', 0, 512, S.pj)
                    CP('act', S.raw[:, 0:512], pb[S.pj][:, :], [PB[S.pj]], [('raw', i)])
                    prep(S, _view(S.raw[:], 0, [[64, 8], [1, 64]]), 8, 64, naq_s[:], _view(S.yb, 0, [[64, 8], [1, 64]]))
                    to_fm(S, [S.yb[:, c * 128:(c + 1) * 128] for c in range(4)], 128, lambda g0, n: qT[:, g0:g0 + n, ti * 128:(ti + 1) * 128], ('fm',))

                interleaved(kv_tile, list(range(NT)))
                HCOMP[0] = False
                for tiles in chunks:
                    N = 128 * len(tiles)
                    lat = tiles[0] < NL
                    interleaved(q_tile, list(enumerate(tiles)))
                    for h in range(8):
                        ho, hc = (h % 2) * 64, h // 2
                        units = [dict(kT=kT[ho:ho + 64, hc, (NL + i_) * 128:(NL + i_ + 1) * 128], nk=128, c0=0, c1=N, v=vvc[:, i_, h, :])
                                 for i_ in range(2)]
                        if lat:
                            r0 = (tiles[0] * 128) // 64
                            for kr in range(32):
                                rs = [r for r in range(r0, r0 + 8) if min(max(r - 4, 0), 24) <= kr <= min(max(r - 4, 0), 24) + 7]
                                if not rs:
                                    continue
                                ra, rb = rs[0], rs[-1]
                                c0, c1 = (ra - r0) * 64, (rb - r0 + 1) * 64
                                L0 = (7 - (kr - ra)) * 64
                                units.append(dict(kT=kT[ho:ho + 64, hc, kr * 64:(kr + 1) * 64], nk=64, c0=c0, c1=c1, v=vv[0:64, kr, h, :],
                                                  bias=(identb[ho:ho + 64, ho:ho + 64], libt[ho:ho + 64, hc, L0:L0 + (c1 - c0)])))
                        pN, pD = NB[h % 2]
                        attend(N, units, 1.0, lambda c0, c1, ho=ho, hc=hc: qT[ho:ho + 64, hc, c0:c1], 64, (pN, pD), PT)
                        norm_heads(h, N, pN, pD, mix)
                    out_proj(l, mix, 64, 8, wo, tiles, acc, dst, 0)
            elif mx == 'C':
                wi = AR.alloc(8, 768)
                load_w(wi, win_d[:, :, 0:768], 'wi', 'wi', 8)
                wo = AR.alloc(4, 1024)
                load_w(wo, W[l]['w_out'][0:512, :].rearrange("(s p) n -> p s n", p=128), 'wo', 'wo', 4)
                kT = AR.alloc(1, T)
                vvE = AR.alloc(NT, 2, 128)
                vvO = AR.alloc(NT, 2, 128)
                p.op('pool', lambda e: e.memset(vvE, 1.0), [], [('v',)])
                p.op('pool', lambda e: e.memset(vvO, 1.0), [], [('v',)])
                qTs = [AR.alloc(4, 512) for _ in range(2)]
                mix = AR.alloc(4, 512)

                def kv_tile(S, t):
                    lat = t < NL
                    i = S.i
                    if lat:
                        load_cs(S, t, 64)
                    get_h(S, src, t, 0, S.hT, HK(S), HCOMP[0])
                    proj(S.hT, HK(S), wi, 'wi', 512, 256, S.pj)
                    CP('act', S.raw[:, 0:256], pb[S.pj][:, 0:256], [PB[S.pj]], [('raw', i)])
                    prep(S, _view(S.raw[:], 0, [[64, 2], [1, 64]]), 2, 64, gain('gk'), _view(S.yb, 0, [[64, 2], [1, 64]]), rope=(0, 16) if lat else None)
                    to_fm(S, [S.yb[:, 0:128]], 128, lambda g0, n: kT[:, 0:1, t * 128:(t + 1) * 128], ('fm',))
                    CP('pool', vvE[:, t, :, 0:64], _view(S.raw[:], 128, [[64, 2], [1, 64]]), [('raw', i)], [('v',)])
                    CP('pool', vvO[:, t, :, 64:128], _view(S.raw[:], 128, [[64, 2], [1, 64]]), [('raw', i)], [('v',)])

                def q_tile(S, tt, qb):
                    ti, t = tt
                    i = S.i
                    bk = 6 + i
                    load_cs(S, t, 64)
                    get_h(S, src, t, 0, S.hT, HK(S), HCOMP[0])
                    proj(S.hT, HK(S), wi, 'wi', 0, 512, bk)
                    CP('act', S.raw[:, 0:512], pb[bk][:, :], [PB[bk]], [('raw', i)])
                    prep(S, _view(S.raw[:], 0, [[64, 8], [1, 64]]), 8, 64, gain('gq'), _view(S.yb, 0, [[64, 8], [1, 64]]), rope=(0, 16))
                    pt = ptbs[i][:, 0:512]
                    for h in range(8):
                        kvh, g = h // 4, h % 4
                        TR(pt[kvh * 64:(kvh + 1) * 64, g * 128:(g + 1) * 128], S.yb[:, h * 64:(h + 1) * 64], identb[:], [('yb', i), 'identb'],
                           [PB[6 + i]], inc=(h == 7))
                    CP('act', qTs[qb][:, :, ti * 128:(ti + 1) * 128], pt[:, 0:512].rearrange("p (g t) -> p g t", g=4), [PB[6 + i]], [('qT', qb)])

                interleaved(kv_tile, list(range(NT)))
                HCOMP[0] = False

                def att_chunk(cch, tiles):
                    N = 512
                    qT_ = qTs[cch % 2]
                    for h in range(8):
                        kvh, g = h // 4, h % 4
                        ho = kvh * 64
                        vv = vvE if h % 2 == 0 else vvO
                        units = [dict(kT=kT[ho:ho + 64, 0, (NL + i_) * 128:(NL + i_ + 1) * 128], nk=128, c0=0, c1=N, v=vv[:, NL + i_, kvh, :])
                                 for i_ in range(2)]
                        for m in range(max(4 * cch - 1, 0), min(4 * cch + 4, 15) + 1):
                            nlo, nhi = max(m - 1, 4 * cch), min(m + 1, 4 * cch + 3)
                            masks = []
                            for n_ in range(nlo, nhi + 1):
                                if n_ == m + 1:
                                    masks.append(((n_ - nlo) * 128, trib[:, 0:128]))
                                elif n_ == m - 1:
                                    masks.append(((n_ - nlo) * 128, trib[:, 128:256]))
                            units.append(dict(kT=kT[ho:ho + 64, 0, m * 128:(m + 1) * 128], nk=128, c0=(nlo - 4 * cch) * 128,
                                              c1=(nhi - 4 * cch + 1) * 128, v=vv[:, m, kvh, :], masks=masks))
                        bank = (1, 2, 4)[h % 3]
                        attend(N, units, 0.125, lambda c0, c1, ho=ho, g=g: qT_[ho:ho + 64, g, c0:c1], 64, None, PT, fused=bank, qkey=('qT', cch % 2))
                        norm_fused(h, N, bank, mix, add_sink=True)
                    out_proj(l, mix, 128, 4, wo, tiles, acc, dst, 0)
                overlapped_chunks([list(range(c_ * 4, c_ * 4 + 4)) for c_ in range(4)], q_tile, att_chunk)
            else:
                wi = AR.alloc(8, 1536)
                load_w(wi, win_d[:, :, 768:2304], 'wi', 'wi', 8)
                wo = AR.alloc(4, 1024)
                load_w(wo, W[l]['w_out'][512:1024, :].rearrange("(h p) n -> p h n", p=128), 'wo', 'wo', 4)
                kT = AR.alloc(4, T)
                vv = AR.alloc(NT, 4, 128)
                qTs = [AR.alloc(4, 512) for _ in range(2)]
                mix = AR.alloc(4, 512)

                def kv_tile(S, t):
                    lat = t < NL
                    i = S.i
                    if lat:
                        load_cs(S, t, 64)
                    get_h(S, src, t, 0, S.hT, HK(S), HCOMP[0])
                    proj(S.hT, HK(S), wi, 'wi', 512, 512, S.pj)
                    CP('act', S.raw[:, 0:512], pb[S.pj][:, :], [PB[S.pj]], [('raw', i)])
                    prep(S, _view(S.raw[:], 0, [[64, 8], [1, 64]]), 8, 64, gain('dk'), _view(S.yb, 0, [[64, 8], [1, 64]]), rope=(0, 16) if lat else None)
                    to_fm(S, [S.yb[:, c * 128:(c + 1) * 128] for c in range(4)], 128, lambda g0, n: kT[:, g0:g0 + n, t * 128:(t + 1) * 128], ('fm',))
                    proj(S.hT, HK(S), wi, 'wi', 1024, 512, S.xb[0])
                    CP('act', vv[:, t, :, :], pb[S.xb[0]][:, :].rearrange("p (h e) -> p h e", h=4), [PB[S.xb[0]]], [('v',)])

                def q_tile(S, tt, qb):
                    ti, t = tt
                    i = S.i
                    bk = 6 + i
                    load_cs(S, t, 64)
                    get_h(S, src, t, 0, S.hT, HK(S), HCOMP[0])
                    proj(S.hT, HK(S), wi, 'wi', 0, 512, bk)
                    CP('act', S.raw[:, 0:512], pb[bk][:, :], [PB[bk]], [('raw', i)])
                    prep(S, _view(S.raw[:], 0, [[64, 8], [1, 64]]), 8, 64, gain('dq'), _view(S.yb, 0, [[64, 8], [1, 64]]), rope=(0, 16))
                    to_fm(S, [S.yb[:, c * 128:(c + 1) * 128] for c in range(4)], 128, lambda g0, n: qTs[qb][:, g0:g0 + n, ti * 128:(ti + 1) * 128], ('qT', qb))

                interleaved(kv_tile, list(range(NT)))
                HCOMP[0] = False

                def att_chunk(cch, tiles):
                    N = 512
                    qT_ = qTs[cch % 2]
                    for hh in range(4):
                        for cc in range(2):
                            ho = cc * 64
                            units = [dict(kT=kT[ho:ho + 64, hh, kt * 128:(kt + 1) * 128], nk=128, c0=0, c1=N, v=vv[:, kt, hh, :])
                                     for kt in ([16, 17] + list(range(16)))]
                            attend(N, units, 0.125, lambda c0, c1, ho=ho, hh=hh: qT_[ho:ho + 64, hh, c0:c1], 128, NB[cc], PT, qkey=('qT', cch % 2), la=2, sbk=[0, 3, 7])
                        for cc in range(2):
                            pN, pD = NB[cc]
                            p.op('dve', lambda e, cc=cc, pD=pD: e.reciprocal(rec[cc][:, :], pb[pD][:, :]), [PB[pD]], [RECK[cc]])
                            TT('dve', of[cc][:, :], pb[pN][:, :], rec[cc][:, :], ALU.mult, [PB[pN], RECK[cc]], [OFK[cc]])
                        STT(of[0][:, :], of[1][:, :], lamt[:, 3:4], of[0][:, :], ALU.mult, ALU.add, [OFK[0], 'lamt'], [OFK[0]])
                        TT('pool', PT[0][:, :], of[0][:, :], of[0][:, :], ALU.mult, [OFK[0]], [('PT', 0)])
                        MM(pb[0][:, :], onesb[:, :], PT[0][:, :], True, True, [('PT', 0), 'onesb'], [PB[0]])
                        ACT(rec[0][:, :], pb[0][:, :], AF.Ln, [PB[0]], [RECK[0]], bias=EPS, scale=1.0 / 128)
                        ACT(rec[0][:, :], rec[0][:, :], AF.Exp, [RECK[0]], [RECK[0]], scale=-0.5)
                        STT(mix[:, hh, :], of[0][:, :], sublnS[:, 0:1], rec[0][:, :], ALU.mult, ALU.mult, [OFK[0], RECK[0], 'sublnS'], [('mix',)])
                    out_proj(l, mix, 128, 4, wo, tiles, acc, dst, 0)
                overlapped_chunks([list(range(c_ * 4, c_ * 4 + 4)) for c_ in range(4)], q_tile, att_chunk, nsets_b=1)

        def ffn_half(l, hf, hsrc, acc, dst):
            AR.reset()
            p.barrier()
            wg = AR.alloc(8, 1408)
            wu = AR.alloc(8, 1408)
            wd = AR.alloc(11, 1024)
            load_w(wg, W[l]['wg'].rearrange("(c p) n -> p c n", p=128)[:, :, hf * 1408:(hf + 1) * 1408], 'wg', 'wg', 8)
            load_w(wu, W[l]['wu'].rearrange("(c p) n -> p c n", p=128)[:, :, hf * 1408:(hf + 1) * 1408], 'wu', 'wu', 8)
            load_w(wd, W[l]['wd'][hf * 1408:(hf + 1) * 1408, :].rearrange("(f p) n -> p f n", p=128), 'wd', 'wd', 11)
            h2s = [AR.alloc(8, 512) for _ in range(2)]
            act = AR.alloc(11, 512)
            sg = [AR.alloc(512) for _ in range(2)]
            chunks = [list(range(c * 4, c * 4 + 4)) for c in range(4)] + ([[16, 17]] if l == 0 else [])

            if hf == 0:
                for S in SS:
                    S.hT = AR.alloc(8, 128)

                def norm_tile(S, t):
                    get_h(S, hsrc, t, 1, S.hT, ('hT', S.i), True)
                interleaved(norm_tile, [t for tiles in chunks for t in tiles])

            def emit_h_tile(ci_, ti):
                t = chunks[ci_][ti]
                p.dma('sp', h2s[ci_ % 2][:, :, ti * 128:(ti + 1) * 128], hts[1][t].rearrange("p (c n) -> p c n", c=8),
                      reads=[('hts', 1, t)], writes=[('h2', ci_ % 2, ti)], sem=f'h2l{ci_ % 2}_{ti}')

            for ti in range(len(chunks[0])):
                emit_h_tile(0, ti)
            for ci_, tiles in enumerate(chunks):
                N = 128 * len(tiles)
                h2 = h2s[ci_ % 2]
                hks = [('h2', ci_ % 2, ti_) for ti_ in range(len(tiles))]
                nxt = len(chunks[ci_ + 1]) if ci_ + 1 < len(chunks) else 0
                for f in range(11):
                    bg, bu = (f % 2) * 2, (f % 2) * 2 + 1
                    for c in range(8):
                        MM(pb[bg][:, 0:N], wg[:, c, f * 128:(f + 1) * 128], h2[:, c, 0:N], c == 0, c == 7, hks + ['wg'], [PB[bg]], inc=(c == 7))
                    for c in range(8):
                        MM(pb[bu][:, 0:N], wu[:, c, f * 128:(f + 1) * 128], h2[:, c, 0:N], c == 0, c == 7, hks + ['wu'], [PB[bu]], inc=(c == 7))
                    ACT(sg[f % 2][:, 0:N], pb[bg][:, 0:N], AF.Silu, [PB[bg]], [('sg', f % 2)])
                    TT('dve', act[:, f, 0:N], pb[bu][:, 0:N], sg[f % 2][:, 0:N], ALU.mult, [PB[bu], ('sg', f % 2)], [('act',)])
                    if f % 2 == 1 and (f // 2) < nxt:
                        emit_h_tile(ci_ + 1, f // 2)
                for ti, t in enumerate(tiles):
                    j = 0 if t < NL else 1
                    xi2 = load_xa(acc, t)
                    for h2_ in range(2):
                        for f in range(11):
                            bk = (4 + h2_) if ti % 2 == 0 else (2 + h2_)
                            MM(pb[bk][:, :], act[:, f, ti * 128:(ti + 1) * 128], wd[:, f, h2_ * 512:(h2_ + 1) * 512], f == 0, f == 10,
                               [('act',), 'wd'], [PB[bk]], inc=(f == 10))
                    residual_store((4, 5) if ti % 2 == 0 else (2, 3), xi2, 1, j, dst, t)

        seq = []
        if 0 in layers:
            seq += [('ada', 0), ('mix', 0, 'A', 'xin', 'xin', 'xs_a'), ('mix', 0, 'B', 'xin', 'xs_a', 'xs_b'),
                    ('ffn', 0, 0, 'xs_b', 'xs_b', 'xs_c'), ('ffn', 0, 1, 'xs_b', 'xs_c', 'xs_d')]
        if 1 in layers:
            s1 = 'xs_d' if 0 in layers else 'xin'
            seq += [('ada', 1), ('mix', 1, 'C', s1, s1, 'xs_a'), ('mix', 1, 'D', s1, 'xs_a', 'xs_b'),
                    ('ffn', 1, 0, 'xs_b', 'xs_b', 'xs_c'), ('ffn', 1, 1, 'xs_b', 'xs_c', 'y')]
        if stop_after is not None:
            seq = seq[:stop_after]
        last_dst = None
        for st in seq:
            if st[0] == 'ada':
                ada_phase(st[1])
            elif st[0] == 'mix':
                mixer_pass(st[1], st[2], st[3], st[4], st[5])
                last_dst = st[5]
            else:
                ffn_half(st[1], st[2], st[3], st[4], st[5])
                last_dst = st[5]
        if last_dst is None:
            p.dma('sp', xs['y'][0:128, :], Gt[0][0][:], reads=[('G', 0, 0)], writes=[('y', 0)], sem='dbg0')
            p.dma('sp', xs['y'][128:256, :], Gt[0][1][:], reads=[('G', 0, 1)], writes=[('y', 1)], sem='dbg0')
            p.dma('sp', xs['y'][256:384, :], Gt[1][0][:], reads=[('G', 1, 0)], writes=[('y', 2)], sem='dbg0')
            p.dma('sp', xs['y'][384:512, 0:64], AB[:], reads=['AB'], writes=[('y', 3)], sem='dbg0')
            p.finish([('y', t) for t in range(4)])
            p.emit()
            return nc
        if last_dst != 'y':
            for t in range(NL):
                S = SS[t % 2]
                p.dma('sp', S.xt[:], xs[last_dst][t * 128:(t + 1) * 128, :], reads=[(last_dst, t)], writes=[('xt', S.i)], sem=f'xt{S.i}')
                p.dma('sp', xs['y'][t * 128:(t + 1) * 128, :], S.xt[:], reads=[('xt', S.i)], writes=[('y', t)], sem=f'dbg{t % 2}')
        p.finish([('y', t) for t in range(NL)])
        p.emit()
    return nc


_CACHE = {}


def _host_inputs(inputs, layers=(0, 1)):
    f = lambda a: np.ascontiguousarray(np.asarray(a, dtype=np.float32))
    x, c, ctx, c_ctx = f(inputs['x']), f(inputs['c']), f(inputs['ctx']), f(inputs['c_ctx'])
    cs64, cs32 = _rope_tables()
    lib, nam = _na_tables(f(inputs['l0_na_rpb']))
    gl = []
    for n in ['l0_mla_qa_g', 'l0_mla_kva_g', 'l0_mla_qn_g', 'l0_mla_kn_g', 'l0_na_qn_g', 'l0_na_kn_g', 'l1_gqa_qn_g', 'l1_gqa_kn_g',
              'l1_diff_qn_g', 'l1_diff_kn_g', 'l1_gqa_sink', 'l1_diff_lq1', 'l1_diff_lk1', 'l1_diff_lq2', 'l1_diff_lk2']:
        gl.append(f(inputs[n]).reshape(-1))
    gains = np.ascontiguousarray(np.broadcast_to(np.concatenate(gl)[None, :], (128, GW)))
    pp = np.zeros((128, 33), np.float32)
    for i, n in enumerate(['l0_norm1_g', 'l0_norm2_g', 'l1_norm1_g', 'l1_norm2_g']):
        pp[:, i * 8:(i + 1) * 8] = f(inputs[n]).reshape(8, 128).T
    pp[:, 32] = f(inputs['l1_diff_subln_g'])
    ident = np.eye(128, dtype=np.float32)
    jj = np.arange(128)[:, None]
    ii = np.arange(128)[None, :]
    tri = np.concatenate([(ii <= jj), (jj <= ii)], axis=1).astype(np.float32)
    sel = np.zeros((2, 256), np.float32)
    sel[0, 0:128] = 1.0
    sel[1, 128:256] = 1.0
    shared = dict(gains=gains, pp=pp, ident=ident, tri=tri, sel=sel, cs64=cs64, cs32=cs32,
                  nalib=np.ascontiguousarray(lib.reshape(128, 4 * 960)), namask=nam)
    for l in (0, 1):
        shared[f'l{l}_ada_w'] = f(inputs[f'l{l}_ada_w'])
        shared[f'l{l}_ada_b2'] = np.ascontiguousarray(np.broadcast_to(f(inputs[f'l{l}_ada_b'])[None, :], (2, 6 * D)))
        shared[f'l{l}_w_in'] = f(inputs[f'l{l}_w_in'])
        shared[f'l{l}_w_out'] = f(inputs[f'l{l}_w_out'])
        shared[f'l{l}_ffn_w_gate'] = f(inputs[f'l{l}_ffn_w_gate'])
        shared[f'l{l}_ffn_w_up'] = f(inputs[f'l{l}_ffn_w_up'])
        shared[f'l{l}_ffn_w_down'] = f(inputs[f'l{l}_ffn_w_down'])
    shared['l0_mla_w_uq'] = f(inputs['l0_mla_w_uq'])
    shared['l0_mla_w_ukv'] = f(inputs['l0_mla_w_ukv'])
    maps = []
    for b in range(x.shape[0]):
        m = dict(shared)
        m['xin'] = np.ascontiguousarray(np.concatenate([x[b], ctx[b]], axis=0))
        cv = np.stack([c[b], c_ctx], axis=0)
        m['cvecT'] = np.ascontiguousarray(cv.reshape(2, 8, 128).transpose(2, 1, 0).reshape(128, 16))
        maps.append(m)
    return maps


def kernel(**inputs):
    maps = _host_inputs(inputs)
    if 'nc' not in _CACHE:
        _CACHE['nc'] = build()
    res = run_bass_kernel_spmd(_CACHE['nc'], maps, core_ids=list(range(len(maps))))
    return np.stack([np.asarray(r['y'], dtype=np.float32) for r in res.results], axis=0)
```

```python
import os
import numpy as np
import concourse.bass as bass
import concourse.mybir as mybir
from concourse.bass_utils import run_bass_kernel_spmd
from concourse.alu_op_type import AluOpType as ALU
from contextlib import ExitStack

F32 = mybir.dt.float32
BF16 = mybir.dt.bfloat16
AF = mybir.ActivationFunctionType
AX = mybir.AxisListType


STRICT = int(os.environ.get('KSTRICT', '1'))


class Prog:
    ENG = ('pe', 'act', 'dve', 'pool', 'sp')

    def __init__(self, nc):
        self.nc = nc
        self.q = {e: [] for e in self.ENG}
        self.cnt = {}
        self.res = {}
        self.known = {e: {} for e in self.ENG}
        self.pending = {e: {} for e in self.ENG}

    def _need(self, eng, toks, waits, skip=None):
        for s, v in toks.items():
            if s == skip:
                continue
            if self.known[eng].get(s, 0) < v and waits.get(s, 0) < v:
                waits[s] = v

    def _record(self, eng, fn, reads, writes, tok, incspec, is_dma=False):
        waits = {}
        own = 'e_' + eng
        wskip = tok[0] if is_dma else (own if (eng == 'pe' or not STRICT) else None)
        for r in reads:
            st = self.res.get(r)
            if st is not None:
                self._need(eng, st[0], waits, skip=own if eng == 'pe' else None)
        for w in writes:
            st = self.res.get(w)
            if st is not None:
                self._need(eng, st[0], waits, skip=wskip)
                self._need(eng, st[1], waits, skip=wskip)
        for s, v in waits.items():
            self.known[eng][s] = v
        for s, v in self.pending[eng].items():
            if waits.get(s, 0) < v:
                waits[s] = v
        self.pending[eng] = {}
        self.q[eng].append((sorted(waits.items()), fn, incspec))
        for r in reads:
            st = self.res.setdefault(r, [{}, {}])
            if st[1].get(tok[0], 0) < tok[1]:
                st[1][tok[0]] = tok[1]
        for w in writes:
            self.res[w] = [{tok[0]: tok[1]}, {}]

    def capture(self):
        self.cap = []
        return self.cap

    def end_capture(self):
        c = self.cap
        self.cap = None
        return c

    def replay(self, lists):
        idx = [0] * len(lists)
        live = True
        while live:
            live = False
            for k, L in enumerate(lists):
                if idx[k] < len(L):
                    it = L[idx[k]]
                    idx[k] += 1
                    live = True
                    if it[0] == 'op':
                        self.op(*it[1:])
                    else:
                        self.dma(*it[1:-1], **it[-1])

    def op(self, eng, fn, reads=(), writes=(), inc=True):
        if getattr(self, 'cap', None) is not None:
            self.cap.append(('op', eng, fn, list(reads), list(writes), inc))
            return
        own = 'e_' + eng
        c = self.cnt.get(own, 0)
        if inc:
            self.cnt[own] = c + 1
        self._record(eng, fn, reads, writes, (own, c + 1), (own, 1) if inc else None)

    def dma(self, eng, out, in_, reads=(), writes=(), sem=None, **kw):
        if getattr(self, 'cap', None) is not None:
            self.cap.append(('dma', eng, out, in_, list(reads), list(writes), sem, kw))
            return
        s = 'd_' + sem
        c = self.cnt.get(s, 0) + 16
        self.cnt[s] = c
        self._record(eng, lambda e: e.dma_start(out=out, in_=in_, **kw), reads, writes, (s, c), (s, 16), is_dma=True)

    def finish(self, keys):
        self.op('sp', lambda e: e.nop(), reads=list(keys), inc=False)

    def emit(self):
        nc = self.nc
        names = sorted(self.cnt.keys())
        with ExitStack() as es:
            sems = {n: es.enter_context(nc.semaphore(n)) for n in names}
            with nc.Block() as block:
                def body(ename):
                    def f(e):
                        for waits, fn, incspec in self.q[ename]:
                            for s, v in waits:
                                e.wait_ge(sems[s], v)
                            ins = fn(e)
                            if incspec is not None:
                                ins.then_inc(sems[incspec[0]], incspec[1])
                        for s, v in sorted(self.pending[ename].items()):
                            e.wait_ge(sems[s], v)
                    return f
                block.tensor(body('pe'))
                block.scalar(body('act'))
                block.vector(body('dve'))
                block.gpsimd(body('pool'))
                block.sync(body('sp'))

    def check(self):
        sem = {}
        ptr = {e: 0 for e in self.ENG}
        prog = True
        while prog:
            prog = False
            for e in self.ENG:
                while ptr[e] < len(self.q[e]):
                    waits, fn, inc = self.q[e][ptr[e]]
                    if all(sem.get(s_, 0) >= v for s_, v in waits):
                        if inc is not None:
                            sem[inc[0]] = sem.get(inc[0], 0) + inc[1]
                        ptr[e] += 1
                        prog = True
                    else:
                        break
        stuck = {e: (ptr[e], len(self.q[e]), self.q[e][ptr[e]][0]) for e in self.ENG if ptr[e] < len(self.q[e])}
        return stuck, sem

    def barrier(self):
        snap = dict(self.cnt)
        for eng in self.ENG:
            waits = {}
            own = 'e_' + eng
            for s, v in snap.items():
                if s == own:
                    continue
                if self.known[eng].get(s, 0) < v:
                    waits[s] = v
                    self.known[eng][s] = v
            for s, v in waits.items():
                if self.pending[eng].get(s, 0) < v:
                    self.pending[eng][s] = v


D = 1024
T = 2304
NT = 18
NL = 16
FH = 2816
EPS = 1e-6
NEG = -30000.0
GRID_W = 64
G_OFF = {}
_o = 0
for _n, _w in [('qa', 256), ('kva', 128), ('qn96', 96), ('kn96', 96), ('naq', 64), ('nak', 64), ('gq', 64), ('gk', 64),
               ('dq', 64), ('dk', 64), ('sink', 8), ('lq1', 64), ('lk1', 64), ('lq2', 64), ('lk2', 64)]:
    G_OFF[_n] = (_o, _w)
    _o += _w
GW = _o


def _rope_tables():
    pos = np.arange(2048)
    rows, cols = pos // GRID_W, pos % GRID_W

    def tab(h):
        fr = (10000.0 ** (-np.arange(0, h, 2, dtype=np.float32) / np.float32(h))).astype(np.float32)
        ar = rows.astype(np.float32)[:, None] * fr[None, :]
        ac = cols.astype(np.float32)[:, None] * fr[None, :]
        cr, sr, cc, sc = np.cos(ar), np.sin(ar), np.cos(ac), np.sin(ac)
        cos = np.concatenate([cr, cr, cc, cc], axis=1)
        sin = np.concatenate([-sr, sr, -sc, sc], axis=1)
        return np.concatenate([cos, sin], axis=1).astype(np.float32)
    return tab(32), tab(16)


def _na_tables(rpb):
    kc = np.arange(64)[:, None]
    qc = np.arange(64)[None, :]
    c0 = np.clip(qc - 8, 0, 48)
    valid = (kc >= c0) & (kc < c0 + 16)
    offc = np.clip(kc - qc + 15, 0, 30)
    lib = np.zeros((128, 4, 960), np.float32)
    mask = np.zeros((128, 960), np.float32)
    for dr in range(-7, 8):
        col = (7 - dr) * 64
        for h in range(8):
            lib[(h % 2) * 64:(h % 2) * 64 + 64, h // 2, col:col + 64] = rpb[h, dr + 7][offc]
        mask[0:64, col:col + 64] = np.where(valid, 0.0, NEG)
        mask[64:128, col:col + 64] = np.where(valid, 0.0, NEG)
    return lib, mask


def _bc_mid(a, G):
    return bass.AP(a.tensor, a.offset, [list(a.ap[0]), [0, G], list(a.ap[1])])


def _bc_last(a, d):
    return bass.AP(a.tensor, a.offset, [list(a.ap[0]), list(a.ap[1]), [0, d]])


def _view(a, off, dims):
    return bass.AP(a.tensor, a.offset + off, [list(a.ap[0])] + [list(x) for x in dims])


def build(layers=(0, 1), stop_after=None):
    nc = bass.Bass("TRN2", target_bir_lowering=False)
    es = ExitStack()

    def din(name, shape):
        return nc.dram_tensor(name, list(shape), F32, kind="ExternalInput").ap()

    xin = din("xin", [T, D])
    cvecT = din("cvecT", [128, 16])
    gains_d = din("gains", [128, GW])
    pp_d = din("pp", [128, 33])
    ident_d = din("ident", [128, 128])
    tri_d = din("tri", [128, 256])
    sel_d = din("sel", [2, 256])
    cs64_d = din("cs64", [2048, 128])
    cs32_d = din("cs32", [2048, 64])
    lib_d = din("nalib", [128, 4 * 960])
    nam_d = din("namask", [128, 960])
    W = {}
    for l in (0, 1):
        W[l] = dict(
            ada_w=din(f"l{l}_ada_w", [D, 6 * D]), ada_b2=din(f"l{l}_ada_b2", [2, 6 * D]),
            w_in=din(f"l{l}_w_in", [D, 1952 if l == 0 else 2304]), w_out=din(f"l{l}_w_out", [D, D]),
            wg=din(f"l{l}_ffn_w_gate", [D, FH]), wu=din(f"l{l}_ffn_w_up", [D, FH]), wd=din(f"l{l}_ffn_w_down", [FH, D]))
    W[0]['w_uq'] = din("l0_mla_w_uq", [256, 768])
    W[0]['w_ukv'] = din("l0_mla_w_ukv", [128, 1024])
    y_out = nc.dram_tensor("y", [2048, D], F32, kind="ExternalOutput").ap()
    xs = {n: nc.dram_tensor(n, [T, D], F32).ap() for n in ('xs_a', 'xs_b', 'xs_c', 'xs_d')}
    xs['xin'] = xin
    hts = [nc.dram_tensor(f'hts{k}', [NT, 128, D], BF16).ap() for k in range(2)]
    xs['y'] = y_out

    with es:
        def sb(name, shape, dt=F32):
            return es.enter_context(nc.sbuf_tensor(name, list(shape), dt))

        def ps(name, shape, dt=F32):
            return es.enter_context(nc.psum_tensor(name, list(shape), dt))

        p = Prog(nc)
        ident = sb("ident_s", [128, 128])
        identb = sb("identb", [128, 128], BF16)
        onesb = sb("onesb", [128, 128], BF16)
        trib = sb("trib", [128, 256], BF16)
        sel = sb("sel_s", [2, 256])
        gains = sb("gains_s", [128, GW])
        pp = sb("pp_s", [128, 33])
        cT = sb("cT", [128, 16])
        cTb = sb("cTb", [128, 16], BF16)
        modT = sb("modT", [128, 64])
        AB = sb("AB", [128, 64])
        Gt = [[sb(f"G{k}{j}", [128, D]) for j in range(2)] for k in range(2)]
        small = sb("small", [128, 64])
        lamt = sb("lamt", [128, 8])
        es_sink = sb("es_sink", [128, 8])
        sublnS = sb("sublnS", [128, 1])
        naq_s = sb("naq_s", [128, 64])
        xa = [sb("xa0", [128, D])]
        recb = [sb(f"recb{i}", [128, 512]) for i in range(2)]
        xo = [sb(f"xo{i}", [128, D]) for i in range(2)]
        mrow = [sb(f"mrow{i}", [2, 512]) for i in range(2)]
        brow = [sb(f"brow{i}", [2, 512]) for i in range(2)]
        NS = 2

        class SSet:
            pass
        SS = []
        for i in range(NS):
            S_ = SSet()
            S_.i = i
            S_.xt = sb(f"xt{i}", [128, D])
            S_.buf = sb(f"buf{i}", [128, D])
            S_.raw = sb(f"raw{i}", [128, D])
            S_.y1 = sb(f"y1{i}", [128, D])
            S_.cs = sb(f"cs{i}", [128, 192])
            S_.small = small[:, i * 32:(i + 1) * 32]
            S_.xb = (3 * i, 3 * i + 1)
            S_.pj = 3 * i + 2
            S_.k = lambda n, i=i: (n, i)
            SS.append(S_)
        junk = SS[0].buf
        raw = SS[0].raw
        rec = recb
        of = [xo[1][:, 0:512], xo[1][:, 512:1024]]
        RECK = [('rec', 0), ('rec', 1)]
        OFK = [('xo', 1), ('xo', 1)]
        ARN = 63800
        arena = sb("arena", [128, ARN], BF16)
        pb = [ps(f"pb{i}", [128, 512]) for i in range(8)]
        ptbs = [pb[6][:, :].bitcast(BF16), pb[7][:, :].bitcast(BF16)]
        PB = [('pb', i) for i in range(8)]

        class Arena:
            def __init__(self):
                self.off = 0

            def reset(self):
                self.off = 0

            def alloc(self, *free):
                n = int(np.prod(free))
                a = arena[:, self.off:self.off + n]
                self.off += n
                assert self.off <= ARN, self.off
                if len(free) == 2:
                    a = a.rearrange("p (a b) -> p a b", a=free[0])
                elif len(free) == 3:
                    a = a.rearrange("p (a b c) -> p a b c", a=free[0], b=free[1])
                return a
        AR = Arena()

        def MM(out, lhsT, rhs, start, stop, reads, writes, inc=True):
            p.op('pe', lambda e: e.matmul(out, lhsT, rhs, start=start, stop=stop), reads, writes, inc)

        def TR(out, in_, idn, reads, writes, inc=True):
            p.op('pe', lambda e: e.transpose(out, in_, idn), reads, writes, inc)

        def ACT(out, in_, func, reads, writes, bias=None, scale=None):
            kw = {}
            if bias is not None:
                kw['bias'] = bias
            if scale is not None:
                kw['scale'] = scale
            p.op('act', lambda e: e.activation(out, in_, func, **kw), reads, writes)

        def TT(eng, out, a, b, op, reads, writes):
            p.op(eng, lambda e: e.tensor_tensor(out, a, b, op), reads, writes)

        def TS(eng, out, a, s1, s2, op0, op1, reads, writes):
            if s2 is None:
                p.op(eng, lambda e: e.tensor_scalar(out, a, s1, None, op0), reads, writes)
            else:
                p.op(eng, lambda e: e.tensor_scalar(out, a, s1, s2, op0, op1), reads, writes)

        def STT(out, a, s, b, op0, op1, reads, writes):
            p.op('dve', lambda e: e.scalar_tensor_tensor(out, a, s, b, op0, op1), reads, writes)

        def CP(eng, out, in_, reads, writes):
            if eng == 'act':
                ACT(out, in_, AF.Copy, reads, writes)
            else:
                p.op(eng, lambda e: e.tensor_copy(out, in_), reads, writes)

        def RED(out, in_, reads, writes):
            p.op('dve', lambda e: e.tensor_reduce(out, in_, AX.X, ALU.add), reads, writes)

        def RSTD(ap, n_inv, reads_writes):
            ACT(ap, ap, AF.Ln, [reads_writes], [reads_writes], bias=EPS, scale=n_inv)
            ACT(ap, ap, AF.Exp, [reads_writes], [reads_writes], scale=-0.5)

        def gain(name):
            o, w = G_OFF[name]
            return gains[:, o:o + w]

        p.dma('sp', ident[:], ident_d, writes=['ident'], sem='c0')
        p.dma('sp', sel[:], sel_d, writes=['sel'], sem='c1')
        p.dma('sp', gains[:], gains_d, writes=['gains'], sem='c2')
        p.dma('sp', pp[:], pp_d, writes=['pp'], sem='c3')
        p.dma('sp', cT[:], cvecT, writes=['cT'], sem='c4')
        p.dma('pool', identb[:], ident_d, writes=['identb'], sem='c5')
        p.dma('pool', trib[:], tri_d, writes=['trib'], sem='c6')
        p.op('pool', lambda e: e.memset(onesb[:], 1.0), writes=['onesb'])
        ACT(cTb[:], cT[:], AF.Silu, ['cT'], ['cTb'])
        TS('dve', naq_s[:], gain('naq'), 0.125, None, ALU.mult, None, ['gains'], ['naq_s'])
        lam_init = 0.8 - 0.6 * float(np.exp(-0.3 * 1))
        TT('dve', junk[:, 0:64], gain('lq1'), gain('lk1'), ALU.mult, ['gains'], [('buf', 0)])
        RED(lamt[:, 0:1], junk[:, 0:64], [('buf', 0)], ['lamt'])
        TT('dve', junk[:, 64:128], gain('lq2'), gain('lk2'), ALU.mult, ['gains'], [('buf', 0)])
        RED(lamt[:, 1:2], junk[:, 64:128], [('buf', 0)], ['lamt'])
        ACT(lamt[:, 0:2], lamt[:, 0:2], AF.Exp, ['lamt'], ['lamt'])
        TT('dve', lamt[:, 2:3], lamt[:, 1:2], lamt[:, 0:1], ALU.subtract, ['lamt'], ['lamt'])
        TS('dve', lamt[:, 3:4], lamt[:, 2:3], -lam_init, None, ALU.add, None, ['lamt'], ['lamt'])
        ACT(es_sink[:], gain('sink'), AF.Exp, ['gains'], ['es_sink'])
        TS('dve', sublnS[:], pp[:, 32:33], 1.0 - lam_init, None, ALU.mult, None, ['pp'], ['sublnS'])

        def ada_phase(l):
            AR.reset()
            p.barrier()
            aw = [AR.alloc(8, 512) for _ in range(3)]
            adaw = W[l]['ada_w'].rearrange("(c p) n -> p c n", p=128)
            for n in range(12):
                s = n % 3
                b2 = n % 2
                for c in range(8):
                    p.dma('pool', aw[s][:, c, :], adaw[:, c, n * 512:(n + 1) * 512], writes=[('aw', s)], sem=f'aw{s}')
                p.dma('sp', brow[b2][:], W[l]['ada_b2'][:, n * 512:(n + 1) * 512], writes=[('brow', b2)], sem=f'brow{b2}')
                for c in range(8):
                    MM(pb[0][0:2, :], cTb[:, c * 2:c * 2 + 2], aw[s][:, c, :], c == 0, c == 7,
                       ['cTb', ('aw', s)], [PB[0]], inc=(c == 7))
                TT('dve', mrow[b2][:], pb[0][0:2, :], brow[b2][:], ALU.add, [PB[0], ('brow', b2)], [('mrow', b2)])
                sec, half = n // 2, n % 2
                if sec in (2, 5):
                    k = 0 if sec == 2 else 1
                    for j in range(2):
                        MM(pb[1 + j][:, :], sel[0:2, j * 128:(j + 1) * 128], mrow[b2][:], True, True,
                           ['sel', ('mrow', b2)], [PB[1 + j]])
                        CP('act', Gt[k][j][:, half * 512:(half + 1) * 512], pb[1 + j][:, :], [PB[1 + j]], [('G', k, j)])
                else:
                    si = {0: 0, 1: 1, 3: 2, 4: 3}[sec]
                    for q in range(4):
                        c = half * 4 + q
                        TR(pb[3][:, (si * 8 + c) * 2:(si * 8 + c) * 2 + 2], mrow[b2][0:2, q * 128:(q + 1) * 128], ident[0:2, 0:2],
                           [('mrow', b2), 'ident'], [PB[3]])
            CP('dve', modT[:], pb[3][:, 0:64], [PB[3]], ['modT'])
            for k in range(2):
                sh = modT[:, (2 * k) * 16:(2 * k) * 16 + 16]
                sc = modT[:, (2 * k + 1) * 16:(2 * k + 1) * 16 + 16]
                gn = pp[:, l * 16 + k * 8:l * 16 + k * 8 + 8]
                A = AB[:, k * 32:k * 32 + 16]
                B = AB[:, k * 32 + 16:k * 32 + 32]
                TS('dve', A, sc, 1.0, None, ALU.add, None, ['modT'], ['AB'])
                TT('dve', A.rearrange("p (c j) -> p c j", j=2), A.rearrange("p (c j) -> p c j", j=2), _bc_last(gn, 2), ALU.mult,
                   ['AB', 'pp'], ['AB'])
                CP('dve', B, sh, ['modT'], ['AB'])

        state = dict(xa=0, xo=0)

        def emit_h(S, src, t, k, dst, hkey, banks=None):
            j = 0 if t < NL else 1
            i = S.i
            xb = banks if banks is not None else S.xb
            X = S.xt
            p.dma('sp', X[:], xs[src][t * 128:(t + 1) * 128, :], reads=[(src, t)], writes=[('xt', i)], sem=f'xt{i}')
            TT('dve', S.buf[:], X[:], X[:], ALU.mult, [('xt', i)], [('buf', i)])
            RED(S.small[:, 0:1], S.buf[:], [('buf', i)], [('small', i)])
            RSTD(S.small[:, 0:1], 1.0 / D, ('small', i))
            ACT(S.buf[:], X[:], AF.Identity, [('xt', i), ('small', i)], [('buf', i)], scale=S.small[:, 0:1])
            for rnd in range(2):
                for c in range(rnd * 4, rnd * 4 + 4):
                    TR(pb[xb[c // 4]][:, (c % 4) * 128:(c % 4 + 1) * 128], S.buf[:, c * 128:(c + 1) * 128], ident[:],
                       [('buf', i), 'ident'], [PB[xb[c // 4]]], inc=(c % 4 == 3))
                for c in range(rnd * 4, rnd * 4 + 4):
                    ACT(dst[:, c, :], pb[xb[c // 4]][:, (c % 4) * 128:(c % 4 + 1) * 128], AF.Identity,
                        [PB[xb[c // 4]], 'AB'], [hkey], scale=AB[:, k * 32 + c * 2 + j:k * 32 + c * 2 + j + 1],
                        bias=AB[:, k * 32 + 16 + c * 2 + j:k * 32 + 16 + c * 2 + j + 1])

        def get_h(S, src, t, k, dst, hkey, compute, banks=None):
            i = S.i
            if compute:
                emit_h(S, src, t, k, dst, hkey, banks=banks)
                p.dma('sp', hts[k][t].rearrange("p (c n) -> p c n", c=8), dst, reads=[hkey], writes=[('hts', k, t)], sem=f'hs{i}')
            else:
                p.dma('sp', dst, hts[k][t].rearrange("p (c n) -> p c n", c=8), reads=[('hts', k, t)], writes=[hkey], sem=f'hl{i}')

        def proj(hT, hkey, wsb, wkey, col0, ncols, bank, M0=0, M1=128):
            for c in range(8):
                MM(pb[bank][0:M1 - M0, 0:ncols], hT[:, c, M0:M1], wsb[:, c, col0:col0 + ncols], c == 0, c == 7,
                   [hkey, wkey], [PB[bank]], inc=(c == 7))

        def load_cs(S, t, which):
            i = S.i
            if which == 64:
                p.dma('sp', S.cs[:, 0:128], cs64_d[t * 128:(t + 1) * 128, :], writes=[('cs', i)], sem=f'cs{i}')
            else:
                p.dma('sp', S.cs[:, 0:64], cs32_d[t * 128:(t + 1) * 128, :], writes=[('cs', i)], sem=f'cs{i}')

        def prep(S, src, G, d, gain_ap, out_b, rope=None):
            i = S.i
            kr, kb, ky, ks, kyb = ('raw', i), ('buf', i), ('y1', i), ('small', i), ('yb', i)
            sq = _view(S.buf[:], 0, [[d, G], [1, d]])
            TT('dve', sq, src, src, ALU.mult, [kr], [kb])
            RED(S.small[:, 8:8 + G], sq, [kb], [ks])
            RSTD(S.small[:, 8:8 + G], 1.0 / d, ks)
            yv = _view(S.y1[:], 0, [[d, G], [1, d]])
            TT('dve', yv, src, _bc_last(S.small[:, 8:8 + G], d), ALU.mult, [kr, ks], [ky])
            if rope is None:
                TT('pool', out_b, yv, _bc_mid(gain_ap, G), ALU.mult, [ky, 'gains', 'naq_s'], [kyb])
                return
            off, bs = rope
            TT('pool', yv, yv, _bc_mid(gain_ap, G), ALU.mult, [ky, 'gains'], [ky])
            if off > 0:
                CP('act', out_b[:, :, 0:off], _view(S.y1[:], 0, [[d, G], [1, off]]), [ky], [kyb])
            w = 4 * bs
            r = _view(S.y1[:], off, [[d, G], [1, w]])
            cosv = _bc_mid(S.cs[:, 0:w], G)
            ta = _view(S.buf[:], 0, [[w, G], [1, w]])
            TT('dve', ta, r, cosv, ALU.mult, [ky, ('cs', i)], [kb])
            for s_ in range(2):
                o_ = _view(S.buf[:], 512 + s_ * bs, [[w, G], [2 * bs, 2], [1, bs]])
                i0 = _view(S.y1[:], off + (1 - s_) * bs, [[d, G], [2 * bs, 2], [1, bs]])
                i1 = _view(S.cs[:], w + s_ * bs, [[0, G], [2 * bs, 2], [1, bs]])
                TT('pool', o_, i0, i1, ALU.mult, [ky, ('cs', i)], [kb])
            tb = _view(S.buf[:], 512, [[w, G], [1, w]])
            TT('dve', out_b[:, :, off:off + w], ta, tb, ALU.add, [kb], [kyb])

        def to_fm(S, ins, width, dst_of, dkey):
            i = S.i
            pt = ptbs[i][:, 0:512]
            for g0 in range(0, len(ins), 4):
                grp = ins[g0:g0 + 4]
                n = len(grp)
                for g, a in enumerate(grp):
                    TR(pt[0:width, g * 128:(g + 1) * 128], a, identb[:], [('yb', i), 'identb'], [PB[6 + i]], inc=(g == n - 1))
                CP('act', dst_of(g0, n), pt[0:width, 0:n * 128].rearrange("p (g t) -> p g t", g=n), [PB[6 + i]], [dkey])

        def residual_store(banks, xacc_i, Gk, j, dst, t):
            i = 0
            for hf in range(2):
                TT('dve', xo[i][:, hf * 512:(hf + 1) * 512], pb[banks[hf]][:, :], Gt[Gk][j][:, hf * 512:(hf + 1) * 512], ALU.mult,
                   [PB[banks[hf]], ('G', Gk, j)], [('xo', i)])
            TT('pool', xo[i][:], xo[i][:], xa[xacc_i][:], ALU.add, [('xo', i), ('xa', xacc_i)], [('xo', i)])
            if dst == 'y':
                if t < NL:
                    p.dma('sp', xs['y'][t * 128:(t + 1) * 128, :], xo[i][:], reads=[('xo', i)], writes=[(dst, t)], sem=f'xo{i}')
            else:
                p.dma('sp', xs[dst][t * 128:(t + 1) * 128, :], xo[i][:], reads=[('xo', i)], writes=[(dst, t)], sem=f'xo{i}')

        def load_xa(src, t):
            i = 0
            p.dma('sp', xa[i][:], xs[src][t * 128:(t + 1) * 128, :], reads=[(src, t)], writes=[('xa', i)], sem=f'xa{i}')
            return i

        def load_w(dst3, src3, key, sem, nch):
            for c in range(nch):
                p.dma('pool', dst3[:, c, :], src3[:, c, :], writes=[key], sem=sem)

        def merged(fn, items, nsets=NS):
            out = []
            for g0 in range(0, len(items), nsets):
                lists = []
                for k_, it in enumerate(items[g0:g0 + nsets]):
                    p.capture()
                    fn(SS[k_], it)
                    lists.append(p.end_capture())
                idx = [0] * len(lists)
                live = True
                while live:
                    live = False
                    for k_, L in enumerate(lists):
                        if idx[k_] < len(L):
                            out.append(L[idx[k_]])
                            idx[k_] += 1
                            live = True
            return out

        def overlapped_chunks(chunks_, q_tile_, att_chunk_, nsets_b=NS):
            p.replay([merged(lambda S, tt: q_tile_(S, tt, 0), list(enumerate(chunks_[0])))])
            for ci, tiles in enumerate(chunks_):
                p.capture()
                att_chunk_(ci, tiles)
                A = p.end_capture()
                B = merged(lambda S, tt: q_tile_(S, tt, (ci + 1) % 2), list(enumerate(chunks_[ci + 1])), nsets_b) if ci + 1 < len(chunks_) else []
                M = []
                nb_, ib = len(B), 0
                for ka, ia in enumerate(A):
                    M.append(ia)
                    tgt = ((ka + 1) * nb_) // max(len(A), 1)
                    while ib < tgt:
                        M.append(B[ib])
                        ib += 1
                M.extend(B[ib:])
                p.replay([M])

        def interleaved(fn, items):
            for g0 in range(0, len(items), NS):
                lists = []
                for k_, it in enumerate(items[g0:g0 + NS]):
                    p.capture()
                    fn(SS[k_], it)
                    lists.append(p.end_capture())
                p.replay(lists)

        def attend(N, units, scale, qT_of, dv, nb, PT, fused=None, qkey=('fm',), la=None, sbk=None):
            pN, pD = nb if nb is not None else (None, None)
            nu = len(units)

            LA = int(os.environ.get('ATT_LA', '2')) if la is None else la
            DUP = int(os.environ.get('ATT_DUP', '0'))
            SBK = (sbk if sbk is not None else ([0, 3, 5] if fused is not None else [0, 3, 6]))[:LA + 1]

            def emit_S(i):
                u = units[i]
                sbk = SBK[i % len(SBK)]
                pS = pb[sbk]
                nk, c0, c1 = u['nk'], u['c0'], u['c1']
                n = c1 - c0
                for rep in range(1 + DUP):
                    MM(pS[0:nk, 0:n], u['kT'], qT_of(c0, c1), True, u.get('bias') is None, [('fm',), qkey], [PB[sbk]], inc=(u.get('bias') is None))
                    if u.get('bias') is not None:
                        MM(pS[0:nk, 0:n], u['bias'][0], u['bias'][1], False, True, ['identb', 'lib'], [PB[sbk]])

            for i0 in range(min(LA, nu)):
                emit_S(i0)
            for i, u in enumerate(units):
                sbk = SBK[i % len(SBK)]
                pS = pb[sbk]
                nk, c0, c1 = u['nk'], u['c0'], u['c1']
                n = c1 - c0
                PTi = i % 4
                ACT(PT[PTi][0:nk, 0:n], pS[0:nk, 0:n], AF.Exp, [PB[sbk]], [('PT', PTi)], scale=scale)
                if i + LA < nu:
                    emit_S(i + LA)
                for (mc, mk) in u.get('masks', []):
                    TT('pool', PT[PTi][:, mc:mc + 128], PT[PTi][:, mc:mc + 128], mk, ALU.mult, [('PT', PTi), 'trib'], [('PT', PTi)])
                if fused is not None:
                    MM(pb[fused][:, c0:c1], u['v'], PT[PTi][0:nk, 0:n], i == 0, i == nu - 1, [('PT', PTi), ('v',)], [PB[fused]], inc=True)
                    continue
                MM(pb[pN][0:dv, c0:c1], u['v'], PT[PTi][0:nk, 0:n], i == 0, i == nu - 1, [('PT', PTi), ('v',)], [PB[pN]], inc=False)
                MM(pb[pD][0:dv, c0:c1], onesb[0:nk, 0:dv], PT[PTi][0:nk, 0:n], i == 0, i == nu - 1, [('PT', PTi), 'onesb'], [PB[pD]],
                   inc=True)

        def norm_fused(h, N, bank, mix, add_sink=False):
            par = h % 2
            dlo, nlo = (64, 0) if par == 0 else (0, 64)
            r_ = rec[par]
            rk = RECK[par]
            if add_sink:
                TS('dve', r_[dlo:dlo + 64, 0:N], pb[bank][dlo:dlo + 64, 0:N], es_sink[dlo:dlo + 64, h:h + 1], None, ALU.add, None,
                   [PB[bank], 'es_sink'], [rk])
                p.op('dve', lambda e: e.reciprocal(r_[dlo:dlo + 64, 0:N], r_[dlo:dlo + 64, 0:N]), [rk], [rk])
            else:
                p.op('dve', lambda e: e.reciprocal(r_[dlo:dlo + 64, 0:N], pb[bank][dlo:dlo + 64, 0:N]), [PB[bank]], [rk])
            TT('dve', mix[nlo:nlo + 64, h // 2, 0:N], pb[bank][nlo:nlo + 64, 0:N], r_[dlo:dlo + 64, 0:N], ALU.mult, [PB[bank], rk], [('mix',)])

        def norm_heads(h, N, pN, pD, mix, add_sink=None):
            r_ = rec[h % 2]
            rk = RECK[h % 2]
            if add_sink is None:
                p.op('dve', lambda e: e.reciprocal(r_[0:64, 0:N], pb[pD][0:64, 0:N]), [PB[pD]], [rk])
            else:
                TS('dve', r_[0:64, 0:N], pb[pD][0:64, 0:N], add_sink, None, ALU.add, None, [PB[pD], 'es_sink'], [rk])
                p.op('dve', lambda e: e.reciprocal(r_[0:64, 0:N], r_[0:64, 0:N]), [rk], [rk])
            TT('dve', mix[0:64, h, 0:N], pb[pN][0:64, 0:N], r_[0:64, 0:N], ALU.mult, [PB[pN], rk], [('mix',)])

        def out_proj(l, mixv, K, nslot, wo, tiles, xacc_src, dst, Gk):
            for ti, t in enumerate(tiles):
                j = 0 if t < NL else 1
                xi = load_xa(xacc_src, t)
                for hf in range(2):
                    for s_ in range(nslot):
                        bk = (1 + hf) if ti % 2 == 0 else (4 + hf)
                        MM(pb[bk][:, :], mixv[0:K, s_, ti * 128:(ti + 1) * 128], wo[0:K, s_, hf * 512:(hf + 1) * 512],
                           s_ == 0, s_ == nslot - 1, [('mix',), 'wo'], [PB[bk]], inc=(s_ == nslot - 1))
                residual_store((1, 2) if ti % 2 == 0 else (4, 5), xi, Gk, j, dst, t)

        def mixer_pass(l, mx, src, acc, dst):
            AR.reset()
            p.barrier()
            ctxq = (l == 0)
            win_d = W[l]['w_in'].rearrange("(c p) n -> p c n", p=128)
            PT = [AR.alloc(512) for _ in range(4)]
            for S in SS:
                S.hT = AR.alloc(8, 128)
                S.yb = AR.alloc(1024)
                S.cT = AR.alloc(2, 128)
            HK = lambda S: ('hT', S.i)
            HCOMP = [mx in ('A', 'C')]
            chunks = [list(range(c * 4, c * 4 + 4)) for c in range(4)] + ([[16, 17]] if ctxq else [])
            NB = [(1, 2), (4, 5)]
            if mx == 'A':
                wi = AR.alloc(8, 416)
                load_w(wi, win_d[:, :, 0:416], 'wi', 'wi', 8)
                wuq = AR.alloc(2, 768)
                load_w(wuq, W[0]['w_uq'].rearrange("(c p) n -> p c n", p=128), 'wuq', 'wuq', 2)
                wukv = AR.alloc(1, 1024)
                load_w(wukv, W[0]['w_ukv'].rearrange("(c p) n -> p c n", p=128), 'wukv', 'wukv', 1)
                wo = AR.alloc(4, 1024)
                load_w(wo, W[l]['w_out'][0:512, :].rearrange("(s p) n -> p s n", p=128), 'wo', 'wo', 4)
                kT = AR.alloc(8, T)
                vv = AR.alloc(NT, 8, 128)
                p.op('pool', lambda e: e.memset(vv, 1.0), [], [('v',)])
                qTs = [AR.alloc(8, 512) for _ in range(2)]
                mix = AR.alloc(4, 512)

                def kv_tile(S, t):
                    lat = t < NL
                    i = S.i
                    if lat:
                        load_cs(S, t, 32)
                    get_h(S, src, t, 0, S.hT, HK(S), HCOMP[0])
                    proj(S.hT, HK(S), wi, 'wi', 256, 160, S.pj)
                    CP('act', S.raw[:, 0:160], pb[S.pj][:, 0:160], [PB[S.pj]], [('raw', i)])
                    prep(S, _view(S.raw[:], 0, [[128, 1], [1, 128]]), 1, 128, gain('kva'), _view(S.yb, 0, [[128, 1], [1, 128]]))
                    to_fm(S, [S.yb[:, 0:128]], 128, lambda g0, n: S.cT[:, 0:1, :], ('cT', i))
                    for hf in range(2):
                        MM(pb[S.xb[hf]][:, :], S.cT[:, 0, :], wukv[:, 0, hf * 512:(hf + 1) * 512], True, True, [('cT', i), 'wukv'], [PB[S.xb[hf]]])
                    for hf in range(2):
                        kvv = pb[S.xb[hf]][:, :].rearrange("p (h e) -> p h e", h=4)
                        for par in range(2):
                            CP('act', _view(vv, (t * 8 + hf * 4 + par) * 128 + par * 64, [[256, 2], [1, 64]]),
                               _view(pb[S.xb[hf]][:, :], par * 128 + 64, [[256, 2], [1, 64]]), [PB[S.xb[hf]]], [('v',)])
                        CP('act', _view(S.raw[:], 256 + hf * 4 * 96, [[96, 4], [1, 64]]), kvv[:, :, 0:64], [PB[S.xb[hf]]], [('raw', i)])
                    CP('pool', _view(S.raw[:], 256 + 64, [[96, 8], [1, 32]]), _bc_mid(S.raw[:, 128:160], 8), [('raw', i)], [('raw', i)])
                    ybv = _view(S.yb, 0, [[96, 8], [1, 96]])
                    prep(S, _view(S.raw[:], 256, [[96, 8], [1, 96]]), 8, 96, gain('kn96'), ybv, rope=(64, 8) if lat else None)
                    to_fm(S, [ybv[:, h, :] for h in range(8)], 96, lambda g0, n: kT[0:96, g0:g0 + n, t * 128:(t + 1) * 128], ('fm',))

                def q_tile(S, tt, qb):
                    ti, t = tt
                    lat = t < NL
                    i = S.i
                    bk = 6 + i
                    if lat:
                        load_cs(S, t, 32)
                    get_h(S, src, t, 0, S.hT, HK(S), HCOMP[0])
                    proj(S.hT, HK(S), wi, 'wi', 0, 256, bk)
                    CP('act', S.raw[:, 0:256], pb[bk][:, 0:256], [PB[bk]], [('raw', i)])
                    prep(S, _view(S.raw[:], 0, [[256, 1], [1, 256]]), 1, 256, gain('qa'), _view(S.yb, 0, [[256, 1], [1, 256]]))
                    to_fm(S, [S.yb[:, 0:128], S.yb[:, 128:256]], 128, lambda g0, n: S.cT[:, :, :], ('cT', i))
                    for (c0_, n_) in ((0, 512), (512, 256)):
                        for c in range(2):
                            MM(pb[bk][:, 0:n_], S.cT[:, c, :], wuq[:, c, c0_:c0_ + n_], c == 0, c == 1, [('cT', i), 'wuq'], [PB[bk]], inc=(c == 1))
                        CP('act', S.raw[:, c0_:c0_ + n_], pb[bk][:, 0:n_], [PB[bk]], [('raw', i)])
                    ybv = _view(S.yb, 0, [[96, 8], [1, 96]])
                    prep(S, _view(S.raw[:], 0, [[96, 8], [1, 96]]), 8, 96, gain('qn96'), ybv, rope=(64, 8) if lat else None)
                    to_fm(S, [ybv[:, h, :] for h in range(8)], 96, lambda g0, n: qTs[qb][0:96, g0:g0 + n, ti * 128:(ti + 1) * 128], ('qT', qb))

                KS = os.environ.get('KSTOP', '')
                interleaved(kv_tile, list(range(2 if KS == 'kv1' else NT)))
                HCOMP[0] = False
                if KS in ('kv1', 'kv'):
                    chunks = []

                def att_chunk(ci, tiles):
                    N = 128 * len(tiles)
                    lat = tiles[0] < NL
                    qT_ = qTs[ci % 2]
                    ktiles = [16, 17] + (list(range(16)) if lat else [])
                    for h in range(8):
                        units = [dict(kT=kT[0:96, h, kt * 128:(kt + 1) * 128], nk=128, c0=0, c1=N, v=vv[:, kt, h, :]) for kt in ktiles]
                        bank = (1, 2, 4)[h % 3]
                        attend(N, units, 96 ** -0.5, lambda c0, c1, h=h: qT_[0:96, h, c0:c1], 64, None, PT, fused=bank, qkey=('qT', ci % 2), la=3, sbk=[0, 3, 5, 7])
                        norm_fused(h, N, bank, mix)
                    out_proj(l, mix, 128, 4, wo, tiles, acc, dst, 0)
                if chunks:
                    overlapped_chunks(chunks, q_tile, att_chunk, nsets_b=1)
            elif mx == 'B':
                wi = AR.alloc(8, 1536)
                load_w(wi, win_d[:, :, 416:1952], 'wi', 'wi', 8)
                wo = AR.alloc(8, 1024)
                load_w(wo[0:64], W[l]['w_out'][512:1024, :].rearrange("(h p) n -> p h n", p=64), 'wo', 'wo', 8)
                kT = AR.alloc(4, T)
                vv = AR.alloc(32, 8, 64)
                vvc = AR.alloc(2, 8, 64)
                qT = AR.alloc(4, 512)
                mix = AR.alloc(8, 512)
                libt = AR.alloc(4, 960)
                p.dma('sp', SS[1].raw[:, 0:960], nam_d, writes=[('raw', 1)], sem='c7')
                for hc in range(4):
                    p.dma('sp', SS[0].raw[:, 0:960], lib_d[:, hc * 960:(hc + 1) * 960], writes=[('raw', 0)], sem='c8')
                    TT('pool', libt[:, hc, :], SS[0].raw[:, 0:960], SS[1].raw[:, 0:960], ALU.add, [('raw', 0), ('raw', 1)], ['lib'])

                def kv_tile(S, t):
                    lat = t < NL
                    i = S.i
                    get_h(S, src, t, 0, S.hT, HK(S), HCOMP[0])
                    proj(S.hT, HK(S), wi, 'wi', 512, 512, S.pj)
                    CP('act', S.raw[:, 0:512], pb[S.pj][:, :], [PB[S.pj]], [('raw', i)])
                    prep(S, _view(S.raw[:], 0, [[64, 8], [1, 64]]), 8, 64, gain('nak'), _view(S.yb, 0, [[64, 8], [1, 64]]))
                    to_fm(S, [S.yb[:, c * 128:(c + 1) * 128] for c in range(4)], 128, lambda g0, n: kT[:, g0:g0 + n, t * 128:(t + 1) * 128], ('fm',))
                    if lat:
                        for hf in range(2):
                            proj(S.hT, HK(S), wi, 'wi', 1024, 512, S.xb[hf], M0=hf * 64, M1=hf * 64 + 64)
                            CP('act', vv[0:64, 2 * t + hf, :, :], pb[S.xb[hf]][0:64, :].rearrange("p (h e) -> p h e", h=8), [PB[S.xb[hf]]], [('v',)])
                    else:
                        proj(S.hT, HK(S), wi, 'wi', 1024, 512, S.xb[0])
                        CP('act', vvc[:, t - NL, :, :], pb[S.xb[0]][:, :].rearrange("p (h e) -> p h e", h=8), [PB[S.xb[0]]], [('v',)])

                def q_tile(S, tt):
                    ti, t = tt
                    i = S.i
                    get_h(S, src, t, 0, S.hT, HK(S), HCOMP[0])
                    proj(S.hT, HK(S), wi, 'wi', 0, 512, S.pj)
                    CP('act', S.raw[:, 0:512], pb[S.pj][:, :], [PB[S.pj]], [('raw', i)])
                    prep(S, _view(S.raw[:], 0, [[64, 8], [1, 64]]), 8, 64, naq_s[:], _view(S.yb, 0, [[64, 8], [1, 64]]))
                    to_fm(S, [S.yb[:, c * 128:(c + 1) * 128] for c in range(4)], 128, lambda g0, n: qT[:, g0:g0 + n, ti * 128:(ti + 1) * 128], ('fm',))

                interleaved(kv_tile, list(range(NT)))
                HCOMP[0] = False
                for tiles in chunks:
                    N = 128 * len(tiles)
                    lat = tiles[0] < NL
                    interleaved(q_tile, list(enumerate(tiles)))
                    for h in range(8):
                        ho, hc = (h % 2) * 64, h // 2
                        units = [dict(kT=kT[ho:ho + 64, hc, (NL + i_) * 128:(NL + i_ + 1) * 128], nk=128, c0=0, c1=N, v=vvc[:, i_, h, :])
                                 for i_ in range(2)]
                        if lat:
                            r0 = (tiles[0] * 128) // 64
                            for kr in range(32):
                                rs = [r for r in range(r0, r0 + 8) if min(max(r - 4, 0), 24) <= kr <= min(max(r - 4, 0), 24) + 7]
                                if not rs:
                                    continue
                                ra, rb = rs[0], rs[-1]
                                c0, c1 = (ra - r0) * 64, (rb - r0 + 1) * 64
                                L0 = (7 - (kr - ra)) * 64
                                units.append(dict(kT=kT[ho:ho + 64, hc, kr * 64:(kr + 1) * 64], nk=64, c0=c0, c1=c1, v=vv[0:64, kr, h, :],
                                                  bias=(identb[ho:ho + 64, ho:ho + 64], libt[ho:ho + 64, hc, L0:L0 + (c1 - c0)])))
                        pN, pD = NB[h % 2]
                        attend(N, units, 1.0, lambda c0, c1, ho=ho, hc=hc: qT[ho:ho + 64, hc, c0:c1], 64, (pN, pD), PT)
                        norm_heads(h, N, pN, pD, mix)
                    out_proj(l, mix, 64, 8, wo, tiles, acc, dst, 0)
            elif mx == 'C':
                wi = AR.alloc(8, 768)
                load_w(wi, win_d[:, :, 0:768], 'wi', 'wi', 8)
                wo = AR.alloc(4, 1024)
                load_w(wo, W[l]['w_out'][0:512, :].rearrange("(s p) n -> p s n", p=128), 'wo', 'wo', 4)
                kT = AR.alloc(1, T)
                vvE = AR.alloc(NT, 2, 128)
                vvO = AR.alloc(NT, 2, 128)
                p.op('pool', lambda e: e.memset(vvE, 1.0), [], [('v',)])
                p.op('pool', lambda e: e.memset(vvO, 1.0), [], [('v',)])
                qTs = [AR.alloc(4, 512) for _ in range(2)]
                mix = AR.alloc(4, 512)

                def kv_tile(S, t):
                    lat = t < NL
                    i = S.i
                    if lat:
                        load_cs(S, t, 64)
                    get_h(S, src, t, 0, S.hT, HK(S), HCOMP[0])
                    proj(S.hT, HK(S), wi, 'wi', 512, 256, S.pj)
                    CP('act', S.raw[:, 0:256], pb[S.pj][:, 0:256], [PB[S.pj]], [('raw', i)])
                    prep(S, _view(S.raw[:], 0, [[64, 2], [1, 64]]), 2, 64, gain('gk'), _view(S.yb, 0, [[64, 2], [1, 64]]), rope=(0, 16) if lat else None)
                    to_fm(S, [S.yb[:, 0:128]], 128, lambda g0, n: kT[:, 0:1, t * 128:(t + 1) * 128], ('fm',))
                    CP('pool', vvE[:, t, :, 0:64], _view(S.raw[:], 128, [[64, 2], [1, 64]]), [('raw', i)], [('v',)])
                    CP('pool', vvO[:, t, :, 64:128], _view(S.raw[:], 128, [[64, 2], [1, 64]]), [('raw', i)], [('v',)])

                def q_tile(S, tt, qb):
                    ti, t = tt
                    i = S.i
                    bk = 6 + i
                    load_cs(S, t, 64)
                    get_h(S, src, t, 0, S.hT, HK(S), HCOMP[0])
                    proj(S.hT, HK(S), wi, 'wi', 0, 512, bk)
                    CP('act', S.raw[:, 0:512], pb[bk][:, :], [PB[bk]], [('raw', i)])
                    prep(S, _view(S.raw[:], 0, [[64, 8], [1, 64]]), 8, 64, gain('gq'), _view(S.yb, 0, [[64, 8], [1, 64]]), rope=(0, 16))
                    pt = ptbs[i][:, 0:512]
                    for h in range(8):
                        kvh, g = h // 4, h % 4
                        TR(pt[kvh * 64:(kvh + 1) * 64, g * 128:(g + 1) * 128], S.yb[:, h * 64:(h + 1) * 64], identb[:], [('yb', i), 'identb'],
                           [PB[6 + i]], inc=(h == 7))
                    CP('act', qTs[qb][:, :, ti * 128:(ti + 1) * 128], pt[:, 0:512].rearrange("p (g t) -> p g t", g=4), [PB[6 + i]], [('qT', qb)])

                interleaved(kv_tile, list(range(NT)))
                HCOMP[0] = False

                def att_chunk(cch, tiles):
                    N = 512
                    qT_ = qTs[cch % 2]
                    for h in range(8):
                        kvh, g = h // 4, h % 4
                        ho = kvh * 64
                        vv = vvE if h % 2 == 0 else vvO
                        units = [dict(kT=kT[ho:ho + 64, 0, (NL + i_) * 128:(NL + i_ + 1) * 128], nk=128, c0=0, c1=N, v=vv[:, NL + i_, kvh, :])
                                 for i_ in range(2)]
                        for m in range(max(4 * cch - 1, 0), min(4 * cch + 4, 15) + 1):
                            nlo, nhi = max(m - 1, 4 * cch), min(m + 1, 4 * cch + 3)
                            masks = []
                            for n_ in range(nlo, nhi + 1):
                                if n_ == m + 1:
                                    masks.append(((n_ - nlo) * 128, trib[:, 0:128]))
                                elif n_ == m - 1:
                                    masks.append(((n_ - nlo) * 128, trib[:, 128:256]))
                            units.append(dict(kT=kT[ho:ho + 64, 0, m * 128:(m + 1) * 128], nk=128, c0=(nlo - 4 * cch) * 128,
                                              c1=(nhi - 4 * cch + 1) * 128, v=vv[:, m, kvh, :], masks=masks))
                        bank = (1, 2, 4)[h % 3]
                        attend(N, units, 0.125, lambda c0, c1, ho=ho, g=g: qT_[ho:ho + 64, g, c0:c1], 64, None, PT, fused=bank, qkey=('qT', cch % 2), la=3, sbk=[0, 3, 5, 7])
                        norm_fused(h, N, bank, mix, add_sink=True)
                    out_proj(l, mix, 128, 4, wo, tiles, acc, dst, 0)
                overlapped_chunks([list(range(c_ * 4, c_ * 4 + 4)) for c_ in range(4)], q_tile, att_chunk, nsets_b=1)
            else:
                wi = AR.alloc(8, 1536)
                load_w(wi, win_d[:, :, 768:2304], 'wi', 'wi', 8)
                wo = AR.alloc(4, 1024)
                load_w(wo, W[l]['w_out'][512:1024, :].rearrange("(h p) n -> p h n", p=128), 'wo', 'wo', 4)
                kT = AR.alloc(4, T)
                vv = AR.alloc(NT, 4, 128)
                qTs = [AR.alloc(4, 512) for _ in range(2)]
                mix = AR.alloc(4, 512)

                def kv_tile(S, t):
                    lat = t < NL
                    i = S.i
                    if lat:
                        load_cs(S, t, 64)
                    get_h(S, src, t, 0, S.hT, HK(S), HCOMP[0])
                    proj(S.hT, HK(S), wi, 'wi', 512, 512, S.pj)
                    CP('act', S.raw[:, 0:512], pb[S.pj][:, :], [PB[S.pj]], [('raw', i)])
                    prep(S, _view(S.raw[:], 0, [[64, 8], [1, 64]]), 8, 64, gain('dk'), _view(S.yb, 0, [[64, 8], [1, 64]]), rope=(0, 16) if lat else None)
                    to_fm(S, [S.yb[:, c * 128:(c + 1) * 128] for c in range(4)], 128, lambda g0, n: kT[:, g0:g0 + n, t * 128:(t + 1) * 128], ('fm',))
                    proj(S.hT, HK(S), wi, 'wi', 1024, 512, S.xb[0])
                    CP('act', vv[:, t, :, :], pb[S.xb[0]][:, :].rearrange("p (h e) -> p h e", h=4), [PB[S.xb[0]]], [('v',)])

                def q_tile(S, tt, qb):
                    ti, t = tt
                    i = S.i
                    bk = 6 + i
                    load_cs(S, t, 64)
                    get_h(S, src, t, 0, S.hT, HK(S), HCOMP[0])
                    proj(S.hT, HK(S), wi, 'wi', 0, 512, bk)
                    CP('act', S.raw[:, 0:512], pb[bk][:, :], [PB[bk]], [('raw', i)])
                    prep(S, _view(S.raw[:], 0, [[64, 8], [1, 64]]), 8, 64, gain('dq'), _view(S.yb, 0, [[64, 8], [1, 64]]), rope=(0, 16))
                    to_fm(S, [S.yb[:, c * 128:(c + 1) * 128] for c in range(4)], 128, lambda g0, n: qTs[qb][:, g0:g0 + n, ti * 128:(ti + 1) * 128], ('qT', qb))

                interleaved(kv_tile, list(range(NT)))
                HCOMP[0] = False

                def att_chunk(cch, tiles):
                    N = 512
                    qT_ = qTs[cch % 2]
                    for hh in range(4):
                        for cc in range(2):
                            ho = cc * 64
                            units = [dict(kT=kT[ho:ho + 64, hh, kt * 128:(kt + 1) * 128], nk=128, c0=0, c1=N, v=vv[:, kt, hh, :])
                                     for kt in ([16, 17] + list(range(16)))]
                            attend(N, units, 0.125, lambda c0, c1, ho=ho, hh=hh: qT_[ho:ho + 64, hh, c0:c1], 128, NB[cc], PT, qkey=('qT', cch % 2), la=2, sbk=[0, 3, 7])
                        for cc in range(2):
                            pN, pD = NB[cc]
                            p.op('dve', lambda e, cc=cc, pD=pD: e.reciprocal(rec[cc][:, :], pb[pD][:, :]), [PB[pD]], [RECK[cc]])
                            TT('dve', of[cc][:, :], pb[pN][:, :], rec[cc][:, :], ALU.mult, [PB[pN], RECK[cc]], [OFK[cc]])
                        STT(of[0][:, :], of[1][:, :], lamt[:, 3:4], of[0][:, :], ALU.mult, ALU.add, [OFK[0], 'lamt'], [OFK[0]])
                        TT('pool', PT[0][:, :], of[0][:, :], of[0][:, :], ALU.mult, [OFK[0]], [('PT', 0)])
                        MM(pb[0][:, :], onesb[:, :], PT[0][:, :], True, True, [('PT', 0), 'onesb'], [PB[0]])
                        ACT(rec[0][:, :], pb[0][:, :], AF.Ln, [PB[0]], [RECK[0]], bias=EPS, scale=1.0 / 128)
                        ACT(rec[0][:, :], rec[0][:, :], AF.Exp, [RECK[0]], [RECK[0]], scale=-0.5)
                        STT(mix[:, hh, :], of[0][:, :], sublnS[:, 0:1], rec[0][:, :], ALU.mult, ALU.mult, [OFK[0], RECK[0], 'sublnS'], [('mix',)])
                    out_proj(l, mix, 128, 4, wo, tiles, acc, dst, 0)
                overlapped_chunks([list(range(c_ * 4, c_ * 4 + 4)) for c_ in range(4)], q_tile, att_chunk, nsets_b=1)

        def ffn_half(l, hf, hsrc, acc, dst):
            AR.reset()
            p.barrier()
            wg = AR.alloc(8, 1408)
            wu = AR.alloc(8, 1408)
            wd = AR.alloc(11, 1024)
            load_w(wg, W[l]['wg'].rearrange("(c p) n -> p c n", p=128)[:, :, hf * 1408:(hf + 1) * 1408], 'wg', 'wg', 8)
            load_w(wu, W[l]['wu'].rearrange("(c p) n -> p c n", p=128)[:, :, hf * 1408:(hf + 1) * 1408], 'wu', 'wu', 8)
            load_w(wd, W[l]['wd'][hf * 1408:(hf + 1) * 1408, :].rearrange("(f p) n -> p f n", p=128), 'wd', 'wd', 11)
            h2s = [AR.alloc(8, 512) for _ in range(2)]
            act = AR.alloc(11, 512)
            sg = [AR.alloc(512) for _ in range(2)]
            chunks = [list(range(c * 4, c * 4 + 4)) for c in range(4)] + ([[16, 17]] if l == 0 else [])

            if hf == 0:
                for S in SS:
                    S.hT = AR.alloc(8, 128)

                def norm_tile(S, t):
                    get_h(S, hsrc, t, 1, S.hT, ('hT', S.i), True)
                interleaved(norm_tile, [t for tiles in chunks for t in tiles])

            def emit_h_tile(ci_, ti):
                t = chunks[ci_][ti]
                p.dma('sp', h2s[ci_ % 2][:, :, ti * 128:(ti + 1) * 128], hts[1][t].rearrange("p (c n) -> p c n", c=8),
                      reads=[('hts', 1, t)], writes=[('h2', ci_ % 2, ti)], sem=f'h2l{ci_ % 2}_{ti}')

            for ti in range(len(chunks[0])):
                emit_h_tile(0, ti)
            for ci_, tiles in enumerate(chunks):
                N = 128 * len(tiles)
                h2 = h2s[ci_ % 2]
                hks = [('h2', ci_ % 2, ti_) for ti_ in range(len(tiles))]
                nxt = len(chunks[ci_ + 1]) if ci_ + 1 < len(chunks) else 0
                for f in range(11):
                    bg, bu = (f % 2) * 2, (f % 2) * 2 + 1
                    for c in range(8):
                        MM(pb[bg][:, 0:N], wg[:, c, f * 128:(f + 1) * 128], h2[:, c, 0:N], c == 0, c == 7, hks + ['wg'], [PB[bg]], inc=(c == 7))
                    for c in range(8):
                        MM(pb[bu][:, 0:N], wu[:, c, f * 128:(f + 1) * 128], h2[:, c, 0:N], c == 0, c == 7, hks + ['wu'], [PB[bu]], inc=(c == 7))
                    ACT(sg[f % 2][:, 0:N], pb[bg][:, 0:N], AF.Silu, [PB[bg]], [('sg', f % 2)])
                    TT('dve', act[:, f, 0:N], pb[bu][:, 0:N], sg[f % 2][:, 0:N], ALU.mult, [PB[bu], ('sg', f % 2)], [('act',)])
                    if f % 2 == 1 and (f // 2) < nxt:
                        emit_h_tile(ci_ + 1, f // 2)
                for ti, t in enumerate(tiles):
                    j = 0 if t < NL else 1
                    xi2 = load_xa(acc, t)
                    for h2_ in range(2):
                        for f in range(11):
                            bk = (4 + h2_) if ti % 2 == 0 else (2 + h2_)
                            MM(pb[bk][:, :], act[:, f, ti * 128:(ti + 1) * 128], wd[:, f, h2_ * 512:(h2_ + 1) * 512], f == 0, f == 10,
                               [('act',), 'wd'], [PB[bk]], inc=(f == 10))
                    residual_store((4, 5) if ti % 2 == 0 else (2, 3), xi2, 1, j, dst, t)

        seq = []
        if 0 in layers:
            seq += [('ada', 0), ('mix', 0, 'A', 'xin', 'xin', 'xs_a'), ('mix', 0, 'B', 'xin', 'xs_a', 'xs_b'),
                    ('ffn', 0, 0, 'xs_b', 'xs_b', 'xs_c'), ('ffn', 0, 1, 'xs_b', 'xs_c', 'xs_d')]
        if 1 in layers:
            s1 = 'xs_d' if 0 in layers else 'xin'
            seq += [('ada', 1), ('mix', 1, 'C', s1, s1, 'xs_a'), ('mix', 1, 'D', s1, 'xs_a', 'xs_b'),
                    ('ffn', 1, 0, 'xs_b', 'xs_b', 'xs_c'), ('ffn', 1, 1, 'xs_b', 'xs_c', 'y')]
        if stop_after is not None:
            seq = seq[:stop_after]
        last_dst = None
        for st in seq:
            if st[0] == 'ada':
                ada_phase(st[1])
            elif st[0] == 'mix':
                mixer_pass(st[1], st[2], st[3], st[4], st[5])
                last_dst = st[5]
            else:
                ffn_half(st[1], st[2], st[3], st[4], st[5])
                last_dst = st[5]
        if last_dst is None:
            p.dma('sp', xs['y'][0:128, :], Gt[0][0][:], reads=[('G', 0, 0)], writes=[('y', 0)], sem='dbg0')
            p.dma('sp', xs['y'][128:256, :], Gt[0][1][:], reads=[('G', 0, 1)], writes=[('y', 1)], sem='dbg0')
            p.dma('sp', xs['y'][256:384, :], Gt[1][0][:], reads=[('G', 1, 0)], writes=[('y', 2)], sem='dbg0')
            p.dma('sp', xs['y'][384:512, 0:64], AB[:], reads=['AB'], writes=[('y', 3)], sem='dbg0')
            p.finish([('y', t) for t in range(4)])
            p.emit()
            return nc
        if last_dst != 'y':
            for t in range(NL):
                S = SS[t % 2]
                p.dma('sp', S.xt[:], xs[last_dst][t * 128:(t + 1) * 128, :], reads=[(last_dst, t)], writes=[('xt', S.i)], sem=f'xt{S.i}')
                p.dma('sp', xs['y'][t * 128:(t + 1) * 128, :], S.xt[:], reads=[('xt', S.i)], writes=[('y', t)], sem=f'dbg{t % 2}')
        p.finish([('y', t) for t in range(NL)])
        p.emit()
    return nc


_CACHE = {}


def _host_inputs(inputs, layers=(0, 1)):
    f = lambda a: np.ascontiguousarray(np.asarray(a, dtype=np.float32))
    x, c, ctx, c_ctx = f(inputs['x']), f(inputs['c']), f(inputs['ctx']), f(inputs['c_ctx'])
    cs64, cs32 = _rope_tables()
    lib, nam = _na_tables(f(inputs['l0_na_rpb']))
    gl = []
    for n in ['l0_mla_qa_g', 'l0_mla_kva_g', 'l0_mla_qn_g', 'l0_mla_kn_g', 'l0_na_qn_g', 'l0_na_kn_g', 'l1_gqa_qn_g', 'l1_gqa_kn_g',
              'l1_diff_qn_g', 'l1_diff_kn_g', 'l1_gqa_sink', 'l1_diff_lq1', 'l1_diff_lk1', 'l1_diff_lq2', 'l1_diff_lk2']:
        gl.append(f(inputs[n]).reshape(-1))
    gains = np.ascontiguousarray(np.broadcast_to(np.concatenate(gl)[None, :], (128, GW)))
    pp = np.zeros((128, 33), np.float32)
    for i, n in enumerate(['l0_norm1_g', 'l0_norm2_g', 'l1_norm1_g', 'l1_norm2_g']):
        pp[:, i * 8:(i + 1) * 8] = f(inputs[n]).reshape(8, 128).T
    pp[:, 32] = f(inputs['l1_diff_subln_g'])
    ident = np.eye(128, dtype=np.float32)
    jj = np.arange(128)[:, None]
    ii = np.arange(128)[None, :]
    tri = np.concatenate([(ii <= jj), (jj <= ii)], axis=1).astype(np.float32)
    sel = np.zeros((2, 256), np.float32)
    sel[0, 0:128] = 1.0
    sel[1, 128:256] = 1.0
    shared = dict(gains=gains, pp=pp, ident=ident, tri=tri, sel=sel, cs64=cs64, cs32=cs32,
                  nalib=np.ascontiguousarray(lib.reshape(128, 4 * 960)), namask=nam)
    for l in (0, 1):
        shared[f'l{l}_ada_w'] = f(inputs[f'l{l}_ada_w'])
        shared[f'l{l}_ada_b2'] = np.ascontiguousarray(np.broadcast_to(f(inputs[f'l{l}_ada_b'])[None, :], (2, 6 * D)))
        shared[f'l{l}_w_in'] = f(inputs[f'l{l}_w_in'])
        shared[f'l{l}_w_out'] = f(inputs[f'l{l}_w_out'])
        shared[f'l{l}_ffn_w_gate'] = f(inputs[f'l{l}_ffn_w_gate'])
        shared[f'l{l}_ffn_w_up'] = f(inputs[f'l{l}_ffn_w_up'])
        shared[f'l{l}_ffn_w_down'] = f(inputs[f'l{l}_ffn_w_down'])
    shared['l0_mla_w_uq'] = f(inputs['l0_mla_w_uq'])
    shared['l0_mla_w_ukv'] = f(inputs['l0_mla_w_ukv'])
    maps = []
    for b in range(x.shape[0]):
        m = dict(shared)
        m['xin'] = np.ascontiguousarray(np.concatenate([x[b], ctx[b]], axis=0))
        cv = np.stack([c[b], c_ctx], axis=0)
        m['cvecT'] = np.ascontiguousarray(cv.reshape(2, 8, 128).transpose(2, 1, 0).reshape(128, 16))
        maps.append(m)
    return maps


def kernel(**inputs):
    maps = _host_inputs(inputs)
    if 'nc' not in _CACHE:
        _CACHE['nc'] = build()
    res = run_bass_kernel_spmd(_CACHE['nc'], maps, core_ids=list(range(len(maps))))
    return np.stack([np.asarray(r['y'], dtype=np.float32) for r in res.results], axis=0)
```

```python
import os
import numpy as np
import concourse.bass as bass
import concourse.mybir as mybir
from concourse.bass_utils import run_bass_kernel_spmd
from concourse.alu_op_type import AluOpType as ALU
from contextlib import ExitStack

F32 = mybir.dt.float32
BF16 = mybir.dt.bfloat16
AF = mybir.ActivationFunctionType
AX = mybir.AxisListType


STRICT = int(os.environ.get('KSTRICT', '1'))


class Prog:
    ENG = ('pe', 'act', 'dve', 'pool', 'sp')

    def __init__(self, nc):
        self.nc = nc
        self.q = {e: [] for e in self.ENG}
        self.cnt = {}
        self.res = {}
        self.known = {e: {} for e in self.ENG}
        self.pending = {e: {} for e in self.ENG}

    def _need(self, eng, toks, waits, skip=None):
        for s, v in toks.items():
            if s == skip:
                continue
            if self.known[eng].get(s, 0) < v and waits.get(s, 0) < v:
                waits[s] = v

    def _record(self, eng, fn, reads, writes, tok, incspec, is_dma=False):
        waits = {}
        own = 'e_' + eng
        wskip = tok[0] if is_dma else (own if (eng == 'pe' or not STRICT) else None)
        for r in reads:
            st = self.res.get(r)
            if st is not None:
                self._need(eng, st[0], waits, skip=own if eng == 'pe' else None)
        for w in writes:
            st = self.res.get(w)
            if st is not None:
                self._need(eng, st[0], waits, skip=wskip)
                self._need(eng, st[1], waits, skip=wskip)
        for s, v in waits.items():
            self.known[eng][s] = v
        for s, v in self.pending[eng].items():
            if waits.get(s, 0) < v:
                waits[s] = v
        self.pending[eng] = {}
        self.q[eng].append((sorted(waits.items()), fn, incspec))
        for r in reads:
            st = self.res.setdefault(r, [{}, {}])
            if st[1].get(tok[0], 0) < tok[1]:
                st[1][tok[0]] = tok[1]
        for w in writes:
            self.res[w] = [{tok[0]: tok[1]}, {}]

    def capture(self):
        self.cap = []
        return self.cap

    def end_capture(self):
        c = self.cap
        self.cap = None
        return c

    def replay(self, lists):
        idx = [0] * len(lists)
        live = True
        while live:
            live = False
            for k, L in enumerate(lists):
                if idx[k] < len(L):
                    it = L[idx[k]]
                    idx[k] += 1
                    live = True
                    if it[0] == 'op':
                        self.op(*it[1:])
                    else:
                        self.dma(*it[1:-1], **it[-1])

    def op(self, eng, fn, reads=(), writes=(), inc=True):
        if getattr(self, 'cap', None) is not None:
            self.cap.append(('op', eng, fn, list(reads), list(writes), inc))
            return
        own = 'e_' + eng
        c = self.cnt.get(own, 0)
        if inc:
            self.cnt[own] = c + 1
        self._record(eng, fn, reads, writes, (own, c + 1), (own, 1) if inc else None)

    def dma(self, eng, out, in_, reads=(), writes=(), sem=None, **kw):
        if getattr(self, 'cap', None) is not None:
            self.cap.append(('dma', eng, out, in_, list(reads), list(writes), sem, kw))
            return
        s = 'd_' + sem
        c = self.cnt.get(s, 0) + 16
        self.cnt[s] = c
        self._record(eng, lambda e: e.dma_start(out=out, in_=in_, **kw), reads, writes, (s, c), (s, 16), is_dma=True)

    def finish(self, keys):
        self.op('sp', lambda e: e.nop(), reads=list(keys), inc=False)

    def emit(self):
        nc = self.nc
        names = sorted(self.cnt.keys())
        with ExitStack() as es:
            sems = {n: es.enter_context(nc.semaphore(n)) for n in names}
            with nc.Block() as block:
                def body(ename):
                    def f(e):
                        for waits, fn, incspec in self.q[ename]:
                            for s, v in waits:
                                e.wait_ge(sems[s], v)
                            ins = fn(e)
                            if incspec is not None:
                                ins.then_inc(sems[incspec[0]], incspec[1])
                        for s, v in sorted(self.pending[ename].items()):
                            e.wait_ge(sems[s], v)
                    return f
                block.tensor(body('pe'))
                block.scalar(body('act'))
                block.vector(body('dve'))
                block.gpsimd(body('pool'))
                block.sync(body('sp'))

    def check(self):
        sem = {}
        ptr = {e: 0 for e in self.ENG}
        prog = True
        while prog:
            prog = False
            for e in self.ENG:
                while ptr[e] < len(self.q[e]):
                    waits, fn, inc = self.q[e][ptr[e]]
                    if all(sem.get(s_, 0) >= v for s_, v in waits):
                        if inc is not None:
                            sem[inc[0]] = sem.get(inc[0], 0) + inc[1]
                        ptr[e] += 1
                        prog = True
                    else:
                        break
        stuck = {e: (ptr[e], len(self.q[e]), self.q[e][ptr[e]][0]) for e in self.ENG if ptr[e] < len(self.q[e])}
        return stuck, sem

    def barrier(self):
        snap = dict(self.cnt)
        for eng in self.ENG:
            waits = {}
            own = 'e_' + eng
            for s, v in snap.items():
                if s == own:
                    continue
                if self.known[eng].get(s, 0) < v:
                    waits[s] = v
                    self.known[eng][s] = v
            for s, v in waits.items():
                if self.pending[eng].get(s, 0) < v:
                    self.pending[eng][s] = v


D = 1024
T = 2304
NT = 18
NL = 16
FH = 2816
EPS = 1e-6
NEG = -30000.0
GRID_W = 64
G_OFF = {}
_o = 0
for _n, _w in [('qa', 256), ('kva', 128), ('qn96', 96), ('kn96', 96), ('naq', 64), ('nak', 64), ('gq', 64), ('gk', 64),
               ('dq', 64), ('dk', 64), ('sink', 8), ('lq1', 64), ('lk1', 64), ('lq2', 64), ('lk2', 64)]:
    G_OFF[_n] = (_o, _w)
    _o += _w
GW = _o


def _rope_tables():
    pos = np.arange(2048)
    rows, cols = pos // GRID_W, pos % GRID_W

    def tab(h):
        fr = (10000.0 ** (-np.arange(0, h, 2, dtype=np.float32) / np.float32(h))).astype(np.float32)
        ar = rows.astype(np.float32)[:, None] * fr[None, :]
        ac = cols.astype(np.float32)[:, None] * fr[None, :]
        cr, sr, cc, sc = np.cos(ar), np.sin(ar), np.cos(ac), np.sin(ac)
        cos = np.concatenate([cr, cr, cc, cc], axis=1)
        sin = np.concatenate([-sr, sr, -sc, sc], axis=1)
        return np.concatenate([cos, sin], axis=1).astype(np.float32)
    return tab(32), tab(16)


def _na_tables(rpb):
    kc = np.arange(64)[:, None]
    qc = np.arange(64)[None, :]
    c0 = np.clip(qc - 8, 0, 48)
    valid = (kc >= c0) & (kc < c0 + 16)
    offc = np.clip(kc - qc + 15, 0, 30)
    lib = np.zeros((128, 4, 960), np.float32)
    mask = np.zeros((128, 960), np.float32)
    for dr in range(-7, 8):
        col = (7 - dr) * 64
        for h in range(8):
            lib[(h % 2) * 64:(h % 2) * 64 + 64, h // 2, col:col + 64] = rpb[h, dr + 7][offc]
        mask[0:64, col:col + 64] = np.where(valid, 0.0, NEG)
        mask[64:128, col:col + 64] = np.where(valid, 0.0, NEG)
    return lib, mask


def _bc_mid(a, G):
    return bass.AP(a.tensor, a.offset, [list(a.ap[0]), [0, G], list(a.ap[1])])


def _bc_last(a, d):
    return bass.AP(a.tensor, a.offset, [list(a.ap[0]), list(a.ap[1]), [0, d]])


def _view(a, off, dims):
    return bass.AP(a.tensor, a.offset + off, [list(a.ap[0])] + [list(x) for x in dims])


def build(layers=(0, 1), stop_after=None):
    nc = bass.Bass("TRN2", target_bir_lowering=False)
    es = ExitStack()

    def din(name, shape):
        return nc.dram_tensor(name, list(shape), F32, kind="ExternalInput").ap()

    xin = din("xin", [T, D])
    cvecT = din("cvecT", [128, 16])
    gains_d = din("gains", [128, GW])
    pp_d = din("pp", [128, 33])
    ident_d = din("ident", [128, 128])
    tri_d = din("tri", [128, 256])
    sel_d = din("sel", [2, 256])
    cs64_d = din("cs64", [2048, 128])
    cs32_d = din("cs32", [2048, 64])
    lib_d = din("nalib", [128, 4 * 960])
    nam_d = din("namask", [128, 960])
    W = {}
    for l in (0, 1):
        W[l] = dict(
            ada_w=din(f"l{l}_ada_w", [D, 6 * D]), ada_b2=din(f"l{l}_ada_b2", [2, 6 * D]),
            w_in=din(f"l{l}_w_in", [D, 1952 if l == 0 else 2304]), w_out=din(f"l{l}_w_out", [D, D]),
            wg=din(f"l{l}_ffn_w_gate", [D, FH]), wu=din(f"l{l}_ffn_w_up", [D, FH]), wd=din(f"l{l}_ffn_w_down", [FH, D]))
    W[0]['w_uq'] = din("l0_mla_w_uq", [256, 768])
    W[0]['w_ukv'] = din("l0_mla_w_ukv", [128, 1024])
    y_out = nc.dram_tensor("y", [2048, D], F32, kind="ExternalOutput").ap()
    xs = {n: nc.dram_tensor(n, [T, D], F32).ap() for n in ('xs_a', 'xs_b', 'xs_c', 'xs_d')}
    xs['xin'] = xin
    hts = [nc.dram_tensor(f'hts{k}', [NT, 128, D], BF16).ap() for k in range(2)]
    xs['y'] = y_out

    with es:
        def sb(name, shape, dt=F32):
            return es.enter_context(nc.sbuf_tensor(name, list(shape), dt))

        def ps(name, shape, dt=F32):
            return es.enter_context(nc.psum_tensor(name, list(shape), dt))

        p = Prog(nc)
        ident = sb("ident_s", [128, 128])
        identb = sb("identb", [128, 128], BF16)
        onesb = sb("onesb", [128, 128], BF16)
        trib = sb("trib", [128, 256], BF16)
        sel = sb("sel_s", [2, 256])
        gains = sb("gains_s", [128, GW])
        pp = sb("pp_s", [128, 33])
        cT = sb("cT", [128, 16])
        cTb = sb("cTb", [128, 16], BF16)
        modT = sb("modT", [128, 64])
        AB = sb("AB", [128, 64])
        Gt = [[sb(f"G{k}{j}", [128, D]) for j in range(2)] for k in range(2)]
        small = sb("small", [128, 64])
        lamt = sb("lamt", [128, 8])
        es_sink = sb("es_sink", [128, 8])
        sublnS = sb("sublnS", [128, 1])
        naq_s = sb("naq_s", [128, 64])
        xa = [sb("xa0", [128, D])]
        recb = [sb(f"recb{i}", [128, 512]) for i in range(2)]
        xo = [sb(f"xo{i}", [128, D]) for i in range(2)]
        mrow = [sb(f"mrow{i}", [2, 512]) for i in range(2)]
        brow = [sb(f"brow{i}", [2, 512]) for i in range(2)]
        NS = 2

        class SSet:
            pass
        SS = []
        for i in range(NS):
            S_ = SSet()
            S_.i = i
            S_.xt = sb(f"xt{i}", [128, D])
            S_.buf = sb(f"buf{i}", [128, D])
            S_.raw = sb(f"raw{i}", [128, D])
            S_.y1 = sb(f"y1{i}", [128, D])
            S_.cs = sb(f"cs{i}", [128, 192])
            S_.small = small[:, i * 32:(i + 1) * 32]
            S_.xb = (3 * i, 3 * i + 1)
            S_.pj = 3 * i + 2
            S_.k = lambda n, i=i: (n, i)
            SS.append(S_)
        junk = SS[0].buf
        raw = SS[0].raw
        rec = recb
        of = [xo[1][:, 0:512], xo[1][:, 512:1024]]
        RECK = [('rec', 0), ('rec', 1)]
        OFK = [('xo', 1), ('xo', 1)]
        ARN = 63800
        arena = sb("arena", [128, ARN], BF16)
        pb = [ps(f"pb{i}", [128, 512]) for i in range(8)]
        ptbs = [pb[6][:, :].bitcast(BF16), pb[7][:, :].bitcast(BF16)]
        PB = [('pb', i) for i in range(8)]

        class Arena:
            def __init__(self):
                self.off = 0

            def reset(self):
                self.off = 0

            def alloc(self, *free):
                n = int(np.prod(free))
                a = arena[:, self.off:self.off + n]
                self.off += n
                assert self.off <= ARN, self.off
                if len(free) == 2:
                    a = a.rearrange("p (a b) -> p a b", a=free[0])
                elif len(free) == 3:
                    a = a.rearrange("p (a b c) -> p a b c", a=free[0], b=free[1])
                return a
        AR = Arena()

        def MM(out, lhsT, rhs, start, stop, reads, writes, inc=True):
            p.op('pe', lambda e: e.matmul(out, lhsT, rhs, start=start, stop=stop), reads, writes, inc)

        def TR(out, in_, idn, reads, writes, inc=True):
            p.op('pe', lambda e: e.transpose(out, in_, idn), reads, writes, inc)

        def ACT(out, in_, func, reads, writes, bias=None, scale=None):
            kw = {}
            if bias is not None:
                kw['bias'] = bias
            if scale is not None:
                kw['scale'] = scale
            p.op('act', lambda e: e.activation(out, in_, func, **kw), reads, writes)

        def TT(eng, out, a, b, op, reads, writes):
            p.op(eng, lambda e: e.tensor_tensor(out, a, b, op), reads, writes)

        def TS(eng, out, a, s1, s2, op0, op1, reads, writes):
            if s2 is None:
                p.op(eng, lambda e: e.tensor_scalar(out, a, s1, None, op0), reads, writes)
            else:
                p.op(eng, lambda e: e.tensor_scalar(out, a, s1, s2, op0, op1), reads, writes)

        def STT(out, a, s, b, op0, op1, reads, writes):
            p.op('dve', lambda e: e.scalar_tensor_tensor(out, a, s, b, op0, op1), reads, writes)

        def CP(eng, out, in_, reads, writes):
            if eng == 'act':
                ACT(out, in_, AF.Copy, reads, writes)
            else:
                p.op(eng, lambda e: e.tensor_copy(out, in_), reads, writes)

        def RED(out, in_, reads, writes):
            p.op('dve', lambda e: e.tensor_reduce(out, in_, AX.X, ALU.add), reads, writes)

        def RSTD(ap, n_inv, reads_writes):
            ACT(ap, ap, AF.Ln, [reads_writes], [reads_writes], bias=EPS, scale=n_inv)
            ACT(ap, ap, AF.Exp, [reads_writes], [reads_writes], scale=-0.5)

        def gain(name):
            o, w = G_OFF[name]
            return gains[:, o:o + w]

        p.dma('sp', ident[:], ident_d, writes=['ident'], sem='c0')
        p.dma('sp', sel[:], sel_d, writes=['sel'], sem='c1')
        p.dma('sp', gains[:], gains_d, writes=['gains'], sem='c2')
        p.dma('sp', pp[:], pp_d, writes=['pp'], sem='c3')
        p.dma('sp', cT[:], cvecT, writes=['cT'], sem='c4')
        p.dma('pool', identb[:], ident_d, writes=['identb'], sem='c5')
        p.dma('pool', trib[:], tri_d, writes=['trib'], sem='c6')
        p.op('pool', lambda e: e.memset(onesb[:], 1.0), writes=['onesb'])
        ACT(cTb[:], cT[:], AF.Silu, ['cT'], ['cTb'])
        TS('dve', naq_s[:], gain('naq'), 0.125, None, ALU.mult, None, ['gains'], ['naq_s'])
        lam_init = 0.8 - 0.6 * float(np.exp(-0.3 * 1))
        TT('dve', junk[:, 0:64], gain('lq1'), gain('lk1'), ALU.mult, ['gains'], [('buf', 0)])
        RED(lamt[:, 0:1], junk[:, 0:64], [('buf', 0)], ['lamt'])
        TT('dve', junk[:, 64:128], gain('lq2'), gain('lk2'), ALU.mult, ['gains'], [('buf', 0)])
        RED(lamt[:, 1:2], junk[:, 64:128], [('buf', 0)], ['lamt'])
        ACT(lamt[:, 0:2], lamt[:, 0:2], AF.Exp, ['lamt'], ['lamt'])
        TT('dve', lamt[:, 2:3], lamt[:, 1:2], lamt[:, 0:1], ALU.subtract, ['lamt'], ['lamt'])
        TS('dve', lamt[:, 3:4], lamt[:, 2:3], -lam_init, None, ALU.add, None, ['lamt'], ['lamt'])
        ACT(es_sink[:], gain('sink'), AF.Exp, ['gains'], ['es_sink'])
        TS('dve', sublnS[:], pp[:, 32:33], 1.0 - lam_init, None, ALU.mult, None, ['pp'], ['sublnS'])

        def ada_phase(l):
            AR.reset()
            p.barrier()
            aw = [AR.alloc(8, 512) for _ in range(3)]
            adaw = W[l]['ada_w'].rearrange("(c p) n -> p c n", p=128)
            for n in range(12):
                s = n % 3
                b2 = n % 2
                for c in range(8):
                    p.dma('pool', aw[s][:, c, :], adaw[:, c, n * 512:(n + 1) * 512], writes=[('aw', s)], sem=f'aw{s}')
                p.dma('sp', brow[b2][:], W[l]['ada_b2'][:, n * 512:(n + 1) * 512], writes=[('brow', b2)], sem=f'brow{b2}')
                for c in range(8):
                    MM(pb[0][0:2, :], cTb[:, c * 2:c * 2 + 2], aw[s][:, c, :], c == 0, c == 7,
                       ['cTb', ('aw', s)], [PB[0]], inc=(c == 7))
                TT('dve', mrow[b2][:], pb[0][0:2, :], brow[b2][:], ALU.add, [PB[0], ('brow', b2)], [('mrow', b2)])
                sec, half = n // 2, n % 2
                if sec in (2, 5):
                    k = 0 if sec == 2 else 1
                    for j in range(2):
                        MM(pb[1 + j][:, :], sel[0:2, j * 128:(j + 1) * 128], mrow[b2][:], True, True,
                           ['sel', ('mrow', b2)], [PB[1 + j]])
                        CP('act', Gt[k][j][:, half * 512:(half + 1) * 512], pb[1 + j][:, :], [PB[1 + j]], [('G', k, j)])
                else:
                    si = {0: 0, 1: 1, 3: 2, 4: 3}[sec]
                    for q in range(4):
                        c = half * 4 + q
                        TR(pb[3][:, (si * 8 + c) * 2:(si * 8 + c) * 2 + 2], mrow[b2][0:2, q * 128:(q + 1) * 128], ident[0:2, 0:2],
                           [('mrow', b2), 'ident'], [PB[3]])
            CP('dve', modT[:], pb[3][:, 0:64], [PB[3]], ['modT'])
            for k in range(2):
                sh = modT[:, (2 * k) * 16:(2 * k) * 16 + 16]
                sc = modT[:, (2 * k + 1) * 16:(2 * k + 1) * 16 + 16]
                gn = pp[:, l * 16 + k * 8:l * 16 + k * 8 + 8]
                A = AB[:, k * 32:k * 32 + 16]
                B = AB[:, k * 32 + 16:k * 32 + 32]
                TS('dve', A, sc, 1.0, None, ALU.add, None, ['modT'], ['AB'])
                TT('dve', A.rearrange("p (c j) -> p c j", j=2), A.rearrange("p (c j) -> p c j", j=2), _bc_last(gn, 2), ALU.mult,
                   ['AB', 'pp'], ['AB'])
                CP('dve', B, sh, ['modT'], ['AB'])

        state = dict(xa=0, xo=0)

        def emit_h(S, src, t, k, dst, hkey, banks=None):
            j = 0 if t < NL else 1
            i = S.i
            xb = banks if banks is not None else S.xb
            X = S.xt
            p.dma('sp', X[:], xs[src][t * 128:(t + 1) * 128, :], reads=[(src, t)], writes=[('xt', i)], sem=f'xt{i}')
            TT('dve', S.buf[:], X[:], X[:], ALU.mult, [('xt', i)], [('buf', i)])
            RED(S.small[:, 0:1], S.buf[:], [('buf', i)], [('small', i)])
            RSTD(S.small[:, 0:1], 1.0 / D, ('small', i))
            ACT(S.buf[:], X[:], AF.Identity, [('xt', i), ('small', i)], [('buf', i)], scale=S.small[:, 0:1])
            for rnd in range(2):
                for c in range(rnd * 4, rnd * 4 + 4):
                    TR(pb[xb[c // 4]][:, (c % 4) * 128:(c % 4 + 1) * 128], S.buf[:, c * 128:(c + 1) * 128], ident[:],
                       [('buf', i), 'ident'], [PB[xb[c // 4]]], inc=(c % 4 == 3))
                for c in range(rnd * 4, rnd * 4 + 4):
                    ACT(dst[:, c, :], pb[xb[c // 4]][:, (c % 4) * 128:(c % 4 + 1) * 128], AF.Identity,
                        [PB[xb[c // 4]], 'AB'], [hkey], scale=AB[:, k * 32 + c * 2 + j:k * 32 + c * 2 + j + 1],
                        bias=AB[:, k * 32 + 16 + c * 2 + j:k * 32 + 16 + c * 2 + j + 1])

        def get_h(S, src, t, k, dst, hkey, compute, banks=None):
            i = S.i
            if compute:
                emit_h(S, src, t, k, dst, hkey, banks=banks)
                p.dma('sp', hts[k][t].rearrange("p (c n) -> p c n", c=8), dst, reads=[hkey], writes=[('hts', k, t)], sem=f'hs{i}')
            else:
                p.dma('sp', dst, hts[k][t].rearrange("p (c n) -> p c n", c=8), reads=[('hts', k, t)], writes=[hkey], sem=f'hl{i}')

        def proj(hT, hkey, wsb, wkey, col0, ncols, bank, M0=0, M1=128):
            for c in range(8):
                MM(pb[bank][0:M1 - M0, 0:ncols], hT[:, c, M0:M1], wsb[:, c, col0:col0 + ncols], c == 0, c == 7,
                   [hkey, wkey], [PB[bank]], inc=(c == 7))

        def load_cs(S, t, which):
            i = S.i
            if which == 64:
                p.dma('sp', S.cs[:, 0:128], cs64_d[t * 128:(t + 1) * 128, :], writes=[('cs', i)], sem=f'cs{i}')
            else:
                p.dma('sp', S.cs[:, 0:64], cs32_d[t * 128:(t + 1) * 128, :], writes=[('cs', i)], sem=f'cs{i}')

        def prep(S, src, G, d, gain_ap, out_b, rope=None):
            i = S.i
            kr, kb, ky, ks, kyb = ('raw', i), ('buf', i), ('y1', i), ('small', i), ('yb', i)
            sq = _view(S.buf[:], 0, [[d, G], [1, d]])
            TT('dve', sq, src, src, ALU.mult, [kr], [kb])
            RED(S.small[:, 8:8 + G], sq, [kb], [ks])
            RSTD(S.small[:, 8:8 + G], 1.0 / d, ks)
            yv = _view(S.y1[:], 0, [[d, G], [1, d]])
            TT('dve', yv, src, _bc_last(S.small[:, 8:8 + G], d), ALU.mult, [kr, ks], [ky])
            if rope is None:
                TT('pool', out_b, yv, _bc_mid(gain_ap, G), ALU.mult, [ky, 'gains', 'naq_s'], [kyb])
                return
            off, bs = rope
            TT('pool', yv, yv, _bc_mid(gain_ap, G), ALU.mult, [ky, 'gains'], [ky])
            if off > 0:
                CP('act', out_b[:, :, 0:off], _view(S.y1[:], 0, [[d, G], [1, off]]), [ky], [kyb])
            w = 4 * bs
            r = _view(S.y1[:], off, [[d, G], [1, w]])
            cosv = _bc_mid(S.cs[:, 0:w], G)
            ta = _view(S.buf[:], 0, [[w, G], [1, w]])
            TT('dve', ta, r, cosv, ALU.mult, [ky, ('cs', i)], [kb])
            for s_ in range(2):
                o_ = _view(S.buf[:], 512 + s_ * bs, [[w, G], [2 * bs, 2], [1, bs]])
                i0 = _view(S.y1[:], off + (1 - s_) * bs, [[d, G], [2 * bs, 2], [1, bs]])
                i1 = _view(S.cs[:], w + s_ * bs, [[0, G], [2 * bs, 2], [1, bs]])
                TT('pool', o_, i0, i1, ALU.mult, [ky, ('cs', i)], [kb])
            tb = _view(S.buf[:], 512, [[w, G], [1, w]])
            TT('dve', out_b[:, :, off:off + w], ta, tb, ALU.add, [kb], [kyb])

        def to_fm(S, ins, width, dst_of, dkey):
            i = S.i
            pt = ptbs[i][:, 0:512]
            for g0 in range(0, len(ins), 4):
                grp = ins[g0:g0 + 4]
                n = len(grp)
                for g, a in enumerate(grp):
                    TR(pt[0:width, g * 128:(g + 1) * 128], a, identb[:], [('yb', i), 'identb'], [PB[6 + i]], inc=(g == n - 1))
                CP('act', dst_of(g0, n), pt[0:width, 0:n * 128].rearrange("p (g t) -> p g t", g=n), [PB[6 + i]], [dkey])

        def residual_store(banks, xacc_i, Gk, j, dst, t):
            i = 0
            for hf in range(2):
                TT('dve', xo[i][:, hf * 512:(hf + 1) * 512], pb[banks[hf]][:, :], Gt[Gk][j][:, hf * 512:(hf + 1) * 512], ALU.mult,
                   [PB[banks[hf]], ('G', Gk, j)], [('xo', i)])
            TT('pool', xo[i][:], xo[i][:], xa[xacc_i][:], ALU.add, [('xo', i), ('xa', xacc_i)], [('xo', i)])
            if dst == 'y':
                if t < NL:
                    p.dma('sp', xs['y'][t * 128:(t + 1) * 128, :], xo[i][:], reads=[('xo', i)], writes=[(dst, t)], sem=f'xo{i}')
            else:
                p.dma('sp', xs[dst][t * 128:(t + 1) * 128, :], xo[i][:], reads=[('xo', i)], writes=[(dst, t)], sem=f'xo{i}')

        def load_xa(src, t):
            i = 0
            p.dma('sp', xa[i][:], xs[src][t * 128:(t + 1) * 128, :], reads=[(src, t)], writes=[('xa', i)], sem=f'xa{i}')
            return i

        def load_w(dst3, src3, key, sem, nch):
            for c in range(nch):
                p.dma('pool', dst3[:, c, :], src3[:, c, :], writes=[key], sem=sem)

        def merged(fn, items, nsets=NS):
            out = []
            for g0 in range(0, len(items), nsets):
                lists = []
                for k_, it in enumerate(items[g0:g0 + nsets]):
                    p.capture()
                    fn(SS[k_], it)
                    lists.append(p.end_capture())
                idx = [0] * len(lists)
                live = True
                while live:
                    live = False
                    for k_, L in enumerate(lists):
                        if idx[k_] < len(L):
                            out.append(L[idx[k_]])
                            idx[k_] += 1
                            live = True
            return out

        def overlapped_chunks(chunks_, q_tile_, att_chunk_, nsets_b=NS):
            p.replay([merged(lambda S, tt: q_tile_(S, tt, 0), list(enumerate(chunks_[0])))])
            for ci, tiles in enumerate(chunks_):
                p.capture()
                att_chunk_(ci, tiles)
                A = p.end_capture()
                B = merged(lambda S, tt: q_tile_(S, tt, (ci + 1) % 2), list(enumerate(chunks_[ci + 1])), nsets_b) if ci + 1 < len(chunks_) else []
                M = []
                nb_, ib = len(B), 0
                for ka, ia in enumerate(A):
                    M.append(ia)
                    tgt = ((ka + 1) * nb_) // max(len(A), 1)
                    while ib < tgt:
                        M.append(B[ib])
                        ib += 1
                M.extend(B[ib:])
                p.replay([M])

        def interleaved(fn, items):
            for g0 in range(0, len(items), NS):
                lists = []
                for k_, it in enumerate(items[g0:g0 + NS]):
                    p.capture()
                    fn(SS[k_], it)
                    lists.append(p.end_capture())
                p.replay(lists)

        def attend(N, units, scale, qT_of, dv, nb, PT, fused=None, qkey=('fm',), la=None, sbk=None, hook=None):
            pN, pD = nb if nb is not None else (None, None)
            nu = len(units)

            LA = int(os.environ.get('ATT_LA', '2')) if la is None else la
            DUP = int(os.environ.get('ATT_DUP', '0'))
            SBK = (sbk if sbk is not None else ([0, 3, 5] if fused is not None else [0, 3, 6]))[:LA + 1]

            def emit_S(i):
                u = units[i]
                sbk = SBK[i % len(SBK)]
                pS = pb[sbk]
                nk, c0, c1 = u['nk'], u['c0'], u['c1']
                n = c1 - c0
                for rep in range(1 + DUP):
                    MM(pS[0:nk, 0:n], u['kT'], qT_of(c0, c1), True, u.get('bias') is None, [('fm',), qkey], [PB[sbk]], inc=(u.get('bias') is None))
                    if u.get('bias') is not None:
                        MM(pS[0:nk, 0:n], u['bias'][0], u['bias'][1], False, True, ['identb', 'lib'], [PB[sbk]])

            for i0 in range(min(LA, nu)):
                emit_S(i0)
            for i, u in enumerate(units):
                sbk = SBK[i % len(SBK)]
                pS = pb[sbk]
                nk, c0, c1 = u['nk'], u['c0'], u['c1']
                n = c1 - c0
                PTi = i % 4
                ACT(PT[PTi][0:nk, 0:n], pS[0:nk, 0:n], AF.Exp, [PB[sbk]], [('PT', PTi)], scale=scale)
                if i + LA < nu:
                    emit_S(i + LA)
                for (mc, mk) in u.get('masks', []):
                    TT('pool', PT[PTi][:, mc:mc + 128], PT[PTi][:, mc:mc + 128], mk, ALU.mult, [('PT', PTi), 'trib'], [('PT', PTi)])
                if fused is not None:
                    MM(pb[fused][:, c0:c1], u['v'], PT[PTi][0:nk, 0:n], i == 0, i == nu - 1, [('PT', PTi), ('v',)], [PB[fused]], inc=True)
                    continue
                MM(pb[pN][0:dv, c0:c1], u['v'], PT[PTi][0:nk, 0:n], i == 0, i == nu - 1, [('PT', PTi), ('v',)], [PB[pN]], inc=False)
                MM(pb[pD][0:dv, c0:c1], onesb[0:nk, 0:dv], PT[PTi][0:nk, 0:n], i == 0, i == nu - 1, [('PT', PTi), 'onesb'], [PB[pD]],
                   inc=True)
                if hook is not None and i == 1:
                    hook(0)
                if hook is not None and i == 5:
                    hook(1)

        def norm_fused(h, N, bank, mix, add_sink=False):
            par = h % 2
            dlo, nlo = (64, 0) if par == 0 else (0, 64)
            r_ = rec[par]
            rk = RECK[par]
            if add_sink:
                TS('dve', r_[dlo:dlo + 64, 0:N], pb[bank][dlo:dlo + 64, 0:N], es_sink[dlo:dlo + 64, h:h + 1], None, ALU.add, None,
                   [PB[bank], 'es_sink'], [rk])
                p.op('dve', lambda e: e.reciprocal(r_[dlo:dlo + 64, 0:N], r_[dlo:dlo + 64, 0:N]), [rk], [rk])
            else:
                p.op('dve', lambda e: e.reciprocal(r_[dlo:dlo + 64, 0:N], pb[bank][dlo:dlo + 64, 0:N]), [PB[bank]], [rk])
            TT('dve', mix[nlo:nlo + 64, h // 2, 0:N], pb[bank][nlo:nlo + 64, 0:N], r_[dlo:dlo + 64, 0:N], ALU.mult, [PB[bank], rk], [('mix',)])

        def norm_heads(h, N, pN, pD, mix, add_sink=None):
            r_ = rec[h % 2]
            rk = RECK[h % 2]
            if add_sink is None:
                p.op('dve', lambda e: e.reciprocal(r_[0:64, 0:N], pb[pD][0:64, 0:N]), [PB[pD]], [rk])
            else:
                TS('dve', r_[0:64, 0:N], pb[pD][0:64, 0:N], add_sink, None, ALU.add, None, [PB[pD], 'es_sink'], [rk])
                p.op('dve', lambda e: e.reciprocal(r_[0:64, 0:N], r_[0:64, 0:N]), [rk], [rk])
            TT('dve', mix[0:64, h, 0:N], pb[pN][0:64, 0:N], r_[0:64, 0:N], ALU.mult, [PB[pN], rk], [('mix',)])

        def out_proj(l, mixv, K, nslot, wo, tiles, xacc_src, dst, Gk):
            for ti, t in enumerate(tiles):
                j = 0 if t < NL else 1
                xi = load_xa(xacc_src, t)
                for hf in range(2):
                    for s_ in range(nslot):
                        bk = (1 + hf) if ti % 2 == 0 else (4 + hf)
                        MM(pb[bk][:, :], mixv[0:K, s_, ti * 128:(ti + 1) * 128], wo[0:K, s_, hf * 512:(hf + 1) * 512],
                           s_ == 0, s_ == nslot - 1, [('mix',), 'wo'], [PB[bk]], inc=(s_ == nslot - 1))
                residual_store((1, 2) if ti % 2 == 0 else (4, 5), xi, Gk, j, dst, t)

        def mixer_pass(l, mx, src, acc, dst):
            AR.reset()
            p.barrier()
            ctxq = (l == 0)
            win_d = W[l]['w_in'].rearrange("(c p) n -> p c n", p=128)
            PT = [AR.alloc(512) for _ in range(4)]
            for S in SS:
                S.hT = AR.alloc(8, 128)
                S.yb = AR.alloc(1024)
                S.cT = AR.alloc(2, 128)
            HK = lambda S: ('hT', S.i)
            HCOMP = [mx in ('A', 'C')]
            chunks = [list(range(c * 4, c * 4 + 4)) for c in range(4)] + ([[16, 17]] if ctxq else [])
            NB = [(1, 2), (4, 5)]
            if mx == 'A':
                wi = AR.alloc(8, 416)
                load_w(wi, win_d[:, :, 0:416], 'wi', 'wi', 8)
                wuq = AR.alloc(2, 768)
                load_w(wuq, W[0]['w_uq'].rearrange("(c p) n -> p c n", p=128), 'wuq', 'wuq', 2)
                wukv = AR.alloc(1, 1024)
                load_w(wukv, W[0]['w_ukv'].rearrange("(c p) n -> p c n", p=128), 'wukv', 'wukv', 1)
                wo = AR.alloc(4, 1024)
                load_w(wo, W[l]['w_out'][0:512, :].rearrange("(s p) n -> p s n", p=128), 'wo', 'wo', 4)
                kT = AR.alloc(8, T)
                vv = AR.alloc(NT, 8, 128)
                p.op('pool', lambda e: e.memset(vv, 1.0), [], [('v',)])
                qTs = [AR.alloc(8, 512) for _ in range(2)]
                mix = AR.alloc(4, 512)

                def kv_tile(S, t):
                    lat = t < NL
                    i = S.i
                    if lat:
                        load_cs(S, t, 32)
                    get_h(S, src, t, 0, S.hT, HK(S), HCOMP[0])
                    proj(S.hT, HK(S), wi, 'wi', 256, 160, S.pj)
                    CP('act', S.raw[:, 0:160], pb[S.pj][:, 0:160], [PB[S.pj]], [('raw', i)])
                    prep(S, _view(S.raw[:], 0, [[128, 1], [1, 128]]), 1, 128, gain('kva'), _view(S.yb, 0, [[128, 1], [1, 128]]))
                    to_fm(S, [S.yb[:, 0:128]], 128, lambda g0, n: S.cT[:, 0:1, :], ('cT', i))
                    for hf in range(2):
                        MM(pb[S.xb[hf]][:, :], S.cT[:, 0, :], wukv[:, 0, hf * 512:(hf + 1) * 512], True, True, [('cT', i), 'wukv'], [PB[S.xb[hf]]])
                    for hf in range(2):
                        kvv = pb[S.xb[hf]][:, :].rearrange("p (h e) -> p h e", h=4)
                        for par in range(2):
                            CP('act', _view(vv, (t * 8 + hf * 4 + par) * 128 + par * 64, [[256, 2], [1, 64]]),
                               _view(pb[S.xb[hf]][:, :], par * 128 + 64, [[256, 2], [1, 64]]), [PB[S.xb[hf]]], [('v',)])
                        CP('act', _view(S.raw[:], 256 + hf * 4 * 96, [[96, 4], [1, 64]]), kvv[:, :, 0:64], [PB[S.xb[hf]]], [('raw', i)])
                    CP('pool', _view(S.raw[:], 256 + 64, [[96, 8], [1, 32]]), _bc_mid(S.raw[:, 128:160], 8), [('raw', i)], [('raw', i)])
                    ybv = _view(S.yb, 0, [[96, 8], [1, 96]])
                    prep(S, _view(S.raw[:], 256, [[96, 8], [1, 96]]), 8, 96, gain('kn96'), ybv, rope=(64, 8) if lat else None)
                    to_fm(S, [ybv[:, h, :] for h in range(8)], 96, lambda g0, n: kT[0:96, g0:g0 + n, t * 128:(t + 1) * 128], ('fm',))

                def q_tile(S, tt, qb):
                    ti, t = tt
                    lat = t < NL
                    i = S.i
                    bk = 6 + i
                    if lat:
                        load_cs(S, t, 32)
                    get_h(S, src, t, 0, S.hT, HK(S), HCOMP[0])
                    proj(S.hT, HK(S), wi, 'wi', 0, 256, bk)
                    CP('act', S.raw[:, 0:256], pb[bk][:, 0:256], [PB[bk]], [('raw', i)])
                    prep(S, _view(S.raw[:], 0, [[256, 1], [1, 256]]), 1, 256, gain('qa'), _view(S.yb, 0, [[256, 1], [1, 256]]))
                    to_fm(S, [S.yb[:, 0:128], S.yb[:, 128:256]], 128, lambda g0, n: S.cT[:, :, :], ('cT', i))
                    for (c0_, n_) in ((0, 512), (512, 256)):
                        for c in range(2):
                            MM(pb[bk][:, 0:n_], S.cT[:, c, :], wuq[:, c, c0_:c0_ + n_], c == 0, c == 1, [('cT', i), 'wuq'], [PB[bk]], inc=(c == 1))
                        CP('act', S.raw[:, c0_:c0_ + n_], pb[bk][:, 0:n_], [PB[bk]], [('raw', i)])
                    ybv = _view(S.yb, 0, [[96, 8], [1, 96]])
                    prep(S, _view(S.raw[:], 0, [[96, 8], [1, 96]]), 8, 96, gain('qn96'), ybv, rope=(64, 8) if lat else None)
                    to_fm(S, [ybv[:, h, :] for h in range(8)], 96, lambda g0, n: qTs[qb][0:96, g0:g0 + n, ti * 128:(ti + 1) * 128], ('qT', qb))

                KS = os.environ.get('KSTOP', '')
                interleaved(kv_tile, list(range(2 if KS == 'kv1' else NT)))
                HCOMP[0] = False
                if KS in ('kv1', 'kv'):
                    chunks = []

                def att_chunk(ci, tiles):
                    N = 128 * len(tiles)
                    lat = tiles[0] < NL
                    qT_ = qTs[ci % 2]
                    ktiles = [16, 17] + (list(range(16)) if lat else [])
                    for h in range(8):
                        units = [dict(kT=kT[0:96, h, kt * 128:(kt + 1) * 128], nk=128, c0=0, c1=N, v=vv[:, kt, h, :]) for kt in ktiles]
                        bank = (1, 2, 4)[h % 3]
                        attend(N, units, 96 ** -0.5, lambda c0, c1, h=h: qT_[0:96, h, c0:c1], 64, None, PT, fused=bank, qkey=('qT', ci % 2))
                        norm_fused(h, N, bank, mix)
                    out_proj(l, mix, 128, 4, wo, tiles, acc, dst, 0)
                if chunks:
                    overlapped_chunks(chunks, q_tile, att_chunk)
            elif mx == 'B':
                wi = AR.alloc(8, 1536)
                load_w(wi, win_d[:, :, 416:1952], 'wi', 'wi', 8)
                wo = AR.alloc(8, 1024)
                load_w(wo[0:64], W[l]['w_out'][512:1024, :].rearrange("(h p) n -> p h n", p=64), 'wo', 'wo', 8)
                kT = AR.alloc(4, T)
                vv = AR.alloc(32, 8, 64)
                vvc = AR.alloc(2, 8, 64)
                qT = AR.alloc(4, 512)
                mix = AR.alloc(8, 512)
                libt = AR.alloc(4, 960)
                p.dma('sp', SS[1].raw[:, 0:960], nam_d, writes=[('raw', 1)], sem='c7')
                for hc in range(4):
                    p.dma('sp', SS[0].raw[:, 0:960], lib_d[:, hc * 960:(hc + 1) * 960], writes=[('raw', 0)], sem='c8')
                    TT('pool', libt[:, hc, :], SS[0].raw[:, 0:960], SS[1].raw[:, 0:960], ALU.add, [('raw', 0), ('raw', 1)], ['lib'])

                def kv_tile(S, t):
                    lat = t < NL
                    i = S.i
                    get_h(S, src, t, 0, S.hT, HK(S), HCOMP[0])
                    proj(S.hT, HK(S), wi, 'wi', 512, 512, S.pj)
                    CP('act', S.raw[:, 0:512], pb[S.pj][:, :], [PB[S.pj]], [('raw', i)])
                    prep(S, _view(S.raw[:], 0, [[64, 8], [1, 64]]), 8, 64, gain('nak'), _view(S.yb, 0, [[64, 8], [1, 64]]))
                    to_fm(S, [S.yb[:, c * 128:(c + 1) * 128] for c in range(4)], 128, lambda g0, n: kT[:, g0:g0 + n, t * 128:(t + 1) * 128], ('fm',))
                    if lat:
                        for hf in range(2):
                            proj(S.hT, HK(S), wi, 'wi', 1024, 512, S.xb[hf], M0=hf * 64, M1=hf * 64 + 64)
                            CP('act', vv[0:64, 2 * t + hf, :, :], pb[S.xb[hf]][0:64, :].rearrange("p (h e) -> p h e", h=8), [PB[S.xb[hf]]], [('v',)])
                    else:
                        proj(S.hT, HK(S), wi, 'wi', 1024, 512, S.xb[0])
                        CP('act', vvc[:, t - NL, :, :], pb[S.xb[0]][:, :].rearrange("p (h e) -> p h e", h=8), [PB[S.xb[0]]], [('v',)])

                def q_tile(S, tt):
                    ti, t = tt
                    i = S.i
                    get_h(S, src, t, 0, S.hT, HK(S), HCOMP[0])
                    proj(S.hT, HK(S), wi, 'wi', 0, 512, S.pj)
                    CP('act', S.raw[:, 0:512], pb[S.pj][:, :], [PB[S.pj]], [('raw', i)])
                    prep(S, _view(S.raw[:], 0, [[64, 8], [1, 64]]), 8, 64, naq_s[:], _view(S.yb, 0, [[64, 8], [1, 64]]))
                    to_fm(S, [S.yb[:, c * 128:(c + 1) * 128] for c in range(4)], 128, lambda g0, n: qT[:, g0:g0 + n, ti * 128:(ti + 1) * 128], ('fm',))

                interleaved(kv_tile, list(range(NT)))
                HCOMP[0] = False
                for tiles in chunks:
                    N = 128 * len(tiles)
                    lat = tiles[0] < NL
                    interleaved(q_tile, list(enumerate(tiles)))
                    for h in range(8):
                        ho, hc = (h % 2) * 64, h // 2
                        units = [dict(kT=kT[ho:ho + 64, hc, (NL + i_) * 128:(NL + i_ + 1) * 128], nk=128, c0=0, c1=N, v=vvc[:, i_, h, :])
                                 for i_ in range(2)]
                        if lat:
                            r0 = (tiles[0] * 128) // 64
                            for kr in range(32):
                                rs = [r for r in range(r0, r0 + 8) if min(max(r - 4, 0), 24) <= kr <= min(max(r - 4, 0), 24) + 7]
                                if not rs:
                                    continue
                                ra, rb = rs[0], rs[-1]
                                c0, c1 = (ra - r0) * 64, (rb - r0 + 1) * 64
                                L0 = (7 - (kr - ra)) * 64
                                units.append(dict(kT=kT[ho:ho + 64, hc, kr * 64:(kr + 1) * 64], nk=64, c0=c0, c1=c1, v=vv[0:64, kr, h, :],
                                                  bias=(identb[ho:ho + 64, ho:ho + 64], libt[ho:ho + 64, hc, L0:L0 + (c1 - c0)])))
                        pN, pD = NB[h % 2]
                        attend(N, units, 1.0, lambda c0, c1, ho=ho, hc=hc: qT[ho:ho + 64, hc, c0:c1], 64, (pN, pD), PT)
                        norm_heads(h, N, pN, pD, mix)
                    out_proj(l, mix, 64, 8, wo, tiles, acc, dst, 0)
            elif mx == 'C':
                wi = AR.alloc(8, 768)
                load_w(wi, win_d[:, :, 0:768], 'wi', 'wi', 8)
                wo = AR.alloc(4, 1024)
                load_w(wo, W[l]['w_out'][0:512, :].rearrange("(s p) n -> p s n", p=128), 'wo', 'wo', 4)
                kT = AR.alloc(1, T)
                vvE = AR.alloc(NT, 2, 128)
                vvO = AR.alloc(NT, 2, 128)
                p.op('pool', lambda e: e.memset(vvE, 1.0), [], [('v',)])
                p.op('pool', lambda e: e.memset(vvO, 1.0), [], [('v',)])
                qTs = [AR.alloc(4, 512) for _ in range(2)]
                mix = AR.alloc(4, 512)

                def kv_tile(S, t):
                    lat = t < NL
                    i = S.i
                    if lat:
                        load_cs(S, t, 64)
                    get_h(S, src, t, 0, S.hT, HK(S), HCOMP[0])
                    proj(S.hT, HK(S), wi, 'wi', 512, 256, S.pj)
                    CP('act', S.raw[:, 0:256], pb[S.pj][:, 0:256], [PB[S.pj]], [('raw', i)])
                    prep(S, _view(S.raw[:], 0, [[64, 2], [1, 64]]), 2, 64, gain('gk'), _view(S.yb, 0, [[64, 2], [1, 64]]), rope=(0, 16) if lat else None)
                    to_fm(S, [S.yb[:, 0:128]], 128, lambda g0, n: kT[:, 0:1, t * 128:(t + 1) * 128], ('fm',))
                    CP('pool', vvE[:, t, :, 0:64], _view(S.raw[:], 128, [[64, 2], [1, 64]]), [('raw', i)], [('v',)])
                    CP('pool', vvO[:, t, :, 64:128], _view(S.raw[:], 128, [[64, 2], [1, 64]]), [('raw', i)], [('v',)])

                def q_tile(S, tt, qb):
                    ti, t = tt
                    i = S.i
                    bk = 6 + i
                    load_cs(S, t, 64)
                    get_h(S, src, t, 0, S.hT, HK(S), HCOMP[0])
                    proj(S.hT, HK(S), wi, 'wi', 0, 512, bk)
                    CP('act', S.raw[:, 0:512], pb[bk][:, :], [PB[bk]], [('raw', i)])
                    prep(S, _view(S.raw[:], 0, [[64, 8], [1, 64]]), 8, 64, gain('gq'), _view(S.yb, 0, [[64, 8], [1, 64]]), rope=(0, 16))
                    pt = ptbs[i][:, 0:512]
                    for h in range(8):
                        kvh, g = h // 4, h % 4
                        TR(pt[kvh * 64:(kvh + 1) * 64, g * 128:(g + 1) * 128], S.yb[:, h * 64:(h + 1) * 64], identb[:], [('yb', i), 'identb'],
                           [PB[6 + i]], inc=(h == 7))
                    CP('act', qTs[qb][:, :, ti * 128:(ti + 1) * 128], pt[:, 0:512].rearrange("p (g t) -> p g t", g=4), [PB[6 + i]], [('qT', qb)])

                interleaved(kv_tile, list(range(NT)))
                HCOMP[0] = False

                def att_chunk(cch, tiles):
                    N = 512
                    qT_ = qTs[cch % 2]
                    for h in range(8):
                        kvh, g = h // 4, h % 4
                        ho = kvh * 64
                        vv = vvE if h % 2 == 0 else vvO
                        units = [dict(kT=kT[ho:ho + 64, 0, (NL + i_) * 128:(NL + i_ + 1) * 128], nk=128, c0=0, c1=N, v=vv[:, NL + i_, kvh, :])
                                 for i_ in range(2)]
                        for m in range(max(4 * cch - 1, 0), min(4 * cch + 4, 15) + 1):
                            nlo, nhi = max(m - 1, 4 * cch), min(m + 1, 4 * cch + 3)
                            masks = []
                            for n_ in range(nlo, nhi + 1):
                                if n_ == m + 1:
                                    masks.append(((n_ - nlo) * 128, trib[:, 0:128]))
                                elif n_ == m - 1:
                                    masks.append(((n_ - nlo) * 128, trib[:, 128:256]))
                            units.append(dict(kT=kT[ho:ho + 64, 0, m * 128:(m + 1) * 128], nk=128, c0=(nlo - 4 * cch) * 128,
                                              c1=(nhi - 4 * cch + 1) * 128, v=vv[:, m, kvh, :], masks=masks))
                        bank = (1, 2, 4)[h % 3]
                        attend(N, units, 0.125, lambda c0, c1, ho=ho, g=g: qT_[ho:ho + 64, g, c0:c1], 64, None, PT, fused=bank, qkey=('qT', cch % 2))
                        norm_fused(h, N, bank, mix, add_sink=True)
                    out_proj(l, mix, 128, 4, wo, tiles, acc, dst, 0)
                overlapped_chunks([list(range(c_ * 4, c_ * 4 + 4)) for c_ in range(4)], q_tile, att_chunk)
            else:
                wi = AR.alloc(8, 1536)
                load_w(wi, win_d[:, :, 768:2304], 'wi', 'wi', 8)
                wo = AR.alloc(4, 1024)
                load_w(wo, W[l]['w_out'][512:1024, :].rearrange("(h p) n -> p h n", p=128), 'wo', 'wo', 4)
                kT = AR.alloc(4, T)
                vv = AR.alloc(NT, 4, 128)
                qTs = [AR.alloc(4, 512) for _ in range(2)]
                mix = AR.alloc(4, 512)

                def kv_tile(S, t):
                    lat = t < NL
                    i = S.i
                    if lat:
                        load_cs(S, t, 64)
                    get_h(S, src, t, 0, S.hT, HK(S), HCOMP[0])
                    proj(S.hT, HK(S), wi, 'wi', 512, 512, S.pj)
                    CP('act', S.raw[:, 0:512], pb[S.pj][:, :], [PB[S.pj]], [('raw', i)])
                    prep(S, _view(S.raw[:], 0, [[64, 8], [1, 64]]), 8, 64, gain('dk'), _view(S.yb, 0, [[64, 8], [1, 64]]), rope=(0, 16) if lat else None)
                    to_fm(S, [S.yb[:, c * 128:(c + 1) * 128] for c in range(4)], 128, lambda g0, n: kT[:, g0:g0 + n, t * 128:(t + 1) * 128], ('fm',))
                    proj(S.hT, HK(S), wi, 'wi', 1024, 512, S.xb[0])
                    CP('act', vv[:, t, :, :], pb[S.xb[0]][:, :].rearrange("p (h e) -> p h e", h=4), [PB[S.xb[0]]], [('v',)])

                def q_tile(S, tt, qb):
                    ti, t = tt
                    i = S.i
                    bk = 6 + i
                    load_cs(S, t, 64)
                    get_h(S, src, t, 0, S.hT, HK(S), HCOMP[0])
                    proj(S.hT, HK(S), wi, 'wi', 0, 512, bk)
                    CP('act', S.raw[:, 0:512], pb[bk][:, :], [PB[bk]], [('raw', i)])
                    prep(S, _view(S.raw[:], 0, [[64, 8], [1, 64]]), 8, 64, gain('dq'), _view(S.yb, 0, [[64, 8], [1, 64]]), rope=(0, 16))
                    to_fm(S, [S.yb[:, c * 128:(c + 1) * 128] for c in range(4)], 128, lambda g0, n: qTs[qb][:, g0:g0 + n, ti * 128:(ti + 1) * 128], ('qT', qb))

                interleaved(kv_tile, list(range(NT)))
                HCOMP[0] = False

                def att_chunk(cch, tiles):
                    N = 512
                    qT_ = qTs[cch % 2]
                    sqb = SS[1].yb[:, 0:512]

                    def tail(hh, stage=None):
                        if stage in (None, 0):
                            STT(of[0][:, :], of[1][:, :], lamt[:, 3:4], of[0][:, :], ALU.mult, ALU.add, [OFK[0], 'lamt'], [OFK[0]])
                            TT('pool', sqb, of[0][:, :], of[0][:, :], ALU.mult, [OFK[0]], [('yb', 1)])
                        if stage == 0:
                            return
                        MM(pb[5][:, :], onesb[:, :], sqb, True, True, [('yb', 1), 'onesb'], [PB[5]])
                        ACT(rec[0][:, :], pb[5][:, :], AF.Ln, [PB[5]], [RECK[0]], bias=EPS, scale=1.0 / 128)
                        ACT(rec[0][:, :], rec[0][:, :], AF.Exp, [RECK[0]], [RECK[0]], scale=-0.5)
                        STT(mix[:, hh, :], of[0][:, :], sublnS[:, 0:1], rec[0][:, :], ALU.mult, ALU.mult, [OFK[0], RECK[0], 'sublnS'], [('mix',)])

                    pend = None
                    for hh in range(4):
                        for cc in range(2):
                            ho = cc * 64
                            units = [dict(kT=kT[ho:ho + 64, hh, kt * 128:(kt + 1) * 128], nk=128, c0=0, c1=N, v=vv[:, kt, hh, :])
                                     for kt in ([16, 17] + list(range(16)))]
                            attend(N, units, 0.125, lambda c0, c1, ho=ho, hh=hh: qT_[ho:ho + 64, hh, c0:c1], 128, NB[cc], PT,
                                   qkey=('qT', cch % 2), la=2, sbk=[0, 3, 7], hook=(pend if cc == 0 else None))
                        for cc in range(2):
                            pN, pD = NB[cc]
                            p.op('dve', lambda e, cc=cc, pD=pD: e.reciprocal(rec[cc][:, :], pb[pD][:, :]), [PB[pD]], [RECK[cc]])
                            TT('dve', of[cc][:, :], pb[pN][:, :], rec[cc][:, :], ALU.mult, [PB[pN], RECK[cc]], [OFK[cc]])
                        pend = (lambda stage=None, hh=hh: tail(hh, stage))
                    pend()
                    out_proj(l, mix, 128, 4, wo, tiles, acc, dst, 0)
                overlapped_chunks([list(range(c_ * 4, c_ * 4 + 4)) for c_ in range(4)], q_tile, att_chunk, nsets_b=1)

        def ffn_half(l, hf, hsrc, acc, dst):
            AR.reset()
            p.barrier()
            wg = AR.alloc(8, 1408)
            wu = AR.alloc(8, 1408)
            wd = AR.alloc(11, 1024)
            load_w(wg, W[l]['wg'].rearrange("(c p) n -> p c n", p=128)[:, :, hf * 1408:(hf + 1) * 1408], 'wg', 'wg', 8)
            load_w(wu, W[l]['wu'].rearrange("(c p) n -> p c n", p=128)[:, :, hf * 1408:(hf + 1) * 1408], 'wu', 'wu', 8)
            load_w(wd, W[l]['wd'][hf * 1408:(hf + 1) * 1408, :].rearrange("(f p) n -> p f n", p=128), 'wd', 'wd', 11)
            h2s = [AR.alloc(8, 512) for _ in range(2)]
            act = AR.alloc(11, 512)
            sg = [AR.alloc(512) for _ in range(2)]
            chunks = [list(range(c * 4, c * 4 + 4)) for c in range(4)] + ([[16, 17]] if l == 0 else [])

            if hf == 0:
                for S in SS:
                    S.hT = AR.alloc(8, 128)

                def norm_tile(S, t):
                    get_h(S, hsrc, t, 1, S.hT, ('hT', S.i), True)
                interleaved(norm_tile, [t for tiles in chunks for t in tiles])

            def emit_h_tile(ci_, ti):
                t = chunks[ci_][ti]
                p.dma('sp', h2s[ci_ % 2][:, :, ti * 128:(ti + 1) * 128], hts[1][t].rearrange("p (c n) -> p c n", c=8),
                      reads=[('hts', 1, t)], writes=[('h2', ci_ % 2, ti)], sem=f'h2l{ci_ % 2}_{ti}')

            for ti in range(len(chunks[0])):
                emit_h_tile(0, ti)
            for ci_, tiles in enumerate(chunks):
                N = 128 * len(tiles)
                h2 = h2s[ci_ % 2]
                hks = [('h2', ci_ % 2, ti_) for ti_ in range(len(tiles))]
                nxt = len(chunks[ci_ + 1]) if ci_ + 1 < len(chunks) else 0
                for f in range(11):
                    bg, bu = (f % 2) * 2, (f % 2) * 2 + 1
                    for c in range(8):
                        MM(pb[bg][:, 0:N], wg[:, c, f * 128:(f + 1) * 128], h2[:, c, 0:N], c == 0, c == 7, hks + ['wg'], [PB[bg]], inc=(c == 7))
                    for c in range(8):
                        MM(pb[bu][:, 0:N], wu[:, c, f * 128:(f + 1) * 128], h2[:, c, 0:N], c == 0, c == 7, hks + ['wu'], [PB[bu]], inc=(c == 7))
                    ACT(sg[f % 2][:, 0:N], pb[bg][:, 0:N], AF.Silu, [PB[bg]], [('sg', f % 2)])
                    TT('dve', act[:, f, 0:N], pb[bu][:, 0:N], sg[f % 2][:, 0:N], ALU.mult, [PB[bu], ('sg', f % 2)], [('act',)])
                    if f % 2 == 1 and (f // 2) < nxt:
                        emit_h_tile(ci_ + 1, f // 2)
                for ti, t in enumerate(tiles):
                    j = 0 if t < NL else 1
                    xi2 = load_xa(acc, t)
                    for h2_ in range(2):
                        for f in range(11):
                            bk = (4 + h2_) if ti % 2 == 0 else (2 + h2_)
                            MM(pb[bk][:, :], act[:, f, ti * 128:(ti + 1) * 128], wd[:, f, h2_ * 512:(h2_ + 1) * 512], f == 0, f == 10,
                               [('act',), 'wd'], [PB[bk]], inc=(f == 10))
                    residual_store((4, 5) if ti % 2 == 0 else (2, 3), xi2, 1, j, dst, t)

        seq = []
        if 0 in layers:
            seq += [('ada', 0), ('mix', 0, 'A', 'xin', 'xin', 'xs_a'), ('mix', 0, 'B', 'xin', 'xs_a', 'xs_b'),
                    ('ffn', 0, 0, 'xs_b', 'xs_b', 'xs_c'), ('ffn', 0, 1, 'xs_b', 'xs_c', 'xs_d')]
        if 1 in layers:
            s1 = 'xs_d' if 0 in layers else 'xin'
            seq += [('ada', 1), ('mix', 1, 'C', s1, s1, 'xs_a'), ('mix', 1, 'D', s1, 'xs_a', 'xs_b'),
                    ('ffn', 1, 0, 'xs_b', 'xs_b', 'xs_c'), ('ffn', 1, 1, 'xs_b', 'xs_c', 'y')]
        if stop_after is not None:
            seq = seq[:stop_after]
        last_dst = None
        for st in seq:
            if st[0] == 'ada':
                ada_phase(st[1])
            elif st[0] == 'mix':
                mixer_pass(st[1], st[2], st[3], st[4], st[5])
                last_dst = st[5]
            else:
                ffn_half(st[1], st[2], st[3], st[4], st[5])
                last_dst = st[5]
        if last_dst is None:
            p.dma('sp', xs['y'][0:128, :], Gt[0][0][:], reads=[('G', 0, 0)], writes=[('y', 0)], sem='dbg0')
            p.dma('sp', xs['y'][128:256, :], Gt[0][1][:], reads=[('G', 0, 1)], writes=[('y', 1)], sem='dbg0')
            p.dma('sp', xs['y'][256:384, :], Gt[1][0][:], reads=[('G', 1, 0)], writes=[('y', 2)], sem='dbg0')
            p.dma('sp', xs['y'][384:512, 0:64], AB[:], reads=['AB'], writes=[('y', 3)], sem='dbg0')
            p.finish([('y', t) for t in range(4)])
            p.emit()
            return nc
        if last_dst != 'y':
            for t in range(NL):
                S = SS[t % 2]
                p.dma('sp', S.xt[:], xs[last_dst][t * 128:(t + 1) * 128, :], reads=[(last_dst, t)], writes=[('xt', S.i)], sem=f'xt{S.i}')
                p.dma('sp', xs['y'][t * 128:(t + 1) * 128, :], S.xt[:], reads=[('xt', S.i)], writes=[('y', t)], sem=f'dbg{t % 2}')
        p.finish([('y', t) for t in range(NL)])
        p.emit()
    return nc


_CACHE = {}


def _host_inputs(inputs, layers=(0, 1)):
    f = lambda a: np.ascontiguousarray(np.asarray(a, dtype=np.float32))
    x, c, ctx, c_ctx = f(inputs['x']), f(inputs['c']), f(inputs['ctx']), f(inputs['c_ctx'])
    cs64, cs32 = _rope_tables()
    lib, nam = _na_tables(f(inputs['l0_na_rpb']))
    gl = []
    for n in ['l0_mla_qa_g', 'l0_mla_kva_g', 'l0_mla_qn_g', 'l0_mla_kn_g', 'l0_na_qn_g', 'l0_na_kn_g', 'l1_gqa_qn_g', 'l1_gqa_kn_g',
              'l1_diff_qn_g', 'l1_diff_kn_g', 'l1_gqa_sink', 'l1_diff_lq1', 'l1_diff_lk1', 'l1_diff_lq2', 'l1_diff_lk2']:
        gl.append(f(inputs[n]).reshape(-1))
    gains = np.ascontiguousarray(np.broadcast_to(np.concatenate(gl)[None, :], (128, GW)))
    pp = np.zeros((128, 33), np.float32)
    for i, n in enumerate(['l0_norm1_g', 'l0_norm2_g', 'l1_norm1_g', 'l1_norm2_g']):
        pp[:, i * 8:(i + 1) * 8] = f(inputs[n]).reshape(8, 128).T
    pp[:, 32] = f(inputs['l1_diff_subln_g'])
    ident = np.eye(128, dtype=np.float32)
    jj = np.arange(128)[:, None]
    ii = np.arange(128)[None, :]
    tri = np.concatenate([(ii <= jj), (jj <= ii)], axis=1).astype(np.float32)
    sel = np.zeros((2, 256), np.float32)
    sel[0, 0:128] = 1.0
    sel[1, 128:256] = 1.0
    shared = dict(gains=gains, pp=pp, ident=ident, tri=tri, sel=sel, cs64=cs64, cs32=cs32,
                  nalib=np.ascontiguousarray(lib.reshape(128, 4 * 960)), namask=nam)
    for l in (0, 1):
        shared[f'l{l}_ada_w'] = f(inputs[f'l{l}_ada_w'])
        shared[f'l{l}_ada_b2'] = np.ascontiguousarray(np.broadcast_to(f(inputs[f'l{l}_ada_b'])[None, :], (2, 6 * D)))
        shared[f'l{l}_w_in'] = f(inputs[f'l{l}_w_in'])
        shared[f'l{l}_w_out'] = f(inputs[f'l{l}_w_out'])
        shared[f'l{l}_ffn_w_gate'] = f(inputs[f'l{l}_ffn_w_gate'])
        shared[f'l{l}_ffn_w_up'] = f(inputs[f'l{l}_ffn_w_up'])
        shared[f'l{l}_ffn_w_down'] = f(inputs[f'l{l}_ffn_w_down'])
    shared['l0_mla_w_uq'] = f(inputs['l0_mla_w_uq'])
    shared['l0_mla_w_ukv'] = f(inputs['l0_mla_w_ukv'])
    maps = []
    for b in range(x.shape[0]):
        m = dict(shared)
        m['xin'] = np.ascontiguousarray(np.concatenate([x[b], ctx[b]], axis=0))
        cv = np.stack([c[b], c_ctx], axis=0)
        m['cvecT'] = np.ascontiguousarray(cv.reshape(2, 8, 128).transpose(2, 1, 0).reshape(128, 16))
        maps.append(m)
    return maps


def kernel(**inputs):
    maps = _host_inputs(inputs)
    if 'nc' not in _CACHE:
        _CACHE['nc'] = build()
    res = run_bass_kernel_spmd(_CACHE['nc'], maps, core_ids=list(range(len(maps))))
    return np.stack([np.asarray(r['y'], dtype=np.float32) for r in res.results], axis=0)
```

```python
import os
import numpy as np
import concourse.bass as bass
import concourse.mybir as mybir
from concourse.bass_utils import run_bass_kernel_spmd
from concourse.alu_op_type import AluOpType as ALU
from contextlib import ExitStack

F32 = mybir.dt.float32
BF16 = mybir.dt.bfloat16
AF = mybir.ActivationFunctionType
AX = mybir.AxisListType


STRICT = int(os.environ.get('KSTRICT', '1'))


class Prog:
    ENG = ('pe', 'act', 'dve', 'pool', 'sp')

    def __init__(self, nc):
        self.nc = nc
        self.q = {e: [] for e in self.ENG}
        self.cnt = {}
        self.res = {}
        self.known = {e: {} for e in self.ENG}
        self.pending = {e: {} for e in self.ENG}

    def _need(self, eng, toks, waits, skip=None):
        for s, v in toks.items():
            if s == skip:
                continue
            if self.known[eng].get(s, 0) < v and waits.get(s, 0) < v:
                waits[s] = v

    def _record(self, eng, fn, reads, writes, tok, incspec, is_dma=False):
        waits = {}
        own = 'e_' + eng
        wskip = tok[0] if is_dma else (own if (eng == 'pe' or not STRICT) else None)
        for r in reads:
            st = self.res.get(r)
            if st is not None:
                self._need(eng, st[0], waits, skip=own if eng == 'pe' else None)
        for w in writes:
            st = self.res.get(w)
            if st is not None:
                self._need(eng, st[0], waits, skip=wskip)
                self._need(eng, st[1], waits, skip=wskip)
        for s, v in waits.items():
            self.known[eng][s] = v
        for s, v in self.pending[eng].items():
            if waits.get(s, 0) < v:
                waits[s] = v
        self.pending[eng] = {}
        self.q[eng].append((sorted(waits.items()), fn, incspec))
        for r in reads:
            st = self.res.setdefault(r, [{}, {}])
            if st[1].get(tok[0], 0) < tok[1]:
                st[1][tok[0]] = tok[1]
        for w in writes:
            self.res[w] = [{tok[0]: tok[1]}, {}]

    def capture(self):
        self.cap = []
        return self.cap

    def end_capture(self):
        c = self.cap
        self.cap = None
        return c

    def replay(self, lists):
        idx = [0] * len(lists)
        live = True
        while live:
            live = False
            for k, L in enumerate(lists):
                if idx[k] < len(L):
                    it = L[idx[k]]
                    idx[k] += 1
                    live = True
                    if it[0] == 'op':
                        self.op(*it[1:])
                    else:
                        self.dma(*it[1:-1], **it[-1])

    def op(self, eng, fn, reads=(), writes=(), inc=True):
        if getattr(self, 'cap', None) is not None:
            self.cap.append(('op', eng, fn, list(reads), list(writes), inc))
            return
        own = 'e_' + eng
        c = self.cnt.get(own, 0)
        if inc:
            self.cnt[own] = c + 1
        self._record(eng, fn, reads, writes, (own, c + 1), (own, 1) if inc else None)

    def dma(self, eng, out, in_, reads=(), writes=(), sem=None, **kw):
        if getattr(self, 'cap', None) is not None:
            self.cap.append(('dma', eng, out, in_, list(reads), list(writes), sem, kw))
            return
        s = 'd_' + sem
        c = self.cnt.get(s, 0) + 16
        self.cnt[s] = c
        self._record(eng, lambda e: e.dma_start(out=out, in_=in_, **kw), reads, writes, (s, c), (s, 16), is_dma=True)

    def finish(self, keys):
        self.op('sp', lambda e: e.nop(), reads=list(keys), inc=False)

    def emit(self):
        nc = self.nc
        names = sorted(self.cnt.keys())
        with ExitStack() as es:
            sems = {n: es.enter_context(nc.semaphore(n)) for n in names}
            with nc.Block() as block:
                def body(ename):
                    def f(e):
                        for waits, fn, incspec in self.q[ename]:
                            for s, v in waits:
                                e.wait_ge(sems[s], v)
                            ins = fn(e)
                            if incspec is not None:
                                ins.then_inc(sems[incspec[0]], incspec[1])
                        for s, v in sorted(self.pending[ename].items()):
                            e.wait_ge(sems[s], v)
                    return f
                block.tensor(body('pe'))
                block.scalar(body('act'))
                block.vector(body('dve'))
                block.gpsimd(body('pool'))
                block.sync(body('sp'))

    def check(self):
        sem = {}
        ptr = {e: 0 for e in self.ENG}
        prog = True
        while prog:
            prog = False
            for e in self.ENG:
                while ptr[e] < len(self.q[e]):
                    waits, fn, inc = self.q[e][ptr[e]]
                    if all(sem.get(s_, 0) >= v for s_, v in waits):
                        if inc is not None:
                            sem[inc[0]] = sem.get(inc[0], 0) + inc[1]
                        ptr[e] += 1
                        prog = True
                    else:
                        break
        stuck = {e: (ptr[e], len(self.q[e]), self.q[e][ptr[e]][0]) for e in self.ENG if ptr[e] < len(self.q[e])}
        return stuck, sem

    def barrier(self):
        snap = dict(self.cnt)
        for eng in self.ENG:
            waits = {}
            own = 'e_' + eng
            for s, v in snap.items():
                if s == own:
                    continue
                if self.known[eng].get(s, 0) < v:
                    waits[s] = v
                    self.known[eng][s] = v
            for s, v in waits.items():
                if self.pending[eng].get(s, 0) < v:
                    self.pending[eng][s] = v


D = 1024
T = 2304
NT = 18
NL = 16
FH = 2816
EPS = 1e-6
NEG = -30000.0
GRID_W = 64
G_OFF = {}
_o = 0
for _n, _w in [('qa', 256), ('kva', 128), ('qn96', 96), ('kn96', 96), ('naq', 64), ('nak', 64), ('gq', 64), ('gk', 64),
               ('dq', 64), ('dk', 64), ('sink', 8), ('lq1', 64), ('lk1', 64), ('lq2', 64), ('lk2', 64)]:
    G_OFF[_n] = (_o, _w)
    _o += _w
GW = _o


def _rope_tables():
    pos = np.arange(2048)
    rows, cols = pos // GRID_W, pos % GRID_W

    def tab(h):
        fr = (10000.0 ** (-np.arange(0, h, 2, dtype=np.float32) / np.float32(h))).astype(np.float32)
        ar = rows.astype(np.float32)[:, None] * fr[None, :]
        ac = cols.astype(np.float32)[:, None] * fr[None, :]
        cr, sr, cc, sc = np.cos(ar), np.sin(ar), np.cos(ac), np.sin(ac)
        cos = np.concatenate([cr, cr, cc, cc], axis=1)
        sin = np.concatenate([-sr, sr, -sc, sc], axis=1)
        return np.concatenate([cos, sin], axis=1).astype(np.float32)
    return tab(32), tab(16)


def _na_tables(rpb):
    kc = np.arange(64)[:, None]
    qc = np.arange(64)[None, :]
    c0 = np.clip(qc - 8, 0, 48)
    valid = (kc >= c0) & (kc < c0 + 16)
    offc = np.clip(kc - qc + 15, 0, 30)
    lib = np.zeros((128, 4, 960), np.float32)
    mask = np.zeros((128, 960), np.float32)
    for dr in range(-7, 8):
        col = (7 - dr) * 64
        for h in range(8):
            lib[(h % 2) * 64:(h % 2) * 64 + 64, h // 2, col:col + 64] = rpb[h, dr + 7][offc]
        mask[0:64, col:col + 64] = np.where(valid, 0.0, NEG)
        mask[64:128, col:col + 64] = np.where(valid, 0.0, NEG)
    return lib, mask


def _bc_mid(a, G):
    return bass.AP(a.tensor, a.offset, [list(a.ap[0]), [0, G], list(a.ap[1])])


def _bc_last(a, d):
    return bass.AP(a.tensor, a.offset, [list(a.ap[0]), list(a.ap[1]), [0, d]])


def _view(a, off, dims):
    return bass.AP(a.tensor, a.offset + off, [list(a.ap[0])] + [list(x) for x in dims])


def build(layers=(0, 1), stop_after=None):
    nc = bass.Bass("TRN2", target_bir_lowering=False)
    es = ExitStack()

    def din(name, shape):
        return nc.dram_tensor(name, list(shape), F32, kind="ExternalInput").ap()

    xin = din("xin", [T, D])
    cvecT = din("cvecT", [128, 16])
    gains_d = din("gains", [128, GW])
    pp_d = din("pp", [128, 33])
    ident_d = din("ident", [128, 128])
    tri_d = din("tri", [128, 256])
    sel_d = din("sel", [2, 256])
    cs64_d = din("cs64", [2048, 128])
    cs32_d = din("cs32", [2048, 64])
    lib_d = din("nalib", [128, 4 * 960])
    nam_d = din("namask", [128, 960])
    W = {}
    for l in (0, 1):
        W[l] = dict(
            ada_w=din(f"l{l}_ada_w", [D, 6 * D]), ada_b2=din(f"l{l}_ada_b2", [2, 6 * D]),
            w_in=din(f"l{l}_w_in", [D, 1952 if l == 0 else 2304]), w_out=din(f"l{l}_w_out", [D, D]),
            wg=din(f"l{l}_ffn_w_gate", [D, FH]), wu=din(f"l{l}_ffn_w_up", [D, FH]), wd=din(f"l{l}_ffn_w_down", [FH, D]))
    W[0]['w_uq'] = din("l0_mla_w_uq", [256, 768])
    W[0]['w_ukv'] = din("l0_mla_w_ukv", [128, 1024])
    y_out = nc.dram_tensor("y", [2048, D], F32, kind="ExternalOutput").ap()
    xs = {n: nc.dram_tensor(n, [T, D], F32).ap() for n in ('xs_a', 'xs_b', 'xs_c', 'xs_d')}
    xs['xin'] = xin
    hts = [nc.dram_tensor(f'hts{k}', [NT, 128, D], BF16).ap() for k in range(2)]
    xs['y'] = y_out

    with es:
        def sb(name, shape, dt=F32):
            return es.enter_context(nc.sbuf_tensor(name, list(shape), dt))

        def ps(name, shape, dt=F32):
            return es.enter_context(nc.psum_tensor(name, list(shape), dt))

        p = Prog(nc)
        ident = sb("ident_s", [128, 128])
        identb = sb("identb", [128, 128], BF16)
        onesb = sb("onesb", [128, 128], BF16)
        trib = sb("trib", [128, 256], BF16)
        sel = sb("sel_s", [2, 256])
        gains = sb("gains_s", [128, GW])
        pp = sb("pp_s", [128, 33])
        cT = sb("cT", [128, 16])
        cTb = sb("cTb", [128, 16], BF16)
        modT = sb("modT", [128, 64])
        AB = sb("AB", [128, 64])
        Gt = [[sb(f"G{k}{j}", [128, D]) for j in range(2)] for k in range(2)]
        small = sb("small", [128, 64])
        lamt = sb("lamt", [128, 8])
        es_sink = sb("es_sink", [128, 8])
        sublnS = sb("sublnS", [128, 1])
        naq_s = sb("naq_s", [128, 64])
        xa = [sb("xa0", [128, D])]
        recb = [sb(f"recb{i}", [128, 512]) for i in range(2)]
        xo = [sb(f"xo{i}", [128, D]) for i in range(2)]
        mrow = [sb(f"mrow{i}", [2, 512]) for i in range(2)]
        brow = [sb(f"brow{i}", [2, 512]) for i in range(2)]
        NS = 2

        class SSet:
            pass
        SS = []
        for i in range(NS):
            S_ = SSet()
            S_.i = i
            S_.xt = sb(f"xt{i}", [128, D])
            S_.buf = sb(f"buf{i}", [128, D])
            S_.raw = sb(f"raw{i}", [128, D])
            S_.y1 = sb(f"y1{i}", [128, D])
            S_.cs = sb(f"cs{i}", [128, 192])
            S_.small = small[:, i * 32:(i + 1) * 32]
            S_.xb = (3 * i, 3 * i + 1)
            S_.pj = 3 * i + 2
            S_.k = lambda n, i=i: (n, i)
            SS.append(S_)
        junk = SS[0].buf
        raw = SS[0].raw
        rec = recb
        of = [xo[1][:, 0:512], xo[1][:, 512:1024]]
        RECK = [('rec', 0), ('rec', 1)]
        OFK = [('xo', 1), ('xo', 1)]
        ARN = 63800
        arena = sb("arena", [128, ARN], BF16)
        pb = [ps(f"pb{i}", [128, 512]) for i in range(8)]
        ptbs = [pb[6][:, :].bitcast(BF16), pb[7][:, :].bitcast(BF16)]
        PB = [('pb', i) for i in range(8)]

        class Arena:
            def __init__(self):
                self.off = 0

            def reset(self):
                self.off = 0

            def alloc(self, *free):
                n = int(np.prod(free))
                a = arena[:, self.off:self.off + n]
                self.off += n
                assert self.off <= ARN, self.off
                if len(free) == 2:
                    a = a.rearrange("p (a b) -> p a b", a=free[0])
                elif len(free) == 3:
                    a = a.rearrange("p (a b c) -> p a b c", a=free[0], b=free[1])
                return a
        AR = Arena()

        def MM(out, lhsT, rhs, start, stop, reads, writes, inc=True):
            p.op('pe', lambda e: e.matmul(out, lhsT, rhs, start=start, stop=stop), reads, writes, inc)

        def TR(out, in_, idn, reads, writes, inc=True):
            p.op('pe', lambda e: e.transpose(out, in_, idn), reads, writes, inc)

        def ACT(out, in_, func, reads, writes, bias=None, scale=None):
            kw = {}
            if bias is not None:
                kw['bias'] = bias
            if scale is not None:
                kw['scale'] = scale
            p.op('act', lambda e: e.activation(out, in_, func, **kw), reads, writes)

        def TT(eng, out, a, b, op, reads, writes):
            p.op(eng, lambda e: e.tensor_tensor(out, a, b, op), reads, writes)

        def TS(eng, out, a, s1, s2, op0, op1, reads, writes):
            if s2 is None:
                p.op(eng, lambda e: e.tensor_scalar(out, a, s1, None, op0), reads, writes)
            else:
                p.op(eng, lambda e: e.tensor_scalar(out, a, s1, s2, op0, op1), reads, writes)

        def STT(out, a, s, b, op0, op1, reads, writes):
            p.op('dve', lambda e: e.scalar_tensor_tensor(out, a, s, b, op0, op1), reads, writes)

        def CP(eng, out, in_, reads, writes):
            if eng == 'act':
                ACT(out, in_, AF.Copy, reads, writes)
            else:
                p.op(eng, lambda e: e.tensor_copy(out, in_), reads, writes)

        def RED(out, in_, reads, writes):
            p.op('dve', lambda e: e.tensor_reduce(out, in_, AX.X, ALU.add), reads, writes)

        def RSTD(ap, n_inv, reads_writes):
            ACT(ap, ap, AF.Ln, [reads_writes], [reads_writes], bias=EPS, scale=n_inv)
            ACT(ap, ap, AF.Exp, [reads_writes], [reads_writes], scale=-0.5)

        def gain(name):
            o, w = G_OFF[name]
            return gains[:, o:o + w]

        p.dma('sp', ident[:], ident_d, writes=['ident'], sem='c0')
        p.dma('sp', sel[:], sel_d, writes=['sel'], sem='c1')
        p.dma('sp', gains[:], gains_d, writes=['gains'], sem='c2')
        p.dma('sp', pp[:], pp_d, writes=['pp'], sem='c3')
        p.dma('sp', cT[:], cvecT, writes=['cT'], sem='c4')
        p.dma('pool', identb[:], ident_d, writes=['identb'], sem='c5')
        p.dma('pool', trib[:], tri_d, writes=['trib'], sem='c6')
        p.op('pool', lambda e: e.memset(onesb[:], 1.0), writes=['onesb'])
        ACT(cTb[:], cT[:], AF.Silu, ['cT'], ['cTb'])
        TS('dve', naq_s[:], gain('naq'), 0.125, None, ALU.mult, None, ['gains'], ['naq_s'])
        lam_init = 0.8 - 0.6 * float(np.exp(-0.3 * 1))
        TT('dve', junk[:, 0:64], gain('lq1'), gain('lk1'), ALU.mult, ['gains'], [('buf', 0)])
        RED(lamt[:, 0:1], junk[:, 0:64], [('buf', 0)], ['lamt'])
        TT('dve', junk[:, 64:128], gain('lq2'), gain('lk2'), ALU.mult, ['gains'], [('buf', 0)])
        RED(lamt[:, 1:2], junk[:, 64:128], [('buf', 0)], ['lamt'])
        ACT(lamt[:, 0:2], lamt[:, 0:2], AF.Exp, ['lamt'], ['lamt'])
        TT('dve', lamt[:, 2:3], lamt[:, 1:2], lamt[:, 0:1], ALU.subtract, ['lamt'], ['lamt'])
        TS('dve', lamt[:, 3:4], lamt[:, 2:3], -lam_init, None, ALU.add, None, ['lamt'], ['lamt'])
        ACT(es_sink[:], gain('sink'), AF.Exp, ['gains'], ['es_sink'])
        TS('dve', sublnS[:], pp[:, 32:33], 1.0 - lam_init, None, ALU.mult, None, ['pp'], ['sublnS'])

        def ada_phase(l):
            AR.reset()
            p.barrier()
            aw = [AR.alloc(8, 512) for _ in range(3)]
            adaw = W[l]['ada_w'].rearrange("(c p) n -> p c n", p=128)
            for n in range(12):
                s = n % 3
                b2 = n % 2
                for c in range(8):
                    p.dma('pool', aw[s][:, c, :], adaw[:, c, n * 512:(n + 1) * 512], writes=[('aw', s)], sem=f'aw{s}')
                p.dma('sp', brow[b2][:], W[l]['ada_b2'][:, n * 512:(n + 1) * 512], writes=[('brow', b2)], sem=f'brow{b2}')
                for c in range(8):
                    MM(pb[0][0:2, :], cTb[:, c * 2:c * 2 + 2], aw[s][:, c, :], c == 0, c == 7,
                       ['cTb', ('aw', s)], [PB[0]], inc=(c == 7))
                TT('dve', mrow[b2][:], pb[0][0:2, :], brow[b2][:], ALU.add, [PB[0], ('brow', b2)], [('mrow', b2)])
                sec, half = n // 2, n % 2
                if sec in (2, 5):
                    k = 0 if sec == 2 else 1
                    for j in range(2):
                        MM(pb[1 + j][:, :], sel[0:2, j * 128:(j + 1) * 128], mrow[b2][:], True, True,
                           ['sel', ('mrow', b2)], [PB[1 + j]])
                        CP('act', Gt[k][j][:, half * 512:(half + 1) * 512], pb[1 + j][:, :], [PB[1 + j]], [('G', k, j)])
                else:
                    si = {0: 0, 1: 1, 3: 2, 4: 3}[sec]
                    for q in range(4):
                        c = half * 4 + q
                        TR(pb[3][:, (si * 8 + c) * 2:(si * 8 + c) * 2 + 2], mrow[b2][0:2, q * 128:(q + 1) * 128], ident[0:2, 0:2],
                           [('mrow', b2), 'ident'], [PB[3]])
            CP('dve', modT[:], pb[3][:, 0:64], [PB[3]], ['modT'])
            for k in range(2):
                sh = modT[:, (2 * k) * 16:(2 * k) * 16 + 16]
                sc = modT[:, (2 * k + 1) * 16:(2 * k + 1) * 16 + 16]
                gn = pp[:, l * 16 + k * 8:l * 16 + k * 8 + 8]
                A = AB[:, k * 32:k * 32 + 16]
                B = AB[:, k * 32 + 16:k * 32 + 32]
                TS('dve', A, sc, 1.0, None, ALU.add, None, ['modT'], ['AB'])
                TT('dve', A.rearrange("p (c j) -> p c j", j=2), A.rearrange("p (c j) -> p c j", j=2), _bc_last(gn, 2), ALU.mult,
                   ['AB', 'pp'], ['AB'])
                CP('dve', B, sh, ['modT'], ['AB'])

        state = dict(xa=0, xo=0)

        def emit_h(S, src, t, k, dst, hkey, banks=None):
            j = 0 if t < NL else 1
            i = S.i
            xb = banks if banks is not None else S.xb
            X = S.xt
            p.dma('sp', X[:], xs[src][t * 128:(t + 1) * 128, :], reads=[(src, t)], writes=[('xt', i)], sem=f'xt{i}')
            TT('dve', S.buf[:], X[:], X[:], ALU.mult, [('xt', i)], [('buf', i)])
            RED(S.small[:, 0:1], S.buf[:], [('buf', i)], [('small', i)])
            RSTD(S.small[:, 0:1], 1.0 / D, ('small', i))
            ACT(S.buf[:], X[:], AF.Identity, [('xt', i), ('small', i)], [('buf', i)], scale=S.small[:, 0:1])
            for rnd in range(2):
                for c in range(rnd * 4, rnd * 4 + 4):
                    TR(pb[xb[c // 4]][:, (c % 4) * 128:(c % 4 + 1) * 128], S.buf[:, c * 128:(c + 1) * 128], ident[:],
                       [('buf', i), 'ident'], [PB[xb[c // 4]]], inc=(c % 4 == 3))
                for c in range(rnd * 4, rnd * 4 + 4):
                    ACT(dst[:, c, :], pb[xb[c // 4]][:, (c % 4) * 128:(c % 4 + 1) * 128], AF.Identity,
                        [PB[xb[c // 4]], 'AB'], [hkey], scale=AB[:, k * 32 + c * 2 + j:k * 32 + c * 2 + j + 1],
                        bias=AB[:, k * 32 + 16 + c * 2 + j:k * 32 + 16 + c * 2 + j + 1])

        def get_h(S, src, t, k, dst, hkey, compute, banks=None):
            i = S.i
            if compute:
                emit_h(S, src, t, k, dst, hkey, banks=banks)
                p.dma('sp', hts[k][t].rearrange("p (c n) -> p c n", c=8), dst, reads=[hkey], writes=[('hts', k, t)], sem=f'hs{i}')
            else:
                p.dma('sp', dst, hts[k][t].rearrange("p (c n) -> p c n", c=8), reads=[('hts', k, t)], writes=[hkey], sem=f'hl{i}')

        def proj(hT, hkey, wsb, wkey, col0, ncols, bank, M0=0, M1=128):
            for c in range(8):
                MM(pb[bank][0:M1 - M0, 0:ncols], hT[:, c, M0:M1], wsb[:, c, col0:col0 + ncols], c == 0, c == 7,
                   [hkey, wkey], [PB[bank]], inc=(c == 7))

        def load_cs(S, t, which):
            i = S.i
            if which == 64:
                p.dma('sp', S.cs[:, 0:128], cs64_d[t * 128:(t + 1) * 128, :], writes=[('cs', i)], sem=f'cs{i}')
            else:
                p.dma('sp', S.cs[:, 0:64], cs32_d[t * 128:(t + 1) * 128, :], writes=[('cs', i)], sem=f'cs{i}')

        def prep(S, src, G, d, gain_ap, out_b, rope=None):
            i = S.i
            kr, kb, ky, ks, kyb = ('raw', i), ('buf', i), ('y1', i), ('small', i), ('yb', i)
            sq = _view(S.buf[:], 0, [[d, G], [1, d]])
            TT('dve', sq, src, src, ALU.mult, [kr], [kb])
            RED(S.small[:, 8:8 + G], sq, [kb], [ks])
            RSTD(S.small[:, 8:8 + G], 1.0 / d, ks)
            yv = _view(S.y1[:], 0, [[d, G], [1, d]])
            TT('dve', yv, src, _bc_last(S.small[:, 8:8 + G], d), ALU.mult, [kr, ks], [ky])
            if rope is None:
                TT('pool', out_b, yv, _bc_mid(gain_ap, G), ALU.mult, [ky, 'gains', 'naq_s'], [kyb])
                return
            off, bs = rope
            TT('pool', yv, yv, _bc_mid(gain_ap, G), ALU.mult, [ky, 'gains'], [ky])
            if off > 0:
                CP('act', out_b[:, :, 0:off], _view(S.y1[:], 0, [[d, G], [1, off]]), [ky], [kyb])
            w = 4 * bs
            r = _view(S.y1[:], off, [[d, G], [1, w]])
            cosv = _bc_mid(S.cs[:, 0:w], G)
            ta = _view(S.buf[:], 0, [[w, G], [1, w]])
            TT('dve', ta, r, cosv, ALU.mult, [ky, ('cs', i)], [kb])
            for s_ in range(2):
                o_ = _view(S.buf[:], 512 + s_ * bs, [[w, G], [2 * bs, 2], [1, bs]])
                i0 = _view(S.y1[:], off + (1 - s_) * bs, [[d, G], [2 * bs, 2], [1, bs]])
                i1 = _view(S.cs[:], w + s_ * bs, [[0, G], [2 * bs, 2], [1, bs]])
                TT('pool', o_, i0, i1, ALU.mult, [ky, ('cs', i)], [kb])
            tb = _view(S.buf[:], 512, [[w, G], [1, w]])
            TT('dve', out_b[:, :, off:off + w], ta, tb, ALU.add, [kb], [kyb])

        def to_fm(S, ins, width, dst_of, dkey):
            i = S.i
            pt = ptbs[i][:, 0:512]
            for g0 in range(0, len(ins), 4):
                grp = ins[g0:g0 + 4]
                n = len(grp)
                for g, a in enumerate(grp):
                    TR(pt[0:width, g * 128:(g + 1) * 128], a, identb[:], [('yb', i), 'identb'], [PB[6 + i]], inc=(g == n - 1))
                CP('act', dst_of(g0, n), pt[0:width, 0:n * 128].rearrange("p (g t) -> p g t", g=n), [PB[6 + i]], [dkey])

        def residual_store(banks, xacc_i, Gk, j, dst, t):
            i = 0
            for hf in range(2):
                TT('dve', xo[i][:, hf * 512:(hf + 1) * 512], pb[banks[hf]][:, :], Gt[Gk][j][:, hf * 512:(hf + 1) * 512], ALU.mult,
                   [PB[banks[hf]], ('G', Gk, j)], [('xo', i)])
            TT('pool', xo[i][:], xo[i][:], xa[xacc_i][:], ALU.add, [('xo', i), ('xa', xacc_i)], [('xo', i)])
            if dst == 'y':
                if t < NL:
                    p.dma('sp', xs['y'][t * 128:(t + 1) * 128, :], xo[i][:], reads=[('xo', i)], writes=[(dst, t)], sem=f'xo{i}')
            else:
                p.dma('sp', xs[dst][t * 128:(t + 1) * 128, :], xo[i][:], reads=[('xo', i)], writes=[(dst, t)], sem=f'xo{i}')

        def load_xa(src, t):
            i = 0
            p.dma('sp', xa[i][:], xs[src][t * 128:(t + 1) * 128, :], reads=[(src, t)], writes=[('xa', i)], sem=f'xa{i}')
            return i

        def load_w(dst3, src3, key, sem, nch):
            for c in range(nch):
                p.dma('pool', dst3[:, c, :], src3[:, c, :], writes=[key], sem=sem)

        def merged(fn, items, nsets=NS):
            out = []
            for g0 in range(0, len(items), nsets):
                lists = []
                for k_, it in enumerate(items[g0:g0 + nsets]):
                    p.capture()
                    fn(SS[k_], it)
                    lists.append(p.end_capture())
                idx = [0] * len(lists)
                live = True
                while live:
                    live = False
                    for k_, L in enumerate(lists):
                        if idx[k_] < len(L):
                            out.append(L[idx[k_]])
                            idx[k_] += 1
                            live = True
            return out

        def overlapped_chunks(chunks_, q_tile_, att_chunk_, nsets_b=NS, kv=None):
            if kv is None:
                p.replay([merged(lambda S, tt: q_tile_(S, tt, 0), list(enumerate(chunks_[0])))])
            else:
                kv_tile_, kv_items, HC = kv

                def both(S, it):
                    if it[0] == 'kv':
                        kv_tile_(S, it[1])
                    else:
                        keep = HC[0]
                        HC[0] = False
                        q_tile_(S, it[1], 0)
                        HC[0] = keep
                interleaved(both, [('kv', t_) for t_ in kv_items] + [('q', tt_) for tt_ in enumerate(chunks_[0])])
                HC[0] = False
            for ci, tiles in enumerate(chunks_):
                p.capture()
                att_chunk_(ci, tiles)
                A = p.end_capture()
                B = merged(lambda S, tt: q_tile_(S, tt, (ci + 1) % 2), list(enumerate(chunks_[ci + 1])), nsets_b) if ci + 1 < len(chunks_) else []
                M = []
                nb_, ib = len(B), 0
                for ka, ia in enumerate(A):
                    M.append(ia)
                    tgt = ((ka + 1) * nb_) // max(len(A), 1)
                    while ib < tgt:
                        M.append(B[ib])
                        ib += 1
                M.extend(B[ib:])
                p.replay([M])

        def interleaved(fn, items):
            for g0 in range(0, len(items), NS):
                lists = []
                for k_, it in enumerate(items[g0:g0 + NS]):
                    p.capture()
                    fn(SS[k_], it)
                    lists.append(p.end_capture())
                p.replay(lists)

        def attend(N, units, scale, qT_of, dv, nb, PT, fused=None, qkey=('fm',), la=None, sbk=None, hook=None):
            pN, pD = nb if nb is not None else (None, None)
            nu = len(units)

            LA = int(os.environ.get('ATT_LA', '2')) if la is None else la
            DUP = int(os.environ.get('ATT_DUP', '0'))
            SBK = (sbk if sbk is not None else ([0, 3, 5] if fused is not None else [0, 3, 6]))[:LA + 1]

            def emit_S(i):
                u = units[i]
                sbk = SBK[i % len(SBK)]
                pS = pb[sbk]
                nk, c0, c1 = u['nk'], u['c0'], u['c1']
                n = c1 - c0
                for rep in range(1 + DUP):
                    MM(pS[0:nk, 0:n], u['kT'], qT_of(c0, c1), True, u.get('bias') is None, [('fm',), qkey], [PB[sbk]], inc=(u.get('bias') is None))
                    if u.get('bias') is not None:
                        MM(pS[0:nk, 0:n], u['bias'][0], u['bias'][1], False, True, ['identb', 'lib'], [PB[sbk]])

            for i0 in range(min(LA, nu)):
                emit_S(i0)
            for i, u in enumerate(units):
                sbk = SBK[i % len(SBK)]
                pS = pb[sbk]
                nk, c0, c1 = u['nk'], u['c0'], u['c1']
                n = c1 - c0
                PTi = i % 4
                ACT(PT[PTi][0:nk, 0:n], pS[0:nk, 0:n], AF.Exp, [PB[sbk]], [('PT', PTi)], scale=scale)
                if i + LA < nu:
                    emit_S(i + LA)
                for (mc, mk) in u.get('masks', []):
                    TT('pool', PT[PTi][:, mc:mc + 128], PT[PTi][:, mc:mc + 128], mk, ALU.mult, [('PT', PTi), 'trib'], [('PT', PTi)])
                if fused is not None:
                    MM(pb[fused][:, c0:c1], u['v'], PT[PTi][0:nk, 0:n], i == 0, i == nu - 1, [('PT', PTi), ('v',)], [PB[fused]], inc=True)
                    continue
                MM(pb[pN][0:dv, c0:c1], u['v'], PT[PTi][0:nk, 0:n], i == 0, i == nu - 1, [('PT', PTi), ('v',)], [PB[pN]], inc=False)
                MM(pb[pD][0:dv, c0:c1], onesb[0:nk, 0:dv], PT[PTi][0:nk, 0:n], i == 0, i == nu - 1, [('PT', PTi), 'onesb'], [PB[pD]],
                   inc=True)
                if hook is not None and i == 1:
                    hook(0)
                if hook is not None and i == 5:
                    hook(1)

        def norm_fused(h, N, bank, mix, add_sink=False):
            par = h % 2
            dlo, nlo = (64, 0) if par == 0 else (0, 64)
            r_ = rec[par]
            rk = RECK[par]
            if add_sink:
                TS('dve', r_[dlo:dlo + 64, 0:N], pb[bank][dlo:dlo + 64, 0:N], es_sink[dlo:dlo + 64, h:h + 1], None, ALU.add, None,
                   [PB[bank], 'es_sink'], [rk])
                p.op('dve', lambda e: e.reciprocal(r_[dlo:dlo + 64, 0:N], r_[dlo:dlo + 64, 0:N]), [rk], [rk])
            else:
                p.op('dve', lambda e: e.reciprocal(r_[dlo:dlo + 64, 0:N], pb[bank][dlo:dlo + 64, 0:N]), [PB[bank]], [rk])
            TT('dve', mix[nlo:nlo + 64, h // 2, 0:N], pb[bank][nlo:nlo + 64, 0:N], r_[dlo:dlo + 64, 0:N], ALU.mult, [PB[bank], rk], [('mix',)])

        def norm_heads(h, N, pN, pD, mix, add_sink=None):
            r_ = rec[h % 2]
            rk = RECK[h % 2]
            if add_sink is None:
                p.op('dve', lambda e: e.reciprocal(r_[0:64, 0:N], pb[pD][0:64, 0:N]), [PB[pD]], [rk])
            else:
                TS('dve', r_[0:64, 0:N], pb[pD][0:64, 0:N], add_sink, None, ALU.add, None, [PB[pD], 'es_sink'], [rk])
                p.op('dve', lambda e: e.reciprocal(r_[0:64, 0:N], r_[0:64, 0:N]), [rk], [rk])
            TT('dve', mix[0:64, h, 0:N], pb[pN][0:64, 0:N], r_[0:64, 0:N], ALU.mult, [PB[pN], rk], [('mix',)])

        def out_proj(l, mixv, K, nslot, wo, tiles, xacc_src, dst, Gk):
            for ti, t in enumerate(tiles):
                j = 0 if t < NL else 1
                xi = load_xa(xacc_src, t)
                for hf in range(2):
                    for s_ in range(nslot):
                        bk = (1 + hf) if ti % 2 == 0 else (4 + hf)
                        MM(pb[bk][:, :], mixv[0:K, s_, ti * 128:(ti + 1) * 128], wo[0:K, s_, hf * 512:(hf + 1) * 512],
                           s_ == 0, s_ == nslot - 1, [('mix',), 'wo'], [PB[bk]], inc=(s_ == nslot - 1))
                residual_store((1, 2) if ti % 2 == 0 else (4, 5), xi, Gk, j, dst, t)

        def mixer_pass(l, mx, src, acc, dst):
            AR.reset()
            p.barrier()
            ctxq = (l == 0)
            win_d = W[l]['w_in'].rearrange("(c p) n -> p c n", p=128)
            PT = [AR.alloc(512) for _ in range(4)]
            for S in SS:
                S.hT = AR.alloc(8, 128)
                S.yb = AR.alloc(1024)
                S.cT = AR.alloc(2, 128)
            HK = lambda S: ('hT', S.i)
            HCOMP = [mx in ('A', 'C')]
            chunks = [list(range(c * 4, c * 4 + 4)) for c in range(4)] + ([[16, 17]] if ctxq else [])
            NB = [(1, 2), (4, 5)]
            if mx == 'A':
                wi = AR.alloc(8, 416)
                load_w(wi, win_d[:, :, 0:416], 'wi', 'wi', 8)
                wuq = AR.alloc(2, 768)
                load_w(wuq, W[0]['w_uq'].rearrange("(c p) n -> p c n", p=128), 'wuq', 'wuq', 2)
                wukv = AR.alloc(1, 1024)
                load_w(wukv, W[0]['w_ukv'].rearrange("(c p) n -> p c n", p=128), 'wukv', 'wukv', 1)
                wo = AR.alloc(4, 1024)
                load_w(wo, W[l]['w_out'][0:512, :].rearrange("(s p) n -> p s n", p=128), 'wo', 'wo', 4)
                kT = AR.alloc(8, T)
                vv = AR.alloc(NT, 8, 128)
                p.op('pool', lambda e: e.memset(vv, 1.0), [], [('v',)])
                qTs = [AR.alloc(8, 512) for _ in range(2)]
                mix = AR.alloc(4, 512)

                def kv_tile(S, t):
                    lat = t < NL
                    i = S.i
                    if lat:
                        load_cs(S, t, 32)
                    get_h(S, src, t, 0, S.hT, HK(S), HCOMP[0])
                    proj(S.hT, HK(S), wi, 'wi', 256, 160, S.pj)
                    CP('act', S.raw[:, 0:160], pb[S.pj][:, 0:160], [PB[S.pj]], [('raw', i)])
                    prep(S, _view(S.raw[:], 0, [[128, 1], [1, 128]]), 1, 128, gain('kva'), _view(S.yb, 0, [[128, 1], [1, 128]]))
                    to_fm(S, [S.yb[:, 0:128]], 128, lambda g0, n: S.cT[:, 0:1, :], ('cT', i))
                    for hf in range(2):
                        MM(pb[S.xb[hf]][:, :], S.cT[:, 0, :], wukv[:, 0, hf * 512:(hf + 1) * 512], True, True, [('cT', i), 'wukv'], [PB[S.xb[hf]]])
                    for hf in range(2):
                        kvv = pb[S.xb[hf]][:, :].rearrange("p (h e) -> p h e", h=4)
                        for par in range(2):
                            CP('act', _view(vv, (t * 8 + hf * 4 + par) * 128 + par * 64, [[256, 2], [1, 64]]),
                               _view(pb[S.xb[hf]][:, :], par * 128 + 64, [[256, 2], [1, 64]]), [PB[S.xb[hf]]], [('v',)])
                        CP('act', _view(S.raw[:], 256 + hf * 4 * 96, [[96, 4], [1, 64]]), kvv[:, :, 0:64], [PB[S.xb[hf]]], [('raw', i)])
                    CP('pool', _view(S.raw[:], 256 + 64, [[96, 8], [1, 32]]), _bc_mid(S.raw[:, 128:160], 8), [('raw', i)], [('raw', i)])
                    ybv = _view(S.yb, 0, [[96, 8], [1, 96]])
                    prep(S, _view(S.raw[:], 256, [[96, 8], [1, 96]]), 8, 96, gain('kn96'), ybv, rope=(64, 8) if lat else None)
                    to_fm(S, [ybv[:, h, :] for h in range(8)], 96, lambda g0, n: kT[0:96, g0:g0 + n, t * 128:(t + 1) * 128], ('fm',))

                def q_tile(S, tt, qb):
                    ti, t = tt
                    lat = t < NL
                    i = S.i
                    bk = 6 + i
                    if lat:
                        load_cs(S, t, 32)
                    get_h(S, src, t, 0, S.hT, HK(S), HCOMP[0])
                    proj(S.hT, HK(S), wi, 'wi', 0, 256, bk)
                    CP('act', S.raw[:, 0:256], pb[bk][:, 0:256], [PB[bk]], [('raw', i)])
                    prep(S, _view(S.raw[:], 0, [[256, 1], [1, 256]]), 1, 256, gain('qa'), _view(S.yb, 0, [[256, 1], [1, 256]]))
                    to_fm(S, [S.yb[:, 0:128], S.yb[:, 128:256]], 128, lambda g0, n: S.cT[:, :, :], ('cT', i))
                    for (c0_, n_) in ((0, 512), (512, 256)):
                        for c in range(2):
                            MM(pb[bk][:, 0:n_], S.cT[:, c, :], wuq[:, c, c0_:c0_ + n_], c == 0, c == 1, [('cT', i), 'wuq'], [PB[bk]], inc=(c == 1))
                        CP('act', S.raw[:, c0_:c0_ + n_], pb[bk][:, 0:n_], [PB[bk]], [('raw', i)])
                    ybv = _view(S.yb, 0, [[96, 8], [1, 96]])
                    prep(S, _view(S.raw[:], 0, [[96, 8], [1, 96]]), 8, 96, gain('qn96'), ybv, rope=(64, 8) if lat else None)
                    to_fm(S, [ybv[:, h, :] for h in range(8)], 96, lambda g0, n: qTs[qb][0:96, g0:g0 + n, ti * 128:(ti + 1) * 128], ('qT', qb))

                def att_chunk(ci, tiles):
                    N = 128 * len(tiles)
                    lat = tiles[0] < NL
                    qT_ = qTs[ci % 2]
                    ktiles = [16, 17] + (list(range(16)) if lat else [])
                    for h in range(8):
                        units = [dict(kT=kT[0:96, h, kt * 128:(kt + 1) * 128], nk=128, c0=0, c1=N, v=vv[:, kt, h, :]) for kt in ktiles]
                        bank = (1, 2, 4)[h % 3]
                        attend(N, units, 96 ** -0.5, lambda c0, c1, h=h: qT_[0:96, h, c0:c1], 64, None, PT, fused=bank, qkey=('qT', ci % 2))
                        norm_fused(h, N, bank, mix)
                    out_proj(l, mix, 128, 4, wo, tiles, acc, dst, 0)
                overlapped_chunks(chunks, q_tile, att_chunk, kv=(kv_tile, list(range(NT)), HCOMP))
            elif mx == 'B':
                wi = AR.alloc(8, 1536)
                load_w(wi, win_d[:, :, 416:1952], 'wi', 'wi', 8)
                wo = AR.alloc(8, 1024)
                load_w(wo[0:64], W[l]['w_out'][512:1024, :].rearrange("(h p) n -> p h n", p=64), 'wo', 'wo', 8)
                kT = AR.alloc(4, T)
                vv = AR.alloc(32, 8, 64)
                vvc = AR.alloc(2, 8, 64)
                qT = AR.alloc(4, 512)
                mix = AR.alloc(8, 512)
                libt = AR.alloc(4, 960)
                p.dma('sp', SS[1].raw[:, 0:960], nam_d, writes=[('raw', 1)], sem='c7')
                for hc in range(4):
                    p.dma('sp', SS[0].raw[:, 0:960], lib_d[:, hc * 960:(hc + 1) * 960], writes=[('raw', 0)], sem='c8')
                    TT('pool', libt[:, hc, :], SS[0].raw[:, 0:960], SS[1].raw[:, 0:960], ALU.add, [('raw', 0), ('raw', 1)], ['lib'])

                def kv_tile(S, t):
                    lat = t < NL
                    i = S.i
                    get_h(S, src, t, 0, S.hT, HK(S), HCOMP[0])
                    proj(S.hT, HK(S), wi, 'wi', 512, 512, S.pj)
                    CP('act', S.raw[:, 0:512], pb[S.pj][:, :], [PB[S.pj]], [('raw', i)])
                    prep(S, _view(S.raw[:], 0, [[64, 8], [1, 64]]), 8, 64, gain('nak'), _view(S.yb, 0, [[64, 8], [1, 64]]))
                    to_fm(S, [S.yb[:, c * 128:(c + 1) * 128] for c in range(4)], 128, lambda g0, n: kT[:, g0:g0 + n, t * 128:(t + 1) * 128], ('fm',))
                    if lat:
                        for hf in range(2):
                            proj(S.hT, HK(S), wi, 'wi', 1024, 512, S.xb[hf], M0=hf * 64, M1=hf * 64 + 64)
                            CP('act', vv[0:64, 2 * t + hf, :, :], pb[S.xb[hf]][0:64, :].rearrange("p (h e) -> p h e", h=8), [PB[S.xb[hf]]], [('v',)])
                    else:
                        proj(S.hT, HK(S), wi, 'wi', 1024, 512, S.xb[0])
                        CP('act', vvc[:, t - NL, :, :], pb[S.xb[0]][:, :].rearrange("p (h e) -> p h e", h=8), [PB[S.xb[0]]], [('v',)])

                def q_tile(S, tt):
                    ti, t = tt
                    i = S.i
                    get_h(S, src, t, 0, S.hT, HK(S), HCOMP[0])
                    proj(S.hT, HK(S), wi, 'wi', 0, 512, S.pj)
                    CP('act', S.raw[:, 0:512], pb[S.pj][:, :], [PB[S.pj]], [('raw', i)])
                    prep(S, _view(S.raw[:], 0, [[64, 8], [1, 64]]), 8, 64, naq_s[:], _view(S.yb, 0, [[64, 8], [1, 64]]))
                    to_fm(S, [S.yb[:, c * 128:(c + 1) * 128] for c in range(4)], 128, lambda g0, n: qT[:, g0:g0 + n, ti * 128:(ti + 1) * 128], ('fm',))

                interleaved(kv_tile, list(range(NT)))
                HCOMP[0] = False
                for tiles in chunks:
                    N = 128 * len(tiles)
                    lat = tiles[0] < NL
                    interleaved(q_tile, list(enumerate(tiles)))
                    for h in range(8):
                        ho, hc = (h % 2) * 64, h // 2
                        units = [dict(kT=kT[ho:ho + 64, hc, (NL + i_) * 128:(NL + i_ + 1) * 128], nk=128, c0=0, c1=N, v=vvc[:, i_, h, :])
                                 for i_ in range(2)]
                        if lat:
                            r0 = (tiles[0] * 128) // 64
                            for kr in range(32):
                                rs = [r for r in range(r0, r0 + 8) if min(max(r - 4, 0), 24) <= kr <= min(max(r - 4, 0), 24) + 7]
                                if not rs:
                                    continue
                                ra, rb = rs[0], rs[-1]
                                c0, c1 = (ra - r0) * 64, (rb - r0 + 1) * 64
                                L0 = (7 - (kr - ra)) * 64
                                units.append(dict(kT=kT[ho:ho + 64, hc, kr * 64:(kr + 1) * 64], nk=64, c0=c0, c1=c1, v=vv[0:64, kr, h, :],
                                                  bias=(identb[ho:ho + 64, ho:ho + 64], libt[ho:ho + 64, hc, L0:L0 + (c1 - c0)])))
                        pN, pD = NB[h % 2]
                        attend(N, units, 1.0, lambda c0, c1, ho=ho, hc=hc: qT[ho:ho + 64, hc, c0:c1], 64, (pN, pD), PT)
                        norm_heads(h, N, pN, pD, mix)
                    out_proj(l, mix, 64, 8, wo, tiles, acc, dst, 0)
            elif mx == 'C':
                wi = AR.alloc(8, 768)
                load_w(wi, win_d[:, :, 0:768], 'wi', 'wi', 8)
                wo = AR.alloc(4, 1024)
                load_w(wo, W[l]['w_out'][0:512, :].rearrange("(s p) n -> p s n", p=128), 'wo', 'wo', 4)
                kT = AR.alloc(1, T)
                vvE = AR.alloc(NT, 2, 128)
                vvO = AR.alloc(NT, 2, 128)
                p.op('pool', lambda e: e.memset(vvE, 1.0), [], [('v',)])
                p.op('pool', lambda e: e.memset(vvO, 1.0), [], [('v',)])
                qTs = [AR.alloc(4, 512) for _ in range(2)]
                mix = AR.alloc(4, 512)

                def kv_tile(S, t):
                    lat = t < NL
                    i = S.i
                    if lat:
                        load_cs(S, t, 64)
                    get_h(S, src, t, 0, S.hT, HK(S), HCOMP[0])
                    proj(S.hT, HK(S), wi, 'wi', 512, 256, S.pj)
                    CP('act', S.raw[:, 0:256], pb[S.pj][:, 0:256], [PB[S.pj]], [('raw', i)])
                    prep(S, _view(S.raw[:], 0, [[64, 2], [1, 64]]), 2, 64, gain('gk'), _view(S.yb, 0, [[64, 2], [1, 64]]), rope=(0, 16) if lat else None)
                    to_fm(S, [S.yb[:, 0:128]], 128, lambda g0, n: kT[:, 0:1, t * 128:(t + 1) * 128], ('fm',))
                    CP('pool', vvE[:, t, :, 0:64], _view(S.raw[:], 128, [[64, 2], [1, 64]]), [('raw', i)], [('v',)])
                    CP('pool', vvO[:, t, :, 64:128], _view(S.raw[:], 128, [[64, 2], [1, 64]]), [('raw', i)], [('v',)])

                def q_tile(S, tt, qb):
                    ti, t = tt
                    i = S.i
                    bk = 6 + i
                    load_cs(S, t, 64)
                    get_h(S, src, t, 0, S.hT, HK(S), HCOMP[0])
                    proj(S.hT, HK(S), wi, 'wi', 0, 512, bk)
                    CP('act', S.raw[:, 0:512], pb[bk][:, :], [PB[bk]], [('raw', i)])
                    prep(S, _view(S.raw[:], 0, [[64, 8], [1, 64]]), 8, 64, gain('gq'), _view(S.yb, 0, [[64, 8], [1, 64]]), rope=(0, 16))
                    pt = ptbs[i][:, 0:512]
                    for h in range(8):
                        kvh, g = h // 4, h % 4
                        TR(pt[kvh * 64:(kvh + 1) * 64, g * 128:(g + 1) * 128], S.yb[:, h * 64:(h + 1) * 64], identb[:], [('yb', i), 'identb'],
                           [PB[6 + i]], inc=(h == 7))
                    CP('act', qTs[qb][:, :, ti * 128:(ti + 1) * 128], pt[:, 0:512].rearrange("p (g t) -> p g t", g=4), [PB[6 + i]], [('qT', qb)])

                def att_chunk(cch, tiles):
                    N = 512
                    qT_ = qTs[cch % 2]
                    for h in range(8):
                        kvh, g = h // 4, h % 4
                        ho = kvh * 64
                        vv = vvE if h % 2 == 0 else vvO
                        units = [dict(kT=kT[ho:ho + 64, 0, (NL + i_) * 128:(NL + i_ + 1) * 128], nk=128, c0=0, c1=N, v=vv[:, NL + i_, kvh, :])
                                 for i_ in range(2)]
                        for m in range(max(4 * cch - 1, 0), min(4 * cch + 4, 15) + 1):
                            nlo, nhi = max(m - 1, 4 * cch), min(m + 1, 4 * cch + 3)
                            masks = []
                            for n_ in range(nlo, nhi + 1):
                                if n_ == m + 1:
                                    masks.append(((n_ - nlo) * 128, trib[:, 0:128]))
                                elif n_ == m - 1:
                                    masks.append(((n_ - nlo) * 128, trib[:, 128:256]))
                            units.append(dict(kT=kT[ho:ho + 64, 0, m * 128:(m + 1) * 128], nk=128, c0=(nlo - 4 * cch) * 128,
                                              c1=(nhi - 4 * cch + 1) * 128, v=vv[:, m, kvh, :], masks=masks))
                        bank = (1, 2, 4)[h % 3]
                        attend(N, units, 0.125, lambda c0, c1, ho=ho, g=g: qT_[ho:ho + 64, g, c0:c1], 64, None, PT, fused=bank, qkey=('qT', cch % 2))
                        norm_fused(h, N, bank, mix, add_sink=True)
                    out_proj(l, mix, 128, 4, wo, tiles, acc, dst, 0)
                overlapped_chunks([list(range(c_ * 4, c_ * 4 + 4)) for c_ in range(4)], q_tile, att_chunk, kv=(kv_tile, list(range(NT)), HCOMP))
            else:
                wi = AR.alloc(8, 1536)
                load_w(wi, win_d[:, :, 768:2304], 'wi', 'wi', 8)
                wo = AR.alloc(4, 1024)
                load_w(wo, W[l]['w_out'][512:1024, :].rearrange("(h p) n -> p h n", p=128), 'wo', 'wo', 4)
                kT = AR.alloc(4, T)
                vv = AR.alloc(NT, 4, 128)
                qTs = [AR.alloc(4, 512) for _ in range(2)]
                mix = AR.alloc(4, 512)

                def kv_tile(S, t):
                    lat = t < NL
                    i = S.i
                    if lat:
                        load_cs(S, t, 64)
                    get_h(S, src, t, 0, S.hT, HK(S), HCOMP[0])
                    proj(S.hT, HK(S), wi, 'wi', 512, 512, S.pj)
                    CP('act', S.raw[:, 0:512], pb[S.pj][:, :], [PB[S.pj]], [('raw', i)])
                    prep(S, _view(S.raw[:], 0, [[64, 8], [1, 64]]), 8, 64, gain('dk'), _view(S.yb, 0, [[64, 8], [1, 64]]), rope=(0, 16) if lat else None)
                    to_fm(S, [S.yb[:, c * 128:(c + 1) * 128] for c in range(4)], 128, lambda g0, n: kT[:, g0:g0 + n, t * 128:(t + 1) * 128], ('fm',))
                    proj(S.hT, HK(S), wi, 'wi', 1024, 512, S.xb[0])
                    CP('act', vv[:, t, :, :], pb[S.xb[0]][:, :].rearrange("p (h e) -> p h e", h=4), [PB[S.xb[0]]], [('v',)])

                def q_tile(S, tt, qb):
                    ti, t = tt
                    i = S.i
                    bk = 6 + i
                    load_cs(S, t, 64)
                    get_h(S, src, t, 0, S.hT, HK(S), HCOMP[0])
                    proj(S.hT, HK(S), wi, 'wi', 0, 512, bk)
                    CP('act', S.raw[:, 0:512], pb[bk][:, :], [PB[bk]], [('raw', i)])
                    prep(S, _view(S.raw[:], 0, [[64, 8], [1, 64]]), 8, 64, gain('dq'), _view(S.yb, 0, [[64, 8], [1, 64]]), rope=(0, 16))
                    to_fm(S, [S.yb[:, c * 128:(c + 1) * 128] for c in range(4)], 128, lambda g0, n: qTs[qb][:, g0:g0 + n, ti * 128:(ti + 1) * 128], ('qT', qb))

                def att_chunk(cch, tiles):
                    N = 512
                    qT_ = qTs[cch % 2]
                    sqb = SS[1].yb[:, 0:512]

                    def tail(hh, stage=None):
                        if stage in (None, 0):
                            STT(of[0][:, :], of[1][:, :], lamt[:, 3:4], of[0][:, :], ALU.mult, ALU.add, [OFK[0], 'lamt'], [OFK[0]])
                            TT('pool', sqb, of[0][:, :], of[0][:, :], ALU.mult, [OFK[0]], [('yb', 1)])
                        if stage == 0:
                            return
                        MM(pb[5][:, :], onesb[:, :], sqb, True, True, [('yb', 1), 'onesb'], [PB[5]])
                        ACT(rec[0][:, :], pb[5][:, :], AF.Ln, [PB[5]], [RECK[0]], bias=EPS, scale=1.0 / 128)
                        ACT(rec[0][:, :], rec[0][:, :], AF.Exp, [RECK[0]], [RECK[0]], scale=-0.5)
                        STT(mix[:, hh, :], of[0][:, :], sublnS[:, 0:1], rec[0][:, :], ALU.mult, ALU.mult, [OFK[0], RECK[0], 'sublnS'], [('mix',)])

                    pend = None
                    for hh in range(4):
                        for cc in range(2):
                            ho = cc * 64
                            units = [dict(kT=kT[ho:ho + 64, hh, kt * 128:(kt + 1) * 128], nk=128, c0=0, c1=N, v=vv[:, kt, hh, :])
                                     for kt in ([16, 17] + list(range(16)))]
                            attend(N, units, 0.125, lambda c0, c1, ho=ho, hh=hh: qT_[ho:ho + 64, hh, c0:c1], 128, NB[cc], PT,
                                   qkey=('qT', cch % 2), la=2, sbk=[0, 3, 7], hook=(pend if cc == 0 else None))
                        for cc in range(2):
                            pN, pD = NB[cc]
                            p.op('dve', lambda e, cc=cc, pD=pD: e.reciprocal(rec[cc][:, :], pb[pD][:, :]), [PB[pD]], [RECK[cc]])
                            TT('dve', of[cc][:, :], pb[pN][:, :], rec[cc][:, :], ALU.mult, [PB[pN], RECK[cc]], [OFK[cc]])
                        pend = (lambda stage=None, hh=hh: tail(hh, stage))
                    pend()
                    out_proj(l, mix, 128, 4, wo, tiles, acc, dst, 0)
                overlapped_chunks([list(range(c_ * 4, c_ * 4 + 4)) for c_ in range(4)], q_tile, att_chunk, nsets_b=1, kv=(kv_tile, list(range(NT)), HCOMP))

        def ffn_half(l, hf, hsrc, acc, dst):
            AR.reset()
            p.barrier()
            wg = AR.alloc(8, 1408)
            wu = AR.alloc(8, 1408)
            wd = AR.alloc(11, 1024)
            load_w(wg, W[l]['wg'].rearrange("(c p) n -> p c n", p=128)[:, :, hf * 1408:(hf + 1) * 1408], 'wg', 'wg', 8)
            load_w(wu, W[l]['wu'].rearrange("(c p) n -> p c n", p=128)[:, :, hf * 1408:(hf + 1) * 1408], 'wu', 'wu', 8)
            load_w(wd, W[l]['wd'][hf * 1408:(hf + 1) * 1408, :].rearrange("(f p) n -> p f n", p=128), 'wd', 'wd', 11)
            h2s = [AR.alloc(8, 512) for _ in range(2)]
            act = AR.alloc(11, 512)
            sg = [AR.alloc(512) for _ in range(2)]
            chunks = [list(range(c * 4, c * 4 + 4)) for c in range(4)] + ([[16, 17]] if l == 0 else [])

            if hf == 0:
                for S in SS:
                    S.hT = AR.alloc(8, 128)

                def norm_tile(S, t):
                    get_h(S, hsrc, t, 1, S.hT, ('hT', S.i), True)
                interleaved(norm_tile, [t for tiles in chunks for t in tiles])

            def emit_h_tile(ci_, ti):
                t = chunks[ci_][ti]
                p.dma('sp', h2s[ci_ % 2][:, :, ti * 128:(ti + 1) * 128], hts[1][t].rearrange("p (c n) -> p c n", c=8),
                      reads=[('hts', 1, t)], writes=[('h2', ci_ % 2, ti)], sem=f'h2l{ci_ % 2}_{ti}')

            for ti in range(len(chunks[0])):
                emit_h_tile(0, ti)
            for ci_, tiles in enumerate(chunks):
                N = 128 * len(tiles)
                h2 = h2s[ci_ % 2]
                hks = [('h2', ci_ % 2, ti_) for ti_ in range(len(tiles))]
                nxt = len(chunks[ci_ + 1]) if ci_ + 1 < len(chunks) else 0
                for f in range(11):
                    bg, bu = (f % 2) * 2, (f % 2) * 2 + 1
                    for c in range(8):
                        MM(pb[bg][:, 0:N], wg[:, c, f * 128:(f + 1) * 128], h2[:, c, 0:N], c == 0, c == 7, hks + ['wg'], [PB[bg]], inc=(c == 7))
                    for c in range(8):
                        MM(pb[bu][:, 0:N], wu[:, c, f * 128:(f + 1) * 128], h2[:, c, 0:N], c == 0, c == 7, hks + ['wu'], [PB[bu]], inc=(c == 7))
                    ACT(sg[f % 2][:, 0:N], pb[bg][:, 0:N], AF.Silu, [PB[bg]], [('sg', f % 2)])
                    TT('dve', act[:, f, 0:N], pb[bu][:, 0:N], sg[f % 2][:, 0:N], ALU.mult, [PB[bu], ('sg', f % 2)], [('act',)])
                    if f % 2 == 1 and (f // 2) < nxt:
                        emit_h_tile(ci_ + 1, f // 2)
                for ti, t in enumerate(tiles):
                    j = 0 if t < NL else 1
                    xi2 = load_xa(acc, t)
                    for h2_ in range(2):
                        for f in range(11):
                            bk = (4 + h2_) if ti % 2 == 0 else (2 + h2_)
                            MM(pb[bk][:, :], act[:, f, ti * 128:(ti + 1) * 128], wd[:, f, h2_ * 512:(h2_ + 1) * 512], f == 0, f == 10,
                               [('act',), 'wd'], [PB[bk]], inc=(f == 10))
                    residual_store((4, 5) if ti % 2 == 0 else (2, 3), xi2, 1, j, dst, t)

        seq = []
        if 0 in layers:
            seq += [('ada', 0), ('mix', 0, 'A', 'xin', 'xin', 'xs_a'), ('mix', 0, 'B', 'xin', 'xs_a', 'xs_b'),
                    ('ffn', 0, 0, 'xs_b', 'xs_b', 'xs_c'), ('ffn', 0, 1, 'xs_b', 'xs_c', 'xs_d')]
        if 1 in layers:
            s1 = 'xs_d' if 0 in layers else 'xin'
            seq += [('ada', 1), ('mix', 1, 'C', s1, s1, 'xs_a'), ('mix', 1, 'D', s1, 'xs_a', 'xs_b'),
                    ('ffn', 1, 0, 'xs_b', 'xs_b', 'xs_c'), ('ffn', 1, 1, 'xs_b', 'xs_c', 'y')]
        if stop_after is not None:
            seq = seq[:stop_after]
        last_dst = None
        for st in seq:
            if st[0] == 'ada':
                ada_phase(st[1])
            elif st[0] == 'mix':
                mixer_pass(st[1], st[2], st[3], st[4], st[5])
                last_dst = st[5]
            else:
                ffn_half(st[1], st[2], st[3], st[4], st[5])
                last_dst = st[5]
        if last_dst is None:
            p.dma('sp', xs['y'][0:128, :], Gt[0][0][:], reads=[('G', 0, 0)], writes=[('y', 0)], sem='dbg0')
            p.dma('sp', xs['y'][128:256, :], Gt[0][1][:], reads=[('G', 0, 1)], writes=[('y', 1)], sem='dbg0')
            p.dma('sp', xs['y'][256:384, :], Gt[1][0][:], reads=[('G', 1, 0)], writes=[('y', 2)], sem='dbg0')
            p.dma('sp', xs['y'][384:512, 0:64], AB[:], reads=['AB'], writes=[('y', 3)], sem='dbg0')
            p.finish([('y', t) for t in range(4)])
            p.emit()
            return nc
        if last_dst != 'y':
            for t in range(NL):
                S = SS[t % 2]
                p.dma('sp', S.xt[:], xs[last_dst][t * 128:(t + 1) * 128, :], reads=[(last_dst, t)], writes=[('xt', S.i)], sem=f'xt{S.i}')
                p.dma('sp', xs['y'][t * 128:(t + 1) * 128, :], S.xt[:], reads=[('xt', S.i)], writes=[('y', t)], sem=f'dbg{t % 2}')
        p.finish([('y', t) for t in range(NL)])
        p.emit()
    return nc


_CACHE = {}


def _host_inputs(inputs, layers=(0, 1)):
    f = lambda a: np.ascontiguousarray(np.asarray(a, dtype=np.float32))
    x, c, ctx, c_ctx = f(inputs['x']), f(inputs['c']), f(inputs['ctx']), f(inputs['c_ctx'])
    cs64, cs32 = _rope_tables()
    lib, nam = _na_tables(f(inputs['l0_na_rpb']))
    gl = []
    for n in ['l0_mla_qa_g', 'l0_mla_kva_g', 'l0_mla_qn_g', 'l0_mla_kn_g', 'l0_na_qn_g', 'l0_na_kn_g', 'l1_gqa_qn_g', 'l1_gqa_kn_g',
              'l1_diff_qn_g', 'l1_diff_kn_g', 'l1_gqa_sink', 'l1_diff_lq1', 'l1_diff_lk1', 'l1_diff_lq2', 'l1_diff_lk2']:
        gl.append(f(inputs[n]).reshape(-1))
    gains = np.ascontiguousarray(np.broadcast_to(np.concatenate(gl)[None, :], (128, GW)))
    pp = np.zeros((128, 33), np.float32)
    for i, n in enumerate(['l0_norm1_g', 'l0_norm2_g', 'l1_norm1_g', 'l1_norm2_g']):
        pp[:, i * 8:(i + 1) * 8] = f(inputs[n]).reshape(8, 128).T
    pp[:, 32] = f(inputs['l1_diff_subln_g'])
    ident = np.eye(128, dtype=np.float32)
    jj = np.arange(128)[:, None]
    ii = np.arange(128)[None, :]
    tri = np.concatenate([(ii <= jj), (jj <= ii)], axis=1).astype(np.float32)
    sel = np.zeros((2, 256), np.float32)
    sel[0, 0:128] = 1.0
    sel[1, 128:256] = 1.0
    shared = dict(gains=gains, pp=pp, ident=ident, tri=tri, sel=sel, cs64=cs64, cs32=cs32,
                  nalib=np.ascontiguousarray(lib.reshape(128, 4 * 960)), namask=nam)
    for l in (0, 1):
        shared[f'l{l}_ada_w'] = f(inputs[f'l{l}_ada_w'])
        shared[f'l{l}_ada_b2'] = np.ascontiguousarray(np.broadcast_to(f(inputs[f'l{l}_ada_b'])[None, :], (2, 6 * D)))
        shared[f'l{l}_w_in'] = f(inputs[f'l{l}_w_in'])
        shared[f'l{l}_w_out'] = f(inputs[f'l{l}_w_out'])
        shared[f'l{l}_ffn_w_gate'] = f(inputs[f'l{l}_ffn_w_gate'])
        shared[f'l{l}_ffn_w_up'] = f(inputs[f'l{l}_ffn_w_up'])
        shared[f'l{l}_ffn_w_down'] = f(inputs[f'l{l}_ffn_w_down'])
    shared['l0_mla_w_uq'] = f(inputs['l0_mla_w_uq'])
    shared['l0_mla_w_ukv'] = f(inputs['l0_mla_w_ukv'])
    maps = []
    for b in range(x.shape[0]):
        m = dict(shared)
        m['xin'] = np.ascontiguousarray(np.concatenate([x[b], ctx[b]], axis=0))
        cv = np.stack([c[b], c_ctx], axis=0)
        m['cvecT'] = np.ascontiguousarray(cv.reshape(2, 8, 128).transpose(2, 1, 0).reshape(128, 16))
        maps.append(m)
    return maps


def kernel(**inputs):
    maps = _host_inputs(inputs)
    if 'nc' not in _CACHE:
        _CACHE['nc'] = build()
    res = run_bass_kernel_spmd(_CACHE['nc'], maps, core_ids=list(range(len(maps))))
    return np.stack([np.asarray(r['y'], dtype=np.float32) for r in res.results], axis=0)
```

```python
import os
import numpy as np
import concourse.bass as bass
import concourse.mybir as mybir
from concourse.bass_utils import run_bass_kernel_spmd
from concourse.alu_op_type import AluOpType as ALU
from contextlib import ExitStack

F32 = mybir.dt.float32
BF16 = mybir.dt.bfloat16
AF = mybir.ActivationFunctionType
AX = mybir.AxisListType


STRICT = int(os.environ.get('KSTRICT', '1'))


class Prog:
    ENG = ('pe', 'act', 'dve', 'pool', 'sp')

    def __init__(self, nc):
        self.nc = nc
        self.q = {e: [] for e in self.ENG}
        self.cnt = {}
        self.res = {}
        self.known = {e: {} for e in self.ENG}
        self.pending = {e: {} for e in self.ENG}

    def _need(self, eng, toks, waits, skip=None):
        for s, v in toks.items():
            if s == skip:
                continue
            if self.known[eng].get(s, 0) < v and waits.get(s, 0) < v:
                waits[s] = v

    def _record(self, eng, fn, reads, writes, tok, incspec, is_dma=False):
        waits = {}
        own = 'e_' + eng
        wskip = tok[0] if is_dma else (own if (eng == 'pe' or not STRICT) else None)
        for r in reads:
            st = self.res.get(r)
            if st is not None:
                self._need(eng, st[0], waits, skip=own if eng == 'pe' else None)
        for w in writes:
            st = self.res.get(w)
            if st is not None:
                self._need(eng, st[0], waits, skip=wskip)
                self._need(eng, st[1], waits, skip=wskip)
        for s, v in waits.items():
            self.known[eng][s] = v
        for s, v in self.pending[eng].items():
            if waits.get(s, 0) < v:
                waits[s] = v
        self.pending[eng] = {}
        self.q[eng].append((sorted(waits.items()), fn, incspec))
        for r in reads:
            st = self.res.setdefault(r, [{}, {}])
            if st[1].get(tok[0], 0) < tok[1]:
                st[1][tok[0]] = tok[1]
        for w in writes:
            self.res[w] = [{tok[0]: tok[1]}, {}]

    def capture(self):
        self.cap = []
        return self.cap

    def end_capture(self):
        c = self.cap
        self.cap = None
        return c

    def replay(self, lists):
        idx = [0] * len(lists)
        live = True
        while live:
            live = False
            for k, L in enumerate(lists):
                if idx[k] < len(L):
                    it = L[idx[k]]
                    idx[k] += 1
                    live = True
                    if it[0] == 'op':
                        self.op(*it[1:])
                    else:
                        self.dma(*it[1:-1], **it[-1])

    def op(self, eng, fn, reads=(), writes=(), inc=True):
        if getattr(self, 'cap', None) is not None:
            self.cap.append(('op', eng, fn, list(reads), list(writes), inc))
            return
        own = 'e_' + eng
        c = self.cnt.get(own, 0)
        if inc:
            self.cnt[own] = c + 1
        self._record(eng, fn, reads, writes, (own, c + 1), (own, 1) if inc else None)

    def dma(self, eng, out, in_, reads=(), writes=(), sem=None, **kw):
        if getattr(self, 'cap', None) is not None:
            self.cap.append(('dma', eng, out, in_, list(reads), list(writes), sem, kw))
            return
        s = 'd_' + sem
        c = self.cnt.get(s, 0) + 16
        self.cnt[s] = c
        self._record(eng, lambda e: e.dma_start(out=out, in_=in_, **kw), reads, writes, (s, c), (s, 16), is_dma=True)

    def finish(self, keys):
        self.op('sp', lambda e: e.nop(), reads=list(keys), inc=False)

    def emit(self):
        nc = self.nc
        names = sorted(self.cnt.keys())
        with ExitStack() as es:
            sems = {n: es.enter_context(nc.semaphore(n)) for n in names}
            with nc.Block() as block:
                def body(ename):
                    def f(e):
                        for waits, fn, incspec in self.q[ename]:
                            for s, v in waits:
                                e.wait_ge(sems[s], v)
                            ins = fn(e)
                            if incspec is not None:
                                ins.then_inc(sems[incspec[0]], incspec[1])
                        for s, v in sorted(self.pending[ename].items()):
                            e.wait_ge(sems[s], v)
                    return f
                block.tensor(body('pe'))
                block.scalar(body('act'))
                block.vector(body('dve'))
                block.gpsimd(body('pool'))
                block.sync(body('sp'))

    def check(self):
        sem = {}
        ptr = {e: 0 for e in self.ENG}
        prog = True
        while prog:
            prog = False
            for e in self.ENG:
                while ptr[e] < len(self.q[e]):
                    waits, fn, inc = self.q[e][ptr[e]]
                    if all(sem.get(s_, 0) >= v for s_, v in waits):
                        if inc is not None:
                            sem[inc[0]] = sem.get(inc[0], 0) + inc[1]
                        ptr[e] += 1
                        prog = True
                    else:
                        break
        stuck = {e: (ptr[e], len(self.q[e]), self.q[e][ptr[e]][0]) for e in self.ENG if ptr[e] < len(self.q[e])}
        return stuck, sem

    def barrier(self):
        snap = dict(self.cnt)
        for eng in self.ENG:
            waits = {}
            own = 'e_' + eng
            for s, v in snap.items():
                if s == own:
                    continue
                if self.known[eng].get(s, 0) < v:
                    waits[s] = v
                    self.known[eng][s] = v
            for s, v in waits.items():
                if self.pending[eng].get(s, 0) < v:
                    self.pending[eng][s] = v


D = 1024
T = 2304
NT = 18
NL = 16
FH = 2816
EPS = 1e-6
NEG = -30000.0
GRID_W = 64
G_OFF = {}
_o = 0
for _n, _w in [('qa', 256), ('kva', 128), ('qn96', 96), ('kn96', 96), ('naq', 64), ('nak', 64), ('gq', 64), ('gk', 64),
               ('dq', 64), ('dk', 64), ('sink', 8), ('lq1', 64), ('lk1', 64), ('lq2', 64), ('lk2', 64)]:
    G_OFF[_n] = (_o, _w)
    _o += _w
GW = _o


def _rope_tables():
    pos = np.arange(2048)
    rows, cols = pos // GRID_W, pos % GRID_W

    def tab(h):
        fr = (10000.0 ** (-np.arange(0, h, 2, dtype=np.float32) / np.float32(h))).astype(np.float32)
        ar = rows.astype(np.float32)[:, None] * fr[None, :]
        ac = cols.astype(np.float32)[:, None] * fr[None, :]
        cr, sr, cc, sc = np.cos(ar), np.sin(ar), np.cos(ac), np.sin(ac)
        cos = np.concatenate([cr, cr, cc, cc], axis=1)
        sin = np.concatenate([-sr, sr, -sc, sc], axis=1)
        return np.concatenate([cos, sin], axis=1).astype(np.float32)
    return tab(32), tab(16)


def _na_tables(rpb):
    kc = np.arange(64)[:, None]
    qc = np.arange(64)[None, :]
    c0 = np.clip(qc - 8, 0, 48)
    valid = (kc >= c0) & (kc < c0 + 16)
    offc = np.clip(kc - qc + 15, 0, 30)
    lib = np.zeros((128, 4, 960), np.float32)
    mask = np.zeros((128, 960), np.float32)
    for dr in range(-7, 8):
        col = (7 - dr) * 64
        for h in range(8):
            lib[(h % 2) * 64:(h % 2) * 64 + 64, h // 2, col:col + 64] = rpb[h, dr + 7][offc]
        mask[0:64, col:col + 64] = np.where(valid, 0.0, NEG)
        mask[64:128, col:col + 64] = np.where(valid, 0.0, NEG)
    return lib, mask


def _bc_mid(a, G):
    return bass.AP(a.tensor, a.offset, [list(a.ap[0]), [0, G], list(a.ap[1])])


def _bc_last(a, d):
    return bass.AP(a.tensor, a.offset, [list(a.ap[0]), list(a.ap[1]), [0, d]])


def _view(a, off, dims):
    return bass.AP(a.tensor, a.offset + off, [list(a.ap[0])] + [list(x) for x in dims])


def build(layers=(0, 1), stop_after=None):
    nc = bass.Bass("TRN2", target_bir_lowering=False)
    es = ExitStack()

    def din(name, shape):
        return nc.dram_tensor(name, list(shape), F32, kind="ExternalInput").ap()

    xin = din("xin", [T, D])
    cvecT = din("cvecT", [128, 16])
    gains_d = din("gains", [128, GW])
    pp_d = din("pp", [128, 33])
    ident_d = din("ident", [128, 128])
    tri_d = din("tri", [128, 256])
    sel_d = din("sel", [2, 256])
    cs64_d = din("cs64", [2048, 128])
    cs32_d = din("cs32", [2048, 64])
    lib_d = din("nalib", [128, 4 * 960])
    nam_d = din("namask", [128, 960])
    W = {}
    for l in (0, 1):
        W[l] = dict(
            ada_w=din(f"l{l}_ada_w", [D, 6 * D]), ada_b2=din(f"l{l}_ada_b2", [2, 6 * D]),
            w_in=din(f"l{l}_w_in", [D, 1952 if l == 0 else 2304]), w_out=din(f"l{l}_w_out", [D, D]),
            wg=din(f"l{l}_ffn_w_gate", [D, FH]), wu=din(f"l{l}_ffn_w_up", [D, FH]), wd=din(f"l{l}_ffn_w_down", [FH, D]))
    W[0]['w_uq'] = din("l0_mla_w_uq", [256, 768])
    W[0]['w_ukv'] = din("l0_mla_w_ukv", [128, 1024])
    y_out = nc.dram_tensor("y", [2048, D], F32, kind="ExternalOutput").ap()
    xs = {n: nc.dram_tensor(n, [T, D], F32).ap() for n in ('xs_a', 'xs_b', 'xs_c', 'xs_d')}
    xs['xin'] = xin
    hts = [nc.dram_tensor(f'hts{k}', [NT, 128, D], BF16).ap() for k in range(2)]
    xs['y'] = y_out

    with es:
        def sb(name, shape, dt=F32):
            return es.enter_context(nc.sbuf_tensor(name, list(shape), dt))

        def ps(name, shape, dt=F32):
            return es.enter_context(nc.psum_tensor(name, list(shape), dt))

        p = Prog(nc)
        ident = sb("ident_s", [128, 128])
        identb = sb("identb", [128, 128], BF16)
        onesb = sb("onesb", [128, 128], BF16)
        trib = sb("trib", [128, 256], BF16)
        sel = sb("sel_s", [2, 256])
        gains = sb("gains_s", [128, GW])
        pp = sb("pp_s", [128, 33])
        cT = sb("cT", [128, 16])
        cTb = sb("cTb", [128, 16], BF16)
        modT = sb("modT", [128, 64])
        AB = sb("AB", [128, 64])
        Gt = [[sb(f"G{k}{j}", [128, D]) for j in range(2)] for k in range(2)]
        small = sb("small", [128, 64])
        lamt = sb("lamt", [128, 8])
        es_sink = sb("es_sink", [128, 8])
        sublnS = sb("sublnS", [128, 1])
        naq_s = sb("naq_s", [128, 64])
        xa = [sb("xa0", [128, D])]
        recb = [sb(f"recb{i}", [128, 512]) for i in range(2)]
        xo = [sb(f"xo{i}", [128, D]) for i in range(2)]
        mrow = [sb(f"mrow{i}", [2, 512]) for i in range(2)]
        brow = [sb(f"brow{i}", [2, 512]) for i in range(2)]
        NS = 2

        class SSet:
            pass
        SS = []
        for i in range(NS):
            S_ = SSet()
            S_.i = i
            S_.xt = sb(f"xt{i}", [128, D])
            S_.buf = sb(f"buf{i}", [128, D])
            S_.raw = sb(f"raw{i}", [128, D])
            S_.y1 = sb(f"y1{i}", [128, D])
            S_.cs = sb(f"cs{i}", [128, 192])
            S_.small = small[:, i * 32:(i + 1) * 32]
            S_.xb = (3 * i, 3 * i + 1)
            S_.pj = 3 * i + 2
            S_.k = lambda n, i=i: (n, i)
            SS.append(S_)
        junk = SS[0].buf
        raw = SS[0].raw
        rec = recb
        of = [xo[1][:, 0:512], xo[1][:, 512:1024]]
        RECK = [('rec', 0), ('rec', 1)]
        OFK = [('xo', 1), ('xo', 1)]
        ARN = 63800
        arena = sb("arena", [128, ARN], BF16)
        pb = [ps(f"pb{i}", [128, 512]) for i in range(8)]
        ptbs = [pb[6][:, :].bitcast(BF16), pb[7][:, :].bitcast(BF16)]
        PB = [('pb', i) for i in range(8)]

        class Arena:
            def __init__(self):
                self.off = 0

            def reset(self):
                self.off = 0

            def alloc(self, *free):
                n = int(np.prod(free))
                a = arena[:, self.off:self.off + n]
                self.off += n
                assert self.off <= ARN, self.off
                if len(free) == 2:
                    a = a.rearrange("p (a b) -> p a b", a=free[0])
                elif len(free) == 3:
                    a = a.rearrange("p (a b c) -> p a b c", a=free[0], b=free[1])
                return a
        AR = Arena()

        def MM(out, lhsT, rhs, start, stop, reads, writes, inc=True):
            p.op('pe', lambda e: e.matmul(out, lhsT, rhs, start=start, stop=stop), reads, writes, inc)

        def TR(out, in_, idn, reads, writes, inc=True):
            p.op('pe', lambda e: e.transpose(out, in_, idn), reads, writes, inc)

        def ACT(out, in_, func, reads, writes, bias=None, scale=None):
            kw = {}
            if bias is not None:
                kw['bias'] = bias
            if scale is not None:
                kw['scale'] = scale
            p.op('act', lambda e: e.activation(out, in_, func, **kw), reads, writes)

        def TT(eng, out, a, b, op, reads, writes):
            p.op(eng, lambda e: e.tensor_tensor(out, a, b, op), reads, writes)

        def TS(eng, out, a, s1, s2, op0, op1, reads, writes):
            if s2 is None:
                p.op(eng, lambda e: e.tensor_scalar(out, a, s1, None, op0), reads, writes)
            else:
                p.op(eng, lambda e: e.tensor_scalar(out, a, s1, s2, op0, op1), reads, writes)

        def STT(out, a, s, b, op0, op1, reads, writes):
            p.op('dve', lambda e: e.scalar_tensor_tensor(out, a, s, b, op0, op1), reads, writes)

        def CP(eng, out, in_, reads, writes):
            if eng == 'act':
                ACT(out, in_, AF.Copy, reads, writes)
            else:
                p.op(eng, lambda e: e.tensor_copy(out, in_), reads, writes)

        def RED(out, in_, reads, writes):
            p.op('dve', lambda e: e.tensor_reduce(out, in_, AX.X, ALU.add), reads, writes)

        def RSTD(ap, n_inv, reads_writes):
            ACT(ap, ap, AF.Ln, [reads_writes], [reads_writes], bias=EPS, scale=n_inv)
            ACT(ap, ap, AF.Exp, [reads_writes], [reads_writes], scale=-0.5)

        def gain(name):
            o, w = G_OFF[name]
            return gains[:, o:o + w]

        p.dma('sp', ident[:], ident_d, writes=['ident'], sem='c0')
        p.dma('sp', sel[:], sel_d, writes=['sel'], sem='c1')
        p.dma('sp', gains[:], gains_d, writes=['gains'], sem='c2')
        p.dma('sp', pp[:], pp_d, writes=['pp'], sem='c3')
        p.dma('sp', cT[:], cvecT, writes=['cT'], sem='c4')
        p.dma('pool', identb[:], ident_d, writes=['identb'], sem='c5')
        p.dma('pool', trib[:], tri_d, writes=['trib'], sem='c6')
        p.op('pool', lambda e: e.memset(onesb[:], 1.0), writes=['onesb'])
        ACT(cTb[:], cT[:], AF.Silu, ['cT'], ['cTb'])
        TS('dve', naq_s[:], gain('naq'), 0.125, None, ALU.mult, None, ['gains'], ['naq_s'])
        lam_init = 0.8 - 0.6 * float(np.exp(-0.3 * 1))
        TT('dve', junk[:, 0:64], gain('lq1'), gain('lk1'), ALU.mult, ['gains'], [('buf', 0)])
        RED(lamt[:, 0:1], junk[:, 0:64], [('buf', 0)], ['lamt'])
        TT('dve', junk[:, 64:128], gain('lq2'), gain('lk2'), ALU.mult, ['gains'], [('buf', 0)])
        RED(lamt[:, 1:2], junk[:, 64:128], [('buf', 0)], ['lamt'])
        ACT(lamt[:, 0:2], lamt[:, 0:2], AF.Exp, ['lamt'], ['lamt'])
        TT('dve', lamt[:, 2:3], lamt[:, 1:2], lamt[:, 0:1], ALU.subtract, ['lamt'], ['lamt'])
        TS('dve', lamt[:, 3:4], lamt[:, 2:3], -lam_init, None, ALU.add, None, ['lamt'], ['lamt'])
        ACT(es_sink[:], gain('sink'), AF.Exp, ['gains'], ['es_sink'])
        TS('dve', sublnS[:], pp[:, 32:33], 1.0 - lam_init, None, ALU.mult, None, ['pp'], ['sublnS'])

        def ada_phase(l):
            AR.reset()
            p.barrier()
            aw = [AR.alloc(8, 512) for _ in range(3)]
            adaw = W[l]['ada_w'].rearrange("(c p) n -> p c n", p=128)
            for n in range(12):
                s = n % 3
                b2 = n % 2
                for c in range(8):
                    p.dma('pool', aw[s][:, c, :], adaw[:, c, n * 512:(n + 1) * 512], writes=[('aw', s)], sem=f'aw{s}')
                p.dma('sp', brow[b2][:], W[l]['ada_b2'][:, n * 512:(n + 1) * 512], writes=[('brow', b2)], sem=f'brow{b2}')
                for c in range(8):
                    MM(pb[0][0:2, :], cTb[:, c * 2:c * 2 + 2], aw[s][:, c, :], c == 0, c == 7,
                       ['cTb', ('aw', s)], [PB[0]], inc=(c == 7))
                TT('dve', mrow[b2][:], pb[0][0:2, :], brow[b2][:], ALU.add, [PB[0], ('brow', b2)], [('mrow', b2)])
                sec, half = n // 2, n % 2
                if sec in (2, 5):
                    k = 0 if sec == 2 else 1
                    for j in range(2):
                        MM(pb[1 + j][:, :], sel[0:2, j * 128:(j + 1) * 128], mrow[b2][:], True, True,
                           ['sel', ('mrow', b2)], [PB[1 + j]])
                        CP('act', Gt[k][j][:, half * 512:(half + 1) * 512], pb[1 + j][:, :], [PB[1 + j]], [('G', k, j)])
                else:
                    si = {0: 0, 1: 1, 3: 2, 4: 3}[sec]
                    for q in range(4):
                        c = half * 4 + q
                        TR(pb[3][:, (si * 8 + c) * 2:(si * 8 + c) * 2 + 2], mrow[b2][0:2, q * 128:(q + 1) * 128], ident[0:2, 0:2],
                           [('mrow', b2), 'ident'], [PB[3]])
            CP('dve', modT[:], pb[3][:, 0:64], [PB[3]], ['modT'])
            for k in range(2):
                sh = modT[:, (2 * k) * 16:(2 * k) * 16 + 16]
                sc = modT[:, (2 * k + 1) * 16:(2 * k + 1) * 16 + 16]
                gn = pp[:, l * 16 + k * 8:l * 16 + k * 8 + 8]
                A = AB[:, k * 32:k * 32 + 16]
                B = AB[:, k * 32 + 16:k * 32 + 32]
                TS('dve', A, sc, 1.0, None, ALU.add, None, ['modT'], ['AB'])
                TT('dve', A.rearrange("p (c j) -> p c j", j=2), A.rearrange("p (c j) -> p c j", j=2), _bc_last(gn, 2), ALU.mult,
                   ['AB', 'pp'], ['AB'])
                CP('dve', B, sh, ['modT'], ['AB'])

        state = dict(xa=0, xo=0)

        def emit_h(S, src, t, k, dst, hkey, banks=None):
            j = 0 if t < NL else 1
            i = S.i
            xb = banks if banks is not None else S.xb
            X = S.xt
            p.dma('sp', X[:], xs[src][t * 128:(t + 1) * 128, :], reads=[(src, t)], writes=[('xt', i)], sem=f'xt{i}')
            TT('dve', S.buf[:], X[:], X[:], ALU.mult, [('xt', i)], [('buf', i)])
            RED(S.small[:, 0:1], S.buf[:], [('buf', i)], [('small', i)])
            RSTD(S.small[:, 0:1], 1.0 / D, ('small', i))
            ACT(S.buf[:], X[:], AF.Identity, [('xt', i), ('small', i)], [('buf', i)], scale=S.small[:, 0:1])
            for rnd in range(2):
                for c in range(rnd * 4, rnd * 4 + 4):
                    TR(pb[xb[c // 4]][:, (c % 4) * 128:(c % 4 + 1) * 128], S.buf[:, c * 128:(c + 1) * 128], ident[:],
                       [('buf', i), 'ident'], [PB[xb[c // 4]]], inc=(c % 4 == 3))
                for c in range(rnd * 4, rnd * 4 + 4):
                    ACT(dst[:, c, :], pb[xb[c // 4]][:, (c % 4) * 128:(c % 4 + 1) * 128], AF.Identity,
                        [PB[xb[c // 4]], 'AB'], [hkey], scale=AB[:, k * 32 + c * 2 + j:k * 32 + c * 2 + j + 1],
                        bias=AB[:, k * 32 + 16 + c * 2 + j:k * 32 + 16 + c * 2 + j + 1])

        def get_h(S, src, t, k, dst, hkey, compute, banks=None):
            i = S.i
            if compute:
                emit_h(S, src, t, k, dst, hkey, banks=banks)
                p.dma('sp', hts[k][t].rearrange("p (c n) -> p c n", c=8), dst, reads=[hkey], writes=[('hts', k, t)], sem=f'hs{i}')
            else:
                p.dma('sp', dst, hts[k][t].rearrange("p (c n) -> p c n", c=8), reads=[('hts', k, t)], writes=[hkey], sem=f'hl{i}')

        def proj(hT, hkey, wsb, wkey, col0, ncols, bank, M0=0, M1=128):
            for c in range(8):
                MM(pb[bank][0:M1 - M0, 0:ncols], hT[:, c, M0:M1], wsb[:, c, col0:col0 + ncols], c == 0, c == 7,
                   [hkey, wkey], [PB[bank]], inc=(c == 7))

        def load_cs(S, t, which):
            i = S.i
            if which == 64:
                p.dma('sp', S.cs[:, 0:128], cs64_d[t * 128:(t + 1) * 128, :], writes=[('cs', i)], sem=f'cs{i}')
            else:
                p.dma('sp', S.cs[:, 0:64], cs32_d[t * 128:(t + 1) * 128, :], writes=[('cs', i)], sem=f'cs{i}')

        def prep(S, src, G, d, gain_ap, out_b, rope=None):
            i = S.i
            kr, kb, ky, ks, kyb = ('raw', i), ('buf', i), ('y1', i), ('small', i), ('yb', i)
            sq = _view(S.buf[:], 0, [[d, G], [1, d]])
            TT('dve', sq, src, src, ALU.mult, [kr], [kb])
            RED(S.small[:, 8:8 + G], sq, [kb], [ks])
            RSTD(S.small[:, 8:8 + G], 1.0 / d, ks)
            yv = _view(S.y1[:], 0, [[d, G], [1, d]])
            TT('dve', yv, src, _bc_last(S.small[:, 8:8 + G], d), ALU.mult, [kr, ks], [ky])
            if rope is None:
                TT('pool', out_b, yv, _bc_mid(gain_ap, G), ALU.mult, [ky, 'gains', 'naq_s'], [kyb])
                return
            off, bs = rope
            TT('pool', yv, yv, _bc_mid(gain_ap, G), ALU.mult, [ky, 'gains'], [ky])
            if off > 0:
                CP('act', out_b[:, :, 0:off], _view(S.y1[:], 0, [[d, G], [1, off]]), [ky], [kyb])
            w = 4 * bs
            r = _view(S.y1[:], off, [[d, G], [1, w]])
            cosv = _bc_mid(S.cs[:, 0:w], G)
            ta = _view(S.buf[:], 0, [[w, G], [1, w]])
            TT('dve', ta, r, cosv, ALU.mult, [ky, ('cs', i)], [kb])
            for s_ in range(2):
                o_ = _view(S.buf[:], 512 + s_ * bs, [[w, G], [2 * bs, 2], [1, bs]])
                i0 = _view(S.y1[:], off + (1 - s_) * bs, [[d, G], [2 * bs, 2], [1, bs]])
                i1 = _view(S.cs[:], w + s_ * bs, [[0, G], [2 * bs, 2], [1, bs]])
                TT('pool', o_, i0, i1, ALU.mult, [ky, ('cs', i)], [kb])
            tb = _view(S.buf[:], 512, [[w, G], [1, w]])
            TT('dve', out_b[:, :, off:off + w], ta, tb, ALU.add, [kb], [kyb])

        def to_fm(S, ins, width, dst_of, dkey):
            i = S.i
            pt = ptbs[i][:, 0:512]
            for g0 in range(0, len(ins), 4):
                grp = ins[g0:g0 + 4]
                n = len(grp)
                for g, a in enumerate(grp):
                    TR(pt[0:width, g * 128:(g + 1) * 128], a, identb[:], [('yb', i), 'identb'], [PB[6 + i]], inc=(g == n - 1))
                CP('act', dst_of(g0, n), pt[0:width, 0:n * 128].rearrange("p (g t) -> p g t", g=n), [PB[6 + i]], [dkey])

        def residual_store(banks, xacc_i, Gk, j, dst, t):
            i = 0
            for hf in range(2):
                TT('dve', xo[i][:, hf * 512:(hf + 1) * 512], pb[banks[hf]][:, :], Gt[Gk][j][:, hf * 512:(hf + 1) * 512], ALU.mult,
                   [PB[banks[hf]], ('G', Gk, j)], [('xo', i)])
            TT('pool', xo[i][:], xo[i][:], xa[xacc_i][:], ALU.add, [('xo', i), ('xa', xacc_i)], [('xo', i)])
            if dst == 'y':
                if t < NL:
                    p.dma('sp', xs['y'][t * 128:(t + 1) * 128, :], xo[i][:], reads=[('xo', i)], writes=[(dst, t)], sem=f'xo{i}')
            else:
                p.dma('sp', xs[dst][t * 128:(t + 1) * 128, :], xo[i][:], reads=[('xo', i)], writes=[(dst, t)], sem=f'xo{i}')

        def load_xa(src, t):
            i = 0
            p.dma('sp', xa[i][:], xs[src][t * 128:(t + 1) * 128, :], reads=[(src, t)], writes=[('xa', i)], sem=f'xa{i}')
            return i

        def load_w(dst3, src3, key, sem, nch):
            for c in range(nch):
                p.dma('pool', dst3[:, c, :], src3[:, c, :], writes=[key], sem=sem)

        def merged(fn, items, nsets=NS):
            out = []
            for g0 in range(0, len(items), nsets):
                lists = []
                for k_, it in enumerate(items[g0:g0 + nsets]):
                    p.capture()
                    fn(SS[k_], it)
                    lists.append(p.end_capture())
                idx = [0] * len(lists)
                live = True
                while live:
                    live = False
                    for k_, L in enumerate(lists):
                        if idx[k_] < len(L):
                            out.append(L[idx[k_]])
                            idx[k_] += 1
                            live = True
            return out

        def overlapped_chunks(chunks_, q_tile_, att_chunk_, nsets_b=NS):
            p.replay([merged(lambda S, tt: q_tile_(S, tt, 0), list(enumerate(chunks_[0])))])
            for ci, tiles in enumerate(chunks_):
                p.capture()
                att_chunk_(ci, tiles)
                A = p.end_capture()
                B = merged(lambda S, tt: q_tile_(S, tt, (ci + 1) % 2), list(enumerate(chunks_[ci + 1])), nsets_b) if ci + 1 < len(chunks_) else []
                M = []
                nb_, ib = len(B), 0
                for ka, ia in enumerate(A):
                    M.append(ia)
                    tgt = ((ka + 1) * nb_) // max(len(A), 1)
                    while ib < tgt:
                        M.append(B[ib])
                        ib += 1
                M.extend(B[ib:])
                p.replay([M])

        def interleaved(fn, items):
            for g0 in range(0, len(items), NS):
                lists = []
                for k_, it in enumerate(items[g0:g0 + NS]):
                    p.capture()
                    fn(SS[k_], it)
                    lists.append(p.end_capture())
                p.replay(lists)

        def attend(N, units, scale, qT_of, dv, nb, PT, fused=None, qkey=('fm',), la=None, sbk=None, hook=None):
            pN, pD = nb if nb is not None else (None, None)
            nu = len(units)

            LA = int(os.environ.get('ATT_LA', '2')) if la is None else la
            DUP = int(os.environ.get('ATT_DUP', '0'))
            SBK = (sbk if sbk is not None else ([0, 3, 5] if fused is not None else [0, 3, 6]))[:LA + 1]

            def emit_S(i):
                u = units[i]
                sbk = SBK[i % len(SBK)]
                pS = pb[sbk]
                nk, c0, c1 = u['nk'], u['c0'], u['c1']
                n = c1 - c0
                for rep in range(1 + DUP):
                    MM(pS[0:nk, 0:n], u['kT'], qT_of(c0, c1), True, u.get('bias') is None, [('fm',), qkey], [PB[sbk]], inc=(u.get('bias') is None))
                    if u.get('bias') is not None:
                        MM(pS[0:nk, 0:n], u['bias'][0], u['bias'][1], False, True, ['identb', 'lib'], [PB[sbk]])

            for i0 in range(min(LA, nu)):
                emit_S(i0)
            for i, u in enumerate(units):
                sbk = SBK[i % len(SBK)]
                pS = pb[sbk]
                nk, c0, c1 = u['nk'], u['c0'], u['c1']
                n = c1 - c0
                PTi = i % 4
                ACT(PT[PTi][0:nk, 0:n], pS[0:nk, 0:n], AF.Exp, [PB[sbk]], [('PT', PTi)], scale=scale)
                if i + LA < nu:
                    emit_S(i + LA)
                for (mc, mk) in u.get('masks', []):
                    TT('pool', PT[PTi][:, mc:mc + 128], PT[PTi][:, mc:mc + 128], mk, ALU.mult, [('PT', PTi), 'trib'], [('PT', PTi)])
                if fused is not None:
                    MM(pb[fused][:, c0:c1], u['v'], PT[PTi][0:nk, 0:n], i == 0, i == nu - 1, [('PT', PTi), ('v',)], [PB[fused]], inc=True)
                    continue
                MM(pb[pN][0:dv, c0:c1], u['v'], PT[PTi][0:nk, 0:n], i == 0, i == nu - 1, [('PT', PTi), ('v',)], [PB[pN]], inc=False)
                MM(pb[pD][0:dv, c0:c1], onesb[0:nk, 0:dv], PT[PTi][0:nk, 0:n], i == 0, i == nu - 1, [('PT', PTi), 'onesb'], [PB[pD]],
                   inc=True)
                if hook is not None and i == 1:
                    hook(0)
                if hook is not None and i == 5:
                    hook(1)

        def norm_fused(h, N, bank, mix, add_sink=False):
            par = h % 2
            dlo, nlo = (64, 0) if par == 0 else (0, 64)
            r_ = rec[par]
            rk = RECK[par]
            if add_sink:
                TS('dve', r_[dlo:dlo + 64, 0:N], pb[bank][dlo:dlo + 64, 0:N], es_sink[dlo:dlo + 64, h:h + 1], None, ALU.add, None,
                   [PB[bank], 'es_sink'], [rk])
                p.op('dve', lambda e: e.reciprocal(r_[dlo:dlo + 64, 0:N], r_[dlo:dlo + 64, 0:N]), [rk], [rk])
            else:
                p.op('dve', lambda e: e.reciprocal(r_[dlo:dlo + 64, 0:N], pb[bank][dlo:dlo + 64, 0:N]), [PB[bank]], [rk])
            TT('dve', mix[nlo:nlo + 64, h // 2, 0:N], pb[bank][nlo:nlo + 64, 0:N], r_[dlo:dlo + 64, 0:N], ALU.mult, [PB[bank], rk], [('mix',)])

        def norm_heads(h, N, pN, pD, mix, add_sink=None):
            r_ = rec[h % 2]
            rk = RECK[h % 2]
            if add_sink is None:
                p.op('dve', lambda e: e.reciprocal(r_[0:64, 0:N], pb[pD][0:64, 0:N]), [PB[pD]], [rk])
            else:
                TS('dve', r_[0:64, 0:N], pb[pD][0:64, 0:N], add_sink, None, ALU.add, None, [PB[pD], 'es_sink'], [rk])
                p.op('dve', lambda e: e.reciprocal(r_[0:64, 0:N], r_[0:64, 0:N]), [rk], [rk])
            TT('dve', mix[0:64, h, 0:N], pb[pN][0:64, 0:N], r_[0:64, 0:N], ALU.mult, [PB[pN], rk], [('mix',)])

        def out_proj(l, mixv, K, nslot, wo, tiles, xacc_src, dst, Gk):
            for ti, t in enumerate(tiles):
                j = 0 if t < NL else 1
                xi = load_xa(xacc_src, t)
                for hf in range(2):
                    for s_ in range(nslot):
                        bk = (1 + hf) if ti % 2 == 0 else (4 + hf)
                        MM(pb[bk][:, :], mixv[0:K, s_, ti * 128:(ti + 1) * 128], wo[0:K, s_, hf * 512:(hf + 1) * 512],
                           s_ == 0, s_ == nslot - 1, [('mix',), 'wo'], [PB[bk]], inc=(s_ == nslot - 1))
                residual_store((1, 2) if ti % 2 == 0 else (4, 5), xi, Gk, j, dst, t)

        def mixer_pass(l, mx, src, acc, dst):
            AR.reset()
            p.barrier()
            ctxq = (l == 0)
            win_d = W[l]['w_in'].rearrange("(c p) n -> p c n", p=128)
            PT = [AR.alloc(512) for _ in range(4)]
            for S in SS:
                S.hT = AR.alloc(8, 128)
                S.yb = AR.alloc(1024)
                S.cT = AR.alloc(2, 128)
            HK = lambda S: ('hT', S.i)
            HCOMP = [mx in ('A', 'C')]
            chunks = [list(range(c * 4, c * 4 + 4)) for c in range(4)] + ([[16, 17]] if ctxq else [])
            NB = [(1, 2), (4, 5)]
            if mx == 'A':
                wi = AR.alloc(8, 416)
                load_w(wi, win_d[:, :, 0:416], 'wi', 'wi', 8)
                wuq = AR.alloc(2, 768)
                load_w(wuq, W[0]['w_uq'].rearrange("(c p) n -> p c n", p=128), 'wuq', 'wuq', 2)
                wukv = AR.alloc(1, 1024)
                load_w(wukv, W[0]['w_ukv'].rearrange("(c p) n -> p c n", p=128), 'wukv', 'wukv', 1)
                wo = AR.alloc(4, 1024)
                load_w(wo, W[l]['w_out'][0:512, :].rearrange("(s p) n -> p s n", p=128), 'wo', 'wo', 4)
                kT = AR.alloc(8, T)
                vv = AR.alloc(NT, 8, 128)
                p.op('pool', lambda e: e.memset(vv, 1.0), [], [('v',)])
                qTs = [AR.alloc(8, 512) for _ in range(2)]
                mix = AR.alloc(4, 512)

                def kv_tile(S, t):
                    lat = t < NL
                    i = S.i
                    if lat:
                        load_cs(S, t, 32)
                    get_h(S, src, t, 0, S.hT, HK(S), HCOMP[0])
                    proj(S.hT, HK(S), wi, 'wi', 256, 160, S.pj)
                    CP('act', S.raw[:, 0:160], pb[S.pj][:, 0:160], [PB[S.pj]], [('raw', i)])
                    prep(S, _view(S.raw[:], 0, [[128, 1], [1, 128]]), 1, 128, gain('kva'), _view(S.yb, 0, [[128, 1], [1, 128]]))
                    to_fm(S, [S.yb[:, 0:128]], 128, lambda g0, n: S.cT[:, 0:1, :], ('cT', i))
                    for hf in range(2):
                        MM(pb[S.xb[hf]][:, :], S.cT[:, 0, :], wukv[:, 0, hf * 512:(hf + 1) * 512], True, True, [('cT', i), 'wukv'], [PB[S.xb[hf]]])
                    for hf in range(2):
                        kvv = pb[S.xb[hf]][:, :].rearrange("p (h e) -> p h e", h=4)
                        for par in range(2):
                            CP('act', _view(vv, (t * 8 + hf * 4 + par) * 128 + par * 64, [[256, 2], [1, 64]]),
                               _view(pb[S.xb[hf]][:, :], par * 128 + 64, [[256, 2], [1, 64]]), [PB[S.xb[hf]]], [('v',)])
                        CP('act', _view(S.raw[:], 256 + hf * 4 * 96, [[96, 4], [1, 64]]), kvv[:, :, 0:64], [PB[S.xb[hf]]], [('raw', i)])
                    CP('pool', _view(S.raw[:], 256 + 64, [[96, 8], [1, 32]]), _bc_mid(S.raw[:, 128:160], 8), [('raw', i)], [('raw', i)])
                    ybv = _view(S.yb, 0, [[96, 8], [1, 96]])
                    prep(S, _view(S.raw[:], 256, [[96, 8], [1, 96]]), 8, 96, gain('kn96'), ybv, rope=(64, 8) if lat else None)
                    to_fm(S, [ybv[:, h, :] for h in range(8)], 96, lambda g0, n: kT[0:96, g0:g0 + n, t * 128:(t + 1) * 128], ('fm',))

                def q_tile(S, tt, qb):
                    ti, t = tt
                    lat = t < NL
                    i = S.i
                    bk = 6 + i
                    if lat:
                        load_cs(S, t, 32)
                    get_h(S, src, t, 0, S.hT, HK(S), HCOMP[0])
                    proj(S.hT, HK(S), wi, 'wi', 0, 256, bk)
                    CP('act', S.raw[:, 0:256], pb[bk][:, 0:256], [PB[bk]], [('raw', i)])
                    prep(S, _view(S.raw[:], 0, [[256, 1], [1, 256]]), 1, 256, gain('qa'), _view(S.yb, 0, [[256, 1], [1, 256]]))
                    to_fm(S, [S.yb[:, 0:128], S.yb[:, 128:256]], 128, lambda g0, n: S.cT[:, :, :], ('cT', i))
                    for (c0_, n_) in ((0, 512), (512, 256)):
                        for c in range(2):
                            MM(pb[bk][:, 0:n_], S.cT[:, c, :], wuq[:, c, c0_:c0_ + n_], c == 0, c == 1, [('cT', i), 'wuq'], [PB[bk]], inc=(c == 1))
                        CP('act', S.raw[:, c0_:c0_ + n_], pb[bk][:, 0:n_], [PB[bk]], [('raw', i)])
                    ybv = _view(S.yb, 0, [[96, 8], [1, 96]])
                    prep(S, _view(S.raw[:], 0, [[96, 8], [1, 96]]), 8, 96, gain('qn96'), ybv, rope=(64, 8) if lat else None)
                    to_fm(S, [ybv[:, h, :] for h in range(8)], 96, lambda g0, n: qTs[qb][0:96, g0:g0 + n, ti * 128:(ti + 1) * 128], ('qT', qb))

                KS = os.environ.get('KSTOP', '')
                interleaved(kv_tile, list(range(2 if KS == 'kv1' else NT)))
                HCOMP[0] = False
                if KS in ('kv1', 'kv'):
                    chunks = []

                def att_chunk(ci, tiles):
                    N = 128 * len(tiles)
                    lat = tiles[0] < NL
                    qT_ = qTs[ci % 2]
                    ktiles = [16, 17] + (list(range(16)) if lat else [])
                    for h in range(8):
                        units = [dict(kT=kT[0:96, h, kt * 128:(kt + 1) * 128], nk=128, c0=0, c1=N, v=vv[:, kt, h, :]) for kt in ktiles]
                        bank = (1, 2, 4)[h % 3]
                        attend(N, units, 96 ** -0.5, lambda c0, c1, h=h: qT_[0:96, h, c0:c1], 64, None, PT, fused=bank, qkey=('qT', ci % 2))
                        norm_fused(h, N, bank, mix)
                    out_proj(l, mix, 128, 4, wo, tiles, acc, dst, 0)
                if chunks:
                    overlapped_chunks(chunks, q_tile, att_chunk)
            elif mx == 'B':
                wi = AR.alloc(8, 1536)
                load_w(wi, win_d[:, :, 416:1952], 'wi', 'wi', 8)
                wo = AR.alloc(8, 1024)
                load_w(wo[0:64], W[l]['w_out'][512:1024, :].rearrange("(h p) n -> p h n", p=64), 'wo', 'wo', 8)
                kT = AR.alloc(4, T)
                vv = AR.alloc(32, 8, 64)
                vvc = AR.alloc(2, 8, 64)
                qT = AR.alloc(4, 512)
                mix = AR.alloc(8, 512)
                libt = AR.alloc(4, 960)
                p.dma('sp', SS[1].raw[:, 0:960], nam_d, writes=[('raw', 1)], sem='c7')
                for hc in range(4):
                    p.dma('sp', SS[0].raw[:, 0:960], lib_d[:, hc * 960:(hc + 1) * 960], writes=[('raw', 0)], sem='c8')
                    TT('pool', libt[:, hc, :], SS[0].raw[:, 0:960], SS[1].raw[:, 0:960], ALU.add, [('raw', 0), ('raw', 1)], ['lib'])

                def kv_tile(S, t):
                    lat = t < NL
                    i = S.i
                    get_h(S, src, t, 0, S.hT, HK(S), HCOMP[0])
                    proj(S.hT, HK(S), wi, 'wi', 512, 512, S.pj)
                    CP('act', S.raw[:, 0:512], pb[S.pj][:, :], [PB[S.pj]], [('raw', i)])
                    prep(S, _view(S.raw[:], 0, [[64, 8], [1, 64]]), 8, 64, gain('nak'), _view(S.yb, 0, [[64, 8], [1, 64]]))
                    to_fm(S, [S.yb[:, c * 128:(c + 1) * 128] for c in range(4)], 128, lambda g0, n: kT[:, g0:g0 + n, t * 128:(t + 1) * 128], ('fm',))
                    if lat:
                        for hf in range(2):
                            proj(S.hT, HK(S), wi, 'wi', 1024, 512, S.xb[hf], M0=hf * 64, M1=hf * 64 + 64)
                            CP('act', vv[0:64, 2 * t + hf, :, :], pb[S.xb[hf]][0:64, :].rearrange("p (h e) -> p h e", h=8), [PB[S.xb[hf]]], [('v',)])
                    else:
                        proj(S.hT, HK(S), wi, 'wi', 1024, 512, S.xb[0])
                        CP('act', vvc[:, t - NL, :, :], pb[S.xb[0]][:, :].rearrange("p (h e) -> p h e", h=8), [PB[S.xb[0]]], [('v',)])

                def q_tile(S, tt):
                    ti, t = tt
                    i = S.i
                    get_h(S, src, t, 0, S.hT, HK(S), HCOMP[0])
                    proj(S.hT, HK(S), wi, 'wi', 0, 512, S.pj)
                    CP('act', S.raw[:, 0:512], pb[S.pj][:, :], [PB[S.pj]], [('raw', i)])
                    prep(S, _view(S.raw[:], 0, [[64, 8], [1, 64]]), 8, 64, naq_s[:], _view(S.yb, 0, [[64, 8], [1, 64]]))
                    to_fm(S, [S.yb[:, c * 128:(c + 1) * 128] for c in range(4)], 128, lambda g0, n: qT[:, g0:g0 + n, ti * 128:(ti + 1) * 128], ('fm',))

                interleaved(kv_tile, list(range(NT)))
                HCOMP[0] = False
                for tiles in chunks:
                    N = 128 * len(tiles)
                    lat = tiles[0] < NL
                    interleaved(q_tile, list(enumerate(tiles)))
                    for h in range(8):
                        ho, hc = (h % 2) * 64, h // 2
                        units = [dict(kT=kT[ho:ho + 64, hc, (NL + i_) * 128:(NL + i_ + 1) * 128], nk=128, c0=0, c1=N, v=vvc[:, i_, h, :])
                                 for i_ in range(2)]
                        if lat:
                            r0 = (tiles[0] * 128) // 64
                            for kr in range(32):
                                rs = [r for r in range(r0, r0 + 8) if min(max(r - 4, 0), 24) <= kr <= min(max(r - 4, 0), 24) + 7]
                                if not rs:
                                    continue
                                ra, rb = rs[0], rs[-1]
                                c0, c1 = (ra - r0) * 64, (rb - r0 + 1) * 64
                                L0 = (7 - (kr - ra)) * 64
                                units.append(dict(kT=kT[ho:ho + 64, hc, kr * 64:(kr + 1) * 64], nk=64, c0=c0, c1=c1, v=vv[0:64, kr, h, :],
                                                  bias=(identb[ho:ho + 64, ho:ho + 64], libt[ho:ho + 64, hc, L0:L0 + (c1 - c0)])))
                        pN, pD = NB[h % 2]
                        attend(N, units, 1.0, lambda c0, c1, ho=ho, hc=hc: qT[ho:ho + 64, hc, c0:c1], 64, (pN, pD), PT, la=3, sbk=[0, 3, 6, 7])
                        norm_heads(h, N, pN, pD, mix)
                    out_proj(l, mix, 64, 8, wo, tiles, acc, dst, 0)
            elif mx == 'C':
                wi = AR.alloc(8, 768)
                load_w(wi, win_d[:, :, 0:768], 'wi', 'wi', 8)
                wo = AR.alloc(4, 1024)
                load_w(wo, W[l]['w_out'][0:512, :].rearrange("(s p) n -> p s n", p=128), 'wo', 'wo', 4)
                kT = AR.alloc(1, T)
                vvE = AR.alloc(NT, 2, 128)
                vvO = AR.alloc(NT, 2, 128)
                p.op('pool', lambda e: e.memset(vvE, 1.0), [], [('v',)])
                p.op('pool', lambda e: e.memset(vvO, 1.0), [], [('v',)])
                qTs = [AR.alloc(4, 512) for _ in range(2)]
                mix = AR.alloc(4, 512)

                def kv_tile(S, t):
                    lat = t < NL
                    i = S.i
                    if lat:
                        load_cs(S, t, 64)
                    get_h(S, src, t, 0, S.hT, HK(S), HCOMP[0])
                    proj(S.hT, HK(S), wi, 'wi', 512, 256, S.pj)
                    CP('act', S.raw[:, 0:256], pb[S.pj][:, 0:256], [PB[S.pj]], [('raw', i)])
                    prep(S, _view(S.raw[:], 0, [[64, 2], [1, 64]]), 2, 64, gain('gk'), _view(S.yb, 0, [[64, 2], [1, 64]]), rope=(0, 16) if lat else None)
                    to_fm(S, [S.yb[:, 0:128]], 128, lambda g0, n: kT[:, 0:1, t * 128:(t + 1) * 128], ('fm',))
                    CP('pool', vvE[:, t, :, 0:64], _view(S.raw[:], 128, [[64, 2], [1, 64]]), [('raw', i)], [('v',)])
                    CP('pool', vvO[:, t, :, 64:128], _view(S.raw[:], 128, [[64, 2], [1, 64]]), [('raw', i)], [('v',)])

                def q_tile(S, tt, qb):
                    ti, t = tt
                    i = S.i
                    bk = 6 + i
                    load_cs(S, t, 64)
                    get_h(S, src, t, 0, S.hT, HK(S), HCOMP[0])
                    proj(S.hT, HK(S), wi, 'wi', 0, 512, bk)
                    CP('act', S.raw[:, 0:512], pb[bk][:, :], [PB[bk]], [('raw', i)])
                    prep(S, _view(S.raw[:], 0, [[64, 8], [1, 64]]), 8, 64, gain('gq'), _view(S.yb, 0, [[64, 8], [1, 64]]), rope=(0, 16))
                    pt = ptbs[i][:, 0:512]
                    for h in range(8):
                        kvh, g = h // 4, h % 4
                        TR(pt[kvh * 64:(kvh + 1) * 64, g * 128:(g + 1) * 128], S.yb[:, h * 64:(h + 1) * 64], identb[:], [('yb', i), 'identb'],
                           [PB[6 + i]], inc=(h == 7))
                    CP('act', qTs[qb][:, :, ti * 128:(ti + 1) * 128], pt[:, 0:512].rearrange("p (g t) -> p g t", g=4), [PB[6 + i]], [('qT', qb)])

                interleaved(kv_tile, list(range(NT)))
                HCOMP[0] = False

                def att_chunk(cch, tiles):
                    N = 512
                    qT_ = qTs[cch % 2]
                    for h in range(8):
                        kvh, g = h // 4, h % 4
                        ho = kvh * 64
                        vv = vvE if h % 2 == 0 else vvO
                        units = [dict(kT=kT[ho:ho + 64, 0, (NL + i_) * 128:(NL + i_ + 1) * 128], nk=128, c0=0, c1=N, v=vv[:, NL + i_, kvh, :])
                                 for i_ in range(2)]
                        for m in range(max(4 * cch - 1, 0), min(4 * cch + 4, 15) + 1):
                            nlo, nhi = max(m - 1, 4 * cch), min(m + 1, 4 * cch + 3)
                            masks = []
                            for n_ in range(nlo, nhi + 1):
                                if n_ == m + 1:
                                    masks.append(((n_ - nlo) * 128, trib[:, 0:128]))
                                elif n_ == m - 1:
                                    masks.append(((n_ - nlo) * 128, trib[:, 128:256]))
                            units.append(dict(kT=kT[ho:ho + 64, 0, m * 128:(m + 1) * 128], nk=128, c0=(nlo - 4 * cch) * 128,
                                              c1=(nhi - 4 * cch + 1) * 128, v=vv[:, m, kvh, :], masks=masks))
                        bank = (1, 2, 4)[h % 3]
                        attend(N, units, 0.125, lambda c0, c1, ho=ho, g=g: qT_[ho:ho + 64, g, c0:c1], 64, None, PT, fused=bank, qkey=('qT', cch % 2))
                        norm_fused(h, N, bank, mix, add_sink=True)
                    out_proj(l, mix, 128, 4, wo, tiles, acc, dst, 0)
                overlapped_chunks([list(range(c_ * 4, c_ * 4 + 4)) for c_ in range(4)], q_tile, att_chunk)
            else:
                wi = AR.alloc(8, 1536)
                load_w(wi, win_d[:, :, 768:2304], 'wi', 'wi', 8)
                wo = AR.alloc(4, 1024)
                load_w(wo, W[l]['w_out'][512:1024, :].rearrange("(h p) n -> p h n", p=128), 'wo', 'wo', 4)
                kT = AR.alloc(4, T)
                vv = AR.alloc(NT, 4, 128)
                qTs = [AR.alloc(4, 512) for _ in range(2)]
                mix = AR.alloc(4, 512)

                def kv_tile(S, t):
                    lat = t < NL
                    i = S.i
                    if lat:
                        load_cs(S, t, 64)
                    get_h(S, src, t, 0, S.hT, HK(S), HCOMP[0])
                    proj(S.hT, HK(S), wi, 'wi', 512, 512, S.pj)
                    CP('act', S.raw[:, 0:512], pb[S.pj][:, :], [PB[S.pj]], [('raw', i)])
                    prep(S, _view(S.raw[:], 0, [[64, 8], [1, 64]]), 8, 64, gain('dk'), _view(S.yb, 0, [[64, 8], [1, 64]]), rope=(0, 16) if lat else None)
                    to_fm(S, [S.yb[:, c * 128:(c + 1) * 128] for c in range(4)], 128, lambda g0, n: kT[:, g0:g0 + n, t * 128:(t + 1) * 128], ('fm',))
                    proj(S.hT, HK(S), wi, 'wi', 1024, 512, S.xb[0])
                    CP('act', vv[:, t, :, :], pb[S.xb[0]][:, :].rearrange("p (h e) -> p h e", h=4), [PB[S.xb[0]]], [('v',)])

                def q_tile(S, tt, qb):
                    ti, t = tt
                    i = S.i
                    bk = 6 + i
                    load_cs(S, t, 64)
                    get_h(S, src, t, 0, S.hT, HK(S), HCOMP[0])
                    proj(S.hT, HK(S), wi, 'wi', 0, 512, bk)
                    CP('act', S.raw[:, 0:512], pb[bk][:, :], [PB[bk]], [('raw', i)])
                    prep(S, _view(S.raw[:], 0, [[64, 8], [1, 64]]), 8, 64, gain('dq'), _view(S.yb, 0, [[64, 8], [1, 64]]), rope=(0, 16))
                    to_fm(S, [S.yb[:, c * 128:(c + 1) * 128] for c in range(4)], 128, lambda g0, n: qTs[qb][:, g0:g0 + n, ti * 128:(ti + 1) * 128], ('qT', qb))

                interleaved(kv_tile, list(range(NT)))
                HCOMP[0] = False

                def att_chunk(cch, tiles):
                    N = 512
                    qT_ = qTs[cch % 2]
                    sqb = SS[1].yb[:, 0:512]

                    def tail(hh, stage=None):
                        if stage in (None, 0):
                            STT(of[0][:, :], of[1][:, :], lamt[:, 3:4], of[0][:, :], ALU.mult, ALU.add, [OFK[0], 'lamt'], [OFK[0]])
                            TT('pool', sqb, of[0][:, :], of[0][:, :], ALU.mult, [OFK[0]], [('yb', 1)])
                        if stage == 0:
                            return
                        MM(pb[5][:, :], onesb[:, :], sqb, True, True, [('yb', 1), 'onesb'], [PB[5]])
                        ACT(rec[0][:, :], pb[5][:, :], AF.Ln, [PB[5]], [RECK[0]], bias=EPS, scale=1.0 / 128)
                        ACT(rec[0][:, :], rec[0][:, :], AF.Exp, [RECK[0]], [RECK[0]], scale=-0.5)
                        STT(mix[:, hh, :], of[0][:, :], sublnS[:, 0:1], rec[0][:, :], ALU.mult, ALU.mult, [OFK[0], RECK[0], 'sublnS'], [('mix',)])

                    pend = None
                    for hh in range(4):
                        for cc in range(2):
                            ho = cc * 64
                            units = [dict(kT=kT[ho:ho + 64, hh, kt * 128:(kt + 1) * 128], nk=128, c0=0, c1=N, v=vv[:, kt, hh, :])
                                     for kt in ([16, 17] + list(range(16)))]
                            attend(N, units, 0.125, lambda c0, c1, ho=ho, hh=hh: qT_[ho:ho + 64, hh, c0:c1], 128, NB[cc], PT,
                                   qkey=('qT', cch % 2), la=2, sbk=[0, 3, 7], hook=(pend if cc == 0 else None))
                        for cc in range(2):
                            pN, pD = NB[cc]
                            p.op('dve', lambda e, cc=cc, pD=pD: e.reciprocal(rec[cc][:, :], pb[pD][:, :]), [PB[pD]], [RECK[cc]])
                            TT('dve', of[cc][:, :], pb[pN][:, :], rec[cc][:, :], ALU.mult, [PB[pN], RECK[cc]], [OFK[cc]])
                        pend = (lambda stage=None, hh=hh: tail(hh, stage))
                    pend()
                    out_proj(l, mix, 128, 4, wo, tiles, acc, dst, 0)
                overlapped_chunks([list(range(c_ * 4, c_ * 4 + 4)) for c_ in range(4)], q_tile, att_chunk, nsets_b=1)

        def ffn_half(l, hf, hsrc, acc, dst):
            AR.reset()
            p.barrier()
            wg = AR.alloc(8, 1408)
            wu = AR.alloc(8, 1408)
            wd = AR.alloc(11, 1024)
            load_w(wg, W[l]['wg'].rearrange("(c p) n -> p c n", p=128)[:, :, hf * 1408:(hf + 1) * 1408], 'wg', 'wg', 8)
            load_w(wu, W[l]['wu'].rearrange("(c p) n -> p c n", p=128)[:, :, hf * 1408:(hf + 1) * 1408], 'wu', 'wu', 8)
            load_w(wd, W[l]['wd'][hf * 1408:(hf + 1) * 1408, :].rearrange("(f p) n -> p f n", p=128), 'wd', 'wd', 11)
            h2s = [AR.alloc(8, 512) for _ in range(2)]
            act = AR.alloc(11, 512)
            sg = [AR.alloc(512) for _ in range(2)]
            chunks = [list(range(c * 4, c * 4 + 4)) for c in range(4)] + ([[16, 17]] if l == 0 else [])

            if hf == 0:
                for S in SS:
                    S.hT = AR.alloc(8, 128)

                def norm_tile(S, t):
                    get_h(S, hsrc, t, 1, S.hT, ('hT', S.i), True)
                interleaved(norm_tile, [t for tiles in chunks for t in tiles])

            def emit_h_tile(ci_, ti):
                t = chunks[ci_][ti]
                p.dma('sp', h2s[ci_ % 2][:, :, ti * 128:(ti + 1) * 128], hts[1][t].rearrange("p (c n) -> p c n", c=8),
                      reads=[('hts', 1, t)], writes=[('h2', ci_ % 2, ti)], sem=f'h2l{ci_ % 2}_{ti}')

            for ti in range(len(chunks[0])):
                emit_h_tile(0, ti)
            for ci_, tiles in enumerate(chunks):
                N = 128 * len(tiles)
                h2 = h2s[ci_ % 2]
                hks = [('h2', ci_ % 2, ti_) for ti_ in range(len(tiles))]
                nxt = len(chunks[ci_ + 1]) if ci_ + 1 < len(chunks) else 0
                for f in range(11):
                    bg, bu = (f % 2) * 2, (f % 2) * 2 + 1
                    for c in range(8):
                        MM(pb[bg][:, 0:N], wg[:, c, f * 128:(f + 1) * 128], h2[:, c, 0:N], c == 0, c == 7, hks + ['wg'], [PB[bg]], inc=(c == 7))
                    for c in range(8):
                        MM(pb[bu][:, 0:N], wu[:, c, f * 128:(f + 1) * 128], h2[:, c, 0:N], c == 0, c == 7, hks + ['wu'], [PB[bu]], inc=(c == 7))
                    ACT(sg[f % 2][:, 0:N], pb[bg][:, 0:N], AF.Silu, [PB[bg]], [('sg', f % 2)])
                    TT('dve', act[:, f, 0:N], pb[bu][:, 0:N], sg[f % 2][:, 0:N], ALU.mult, [PB[bu], ('sg', f % 2)], [('act',)])
                    if f % 2 == 1 and (f // 2) < nxt:
                        emit_h_tile(ci_ + 1, f // 2)
                for ti, t in enumerate(tiles):
                    j = 0 if t < NL else 1
                    xi2 = load_xa(acc, t)
                    for h2_ in range(2):
                        for f in range(11):
                            bk = (4 + h2_) if ti % 2 == 0 else (2 + h2_)
                            MM(pb[bk][:, :], act[:, f, ti * 128:(ti + 1) * 128], wd[:, f, h2_ * 512:(h2_ + 1) * 512], f == 0, f == 10,
                               [('act',), 'wd'], [PB[bk]], inc=(f == 10))
                    residual_store((4, 5) if ti % 2 == 0 else (2, 3), xi2, 1, j, dst, t)

        seq = []
        if 0 in layers:
            seq += [('ada', 0), ('mix', 0, 'A', 'xin', 'xin', 'xs_a'), ('mix', 0, 'B', 'xin', 'xs_a', 'xs_b'),
                    ('ffn', 0, 0, 'xs_b', 'xs_b', 'xs_c'), ('ffn', 0, 1, 'xs_b', 'xs_c', 'xs_d')]
        if 1 in layers:
            s1 = 'xs_d' if 0 in layers else 'xin'
            seq += [('ada', 1), ('mix', 1, 'C', s1, s1, 'xs_a'), ('mix', 1, 'D', s1, 'xs_a', 'xs_b'),
                    ('ffn', 1, 0, 'xs_b', 'xs_b', 'xs_c'), ('ffn', 1, 1, 'xs_b', 'xs_c', 'y')]
        if stop_after is not None:
            seq = seq[:stop_after]
        last_dst = None
        for st in seq:
            if st[0] == 'ada':
                ada_phase(st[1])
            elif st[0] == 'mix':
                mixer_pass(st[1], st[2], st[3], st[4], st[5])
                last_dst = st[5]
            else:
                ffn_half(st[1], st[2], st[3], st[4], st[5])
                last_dst = st[5]
        if last_dst is None:
            p.dma('sp', xs['y'][0:128, :], Gt[0][0][:], reads=[('G', 0, 0)], writes=[('y', 0)], sem='dbg0')
            p.dma('sp', xs['y'][128:256, :], Gt[0][1][:], reads=[('G', 0, 1)], writes=[('y', 1)], sem='dbg0')
            p.dma('sp', xs['y'][256:384, :], Gt[1][0][:], reads=[('G', 1, 0)], writes=[('y', 2)], sem='dbg0')
            p.dma('sp', xs['y'][384:512, 0:64], AB[:], reads=['AB'], writes=[('y', 3)], sem='dbg0')
            p.finish([('y', t) for t in range(4)])
            p.emit()
            return nc
        if last_dst != 'y':
            for t in range(NL):
                S = SS[t % 2]
                p.dma('sp', S.xt[:], xs[last_dst][t * 128:(t + 1) * 128, :], reads=[(last_dst, t)], writes=[('xt', S.i)], sem=f'xt{S.i}')
                p.dma('sp', xs['y'][t * 128:(t + 1) * 128, :], S.xt[:], reads=[('xt', S.i)], writes=[('y', t)], sem=f'dbg{t % 2}')
        p.finish([('y', t) for t in range(NL)])
        p.emit()
    return nc


_CACHE = {}


def _host_inputs(inputs, layers=(0, 1)):
    f = lambda a: np.ascontiguousarray(np.asarray(a, dtype=np.float32))
    x, c, ctx, c_ctx = f(inputs['x']), f(inputs['c']), f(inputs['ctx']), f(inputs['c_ctx'])
    cs64, cs32 = _rope_tables()
    lib, nam = _na_tables(f(inputs['l0_na_rpb']))
    gl = []
    for n in ['l0_mla_qa_g', 'l0_mla_kva_g', 'l0_mla_qn_g', 'l0_mla_kn_g', 'l0_na_qn_g', 'l0_na_kn_g', 'l1_gqa_qn_g', 'l1_gqa_kn_g',
              'l1_diff_qn_g', 'l1_diff_kn_g', 'l1_gqa_sink', 'l1_diff_lq1', 'l1_diff_lk1', 'l1_diff_lq2', 'l1_diff_lk2']:
        gl.append(f(inputs[n]).reshape(-1))
    gains = np.ascontiguousarray(np.broadcast_to(np.concatenate(gl)[None, :], (128, GW)))
    pp = np.zeros((128, 33), np.float32)
    for i, n in enumerate(['l0_norm1_g', 'l0_norm2_g', 'l1_norm1_g', 'l1_norm2_g']):
        pp[:, i * 8:(i + 1) * 8] = f(inputs[n]).reshape(8, 128).T
    pp[:, 32] = f(inputs['l1_diff_subln_g'])
    ident = np.eye(128, dtype=np.float32)
    jj = np.arange(128)[:, None]
    ii = np.arange(128)[None, :]
    tri = np.concatenate([(ii <= jj), (jj <= ii)], axis=1).astype(np.float32)
    sel = np.zeros((2, 256), np.float32)
    sel[0, 0:128] = 1.0
    sel[1, 128:256] = 1.0
    shared = dict(gains=gains, pp=pp, ident=ident, tri=tri, sel=sel, cs64=cs64, cs32=cs32,
                  nalib=np.ascontiguousarray(lib.reshape(128, 4 * 960)), namask=nam)
    for l in (0, 1):
        shared[f'l{l}_ada_w'] = f(inputs[f'l{l}_ada_w'])
        shared[f'l{l}_ada_b2'] = np.ascontiguousarray(np.broadcast_to(f(inputs[f'l{l}_ada_b'])[None, :], (2, 6 * D)))
        shared[f'l{l}_w_in'] = f(inputs[f'l{l}_w_in'])
        shared[f'l{l}_w_out'] = f(inputs[f'l{l}_w_out'])
        shared[f'l{l}_ffn_w_gate'] = f(inputs[f'l{l}_ffn_w_gate'])
        shared[f'l{l}_ffn_w_up'] = f(inputs[f'l{l}_ffn_w_up'])
        shared[f'l{l}_ffn_w_down'] = f(inputs[f'l{l}_ffn_w_down'])
    shared['l0_mla_w_uq'] = f(inputs['l0_mla_w_uq'])
    shared['l0_mla_w_ukv'] = f(inputs['l0_mla_w_ukv'])
    maps = []
    for b in range(x.shape[0]):
        m = dict(shared)
        m['xin'] = np.ascontiguousarray(np.concatenate([x[b], ctx[b]], axis=0))
        cv = np.stack([c[b], c_ctx], axis=0)
        m['cvecT'] = np.ascontiguousarray(cv.reshape(2, 8, 128).transpose(2, 1, 0).reshape(128, 16))
        maps.append(m)
    return maps


def kernel(**inputs):
    maps = _host_inputs(inputs)
    if 'nc' not in _CACHE:
        _CACHE['nc'] = build()
    res = run_bass_kernel_spmd(_CACHE['nc'], maps, core_ids=list(range(len(maps))))
    return np.stack([np.asarray(r['y'], dtype=np.float32) for r in res.results], axis=0)
```
